# Optimizing a Trainium2 kernel written in Bass

```python
import math
import jax, jax.numpy as jnp
from jax import lax
import numpy as np

D_MODEL = 1024
BATCH = 8
SEQ = 2048
DEPTH = 4
DEC_BATCH = 128
DEC_SEQ = 1
PAST_LEN = 8192
PAGE_SIZE = 128

N_EVEN = (DEPTH + 1) // 2
N_ODD = DEPTH // 2
EPS = 1e-6
NEG_INF = -1e30
D_FF = 2816

WINDOW = 128
BLOCK = 128
H_A = 8
KV_A = 2
G_A = H_A // KV_A
HD_A = 64
N_BUCKETS = 32
MAX_DIST = 128

H_B = 4
DK_B = 128
DV_B = 128
CONV_W = 4
GDN_CHUNK = 64

D_INNER = 2 * D_MODEL
P_C = 64
H_C = D_INNER // P_C
N_C = 128
G_C = 4
HPG = H_C // G_C
SSD_CHUNK = 128

A_Q = H_A * HD_A
A_K = KV_A * HD_A
A_V = KV_A * HD_A
B_Q = H_B * DK_B
B_K = H_B * DK_B
B_V = H_B * DV_B
B_Z = H_B * DV_B
GDN_CONV_CH = B_Q + B_K + B_V
EVEN_IN = A_Q + A_K + A_V + GDN_CONV_CH + B_Z + 2 * H_B
EVEN_MIX = A_Q + B_V
SSD_CONV_CH = D_INNER + 2 * G_C * N_C
ODD_IN = D_INNER + SSD_CONV_CH + H_C

kernel_name = "hybrid_swa_gdn_ssd_macaron_step"


def rmsnorm(x, g):
    xf = x.astype(jnp.float32)
    y = xf * lax.rsqrt(jnp.mean(xf * xf, axis=-1, keepdims=True) + EPS)
    return (y * g.astype(jnp.float32)).astype(x.dtype)


def l2norm(x):
    xf = x.astype(jnp.float32)
    return xf * lax.rsqrt(jnp.sum(xf * xf, axis=-1, keepdims=True) + EPS)


def swiglu(x, wg, wu, wd):
    return (jax.nn.silu(x @ wg) * (x @ wu)) @ wd


def pad_seq(t, lp):
    return jnp.pad(t, [(0, 0), (0, lp - t.shape[1])] + [(0, 0)] * (t.ndim - 2))


def causal_conv(x, hist, w):
    L = x.shape[1]
    xx = jnp.concatenate([hist.astype(x.dtype), x], axis=1)
    out = xx[:, 0:L] * w[0]
    for i in range(1, CONV_W):
        out = out + xx[:, i:i + L] * w[i]
    return out, xx[:, xx.shape[1] - (CONV_W - 1):]


def t5_bucket(dist):
    max_exact = N_BUCKETS // 2
    df = jnp.maximum(dist, max_exact).astype(jnp.float32)
    large = max_exact + (jnp.log(df / max_exact) / math.log(MAX_DIST / max_exact)
                         * (N_BUCKETS - max_exact)).astype(jnp.int32)
    return jnp.where(dist < max_exact, dist, jnp.minimum(large, N_BUCKETS - 1))


def band_bias(n_q, n_k, offset, rel_bias):
    d = offset + jnp.arange(n_q)[:, None] - jnp.arange(n_k)[None, :]
    valid = (d >= 0) & (d <= WINDOW)
    b = jnp.take(rel_bias.astype(jnp.float32), t5_bucket(jnp.clip(d, 0, WINDOW)), axis=0)
    b = jnp.where(valid[..., None], b, NEG_INF)
    return jnp.moveaxis(b, -1, 0).reshape(KV_A, G_A, n_q, n_k)


def sink_attention(q, k, v, bias, sinks):
    s = jnp.einsum('...qhgd,...khd->...hgqk', q, k).astype(jnp.float32) * (HD_A ** -0.5) + bias
    sk = sinks.astype(jnp.float32)[..., None, None]
    m = jnp.maximum(jnp.max(s, axis=-1, keepdims=True), sk)
    p = jnp.exp(s - m)
    denom = jnp.sum(p, axis=-1, keepdims=True) + jnp.exp(sk - m)
    return jnp.einsum('...hgqk,...khd->...qhgd', (p / denom).astype(v.dtype), v)


def gated_delta_chunked(q, k, v, g, beta, S0):
    Bn, L = q.shape[:2]
    C = min(GDN_CHUNK, L)
    lp = -(-L // C) * C
    nc = lp // C
    q, k, v, g, beta = [pad_seq(t.astype(jnp.float32), lp) for t in (q, k, v, g, beta)]

    def chunks(t):
        return jnp.moveaxis(t.reshape(Bn, nc, C, *t.shape[2:]), 2, 3)

    qc, kc, vc, bc = chunks(q), chunks(k), chunks(v), chunks(beta)
    gc = jnp.cumsum(chunks(g), axis=-1)
    tril = jnp.tril(jnp.ones((C, C), bool))
    strict = jnp.tril(jnp.ones((C, C), bool), -1)
    decay = jnp.exp(jnp.where(tril, gc[..., :, None] - gc[..., None, :], -jnp.inf))
    kb = kc * bc[..., None]
    M = jnp.where(strict, jnp.einsum('...id,...jd->...ij', kb, kc) * decay, 0.0)
    rhs = jnp.concatenate([vc * bc[..., None], kb * jnp.exp(gc)[..., None]], axis=-1)
    sol = lax.linalg.triangular_solve(M + jnp.eye(C, dtype=jnp.float32), rhs,
                                      left_side=True, lower=True, unit_diagonal=True)
    u, w = sol[..., :DV_B], sol[..., DV_B:]
    Aqk = jnp.einsum('...id,...jd->...ij', qc, kc) * decay
    qg = qc * jnp.exp(gc)[..., None]
    kd = kc * jnp.exp(gc[..., -1:] - gc)[..., None]
    gl = jnp.exp(gc[..., -1])

    def step(S, inp):
        u_i, w_i, qg_i, kd_i, A_i, gl_i = inp
        v_new = u_i - jnp.einsum('bhcd,bhde->bhce', w_i, S)
        o = jnp.einsum('bhcd,bhde->bhce', qg_i, S) + jnp.einsum('bhij,bhje->bhie', A_i, v_new)
        S = S * gl_i[..., None, None] + jnp.einsum('bhcd,bhce->bhde', kd_i, v_new)
        return S, o

    mv = lambda t: jnp.moveaxis(t, 1, 0)
    S_fin, o = lax.scan(step, S0.astype(jnp.float32), (mv(u), mv(w), mv(qg), mv(kd), mv(Aqk), mv(gl)))
    o = jnp.transpose(o, (1, 0, 3, 2, 4)).reshape(Bn, lp, H_B, DV_B)[:, :L]
    return o, S_fin


def ssd_chunked(x, dt, A, Bm, Cm, S0):
    Bn, L = x.shape[:2]
    C = min(SSD_CHUNK, L)
    lp = -(-L // C) * C
    nc = lp // C
    x, dt, Bm, Cm = [pad_seq(t.astype(jnp.float32), lp) for t in (x, dt, Bm, Cm)]
    a = (dt * A.astype(jnp.float32)).reshape(Bn, nc, C, G_C, HPG)
    xdt = (x * dt[..., None]).reshape(Bn, nc, C, G_C, HPG, P_C)
    Bc = Bm.reshape(Bn, nc, C, G_C, N_C)
    Cc = Cm.reshape(Bn, nc, C, G_C, N_C)
    acum = jnp.cumsum(a, axis=2)
    tril = jnp.tril(jnp.ones((C, C), bool))[:, :, None, None]
    diff = acum[:, :, :, None] - acum[:, :, None, :]
    Lmat = jnp.exp(jnp.where(tril, diff, -jnp.inf))
    scores = jnp.einsum('bclgn,bcsgn->bclsg', Cc, Bc)[..., None] * Lmat
    y = jnp.einsum('bclsgh,bcsghp->bclghp', scores, xdt)
    decay_to_end = jnp.exp(acum[:, :, -1:] - acum)
    chunk_states = jnp.einsum('bcsgn,bcsgh,bcsghp->bcghpn', Bc, decay_to_end, xdt)
    chunk_decay = jnp.exp(acum[:, :, -1])

    def step(S, inp):
        C_i, acum_i, cs_i, cd_i = inp
        y_off = jnp.einsum('blgn,bghpn,blgh->blghp', C_i, S, jnp.exp(acum_i))
        S = S * cd_i[..., None, None] + cs_i
        return S, y_off

    mv = lambda t: jnp.moveaxis(t, 1, 0)
    S0g = S0.astype(jnp.float32).reshape(Bn, G_C, HPG, P_C, N_C)
    S_fin, y_off = lax.scan(step, S0g, (mv(Cc), mv(acum), mv(chunk_states), mv(chunk_decay)))
    y = (y + jnp.moveaxis(y_off, 0, 1)).reshape(Bn, lp, H_C, P_C)[:, :L]
    return y, S_fin.reshape(Bn, H_C, P_C, N_C)


def even_mixer(h, w_in, w_out, sinks, rel_bias, conv_w, A_log, dt_bias, norm_g, win_k, win_v, conv_hist, S0):
    Bn, L, _ = h.shape
    sizes = [A_Q, A_K, A_V, GDN_CONV_CH, B_Z, H_B, H_B]
    idx = [int(i) for i in np.cumsum(sizes)[:-1]]
    q_a, k_a, v_a, qkv_b, z_b, b_b, a_b = jnp.split(h @ w_in, idx, axis=-1)
    q_a = q_a.reshape(Bn, L, KV_A, G_A, HD_A)
    k_a = k_a.reshape(Bn, L, KV_A, HD_A)
    v_a = v_a.reshape(Bn, L, KV_A, HD_A)
    sk = sinks.reshape(KV_A, G_A)
    if win_k is None:
        nb = L // BLOCK
        qb = q_a.reshape(Bn, nb, BLOCK, KV_A, G_A, HD_A)

        def band(t):
            tb = t.reshape(Bn, nb, BLOCK, KV_A, HD_A)
            prev = jnp.pad(tb, ((0, 0), (1, 0), (0, 0), (0, 0), (0, 0)))[:, :-1]
            return jnp.concatenate([prev, tb], axis=2)

        bias = band_bias(BLOCK, 2 * BLOCK, BLOCK, rel_bias)
        pad_keys = (jnp.arange(nb)[:, None] == 0) & (jnp.arange(2 * BLOCK)[None, :] < BLOCK)
        bias = jnp.where(pad_keys[:, None, None, None, :], NEG_INF, bias[None])
        o_a = sink_attention(qb, band(k_a), band(v_a), bias, sk)
        new_k, new_v = k_a[:, L - WINDOW:], v_a[:, L - WINDOW:]
    else:
        lc = win_k.shape[1]
        kk = jnp.concatenate([win_k.astype(k_a.dtype), k_a], axis=1)
        vv = jnp.concatenate([win_v.astype(v_a.dtype), v_a], axis=1)
        bias = band_bias(L, lc + L, lc, rel_bias)
        o_a = sink_attention(q_a, kk, vv, bias, sk)
        new_k, new_v = kk[:, L:], vv[:, L:]
    o_a = o_a.reshape(Bn, L, A_Q)
    conv, new_conv = causal_conv(qkv_b, conv_hist, conv_w)
    conv = jax.nn.silu(conv)
    q_b, k_b, v_b = jnp.split(conv, [B_Q, B_Q + B_K], axis=-1)
    q_b = l2norm(q_b.reshape(Bn, L, H_B, DK_B)) * (DK_B ** -0.5)
    k_b = l2norm(k_b.reshape(Bn, L, H_B, DK_B))
    v_b = v_b.reshape(Bn, L, H_B, DV_B)
    beta = jax.nn.sigmoid(b_b.astype(jnp.float32))
    g = -jnp.exp(A_log.astype(jnp.float32)) * jax.nn.softplus(a_b.astype(jnp.float32) + dt_bias.astype(jnp.float32))
    o_b, S_new = gated_delta_chunked(q_b, k_b, v_b, g, beta, S0)
    o_b = rmsnorm(o_b.astype(h.dtype), norm_g) * jax.nn.silu(z_b.reshape(Bn, L, H_B, DV_B))
    o = jnp.concatenate([o_a, o_b.reshape(Bn, L, B_V)], axis=-1) @ w_out
    return o, (new_k, new_v, new_conv, S_new.astype(S0.dtype))


def odd_mixer(h, w_in, w_out, conv_w, conv_b, dt_bias, A_log, D_skip, norm_g, conv_hist, S0):
    Bn, L, _ = h.shape
    z, xbc, dt = jnp.split(h @ w_in, [D_INNER, D_INNER + SSD_CONV_CH], axis=-1)
    xbc, new_conv = causal_conv(xbc, conv_hist, conv_w)
    xbc = jax.nn.silu(xbc + conv_b)
    xs, Bm, Cm = jnp.split(xbc, [D_INNER, D_INNER + G_C * N_C], axis=-1)
    xs = xs.reshape(Bn, L, H_C, P_C)
    Bm = Bm.reshape(Bn, L, G_C, N_C)
    Cm = Cm.reshape(Bn, L, G_C, N_C)
    dt = jax.nn.softplus(dt.astype(jnp.float32) + dt_bias.astype(jnp.float32))
    A = -jnp.exp(A_log.astype(jnp.float32))
    y, S_new = ssd_chunked(xs, dt, A, Bm, Cm, S0)
    y = y + D_skip.astype(jnp.float32)[:, None] * xs.astype(jnp.float32)
    y = y.reshape(Bn, L, D_INNER).astype(h.dtype) * jax.nn.silu(z)
    gs = D_INNER // G_C
    y = rmsnorm(y.reshape(Bn, L, G_C, gs), norm_g.reshape(G_C, gs)).reshape(Bn, L, D_INNER)
    return y @ w_out, (new_conv, S_new.astype(S0.dtype))


def trunk(x, states, W):
    (rel_bias, norm_ff1, norm_mix, norm_ff2, norm_final,
     ff1_gate, ff1_up, ff1_down, ff2_gate, ff2_up, ff2_down,
     even_w_in, even_w_out, swa_sinks, gdn_conv_w, gdn_A_log, gdn_dt_bias, gdn_norm,
     ssd_w_in, ssd_w_out, ssd_conv_w, ssd_conv_b, ssd_dt_bias, ssd_A_log, ssd_D, ssd_norm) = W
    Bn = x.shape[0]
    ks, vs, gconv, gssm, sconv, sssm = [], [], [], [], [], []
    for l in range(DEPTH):
        x = x + 0.5 * swiglu(rmsnorm(x, norm_ff1[l]), ff1_gate[l], ff1_up[l], ff1_down[l])
        h = rmsnorm(x, norm_mix[l])
        if l % 2 == 0:
            e = l // 2
            if states is None:
                wk = wv = None
                ch = jnp.zeros((Bn, CONV_W - 1, GDN_CONV_CH), x.dtype)
                S0 = jnp.zeros((Bn, H_B, DK_B, DV_B), x.dtype)
            else:
                wk, wv, ch, S0 = states[0][e], states[1][e], states[2][e], states[3][e]
            o, (nk, nv, nc_, ns) = even_mixer(h, even_w_in[e], even_w_out[e], swa_sinks[e], rel_bias,
                                              gdn_conv_w[e], gdn_A_log[e], gdn_dt_bias[e], gdn_norm[e],
                                              wk, wv, ch, S0)
            ks.append(nk); vs.append(nv); gconv.append(nc_); gssm.append(ns)
        else:
            e = l // 2
            if states is None:
                ch = jnp.zeros((Bn, CONV_W - 1, SSD_CONV_CH), x.dtype)
                S0 = jnp.zeros((Bn, H_C, P_C, N_C), x.dtype)
            else:
                ch, S0 = states[4][e], states[5][e]
            o, (nc_, ns) = odd_mixer(h, ssd_w_in[e], ssd_w_out[e], ssd_conv_w[e], ssd_conv_b[e],
                                     ssd_dt_bias[e], ssd_A_log[e], ssd_D[e], ssd_norm[e], ch, S0)
            sconv.append(nc_); sssm.append(ns)
        x = x + o
        x = x + 0.5 * swiglu(rmsnorm(x, norm_ff2[l]), ff2_gate[l], ff2_up[l], ff2_down[l])
    y = rmsnorm(x, norm_final)
    return (y, jnp.stack(ks), jnp.stack(vs), jnp.stack(gconv), jnp.stack(gssm),
            jnp.stack(sconv), jnp.stack(sssm))


def setup_inputs(seed: int = 0) -> dict:
    key = jax.random.key(seed)
    ks = iter(jax.random.split(key, 64))

    def nrm(shape, s=1.0):
        return s * jax.random.normal(next(ks), shape, jnp.float32)

    def gain(shape):
        return 1.0 + nrm(shape, 0.05)

    def dt_bias_init(shape):
        u = jax.random.uniform(next(ks), shape, jnp.float32)
        dt = jnp.exp(u * (math.log(0.1) - math.log(0.001)) + math.log(0.001))
        return dt + jnp.log(-jnp.expm1(-dt))

    def a_log_init(shape):
        return jnp.log(jax.random.uniform(next(ks), shape, jnp.float32, 1.0, 16.0))

    win = min(WINDOW, PAST_LEN)
    d = D_MODEL
    return {
        "x_prompt": nrm((BATCH, SEQ, d)),
        "x_sample": nrm((DEC_BATCH, DEC_SEQ, d)),
        "cache_swa_k": nrm((N_EVEN, DEC_BATCH, win, KV_A, HD_A)),
        "cache_swa_v": nrm((N_EVEN, DEC_BATCH, win, KV_A, HD_A)),
        "state_gdn_conv": nrm((N_EVEN, DEC_BATCH, CONV_W - 1, GDN_CONV_CH)),
        "state_gdn_ssm": nrm((N_EVEN, DEC_BATCH, H_B, DK_B, DV_B), 0.1),
        "state_ssd_conv": nrm((N_ODD, DEC_BATCH, CONV_W - 1, SSD_CONV_CH)),
        "state_ssd_ssm": nrm((N_ODD, DEC_BATCH, H_C, P_C, N_C), 0.3),
        "rel_bias": nrm((N_BUCKETS, H_A), 0.5),
        "norm_ff1": gain((DEPTH, d)),
        "norm_mix": gain((DEPTH, d)),
        "norm_ff2": gain((DEPTH, d)),
        "norm_final": gain((d,)),
        "ff1_gate": nrm((DEPTH, d, D_FF), d ** -0.5),
        "ff1_up": nrm((DEPTH, d, D_FF), d ** -0.5),
        "ff1_down": nrm((DEPTH, D_FF, d), D_FF ** -0.5),
        "ff2_gate": nrm((DEPTH, d, D_FF), d ** -0.5),
        "ff2_up": nrm((DEPTH, d, D_FF), d ** -0.5),
        "ff2_down": nrm((DEPTH, D_FF, d), D_FF ** -0.5),
        "even_w_in": nrm((N_EVEN, d, EVEN_IN), d ** -0.5),
        "even_w_out": nrm((N_EVEN, EVEN_MIX, d), EVEN_MIX ** -0.5),
        "swa_sinks": nrm((N_EVEN, H_A), 0.5),
        "gdn_conv_w": nrm((N_EVEN, CONV_W, GDN_CONV_CH), CONV_W ** -0.5),
        "gdn_A_log": a_log_init((N_EVEN, H_B)),
        "gdn_dt_bias": dt_bias_init((N_EVEN, H_B)),
        "gdn_norm": gain((N_EVEN, DV_B)),
        "ssd_w_in": nrm((N_ODD, d, ODD_IN), d ** -0.5),
        "ssd_w_out": nrm((N_ODD, D_INNER, d), D_INNER ** -0.5),
        "ssd_conv_w": nrm((N_ODD, CONV_W, SSD_CONV_CH), CONV_W ** -0.5),
        "ssd_conv_b": nrm((N_ODD, SSD_CONV_CH), 0.01),
        "ssd_dt_bias": dt_bias_init((N_ODD, H_C)),
        "ssd_A_log": a_log_init((N_ODD, H_C)),
        "ssd_D": 1.0 + nrm((N_ODD, H_C), 0.1),
        "ssd_norm": gain((N_ODD, D_INNER)),
    }


def reference(x_prompt, x_sample, cache_swa_k, cache_swa_v, state_gdn_conv, state_gdn_ssm,
              state_ssd_conv, state_ssd_ssm, rel_bias, norm_ff1, norm_mix, norm_ff2, norm_final,
              ff1_gate, ff1_up, ff1_down, ff2_gate, ff2_up, ff2_down,
              even_w_in, even_w_out, swa_sinks, gdn_conv_w, gdn_A_log, gdn_dt_bias, gdn_norm,
              ssd_w_in, ssd_w_out, ssd_conv_w, ssd_conv_b, ssd_dt_bias, ssd_A_log, ssd_D, ssd_norm):
    W = (rel_bias, norm_ff1, norm_mix, norm_ff2, norm_final,
         ff1_gate, ff1_up, ff1_down, ff2_gate, ff2_up, ff2_down,
         even_w_in, even_w_out, swa_sinks, gdn_conv_w, gdn_A_log, gdn_dt_bias, gdn_norm,
         ssd_w_in, ssd_w_out, ssd_conv_w, ssd_conv_b, ssd_dt_bias, ssd_A_log, ssd_D, ssd_norm)
    y_prompt, p_k, p_v, p_gconv, p_gssm, p_sconv, p_sssm = trunk(x_prompt, None, W)
    states = (cache_swa_k, cache_swa_v, state_gdn_conv, state_gdn_ssm, state_ssd_conv, state_ssd_ssm)
    y_sample, s_k, s_v, s_gconv, s_gssm, s_sconv, s_sssm = trunk(x_sample, states, W)
    return (y_prompt, y_sample, p_k, p_v, p_gconv, p_gssm, p_sconv, p_sssm,
            s_k, s_v, s_gconv, s_gssm, s_sconv, s_sssm)
```

```python
import math
import numpy as np
import concourse.bass as bass
import concourse.mybir as mybir
from concourse.bass_utils import run_bass_kernel_spmd

F32 = mybir.dt.float32
BF16 = mybir.dt.bfloat16
ALU = mybir.AluOpType
AF = mybir.ActivationFunctionType
AX = mybir.AxisListType

D = 1024
KC = 8
SEQ = 2048
NS = 16
DFF = 2816
FC = 22
EPS = 1e-6
N_CORES = 8
DEPTH = 4


class Buf:
    __slots__ = ("name", "w", "r", "excl")

    def __init__(self, name, excl=False):
        self.name = name
        self.w = None
        self.r = []
        self.excl = excl


class Sched:
    def __init__(self, nc, n_dma_sems=48):
        self.nc = nc
        self.E = {"pe": nc.tensor, "dve": nc.vector, "act": nc.scalar, "pool": nc.gpsimd, "sp": nc.sync}
        self.sems = {}
        self.cnt = {}
        for e in self.E:
            self.sems[e] = nc.alloc_semaphore("s_" + e)
            self.cnt[e] = 0
        self.dsems = []
        for i in range(n_dma_sems):
            k = "d%d" % i
            self.sems[k] = nc.alloc_semaphore("s_" + k)
            self.cnt[k] = 0
            self.dsems.append(k)
        self.dnext = 0
        self.seen = {e: {} for e in self.E}
        self.n_wait = 0
        self.n_ops = 0

    def _wait(self, eng, tick):
        if tick is None:
            return
        k, v = tick
        if self.seen[eng].get(k, 0) >= v:
            return
        self.E[eng].wait_ge(self.sems[k], v)
        self.seen[eng][k] = v
        self.n_wait += 1

    def _deps(self, eng, reads, writes):
        need = {}
        for b in reads:
            if b.w is not None:
                k, v = b.w
                if need.get(k, 0) < v:
                    need[k] = v
            if b.excl:
                for (k, v) in b.r:
                    if k != eng and need.get(k, 0) < v:
                        need[k] = v
        for b in writes:
            if b.w is not None:
                k, v = b.w
                if need.get(k, 0) < v:
                    need[k] = v
            for (k, v) in b.r:
                if need.get(k, 0) < v:
                    need[k] = v
        for k, v in need.items():
            self._wait(eng, (k, v))

    def _commit(self, tick, reads, writes):
        for b in reads:
            b.r.append(tick)
            if len(b.r) > 16:
                m = {}
                for (k, v) in b.r:
                    if m.get(k, 0) < v:
                        m[k] = v
                b.r = list(m.items())
        for b in writes:
            b.w = tick
            b.r = []

    def op(self, eng, fn, reads=(), writes=()):
        self._deps(eng, reads, writes)
        ins = fn()
        self.cnt[eng] += 1
        ins.then_inc(self.sems[eng], 1)
        tick = (eng, self.cnt[eng])
        self._commit(tick, reads, writes)
        self.n_ops += 1
        return tick

    def new_sem(self, k):
        self.sems[k] = self.nc.alloc_semaphore("s_" + k)
        self.cnt[k] = 0

    def barrier(self):
        for e in ("pe", "dve", "act", "sp"):
            for o in ("pe", "dve", "act", "pool"):
                if o != e and self.cnt[o] > 0:
                    self._wait(e, (o, self.cnt[o]))
            for k in self.dsems:
                if self.cnt[k] > 0:
                    self._wait(e, (k, self.cnt[k]))

    def dma(self, q, out=None, in_=None, reads=(), writes=(), multi=None, sem=None):
        pairs = multi if multi is not None else [(out, in_)]
        if sem is not None:
            k = sem
        else:
            k = self.dsems[self.dnext]
            self.dnext = (self.dnext + 1) % len(self.dsems)
        if self.cnt[k] > 0:
            self._wait(q, (k, self.cnt[k]))
        self._deps(q, reads, writes)
        for (o, i) in pairs:
            self.E[q].dma_start(out=o, in_=i, allow_slow_non_contiguous=True).then_inc(self.sems[k], 16)
            self.cnt[k] += 16
        tick = (k, self.cnt[k])
        self._commit(tick, reads, writes)
        return tick

    def finish(self):
        for e in ("pe", "dve", "act", "pool"):
            if self.cnt[e] > 0:
                self._wait("sp", (e, self.cnt[e]))
        for k in list(self.sems):
            if k not in self.E and self.cnt[k] > 0:
                self._wait("sp", (k, self.cnt[k]))


class Tile:
    def __init__(self, nc, name, shape, dtype, psum=False, handle=None):
        if handle is not None:
            self.t = handle
        elif psum:
            self.t = nc.alloc_psum_tensor(name, list(shape), dtype)
        else:
            self.t = nc.alloc_sbuf_tensor(name, list(shape), dtype)
        self.b = Buf(name, excl=psum)

    def __getitem__(self, k):
        return self.t[k]


SLOT_ELEMS = 4096
N_SLOTS = 3


class WStream:
    def __init__(self, nc, S, plan):
        self.nc, self.S = nc, S
        self.plan = plan
        self.slots = [Tile(nc, "wslot%d" % i, [128, SLOT_ELEMS], BF16) for i in range(N_SLOTS)]
        self.dram_buf = Buf("wdram")
        for i in range(N_SLOTS):
            S.new_sem("w%d" % i)
        self.issued = 0
        self.used = 0

    def _issue(self):
        i = self.issued
        ap, n = self.plan[i]
        sl = self.slots[i % N_SLOTS]
        self.S.dma("pool", out=sl[:, 0:n], in_=ap, reads=[self.dram_buf], writes=[sl.b], sem="w%d" % (i % N_SLOTS))
        self.issued += 1

    def get(self, expect_n):
        i = self.used
        while self.issued <= min(i + N_SLOTS - 2, len(self.plan) - 1):
            self._issue()
        assert self.plan[i][1] == expect_n, (i, self.plan[i][1], expect_n)
        self.used += 1
        return self.slots[i % N_SLOTS]


import contextlib

H_A, KV_A, HD_A = 8, 2, 64
H_B = 4
NEG = -1e30
EV_FM = 21
EV_TOK = 392
EW_OFF = [0, 4096, 8192, 12288, 16384, 20480, 21504]
EW_N = [4096, 4096, 4096, 4096, 4096, 1024, 8 * EV_TOK]
EW_TOT = 21504 + 8 * EV_TOK
OW_TOT = 256 + 10 * 4096


class Prog:
    def __init__(self, cfg):
        self.cfg = cfg
        self.nt = cfg.get("ntiles", 4)
        self.seq = self.nt * 512
        self.ntok = self.seq + NS
        self.depth = cfg.get("depth", DEPTH)
        self.layers = cfg.get("layers", None) or list(range(self.depth))
        nc = bass.Bass("TRN2", target_bir_lowering=False)
        self.nc = nc
        self.S = Sched(nc)
        self.stack = None
        self.declare_io()
        self.alloc()
        self.plan_weights()
        self.emit()

    def din(self, name, shape, dtype=F32):
        return self.nc.dram_tensor(name, list(shape), dtype, kind="ExternalInput").ap()

    def dout(self, name, shape, dtype=F32):
        return self.nc.dram_tensor(name, list(shape), dtype, kind="ExternalOutput").ap()

    def declare_io(self):
        sq = self.seq
        self.x_p = self.din("x_p", [sq, D])
        self.x_s = self.din("x_s", [NS, D])
        self.gains = self.din("gains", [128, 13 * KC])
        self.masks = self.din("masks", [128, 8 * 128])
        self.wgu = self.din("wgu", [DEPTH, 2, 11, 128, 4096])
        self.wd = self.din("wd", [DEPTH, 2, 8, 128, 2816])
        self.ewin = self.din("ewin", [2, 128, EW_TOT])
        self.ewout = self.din("ewout", [2, 2, 128, 4096])
        self.econv = self.din("econv", [2, 128, 48])
        self.esm = self.din("esm", [2, 128, 17])
        self.esk = self.din("esk", [2, 128, 1])
        self.swabias = self.din("swabias", [128, 8 * 256])
        self.decbias = self.din("decbias", [128, 129])
        self.c_k = self.din("c_k", [2, NS, 128, 128])
        self.c_v = self.din("c_v", [2, NS, 128, 128])
        self.s_gc = self.din("s_gc", [2, NS, 3, 1536])
        self.s_gs = self.din("s_gs", [2, NS, 4, 128, 128])
        self.owin = self.din("owin", [2, 128, OW_TOT])
        self.owout = self.din("owout", [2, 4, 128, 4096])
        self.oconv = self.din("oconv", [2, 128, 96])
        self.ohead = self.din("ohead", [2, 128, 96])
        self.ofeat = self.din("ofeat", [2, 128, 56])
        self.s_sc = self.din("s_sc", [2, NS, 3, 3072])
        self.s_ss = self.din("s_ss", [2, NS, 2048, 128])
        self.o_psc = self.dout("o_psc", [2, 3, 3072])
        self.o_pss = self.dout("o_pss", [2, 2048, 128])
        self.o_ssc = self.dout("o_ssc", [2, NS, 3, 3072])
        self.o_sss = self.dout("o_sss", [2, NS, 2048, 128])
        self.y_p = self.dout("y_p", [sq, D])
        self.y_s = self.dout("y_s", [NS, D])
        self.o_pk = self.dout("o_pk", [2, 128, 128])
        self.o_pv = self.dout("o_pv", [2, 128, 128])
        self.o_pgc = self.dout("o_pgc", [2, 3, 1536])
        self.o_pgs = self.dout("o_pgs", [2, 4, 128, 128])
        self.o_sk = self.dout("o_sk", [2, NS, 128, 128])
        self.o_sv = self.dout("o_sv", [2, NS, 128, 128])
        self.o_sgc = self.dout("o_sgc", [2, NS, 3, 1536])
        self.o_sgs = self.dout("o_sgs", [2, NS, 4, 128, 128])
        self.Bin = Buf("dram_in")
        self.Bout = Buf("dram_out")

    def alloc(self):
        nc = self.nc
        self.xT = Tile(nc, "xT", [128, KC, self.ntok], F32)
        self.xTb = [Buf("xT_t%d" % i) for i in range(self.nt)] + [Buf("xT_s")]
        self.ident = Tile(nc, "ident", [128, 128], F32)
        self.ident_bf = Tile(nc, "ident_bf", [128, 128], BF16)
        self.ones_bf = Tile(nc, "ones_bf", [128, 128], BF16)
        self.ones_f = Tile(nc, "ones_f", [128, 128], F32)
        self.gn = Tile(nc, "gn", [128, 13 * KC], F32)
        self.mk = Tile(nc, "mk", [128, 8, 128], F32)
        self.P = [Tile(nc, "ps%d" % i, [128, 512], F32, psum=True) for i in range(8)]

    def begin(self):
        assert self.stack is None
        self.stack = contextlib.ExitStack()
        self.nalloc = 0
        self.deferred = []

    def out_dma(self, dst, src, r=()):
        self.deferred.append((dst, src, list(r)))

    def end(self):
        for (dst, src, r) in self.deferred:
            self.dma(dst, src, r=r, w=[self.Bout])
        self.deferred = []
        self.S.barrier()
        self.stack.close()
        self.stack = None

    def T(self, name, shape, dtype):
        self.nalloc += 1
        h = self.stack.enter_context(self.nc.sbuf_tensor("%s_%d" % (name, self.S.n_ops), list(shape), dtype))
        return Tile(self.nc, name, shape, dtype, handle=h)

    def ffn_blocks(self, l, f):
        out = []
        for b in range(11):
            out.append((self.wgu[l, f, b], 4096))
        for b in range(8):
            out.append((self.wd[l, f, b], 2816))
        return out

    def tiles(self):
        ts = []
        for t in range(self.nt):
            segs = [(t * 512, 512, 0)]
            if t == self.nt - 1:
                segs.append((self.seq, NS, 512))
            ts.append(segs)
        return ts

    def even_blocks(self, e):
        out = []
        for _ in range(self.nt + 1):
            for b in range(7):
                out.append((self.ewin[e, :, EW_OFF[b]:EW_OFF[b] + EW_N[b]], EW_N[b]))
            for b in range(2):
                out.append((self.ewout[e, b], 4096))
        return out

    def odd_blocks(self, e):
        out = []
        for _ in range(self.nt + 1):
            out.append((self.owin[e, :, 0:256], 256))
            for b in range(10):
                out.append((self.owin[e, :, 256 + b * 4096:256 + (b + 1) * 4096], 4096))
            for b in range(4):
                out.append((self.owout[e, b], 4096))
        return out

    def mixer_blocks(self, l):
        if l % 2 == 0:
            return self.even_blocks(l // 2)
        return self.odd_blocks(l // 2)

    def plan_weights(self):
        plan = []
        ffn = self.cfg.get("ffn", True)
        for l in self.layers:
            if ffn:
                for _ in self.tiles():
                    plan += self.ffn_blocks(l, 0)
            plan += self.mixer_blocks(l)
            if ffn:
                for _ in self.tiles():
                    plan += self.ffn_blocks(l, 1)
        self.ws = WStream(self.nc, self.S, plan)

    def bl(self, xs):
        return [getattr(x, 'b', x) for x in xs]

    def dve(self, fn, r=(), w=()):
        return self.S.op("dve", fn, self.bl(r), self.bl(w))

    def act(self, fn, r=(), w=()):
        return self.S.op("act", fn, self.bl(r), self.bl(w))

    def pe(self, fn, r=(), w=()):
        return self.S.op("pe", fn, self.bl(r), self.bl(w))

    def pool(self, fn, r=(), w=()):
        return self.S.op("pool", fn, self.bl(r), self.bl(w))

    def dma(self, out, in_, r=(), w=(), q="sp"):
        return self.S.dma(q, out=out, in_=in_, reads=self.bl(r), writes=self.bl(w))

    def gcol(self, idx, kc):
        return self.gn[:, idx * KC + kc: idx * KC + kc + 1]

    def tile_bufs(self, ti):
        b = [self.xTb[ti]]
        if ti == self.nt - 1:
            b.append(self.xTb[self.nt])
        return b

    def col_stats(self, src_fn, nchunk, n, ps, sq, ones, scale, bias_ln, out_ap, exp_bias=0.0, rd=()):
        nc = self.nc
        for c in range(nchunk):
            self.act(lambda c=c: nc.scalar.activation(sq[:, c, 0:n], src_fn(c), AF.Square), r=rd, w=[sq])

        def mm():
            ins = None
            for c in range(nchunk):
                ins = nc.tensor.matmul(ps[:, 0:n], ones[:], sq[:, c, 0:n], start=(c == 0), stop=(c == nchunk - 1))
            return ins
        self.pe(mm, r=[sq, ones], w=[ps])
        return ps

    def rmsnorm_cols(self, xbufs, c0, n, gidx, hn, l0, sq, rstd):
        nc = self.nc
        ps = self.P[7]
        self.act(lambda: nc.scalar.activation(sq[:, :, l0:l0 + n], self.xT[:, :, c0:c0 + n], AF.Square),
                 r=xbufs, w=[sq])

        def mm():
            ins = None
            for kc in range(KC):
                ins = nc.tensor.matmul(ps[:, 0:n], self.ones_bf[:], sq[:, kc, l0:l0 + n],
                                       start=(kc == 0), stop=(kc == KC - 1))
            return ins
        self.pe(mm, r=[sq, self.ones_bf], w=[ps])
        self.act(lambda: nc.scalar.activation(rstd[:, l0:l0 + n], ps[:, 0:n], AF.Ln, bias=EPS, scale=1.0 / D),
                 r=[ps], w=[rstd])
        self.act(lambda: nc.scalar.activation(rstd[:, l0:l0 + n], rstd[:, l0:l0 + n], AF.Exp, scale=-0.5),
                 r=[rstd], w=[rstd])
        for kc in range(KC):
            self.dve(lambda kc=kc: nc.vector.scalar_tensor_tensor(
                out=hn[:, kc, l0:l0 + n], in0=self.xT[:, kc, c0:c0 + n], scalar=self.gcol(gidx, kc),
                in1=rstd[:, l0:l0 + n], op0=ALU.mult, op1=ALU.mult),
                r=xbufs + [rstd, self.gn], w=[hn])

    def ffn_tile(self, l, f, ti, segs, hn, hT, sgs):
        nc = self.nc
        xb = self.tile_bufs(ti)
        it = 0
        for blk in range(11):
            w = self.ws.get(4096)
            wv = w[:, 0:4096].rearrange("p (a k c) -> p a k c", a=2, k=KC)
            for fc in range(2):
                fch = blk * 2 + fc
                for si, (g0, n, l0) in enumerate(segs):
                    if si == 0:
                        pg, pu = self.P[(it % 2)], self.P[2 + (it % 2)]
                    else:
                        pg, pu = self.P[4], self.P[5]

                    def mm(pt, a):
                        ins = None
                        for kc in range(KC):
                            ins = nc.tensor.matmul(pt[:, 0:n], wv[:, a, kc, fc * 128:(fc + 1) * 128],
                                                   hn[:, kc, l0:l0 + n], start=(kc == 0), stop=(kc == KC - 1))
                        return ins
                    self.pe(lambda: mm(pg, 0), r=[w, hn], w=[pg])
                    self.pe(lambda: mm(pu, 1), r=[w, hn], w=[pu])
                    sg = sgs[it % 2]
                    self.act(lambda: nc.scalar.activation(sg[:, 0:n], pg[:, 0:n], AF.Silu), r=[pg], w=[sg])
                    self.dve(lambda: nc.vector.tensor_tensor(out=hT[:, fch, l0:l0 + n], in0=sg[:, 0:n],
                                                             in1=pu[:, 0:n], op=ALU.mult), r=[sg, pu], w=[hT])
                it += 1
        it = 0
        for blk in range(8):
            w = self.ws.get(2816)
            wv = w[:, 0:2816].rearrange("p (k c) -> p k c", k=FC)
            dch = blk
            for si, (g0, n, l0) in enumerate(segs):
                po = self.P[6 + (it % 2)] if si == 0 else self.P[4 + (it % 2)]

                def mm():
                    ins = None
                    for k in range(FC):
                        ins = nc.tensor.matmul(po[:, 0:n], wv[:, k, :], hT[:, k, l0:l0 + n],
                                               start=(k == 0), stop=(k == FC - 1))
                    return ins
                self.pe(mm, r=[w, hT], w=[po])
                xs = self.xT[:, dch, g0:g0 + n]
                self.dve(lambda: nc.vector.scalar_tensor_tensor(out=xs, in0=po[:, 0:n], scalar=0.5, in1=xs,
                                                                op0=ALU.mult, op1=ALU.add), r=[po] + xb, w=xb)
            it += 1

    def ffn(self, l, f):
        self.begin()
        hns = [self.T("hn%d" % i, [128, KC, 528], BF16) for i in range(2)]
        hT = self.T("hT", [128, FC, 528], BF16)
        sq = self.T("sq", [128, KC, 528], BF16)
        rstd = self.T("rstd", [128, 528], F32)
        sgs = [self.T("sg%d" % i, [128, 512], F32) for i in range(2)]
        gidx = l if f == 0 else 8 + l
        for ti, segs in enumerate(self.tiles()):
            hn = hns[ti % 2]
            for (g0, n, l0) in segs:
                self.rmsnorm_cols(self.tile_bufs(ti), g0, n, gidx, hn, l0, sq, rstd)
            self.ffn_tile(l, f, ti, segs, hn, hT, sgs)
        self.end()

    def consts(self):
        nc = self.nc
        self.pool(lambda: nc.gpsimd.memset(self.ident[:], 0.0), w=[self.ident])
        self.pool(lambda: nc.gpsimd.affine_select(self.ident[:], self.ident[:], pattern=[[-1, 128]],
                                                  compare_op=ALU.not_equal, fill=1.0, base=0, channel_multiplier=1),
                  r=[self.ident], w=[self.ident])
        self.pool(lambda: nc.gpsimd.memset(self.ones_bf[:], 1.0), w=[self.ones_bf])
        self.pool(lambda: nc.gpsimd.memset(self.ones_f[:], 1.0), w=[self.ones_f])
        self.dve(lambda: nc.vector.tensor_copy(self.ident_bf[:], self.ident[:]), r=[self.ident], w=[self.ident_bf])
        self.dma(self.gn[:], self.gains, r=[self.Bin], w=[self.gn])
        self.dma(self.mk[:].rearrange("p a b -> p (a b)"), self.masks, r=[self.Bin], w=[self.mk])

    def load_x(self):
        nc = self.nc
        self.begin()
        xins = [self.T("xin%d" % i, [128, D], F32) for i in range(2)]
        ntt = self.seq // 128
        for tt in range(ntt + 1):
            xin = xins[tt % 2]
            if tt < ntt:
                rows, src, c0 = 128, self.x_p[tt * 128:(tt + 1) * 128, :], tt * 128
                xb = self.xTb[tt // 4]
            else:
                rows, src, c0 = NS, self.x_s, self.seq
                xb = self.xTb[self.nt]
            self.dma(xin[0:rows, :], src, r=[self.Bin], w=[xin])
            for half in range(2):
                ps = self.P[(tt % 2) * 2 + half]

                def tr():
                    ins = None
                    for j in range(4):
                        kc = half * 4 + j
                        ins = nc.tensor.transpose(ps[:, j * 128:j * 128 + rows], xin[0:rows, kc * 128:(kc + 1) * 128],
                                                  self.ident[0:rows, 0:rows])
                    return ins
                self.pe(tr, r=[xin, self.ident], w=[ps])
                pv = ps[:, :].rearrange("p (j c) -> p j c", j=4)[:, :, 0:rows]
                dst = self.xT[:, half * 4:half * 4 + 4, c0:c0 + rows]
                if half == 0:
                    self.dve(lambda: nc.vector.tensor_copy(dst, pv), r=[ps], w=[xb])
                else:
                    self.act(lambda: nc.scalar.copy(dst, pv), r=[ps], w=[xb])
        self.end()

    def final(self):
        nc = self.nc
        self.begin()
        sq = self.T("sq", [128, KC, 528], BF16)
        rstd = self.T("rstd", [128, 528], F32)
        yos = [self.T("yo%d" % i, [128, D], F32) for i in range(2)]
        for ti, segs in enumerate(self.tiles()):
            xb = self.tile_bufs(ti)
            for (c0, n, l0) in segs:
                ps = self.P[7]
                self.act(lambda: nc.scalar.activation(sq[:, :, l0:l0 + n], self.xT[:, :, c0:c0 + n], AF.Square),
                         r=xb, w=[sq])

                def mm():
                    ins = None
                    for kc in range(KC):
                        ins = nc.tensor.matmul(ps[:, 0:n], self.ones_bf[:], sq[:, kc, l0:l0 + n],
                                               start=(kc == 0), stop=(kc == KC - 1))
                    return ins
                self.pe(mm, r=[sq, self.ones_bf], w=[ps])
                self.act(lambda: nc.scalar.activation(rstd[:, l0:l0 + n], ps[:, 0:n], AF.Ln, bias=EPS, scale=1.0 / D),
                         r=[ps], w=[rstd])
                self.act(lambda: nc.scalar.activation(rstd[:, l0:l0 + n], rstd[:, l0:l0 + n], AF.Exp, scale=-0.5),
                         r=[rstd], w=[rstd])
                for kc in range(KC):
                    self.dve(lambda kc=kc: nc.vector.scalar_tensor_tensor(
                        out=self.xT[:, kc, c0:c0 + n], in0=self.xT[:, kc, c0:c0 + n], scalar=self.gcol(12, kc),
                        in1=rstd[:, l0:l0 + n], op0=ALU.mult, op1=ALU.mult), r=xb + [rstd, self.gn], w=xb)
        ntt = self.seq // 128
        for tt in range(ntt + 1):
            yo = yos[tt % 2]
            if tt < ntt:
                rows, dst, c0 = 128, self.y_p[tt * 128:(tt + 1) * 128, :], tt * 128
                xb = self.xTb[tt // 4]
            else:
                rows, dst, c0 = NS, self.y_s, self.seq
                xb = self.xTb[self.nt]
            for half in range(2):
                ps = self.P[(tt % 2) * 2 + half]

                def tr():
                    ins = None
                    for j in range(4):
                        kc = half * 4 + j
                        ins = nc.tensor.transpose(ps[0:rows, j * 128:(j + 1) * 128], self.xT[:, kc, c0:c0 + rows],
                                                  self.ident[:])
                    return ins
                self.pe(tr, r=[xb, self.ident], w=[ps])
                if half == 0:
                    self.dve(lambda: nc.vector.tensor_copy(yo[0:rows, 0:512], ps[0:rows, :]), r=[ps], w=[yo])
                else:
                    self.act(lambda: nc.scalar.copy(yo[0:rows, 512:1024], ps[0:rows, :]), r=[ps], w=[yo])
            self.dma(dst, yo[0:rows, :], r=[yo], w=[self.Bout])
        self.end()

    def mixer(self, l):
        if l % 2 == 0:
            self.even_mixer(l)
        else:
            self.odd_mixer(l)

    def emit(self):
        self.consts()
        self.load_x()
        for l in self.layers:
            if self.cfg.get("ffn", True):
                self.ffn(l, 0)
            self.mixer(l)
            if self.cfg.get("ffn", True):
                self.ffn(l, 1)
        self.final()
        self.S.finish()


def even_alloc(self, nc_):
    G = {}
    T = self.T
    G["hn"] = T("hn", [128, KC, nc_], BF16)
    G["sqm"] = T("sqm", [128, KC, nc_], BF16)
    G["rstd"] = T("rstd", [128, nc_], F32)
    G["qT"] = T("qT", [128, 4, nc_], BF16)
    G["acc"] = [T("acc%d" % i, [128, nc_], F32) for i in range(2)]
    G["cq"] = T("cq", [128, 12, nc_], F32)
    G["zs"] = T("zs", [128, 4, nc_], BF16)
    oT = Tile(self.nc, "oT", [128, 4, nc_], F32, handle=G["hn"].t.bitcast(F32).reshape([128, 4, nc_]))
    oT.b = G["hn"].b
    G["oT"] = oT
    G["cw"] = T("cw", [128, 12, 4], F32)
    G["sm"] = T("sm", [128, 17], F32)
    G["eA"] = T("eA", [128, 4], F32)
    G["sq4"] = T("sq4", [128, 2, nc_], BF16)
    return G


def even_common(self, G, e):
    nc = self.nc
    self.dma(G["cw"][:].rearrange("p a b -> p (a b)"), self.econv[e], r=[self.Bin], w=[G["cw"]])
    self.dma(G["sm"][:], self.esm[e], r=[self.Bin], w=[G["sm"]])
    self.act(lambda: nc.scalar.activation(G["eA"][:], G["sm"][:, 0:4], AF.Exp), r=[G["sm"]], w=[G["eA"]])
    self.dve(lambda: nc.vector.tensor_scalar_mul(G["eA"][:], G["eA"][:], -1.0), r=[G["eA"]], w=[G["eA"]])


def even_inproj_fm(self, G, n, conv_fn, k_dst):
    nc = self.nc
    hn = G["hn"]
    j = 0
    for blk in range(6):
        nb = EW_N[blk]
        w = self.ws.get(nb)
        ncols = nb // KC
        wv = w[:, 0:nb].rearrange("p (k c) -> p k c", k=KC)
        for cc in range(ncols // 128):
            ps = self.P[j % 2]

            def mm():
                ins = None
                for kc in range(KC):
                    ins = nc.tensor.matmul(ps[:, 0:n], wv[:, kc, cc * 128:(cc + 1) * 128], hn[:, kc, 0:n],
                                           start=(kc == 0), stop=(kc == KC - 1))
                return ins
            self.pe(mm, r=[w, hn], w=[ps])
            if j < 4:
                self.act(lambda: nc.scalar.copy(G["qT"][:, j, 0:n], ps[:, 0:n]), r=[ps], w=[G["qT"]])
            elif j == 4:
                self.act(lambda: nc.scalar.copy(k_dst, ps[:, 0:n]), r=[ps], w=[G["kTa"]])
            elif j < 17:
                conv_fn(j - 5, ps)
            else:
                self.act(lambda: nc.scalar.activation(G["zs"][:, j - 17, 0:n], ps[:, 0:n], AF.Silu),
                         r=[ps], w=[G["zs"]])
            j += 1
    assert j == EV_FM


def gates_from_ba(self, G, ba_ps_ap, rows, beta_dst, g_dst, tmp, rd, wr):
    nc = self.nc
    sm = G["sm"]
    self.act(lambda: nc.scalar.activation(beta_dst, ba_ps_ap[:, 0:4], AF.Sigmoid), r=rd, w=wr)
    self.dve(lambda: nc.vector.tensor_tensor(out=tmp[0:rows, 0:4], in0=ba_ps_ap[:, 4:8], in1=sm[0:rows, 4:8], op=ALU.add),
             r=rd + [sm], w=[tmp])
    self.act(lambda: nc.scalar.activation(tmp[0:rows, 0:4], tmp[0:rows, 0:4], AF.Exp), r=[tmp], w=[tmp])
    self.act(lambda: nc.scalar.activation(tmp[0:rows, 0:4], tmp[0:rows, 0:4], AF.Ln, bias=1.0, scale=1.0), r=[tmp], w=[tmp])
    self.dve(lambda: nc.vector.tensor_tensor(out=g_dst, in0=tmp[0:rows, 0:4], in1=G["eA"][0:rows, :], op=ALU.mult),
             r=[tmp, G["eA"]], w=wr)


def l2norm_heads(self, G, n):
    nc = self.nc
    cq, sq4 = G["cq"], G["sq4"]
    for c in range(8):
        ps = self.P[c % 2]
        self.act(lambda: nc.scalar.activation(sq4[:, c % 2, 0:n], cq[:, c, 0:n], AF.Square), r=[cq], w=[sq4])
        self.pe(lambda: nc.tensor.matmul(ps[:, 0:n], self.ones_bf[:], sq4[:, c % 2, 0:n], start=True, stop=True),
                r=[sq4, self.ones_bf], w=[ps])
        rs = G["acc"][c % 2]
        self.act(lambda: nc.scalar.activation(rs[:, 0:n], ps[:, 0:n], AF.Ln, bias=EPS, scale=1.0), r=[ps], w=[rs])
        self.act(lambda: nc.scalar.activation(rs[:, 0:n], rs[:, 0:n], AF.Exp, scale=-0.5,
                                              bias=(math.log(128.0 ** -0.5) if c < 4 else 0.0)), r=[rs], w=[rs])
        self.dve(lambda: nc.vector.tensor_tensor(out=cq[:, c, 0:n], in0=cq[:, c, 0:n], in1=rs[:, 0:n], op=ALU.mult),
                 r=[cq, rs], w=[cq])


def gdn_post(self, G, n, e):
    nc = self.nc
    oT, sq4, mix = G["oT"], G["sq4"], G["sqm"]
    for h in range(4):
        ps = self.P[h % 2]
        self.act(lambda: nc.scalar.activation(sq4[:, h % 2, 0:n], oT[:, h, 0:n], AF.Square), r=[oT], w=[sq4])
        self.pe(lambda: nc.tensor.matmul(ps[:, 0:n], self.ones_bf[:], sq4[:, h % 2, 0:n], start=True, stop=True),
                r=[sq4, self.ones_bf], w=[ps])
        rs = G["acc"][h % 2]
        self.act(lambda: nc.scalar.activation(rs[:, 0:n], ps[:, 0:n], AF.Ln, bias=EPS, scale=1.0 / 128.0), r=[ps], w=[rs])
        self.act(lambda: nc.scalar.activation(rs[:, 0:n], rs[:, 0:n], AF.Exp, scale=-0.5), r=[rs], w=[rs])
        self.dve(lambda: nc.vector.scalar_tensor_tensor(out=rs[:, 0:n], in0=oT[:, h, 0:n], scalar=G["sm"][:, 16:17],
                                                        in1=rs[:, 0:n], op0=ALU.mult, op1=ALU.mult),
                 r=[oT, rs, G["sm"]], w=[rs])
        self.dve(lambda: nc.vector.tensor_tensor(out=mix[:, 4 + h, 0:n], in0=rs[:, 0:n], in1=G["zs"][:, h, 0:n], op=ALU.mult),
                 r=[rs, G["zs"]], w=[mix])


def even_outproj(self, G, segs_bufs, c0, n):
    nc = self.nc
    mix = G["sqm"]
    for blk in range(2):
        w = self.ws.get(4096)
        wv = w[:, 0:4096].rearrange("p (k c) -> p k c", k=KC)
        for dc in range(4):
            dch = blk * 4 + dc
            ps = self.P[dch % 2]

            def mm():
                ins = None
                for kc in range(KC):
                    ins = nc.tensor.matmul(ps[:, 0:n], wv[:, kc, dc * 128:(dc + 1) * 128], mix[:, kc, 0:n],
                                           start=(kc == 0), stop=(kc == KC - 1))
                return ins
            self.pe(mm, r=[w, mix], w=[ps])
            xs = self.xT[:, dch, c0:c0 + n]
            self.dve(lambda: nc.vector.tensor_tensor(out=xs, in0=xs, in1=ps[:, 0:n], op=ALU.add),
                     r=[ps] + segs_bufs, w=segs_bufs)


def swa_block(self, G, W, qb, ql):
    nc = self.nc
    mix = G["sqm"]
    sm = G["sm"]
    k0 = (qb - 1) * 128 if qb > 0 else 0
    nk = 256 if qb > 0 else 128
    nkt = nk // 128
    it = 0
    for c in range(4):
        o_ps = self.P[5]
        for kv in range(2):
            h = kv * 4 + c
            rows = slice(kv * 64, (kv + 1) * 64)
            s_ps = self.P[2 + it % 2]
            sb = W["sb"][it % 2]
            pn = W["pn"][it % 2]
            PT = W["PT"][it % 2]
            col = W["col"][it % 2]
            self.pe(lambda: nc.tensor.matmul(s_ps[:, 0:nk], G["qT"][rows, c, ql:ql + 128], G["kTa"][rows, k0:k0 + nk],
                                             start=True, stop=True), r=[G["qT"], G["kTa"]], w=[s_ps])
            self.dve(lambda: nc.vector.scalar_tensor_tensor(out=sb[:, 0:nk], in0=s_ps[:, 0:nk], scalar=0.125,
                                                            in1=G["bias"][:, h, 256 - nk:256], op0=ALU.mult, op1=ALU.add),
                     r=[s_ps, G["bias"]], w=[sb])
            self.dve(lambda: nc.vector.reduce_max(out=col[:, 0:1], in_=sb[:, 0:nk], axis=AX.X), r=[sb], w=[col])
            self.dve(lambda: nc.vector.tensor_tensor(out=col[:, 0:1], in0=col[:, 0:1], in1=sm[:, 8 + h:9 + h], op=ALU.max),
                     r=[col, sm], w=[col])
            self.dve(lambda: nc.vector.tensor_scalar_mul(col[:, 1:2], col[:, 0:1], -1.0), r=[col], w=[col])
            self.act(lambda: nc.scalar.activation(sb[:, 0:nk], sb[:, 0:nk], AF.Exp, bias=col[:, 1:2], scale=1.0),
                     r=[sb, col], w=[sb])
            self.dve(lambda: nc.vector.reduce_sum(out=col[:, 2:3], in_=sb[:, 0:nk], axis=AX.X), r=[sb], w=[col])
            self.act(lambda: nc.scalar.activation(col[:, 3:4], sm[:, 8 + h:9 + h], AF.Exp, bias=col[:, 1:2], scale=1.0),
                     r=[sm, col], w=[col])
            self.dve(lambda: nc.vector.tensor_tensor(out=col[:, 2:3], in0=col[:, 2:3], in1=col[:, 3:4], op=ALU.add),
                     r=[col], w=[col])
            self.dve(lambda: nc.vector.reciprocal(col[:, 4:5], col[:, 2:3]), r=[col], w=[col])
            self.dve(lambda: nc.vector.tensor_scalar_mul(pn[:, 0:nk], sb[:, 0:nk], col[:, 4:5]), r=[sb, col], w=[pn])
            t_ps = self.P[4]
            tv = t_ps[:, :].bitcast(BF16)

            def tr():
                ins = None
                for kt in range(nkt):
                    ins = nc.tensor.transpose(tv[:, kt * 128:(kt + 1) * 128], pn[:, kt * 128:(kt + 1) * 128],
                                              self.ident_bf[:])
                return ins
            self.pe(tr, r=[pn, self.ident_bf], w=[t_ps])
            self.act(lambda: nc.scalar.copy(PT[:, 0:nk], tv[:, 0:nk]), r=[t_ps], w=[PT])

            def pv():
                ins = None
                for kt in range(nkt):
                    ins = nc.tensor.matmul(o_ps[:, 0:128], G["Va"][:, k0 // 128 + kt, kv * 128:(kv + 1) * 128],
                                           PT[:, kt * 128:(kt + 1) * 128],
                                           start=(kv == 0 and kt == 0), stop=(kv == 1 and kt == nkt - 1))
                return ins
            self.pe(pv, r=[G["Va"], PT], w=[o_ps])
            it += 1
        self.act(lambda: nc.scalar.copy(mix[:, c, ql:ql + 128], o_ps[:, 0:128]), r=[o_ps], w=[mix])


def gdn_subtile(self, G, W, st, cs):
    nc = self.nc
    cq = G["cq"]
    mk = self.mk
    ident = self.ident
    beta_t, g_t = G["beta_t"], G["g_t"]
    sc = W["sc"]
    R = W["R"]
    P = self.P
    self.pe(lambda: nc.tensor.matmul(P[2][:, 0:4], mk[:, 0, :], g_t[:, st, :], start=True, stop=True),
            r=[mk, g_t], w=[P[2]])
    self.pe(lambda: nc.tensor.matmul(P[2][:, 4:8], mk[:, 4, :], g_t[:, st, :], start=True, stop=True),
            r=[mk, g_t], w=[P[2]])
    self.dve(lambda: nc.vector.tensor_copy(sc[:, 0:8], P[2][:, 0:8]), r=[P[2]], w=[sc])
    self.dve(lambda: nc.vector.tensor_tensor(out=sc[:, 8:12], in0=sc[:, 4:8], in1=sc[:, 0:4], op=ALU.subtract), r=[sc], w=[sc])
    self.act(lambda: nc.scalar.activation(sc[:, 8:12], sc[:, 8:12], AF.Exp), r=[sc], w=[sc])
    self.act(lambda: nc.scalar.activation(sc[:, 12:16], sc[:, 0:4], AF.Exp), r=[sc], w=[sc])
    self.dve(lambda: nc.vector.tensor_tensor(out=sc[:, 16:20], in0=sc[:, 12:16], in1=beta_t[:, st, :], op=ALU.mult),
             r=[sc, beta_t], w=[sc])
    for h in range(4):
        self.dve(lambda: nc.vector.tensor_scalar_mul(R[:, h, 0:128], mk[:, 0, :], g_t[:, st, h:h + 1]), r=[mk, g_t], w=[R])
        self.dve(lambda: nc.vector.tensor_scalar_mul(R[:, h, 128:256], ident[:], beta_t[:, st, h:h + 1]),
                 r=[ident, beta_t], w=[R])
    for hh in range(2):
        self.pe(lambda: nc.tensor.matmul(P[hh][:, 0:512], self.ones_f[:], R[:, 2 * hh:2 * hh + 2, :], start=True, stop=True),
                r=[self.ones_f, R], w=[P[hh]])

    def gcbc(h):
        return P[h // 2][:, (h % 2) * 256:(h % 2) * 256 + 128]

    def betabc(h):
        return P[h // 2][:, (h % 2) * 256 + 128:(h % 2) * 256 + 256]
    for h in range(4):
        self.act(lambda: nc.scalar.activation(sc[:, 20 + 2 * h:22 + 2 * h], gcbc(h)[:, 63:128:64], AF.Exp),
                 r=[P[h // 2]], w=[sc])
    gstop = self.cfg.get("gdn_stop", 6)
    if gstop <= 1:
        return
    for h in range(4):
        tw = W["set"][0]
        kn = cq[:, 4 + h, cs:cs + 128]
        qn = cq[:, h, cs:cs + 128]
        vv = cq[:, 8 + h, cs:cs + 128]
        bc = P[h // 2]
        self.pe(lambda: nc.tensor.transpose(P[2][:, 128:256], kn, ident[:]), r=[cq, ident], w=[P[2]])
        self.pe(lambda: nc.tensor.transpose(P[2][:, 256:384], vv, ident[:]), r=[cq, ident], w=[P[2]])
        self.dve(lambda: nc.vector.tensor_scalar_mul(tw["kbg"][:], P[2][:, 128:256], sc[:, 16 + h:17 + h]), r=[P[2], sc], w=[tw["kbg"]])
        self.dve(lambda: nc.vector.tensor_scalar_mul(tw["kd"][:], P[2][:, 128:256], sc[:, 8 + h:9 + h]), r=[P[2], sc], w=[tw["kd"]])
        self.dve(lambda: nc.vector.tensor_scalar_mul(tw["vb"][:], P[2][:, 256:384], beta_t[:, st, h:h + 1]),
                 r=[P[2], beta_t], w=[tw["vb"]])
        if gstop <= 2:
            continue
        self.pe(lambda: nc.tensor.matmul(P[3][:, 0:128], kn, kn, start=True, stop=True), r=[cq], w=[P[3]])
        self.pe(lambda: nc.tensor.matmul(P[3][:, 128:256], kn, qn, start=True, stop=True), r=[cq], w=[P[3]])
        self.dve(lambda: nc.vector.scalar_tensor_tensor(out=tw["tA"][:], in0=gcbc(h), scalar=sc[:, h:h + 1], in1=mk[:, 1, :],
                                                        op0=ALU.subtract, op1=ALU.max), r=[bc, sc, mk], w=[tw["tA"]])
        self.act(lambda: nc.scalar.activation(tw["Es"][:], tw["tA"][:], AF.Exp, scale=-1.0), r=[tw["tA"]], w=[tw["Es"]])
        self.dve(lambda: nc.vector.scalar_tensor_tensor(out=tw["tB"][:], in0=gcbc(h), scalar=sc[:, h:h + 1], in1=mk[:, 2, :],
                                                        op0=ALU.subtract, op1=ALU.min), r=[bc, sc, mk], w=[tw["tB"]])
        self.act(lambda: nc.scalar.activation(tw["ETd"][:], tw["tB"][:], AF.Exp), r=[tw["tB"]], w=[tw["ETd"]])
        self.dve(lambda: nc.vector.tensor_tensor(out=tw["ETs"][:], in0=tw["ETd"][:], in1=mk[:, 3, :], op=ALU.mult),
                 r=[tw["ETd"], mk], w=[tw["ETs"]])
        Pc, Qc, X, Y = tw["Pm"][0], tw["Qm"][0], tw["X"][0], tw["Y"][0]
        self.dve(lambda: nc.vector.scalar_tensor_tensor(out=Qc[:], in0=P[3][:, 0:128], scalar=beta_t[:, st, h:h + 1],
                                                        in1=tw["Es"][:], op0=ALU.mult, op1=ALU.mult),
                 r=[P[3], beta_t, tw["Es"]], w=[Qc])
        self.dve(lambda: nc.vector.tensor_tensor(out=tw["tA"][:], in0=P[3][:, 0:128], in1=tw["ETs"][:], op=ALU.mult),
                 r=[P[3], tw["ETs"]], w=[tw["tA"]])
        self.dve(lambda: nc.vector.tensor_tensor(out=Pc[:], in0=tw["tA"][:], in1=betabc(h), op=ALU.mult),
                 r=[tw["tA"], bc], w=[Pc])
        self.dve(lambda: nc.vector.tensor_tensor(out=tw["AqkT"][:], in0=P[3][:, 128:256], in1=tw["ETd"][:], op=ALU.mult),
                 r=[P[3], tw["ETd"]], w=[tw["AqkT"]])
        self.act(lambda: nc.scalar.activation(tw["tB"][:], gcbc(h), AF.Exp), r=[bc], w=[tw["tB"]])
        self.dve(lambda: nc.vector.tensor_tensor(out=tw["qgT"][:], in0=qn, in1=tw["tB"][:], op=ALU.mult),
                 r=[cq, tw["tB"]], w=[tw["qgT"]])
        if gstop <= 3:
            continue
        self.dve(lambda: nc.vector.tensor_tensor(out=X[:], in0=ident[:], in1=Pc[:], op=ALU.subtract), r=[ident, Pc], w=[X])
        self.dve(lambda: nc.vector.tensor_tensor(out=Y[:], in0=ident[:], in1=Qc[:], op=ALU.subtract), r=[ident, Qc], w=[Y])
        for k in range(1, 6):
            bk = P[4 + k % 2]
            last = (k == 5)
            Pn, Qn, Xn, Yn = tw["Pm"][k % 2], tw["Qm"][k % 2], tw["X"][k % 2], tw["Y"][k % 2]

            def sqr():
                ins = nc.tensor.matmul(bk[:, 0:128], Qc[:], Pc[:], start=True, stop=True)
                if not last:
                    ins = nc.tensor.matmul(bk[:, 128:256], Pc[:], Qc[:], start=True, stop=True)
                return ins
            self.pe(sqr, r=[Pc, Qc], w=[bk])
            self.act(lambda: nc.scalar.copy(Pn[:], bk[:, 0:128]), r=[bk], w=[Pn])
            if not last:
                self.dve(lambda: nc.vector.tensor_copy(Qn[:], bk[:, 128:256]), r=[bk], w=[Qn])

            def upd():
                ins = nc.tensor.matmul(bk[:, 256:384], Y[:], Pn[:], start=True, stop=True)
                if not last:
                    ins = nc.tensor.matmul(bk[:, 384:512], X[:], Qn[:], start=True, stop=True)
                return ins
            self.pe(upd, r=[X, Y, Pn] + ([] if last else [Qn]), w=[bk])
            self.dve(lambda: nc.vector.tensor_tensor(out=Xn[:], in0=X[:], in1=bk[:, 256:384], op=ALU.add), r=[X, bk], w=[Xn])
            if not last:
                self.dve(lambda: nc.vector.tensor_tensor(out=Yn[:], in0=Y[:], in1=bk[:, 384:512], op=ALU.add), r=[Y, bk], w=[Yn])
            Pc, Qc, X, Y = Pn, Qn, Xn, Yn
        if gstop <= 4:
            continue
        self.pe(lambda: nc.tensor.matmul(P[6][:, 0:128], X[:], tw["vb"][:], start=True, stop=True), r=[X, tw["vb"]], w=[P[6]])
        self.pe(lambda: nc.tensor.matmul(P[6][:, 128:256], tw["kbg"][:], X[:], start=True, stop=True), r=[X, tw["kbg"]], w=[P[6]])
        self.dve(lambda: nc.vector.tensor_copy(tw["u"][:], P[6][:, 0:128]), r=[P[6]], w=[tw["u"]])
        self.dve(lambda: nc.vector.tensor_copy(tw["wTa"][:, 0:64], P[6][:, 128:192]), r=[P[6]], w=[tw["wTa"]])
        self.dve(lambda: nc.vector.tensor_copy(tw["wTz"][:, 64:128], P[6][:, 192:256]), r=[P[6]], w=[tw["wTz"]])
        Sst, Sbf, oT = G["Sst"], G["Sbf"], G["oT"]
        if gstop <= 5:
            continue
        for ck in range(2):
            rr = slice(ck * 64, (ck + 1) * 64)
            if ck == 0:
                self.pe(lambda: nc.tensor.matmul(P[6][0:64, 256:384], tw["wTa"][:, 0:64], Sbf[:, h, :], start=True, stop=True),
                        r=[tw["wTa"], Sbf], w=[P[6]])
            else:
                self.pe(lambda: nc.tensor.matmul(P[6][:, 256:384], tw["wTz"][:], Sbf[:, h, :], start=True, stop=True),
                        r=[tw["wTz"], Sbf], w=[P[6]])
            self.dve(lambda: nc.vector.tensor_tensor(out=tw["vnew"][rr, :], in0=tw["u"][rr, :], in1=P[6][rr, 256:384],
                                                     op=ALU.subtract), r=[tw["u"], P[6]], w=[tw["vnew"]])

            def omm():
                nc.tensor.matmul(P[7][:, ck * 64:(ck + 1) * 64], Sbf[:, h, :], tw["qgT"][:, rr], start=True, stop=False)
                return nc.tensor.matmul(P[7][:, ck * 64:(ck + 1) * 64], tw["vnew"][rr, :], tw["AqkT"][rr, rr],
                                        start=False, stop=True)
            self.pe(omm, r=[Sbf, tw["qgT"], tw["vnew"], tw["AqkT"]], w=[P[7]])
            self.act(lambda: nc.scalar.copy(oT[:, h, cs + ck * 64:cs + (ck + 1) * 64], P[7][:, ck * 64:(ck + 1) * 64]),
                     r=[P[7]], w=[oT])
            self.pe(lambda: nc.tensor.matmul(P[6][:, 384:512], tw["kd"][rr, :], tw["vnew"][rr, :], start=True, stop=True),
                    r=[tw["kd"], tw["vnew"]], w=[P[6]])
            self.dve(lambda: nc.vector.scalar_tensor_tensor(out=Sst[:, h, :], in0=Sst[:, h, :],
                                                            scalar=sc[:, 20 + 2 * h + ck:21 + 2 * h + ck],
                                                            in1=P[6][:, 384:512], op0=ALU.mult, op1=ALU.add),
                     r=[Sst, sc, P[6]], w=[Sst])
            self.act(lambda: nc.scalar.copy(Sbf[:, h, :], Sst[:, h, :]), r=[Sst], w=[Sbf])


def even_mixer(self, l):
    nc = self.nc
    e = l // 2
    nsub = self.seq // 128
    self.begin()
    G = even_alloc(self, 512)
    T = self.T
    G["pre"] = [T("pre%d" % i, [128, 515], F32) for i in range(2)]
    G["kTa"] = T("kTa", [128, self.seq], BF16)
    G["Va"] = T("Va", [128, nsub, 256], BF16)
    G["carry"] = T("carry", [128, 12, 3], F32)
    G["bias"] = T("bias", [128, 8, 256], F32)
    G["Sst"] = T("Sst", [128, 4, 128], F32)
    G["Sbf"] = T("Sbf", [128, 4, 128], BF16)
    G["beta_t"] = T("beta_t", [128, 4, 4], F32)
    G["g_t"] = T("g_t", [128, 4, 4], F32)
    gtmp = T("gtmp", [128, 4], F32)
    kvo = T("kvo", [128, 256], F32)
    gco = T("gco", [128, 1536], F32)
    W = {"sb": [T("sb%d" % i, [128, 256], F32) for i in range(2)],
         "pn": [T("pn%d" % i, [128, 256], BF16) for i in range(2)],
         "PT": [T("PT%d" % i, [128, 256], BF16) for i in range(2)],
         "col": [T("col%d" % i, [128, 8], F32) for i in range(2)],
         "sc": T("sc", [128, 40], F32),
         "R": T("R", [128, 4, 256], F32),
         "set": []}
    for i in range(1):
        d = {}
        for nm in ("kbg", "vb", "tA", "Es", "tB", "ETd", "ETs", "u"):
            d[nm] = T("%s%d" % (nm, i), [128, 128], F32)
        for nm in ("kd", "AqkT", "qgT", "wTa", "wTz", "vnew"):
            d[nm] = T("%s%d" % (nm, i), [128, 128], BF16)
        for nm in ("Pm", "Qm", "X", "Y"):
            d[nm] = [T("%s%d_%d" % (nm, i, j), [128, 128], F32) for j in range(2)]
        W["set"].append(d)
        self.dve(lambda: nc.vector.memset(d["wTz"][:], 0.0), w=[d["wTz"]])
    even_common(self, G, e)
    self.dma(G["bias"][:].rearrange("p a b -> p (a b)"), self.swabias, r=[self.Bin], w=[G["bias"]])
    self.dve(lambda: nc.vector.memset(G["carry"][:], 0.0), w=[G["carry"]])
    self.dve(lambda: nc.vector.memset(G["Sst"][:], 0.0), w=[G["Sst"]])
    self.dve(lambda: nc.vector.memset(G["Sbf"][:], 0.0), w=[G["Sbf"]])
    cw = G["cw"]
    for ti in range(self.nt):
        c0 = ti * 512
        xb = [self.xTb[ti]]
        self.rmsnorm_cols(xb, c0, 512, 4 + l, G["hn"], 0, G["sqm"], G["rstd"])
        cnt = [0]

        def conv_fn(ch, ps):
            pre = G["pre"][cnt[0] % 2]
            acc = G["acc"][cnt[0] % 2]
            cnt[0] += 1
            self.dve(lambda: nc.vector.tensor_copy(pre[:, 0:3], G["carry"][:, ch, :]), r=[G["carry"]], w=[pre])
            self.act(lambda: nc.scalar.copy(pre[:, 3:515], ps[:, 0:512]), r=[ps], w=[pre])
            self.dve(lambda: nc.vector.tensor_copy(G["carry"][:, ch, :], pre[:, 512:515]), r=[pre], w=[G["carry"]])
            self.dve(lambda: nc.vector.tensor_scalar_mul(acc[:], pre[:, 0:512], cw[:, ch, 0:1]), r=[pre, cw], w=[acc])
            for i in range(1, 4):
                self.dve(lambda: nc.vector.scalar_tensor_tensor(out=acc[:], in0=pre[:, i:i + 512], scalar=cw[:, ch, i:i + 1],
                                                                in1=acc[:], op0=ALU.mult, op1=ALU.add), r=[pre, cw, acc], w=[acc])
            self.act(lambda: nc.scalar.activation(G["cq"][:, ch, :], acc[:], AF.Silu), r=[acc], w=[G["cq"]])
        even_inproj_fm(self, G, 512, conv_fn, G["kTa"][:, c0:c0 + 512])
        w = self.ws.get(EW_N[6])
        wv = w[:, 0:EW_N[6]].rearrange("p (k c) -> p k c", k=KC)
        for st in range(4):
            gs = ti * 4 + st
            ps = self.P[2 + st % 2]

            def mm():
                ins = None
                for kc in range(KC):
                    ins = nc.tensor.matmul(ps[:, 0:EV_TOK], G["hn"][:, kc, st * 128:(st + 1) * 128], wv[:, kc, :],
                                           start=(kc == 0), stop=(kc == KC - 1))
                return ins
            self.pe(mm, r=[w, G["hn"]], w=[ps])
            self.act(lambda: nc.scalar.copy(G["Va"][:, gs, :], ps[:, 0:256]), r=[ps], w=[G["Va"]])
            gates_from_ba(self, G, ps[:, 384:392], 128, G["beta_t"][:, st, :], G["g_t"][:, st, :], gtmp, [ps],
                          [G["beta_t"], G["g_t"]])
            if gs == nsub - 1 and not self.cfg.get("skip_kvo"):
                self.act(lambda: nc.scalar.copy(kvo[:, 0:128], ps[:, 256:384]), r=[ps], w=[kvo])
                self.act(lambda: nc.scalar.copy(kvo[:, 128:256], ps[:, 0:128]), r=[ps], w=[kvo])
                self.dve(lambda: nc.vector.tensor_tensor(out=kvo[:, 128:256], in0=kvo[:, 128:256], in1=ps[:, 128:256], op=ALU.add),
                         r=[ps, kvo], w=[kvo])
                pass
        if self.cfg.get("skip_swa") or self.cfg.get("skip_gdn") or self.cfg.get("gdn_stop", 6) < 6:
            self.dve(lambda: nc.vector.memset(G["sqm"][:], 0.0), w=[G["sqm"]])
            self.dve(lambda: nc.vector.memset(G["oT"][:], 0.0), w=[G["oT"]])
        if not self.cfg.get("skip_swa"):
            for qi in range(4):
                swa_block(self, G, W, ti * 4 + qi, qi * 128)
        l2norm_heads(self, G, 512)
        if not self.cfg.get("skip_gdn"):
            for st in range(4):
                gdn_subtile(self, G, W, st, st * 128)
        gdn_post(self, G, 512, e)
        even_outproj(self, G, xb, c0, 512)
    self.out_dma(self.o_pk[e], kvo[:, 0:128], r=[kvo])
    self.out_dma(self.o_pv[e], kvo[:, 128:256], r=[kvo])
    for ch in range(12):
        self.out_dma(self.o_pgc[e][:, ch * 128:(ch + 1) * 128].rearrange("i p -> p i"), G["carry"][:, ch, :], r=[G["carry"]])
    self.out_dma(self.o_pgs[e].rearrange("h k v -> k h v"), G["Sst"][:], r=[G["Sst"]])
    self.end()
    if self.cfg.get("skip_dec"):
        for b in range(7):
            self.ws.get(EW_N[b])
        for b in range(2):
            self.ws.get(4096)
    else:
        even_decode(self, l)


def even_decode(self, l):
    nc = self.nc
    e = l // 2
    n = NS
    c0 = self.seq
    P = self.P
    ident, mk = self.ident, self.mk
    self.begin()
    T = self.T
    G = even_alloc(self, n)
    G["kTa"] = T("kTs", [128, n], BF16)
    xx = T("xx", [128, 12, n, 4], F32)
    hs = T("hs", [48, 1536], F32)
    gnew = T("gnew", [n, 1536], F32)
    knv = T("knv", [n, 256], F32)
    gts = T("gts", [n, 16], F32)
    Kc = T("Kc", [128, n, 128], F32)
    Vc = T("Vc", [128, n, 128], F32)
    Qb = T("Qb", [128, n, 8], F32)
    KTb = [T("KTb%d" % i, [128, 128], F32) for i in range(2)]
    sT = T("sT", [128, 128], F32)
    kTn = T("kTn", [128, n], F32)
    vTn = T("vTn", [128, n], F32)
    sf = T("sf", [128, 132], F32)
    pnf = T("pnf", [128, 132], F32)
    col = T("dcol", [128, 8], F32)
    PTs = T("PTs", [128, 128], F32)
    R2 = T("R2", [128, 128], F32)
    ov = T("ov", [128, n, 8], F32)
    dbias = T("dbias", [128, 129], F32)
    skc = T("skc", [128, 1], F32)
    R3 = T("R3", [n, 2, 4, n], F32)
    bcs = T("bcs", [128, 2, 4, n], F32)
    Sd = T("Sd", [128, n, 4, 128], F32)
    vnT = T("vnT", [128, 4, n], F32)
    t1 = T("t1", [128, 4, n], F32)
    krow = T("krow", [n, 512], F32)
    vrow = T("vrow", [n, 512], F32)
    Km = [T("Km%d" % i, [n, 512], F32) for i in range(2)]
    tS = T("tS", [128, 4, 128], F32)
    mix = G["sqm"]
    even_common(self, G, e)
    self.dma(dbias[:], self.decbias, r=[self.Bin], w=[dbias])
    self.dma(skc[:], self.esk[e], r=[self.Bin], w=[skc])
    self.dma(hs[:], self.s_gc[e].rearrange("b i c -> (b i) c"), r=[self.Bin], w=[hs])
    self.dma(Kc[:], self.c_k[e].rearrange("b w f -> w b f"), r=[self.Bin], w=[Kc])
    self.dma(Vc[:], self.c_v[e].rearrange("b w f -> w b f"), r=[self.Bin], w=[Vc])
    self.dma(Sd[:], self.s_gs[e].rearrange("b h k v -> k b h v"), r=[self.Bin], w=[Sd])
    for ch in range(12):
        ps = P[2 + ch % 2]
        self.pe(lambda: nc.tensor.transpose(ps[:, 0:48], hs[:, ch * 128:(ch + 1) * 128], ident[0:48, 0:48]),
                r=[hs, ident], w=[ps])
        self.dve(lambda: nc.vector.tensor_copy(xx[:, ch, :, 0:3], ps[:, 0:48].rearrange("p (b i) -> p b i", i=3)),
                 r=[ps], w=[xx])
    xb = [self.xTb[self.nt]]
    self.rmsnorm_cols(xb, c0, n, 4 + l, G["hn"], 0, G["sqm"], G["rstd"])
    cw = G["cw"]
    cnt = [0]

    def conv_fn(ch, ps):
        acc = G["acc"][cnt[0] % 2]
        cnt[0] += 1
        self.act(lambda: nc.scalar.copy(xx[:, ch, :, 3], ps[:, 0:n]), r=[ps], w=[xx])
        self.dve(lambda: nc.vector.tensor_scalar_mul(acc[:, 0:n], xx[:, ch, :, 0], cw[:, ch, 0:1]), r=[xx, cw], w=[acc])
        for i in range(1, 4):
            self.dve(lambda: nc.vector.scalar_tensor_tensor(out=acc[:, 0:n], in0=xx[:, ch, :, i], scalar=cw[:, ch, i:i + 1],
                                                            in1=acc[:, 0:n], op0=ALU.mult, op1=ALU.add), r=[xx, cw, acc], w=[acc])
        self.act(lambda: nc.scalar.activation(G["cq"][:, ch, 0:n], acc[:, 0:n], AF.Silu), r=[acc], w=[G["cq"]])
        pt = P[6]
        self.pe(lambda: nc.tensor.transpose(pt[0:n, (ch % 4) * 128:(ch % 4 + 1) * 128], xx[:, ch, :, 3], ident[:]),
                r=[xx, ident], w=[pt])
        self.dve(lambda: nc.vector.tensor_copy(gnew[:, ch * 128:(ch + 1) * 128], pt[0:n, (ch % 4) * 128:(ch % 4 + 1) * 128]),
                 r=[pt], w=[gnew])
    even_inproj_fm(self, G, n, conv_fn, G["kTa"][:, 0:n])
    w = self.ws.get(EW_N[6])
    wv = w[:, 0:EW_N[6]].rearrange("p (k c) -> p k c", k=KC)
    ps = P[2]

    def mm():
        ins = None
        for kc in range(KC):
            ins = nc.tensor.matmul(ps[0:n, 0:EV_TOK], G["hn"][:, kc, 0:n], wv[:, kc, :], start=(kc == 0), stop=(kc == KC - 1))
        return ins
    self.pe(mm, r=[w, G["hn"]], w=[ps])
    self.act(lambda: nc.scalar.copy(knv[:, 0:128], ps[0:n, 256:384]), r=[ps], w=[knv])
    self.act(lambda: nc.scalar.copy(knv[:, 128:256], ps[0:n, 0:128]), r=[ps], w=[knv])
    self.dve(lambda: nc.vector.tensor_tensor(out=knv[:, 128:256], in0=knv[:, 128:256], in1=ps[0:n, 128:256], op=ALU.add),
             r=[ps, knv], w=[knv])
    gates_from_ba(self, G, ps[0:n, 384:392], n, gts[:, 0:4], gts[:, 4:8], gts_tmp(gts), [ps], [gts])
    self.act(lambda: nc.scalar.activation(gts[:, 8:12], gts[:, 4:8], AF.Exp), r=[gts], w=[gts])
    self.dve(lambda: nc.vector.memset(Qb[:], 0.0), w=[Qb])
    for c in range(4):
        self.dve(lambda: nc.vector.tensor_copy(Qb[0:64, :, c], G["qT"][0:64, c, 0:n]), r=[G["qT"]], w=[Qb])
        self.dve(lambda: nc.vector.tensor_copy(Qb[64:128, :, 4 + c], G["qT"][64:128, c, 0:n]), r=[G["qT"]], w=[Qb])
    self.dve(lambda: nc.vector.tensor_copy(kTn[:], G["kTa"][:, 0:n]), r=[G["kTa"]], w=[kTn])
    for b in range(n):
        kp = P[b % 2]
        kt = KTb[b % 2]
        self.pe(lambda: nc.tensor.transpose(kp[:, 0:128], Kc[:, b, :], ident[:]), r=[Kc, ident], w=[kp])
        self.act(lambda: nc.scalar.copy(kt[:], kp[:, 0:128]), r=[kp], w=[kt])
        self.pe(lambda: nc.tensor.matmul(P[4][:, b * 8:(b + 1) * 8], kt[:], Qb[:, b, :], start=True, stop=True),
                r=[kt, Qb], w=[P[4]])
    self.dve(lambda: nc.vector.tensor_copy(sT[:], P[4][:, 0:128]), r=[P[4]], w=[sT])
    self.pe(lambda: nc.tensor.transpose(P[5][:, 0:128], sT[:], ident[:]), r=[sT, ident], w=[P[5]])
    self.pe(lambda: nc.tensor.matmul(P[5][:, 128:128 + n], Qb[:].rearrange("p b h -> p (b h)"), kTn[:], start=True, stop=True),
            r=[Qb, kTn], w=[P[5]])
    self.dve(lambda: nc.vector.tensor_tensor(out=sf[:, 0:n], in0=P[5][:, 128:128 + n], in1=mk[:, 5, 0:n], op=ALU.mult),
             r=[P[5], mk], w=[sf])
    self.dve(lambda: nc.vector.reduce_sum(out=col[:, 5:6], in_=sf[:, 0:n], axis=AX.X), r=[sf], w=[col])
    self.dve(lambda: nc.vector.scalar_tensor_tensor(out=sf[:, 0:128], in0=P[5][:, 0:128], scalar=0.125, in1=dbias[:, 0:128],
                                                    op0=ALU.mult, op1=ALU.add), r=[P[5], dbias], w=[sf])
    self.dve(lambda: nc.vector.scalar_tensor_tensor(out=sf[:, 128:129], in0=col[:, 5:6], scalar=0.125, in1=dbias[:, 128:129],
                                                    op0=ALU.mult, op1=ALU.add), r=[col, dbias], w=[sf])
    self.dve(lambda: nc.vector.reduce_max(out=col[:, 0:1], in_=sf[:, 0:129], axis=AX.X), r=[sf], w=[col])
    self.dve(lambda: nc.vector.tensor_tensor(out=col[:, 0:1], in0=col[:, 0:1], in1=skc[:, 0:1], op=ALU.max), r=[col, skc], w=[col])
    self.dve(lambda: nc.vector.tensor_scalar_mul(col[:, 1:2], col[:, 0:1], -1.0), r=[col], w=[col])
    self.act(lambda: nc.scalar.activation(sf[:, 0:129], sf[:, 0:129], AF.Exp, bias=col[:, 1:2], scale=1.0), r=[sf, col], w=[sf])
    self.dve(lambda: nc.vector.reduce_sum(out=col[:, 2:3], in_=sf[:, 0:129], axis=AX.X), r=[sf], w=[col])
    self.act(lambda: nc.scalar.activation(col[:, 3:4], skc[:, 0:1], AF.Exp, bias=col[:, 1:2], scale=1.0), r=[skc, col], w=[col])
    self.dve(lambda: nc.vector.tensor_tensor(out=col[:, 2:3], in0=col[:, 2:3], in1=col[:, 3:4], op=ALU.add), r=[col], w=[col])
    self.dve(lambda: nc.vector.reciprocal(col[:, 4:5], col[:, 2:3]), r=[col], w=[col])
    self.dve(lambda: nc.vector.tensor_scalar_mul(pnf[:, 0:129], sf[:, 0:129], col[:, 4:5]), r=[sf, col], w=[pnf])
    self.pe(lambda: nc.tensor.transpose(P[4][:, 128:256], pnf[:, 0:128], ident[:]), r=[pnf, ident], w=[P[4]])
    self.act(lambda: nc.scalar.copy(PTs[:], P[4][:, 128:256]), r=[P[4]], w=[PTs])
    for b in range(n):
        self.pe(lambda: nc.tensor.matmul(P[6][:, 256 + b * 8:256 + (b + 1) * 8], Vc[:, b, :], PTs[:, b * 8:(b + 1) * 8],
                                         start=True, stop=True), r=[Vc, PTs], w=[P[6]])
    self.pe(lambda: nc.tensor.transpose(P[7][:, 0:n], knv[:, 128:256], ident[0:n, 0:n]), r=[knv, ident], w=[P[7]])
    self.dve(lambda: nc.vector.tensor_copy(vTn[:], P[7][:, 0:n]), r=[P[7]], w=[vTn])
    self.dve(lambda: nc.vector.tensor_scalar_mul(R2[:], ident[:], pnf[:, 128:129]), r=[ident, pnf], w=[R2])
    self.pe(lambda: nc.tensor.matmul(P[7][:, 128:256], self.ones_f[:], R2[:], start=True, stop=True), r=[self.ones_f, R2], w=[P[7]])
    self.dve(lambda: nc.vector.tensor_tensor(out=ov[:], in0=P[7][:, 128:256].rearrange("p (b h) -> p b h", h=8),
                                             in1=vTn[:].unsqueeze(2).to_broadcast([128, n, 8]), op=ALU.mult),
             r=[P[7], vTn], w=[ov])
    self.dve(lambda: nc.vector.tensor_tensor(out=ov[:], in0=ov[:], in1=P[6][:, 256:384].rearrange("p (b h) -> p b h", h=8), op=ALU.add),
             r=[ov, P[6]], w=[ov])
    for c in range(4):
        self.dve(lambda: nc.vector.tensor_copy(mix[0:64, c, 0:n], ov[0:64, :, c]), r=[ov], w=[mix])
        self.dve(lambda: nc.vector.tensor_copy(mix[64:128, c, 0:n], ov[64:128, :, 4 + c]), r=[ov], w=[mix])
    l2norm_heads(self, G, n)
    cq = G["cq"]
    for t in range(2):
        src = gts[:, 0:4] if t == 0 else gts[:, 8:12]
        self.dve(lambda: nc.vector.tensor_tensor(out=R3[:, t, :, :], in0=src.unsqueeze(2).to_broadcast([n, 4, n]),
                                                 in1=ident[0:n, 0:n].unsqueeze(1).to_broadcast([n, 4, n]), op=ALU.mult),
                 r=[gts, ident], w=[R3])
    self.pe(lambda: nc.tensor.matmul(P[0][:, 0:128], self.ones_f[0:n, :], R3[:].rearrange("p t h b -> p (t h b)"),
                                     start=True, stop=True), r=[self.ones_f, R3], w=[P[0]])
    self.dve(lambda: nc.vector.tensor_copy(bcs[:].rearrange("p t h b -> p (t h b)"), P[0][:, 0:128]), r=[P[0]], w=[bcs])
    for b in range(n):
        for h in range(4):
            self.pe(lambda: nc.tensor.matmul(P[1][:, h * n + b:h * n + b + 1], Sd[:, b, h, :], cq[:, 4 + h, b:b + 1],
                                             start=True, stop=True), r=[Sd, cq], w=[P[1]])
    kSv = P[1][:, 0:4 * n].rearrange("p (h b) -> p h b", h=4)
    self.dve(lambda: nc.vector.tensor_tensor(out=t1[:], in0=kSv, in1=bcs[:, 1, :, :], op=ALU.mult), r=[P[1], bcs], w=[t1])
    self.dve(lambda: nc.vector.tensor_tensor(out=t1[:], in0=cq[:, 8:12, 0:n], in1=t1[:], op=ALU.subtract), r=[cq, t1], w=[t1])
    self.dve(lambda: nc.vector.tensor_tensor(out=vnT[:], in0=t1[:], in1=bcs[:, 0, :, :], op=ALU.mult), r=[t1, bcs], w=[vnT])
    for h in range(4):
        self.pe(lambda: nc.tensor.transpose(P[2][0:n, h * 128:(h + 1) * 128], cq[:, 4 + h, 0:n], ident[:]), r=[cq, ident], w=[P[2]])
        self.pe(lambda: nc.tensor.transpose(P[3][0:n, h * 128:(h + 1) * 128], vnT[:, h, :], ident[:]), r=[vnT, ident], w=[P[3]])
    self.act(lambda: nc.scalar.copy(krow[:], P[2][0:n, :]), r=[P[2]], w=[krow])
    self.dve(lambda: nc.vector.tensor_copy(vrow[:], P[3][0:n, :]), r=[P[3]], w=[vrow])
    for b in range(n):
        km = Km[b % 2]
        pp = P[4 + b % 2]
        self.dve(lambda: nc.vector.tensor_scalar_mul(km[:], krow[:], ident[0:n, b:b + 1]), r=[krow, ident], w=[km])

        def mm4():
            ins = None
            for h in range(4):
                ins = nc.tensor.matmul(pp[:, h * 128:(h + 1) * 128], km[:, h * 128:(h + 1) * 128], vrow[:, h * 128:(h + 1) * 128],
                                       start=True, stop=True)
            return ins
        self.pe(mm4, r=[km, vrow], w=[pp])
        self.dve(lambda: nc.vector.tensor_tensor(out=tS[:], in0=Sd[:, b, :, :],
                                                 in1=bcs[:, 1, :, b:b + 1].to_broadcast([128, 4, 128]), op=ALU.mult),
                 r=[Sd, bcs], w=[tS])
        self.dve(lambda: nc.vector.tensor_tensor(out=Sd[:, b, :, :], in0=tS[:], in1=pp[:, :].rearrange("p (h v) -> p h v", h=4), op=ALU.add),
                 r=[tS, pp], w=[Sd])
    for b in range(n):
        for h in range(4):
            self.pe(lambda: nc.tensor.matmul(P[1][:, 64 + h * n + b:64 + h * n + b + 1], Sd[:, b, h, :], cq[:, h, b:b + 1],
                                             start=True, stop=True), r=[Sd, cq], w=[P[1]])
    self.act(lambda: nc.scalar.copy(G["oT"][:, :, 0:n], P[1][:, 64:64 + 4 * n].rearrange("p (h b) -> p h b", h=4)), r=[P[1]], w=[G["oT"]])
    gdn_post(self, G, n, e)
    even_outproj(self, G, xb, c0, n)
    self.out_dma(self.o_sk[e][:, 0:127, :], self.c_k[e][:, 1:128, :], r=[self.Bin])
    self.out_dma(self.o_sv[e][:, 0:127, :], self.c_v[e][:, 1:128, :], r=[self.Bin])
    self.out_dma(self.o_sk[e][:, 127, :], knv[:, 0:128], r=[knv])
    self.out_dma(self.o_sv[e][:, 127, :], knv[:, 128:256], r=[knv])
    self.out_dma(self.o_sgc[e][:, 0:2, :], self.s_gc[e][:, 1:3, :], r=[self.Bin])
    self.out_dma(self.o_sgc[e][:, 2, :], gnew[:], r=[gnew])
    self.out_dma(self.o_sgs[e].rearrange("b h k v -> k b h v"), Sd[:], r=[Sd])
    self.end()


def gts_tmp(gts):
    class _V:
        b = gts.b

        def __getitem__(self, k):
            rows, cols = k
            return gts.t[rows, 12 + cols.start:12 + cols.stop]
    return _V()


Prog.even_mixer = even_mixer


OW_DT = 256
H_C, P_C, N_C, G_C = 32, 64, 128, 4


def odd_consts(self, O, e):
    nc = self.nc
    self.dma(O["cw"][:].rearrange("p a b -> p (a b)"), self.oconv[e], r=[self.Bin], w=[O["cw"]])
    self.dma(O["hb"][:], self.ohead[e], r=[self.Bin], w=[O["hb"]])
    self.dma(O["fc"][:], self.ofeat[e], r=[self.Bin], w=[O["fc"]])
    self.act(lambda: nc.scalar.activation(O["hb"][:, 32:64], O["hb"][:, 32:64], AF.Exp), r=[O["hb"]], w=[O["hb"]])
    self.dve(lambda: nc.vector.tensor_scalar_mul(O["hb"][:, 32:64], O["hb"][:, 32:64], -1.0), r=[O["hb"]], w=[O["hb"]])


def odd_alloc(self, n):
    T = self.T
    O = {}
    O["hn"] = T("hn", [128, KC, n], BF16)
    O["sqm"] = T("sqm", [128, KC, n], BF16)
    O["rstd"] = T("rstd", [128, n], F32)
    O["cw"] = T("ocw", [128, 24, 4], F32)
    O["hb"] = T("ohb", [128, 96], F32)
    O["fc"] = T("ofc", [128, 56], F32)
    O["acc"] = [T("oacc%d" % i, [128, n], F32) for i in range(2)]
    O["yT"] = T("yT", [128, 16, n], BF16)
    O["BT"] = T("BT", [128, 4, n], BF16)
    O["CT"] = T("CT", [128, 4, n], BF16)
    O["xc"] = [T("xc%d" % i, [128, n], BF16) for i in range(2)]
    return O


def odd_dt(self, O, ps_ap, rows, dt_dst, a_dst, rd, wr):
    nc = self.nc
    hb = O["hb"]
    self.dve(lambda: nc.vector.tensor_tensor(out=dt_dst, in0=ps_ap, in1=hb[0:rows, 0:32], op=ALU.add), r=rd + [hb], w=wr)
    self.act(lambda: nc.scalar.activation(dt_dst, dt_dst, AF.Exp), r=wr, w=wr)
    self.act(lambda: nc.scalar.activation(dt_dst, dt_dst, AF.Ln, bias=1.0, scale=1.0), r=wr, w=wr)
    self.dve(lambda: nc.vector.tensor_tensor(out=a_dst, in0=dt_dst, in1=hb[0:rows, 32:64], op=ALU.mult), r=wr + [hb], w=wr)


def odd_gate_norm_out(self, O, xbufs, c0, n, zfn):
    nc = self.nc
    yT, fc = O["yT"], O["fc"]
    sq = O["sqm"]
    for j in range(16):
        def consume(ps, j=j):
            zs = O["acc"][j % 2]
            self.act(lambda: nc.scalar.activation(zs[:, 0:n], ps[:, 0:n], AF.Silu), r=[ps], w=[zs])
            self.dve(lambda: nc.vector.tensor_tensor(out=yT[:, j, 0:n], in0=yT[:, j, 0:n], in1=zs[:, 0:n], op=ALU.mult),
                     r=[yT, zs], w=[yT])
        zfn(j, consume)
    for g in range(4):
        ps = self.P[6 + g % 2]
        for jj in range(4):
            j = g * 4 + jj
            self.act(lambda: nc.scalar.activation(sq[:, jj, 0:n], yT[:, j, 0:n], AF.Square), r=[yT], w=[sq])

        def mm():
            ins = None
            for jj in range(4):
                ins = nc.tensor.matmul(ps[:, 0:n], self.ones_bf[:], sq[:, jj, 0:n], start=(jj == 0), stop=(jj == 3))
            return ins
        self.pe(mm, r=[sq, self.ones_bf], w=[ps])
        rs = O["rstd"]
        self.act(lambda: nc.scalar.activation(rs[:, 0:n], ps[:, 0:n], AF.Ln, bias=EPS, scale=1.0 / 512.0), r=[ps], w=[rs])
        self.act(lambda: nc.scalar.activation(rs[:, 0:n], rs[:, 0:n], AF.Exp, scale=-0.5), r=[rs], w=[rs])
        for jj in range(4):
            j = g * 4 + jj
            self.dve(lambda: nc.vector.scalar_tensor_tensor(out=yT[:, j, 0:n], in0=yT[:, j, 0:n], scalar=fc[:, 24 + j:25 + j],
                                                            in1=rs[:, 0:n], op0=ALU.mult, op1=ALU.mult), r=[yT, fc, rs], w=[yT])
    for blk in range(4):
        w = self.ws.get(4096)
        wv = w[:, 0:4096].rearrange("p (k c) -> p k c", k=16)
        for dc in range(2):
            dch = blk * 2 + dc
            ps = self.P[dch % 2]

            def mm2():
                ins = None
                for k in range(16):
                    ins = nc.tensor.matmul(ps[:, 0:n], wv[:, k, dc * 128:(dc + 1) * 128], yT[:, k, 0:n],
                                           start=(k == 0), stop=(k == 15))
                return ins
            self.pe(mm2, r=[w, yT], w=[ps])
            xs = self.xT[:, dch, c0:c0 + n]
            self.dve(lambda: nc.vector.tensor_tensor(out=xs, in0=xs, in1=ps[:, 0:n], op=ALU.add), r=[ps] + xbufs, w=xbufs)


def odd_fm_chunk(self, O, w, cc, n, ps):
    nc = self.nc
    hn = O["hn"]
    wv = w[:, 0:4096].rearrange("p (k c) -> p k c", k=KC)

    def mm():
        ins = None
        for kc in range(KC):
            ins = nc.tensor.matmul(ps[:, 0:n], wv[:, kc, cc * 128:(cc + 1) * 128], hn[:, kc, 0:n],
                                   start=(kc == 0), stop=(kc == KC - 1))
        return ins
    self.pe(mm, r=[w, hn], w=[ps])


def odd_mixer(self, l):
    nc = self.nc
    e = l // 2
    P = self.P
    ident, mk = self.ident, self.mk
    self.begin()
    T = self.T
    O = odd_alloc(self, 512)
    pre = [T("opre%d" % i, [128, 515], F32) for i in range(2)]
    carry = T("ocarry", [128, 24, 3], F32)
    xst = T("xst", [128, 4, 2048], BF16)
    Btok = T("Btok", [128, 4, 512], BF16)
    dtv = T("dtv", [128, 4, 32], F32)
    av = T("av", [128, 4, 32], F32)
    sm = T("osm", [128, 6, 32], F32)
    R4 = [T("R4_%d" % i, [128, 4, 128], F32) for i in range(2)]
    Lt = [T("Lt%d" % i, [128, 4, 128], F32) for i in range(2)]
    Wb = [T("Wb%d" % i, [128, 4, 128], BF16) for i in range(2)]
    xdt = T("xdt", [128, 2048], BF16)
    xdd = T("xdd", [128, 2048], BF16)
    ytk = T("ytk", [128, 2048], BF16)
    tt = [T("ott%d" % i, [128, 512], F32) for i in range(2)]
    ST = T("ST", [128, 2048], F32)
    STb = T("STb", [128, 2048], BF16)
    sto = Tile(self.nc, "sto", [128, 32, 128], F32, handle=xst.t.bitcast(F32).reshape([128, 32, 128]))
    sto.b = xst.b
    odd_consts(self, O, e)
    self.dve(lambda: nc.vector.memset(carry[:], 0.0), w=[carry])
    self.dve(lambda: nc.vector.memset(ST[:], 0.0), w=[ST])
    self.dve(lambda: nc.vector.memset(STb[:], 0.0), w=[STb])
    cw, fc, hb = O["cw"], O["fc"], O["hb"]
    for ti in range(self.nt):
        c0 = ti * 512
        xb = [self.xTb[ti]]
        self.rmsnorm_cols(xb, c0, 512, 4 + l, O["hn"], 0, O["sqm"], O["rstd"])
        w = self.ws.get(OW_DT)
        wv = w[:, 0:OW_DT].rearrange("p (k c) -> p k c", k=KC)
        for st in range(4):
            ps = P[2 + st % 2]

            def mm():
                ins = None
                for kc in range(KC):
                    ins = nc.tensor.matmul(ps[:, 0:32], O["hn"][:, kc, st * 128:(st + 1) * 128], wv[:, kc, :],
                                           start=(kc == 0), stop=(kc == KC - 1))
                return ins
            self.pe(mm, r=[w, O["hn"]], w=[ps])
            odd_dt(self, O, ps[:, 0:32], 128, dtv[:, st, :], av[:, st, :], [ps], [dtv, av])
        for blk in range(6):
            w = self.ws.get(4096)
            for cc in range(4):
                ch = blk * 4 + cc
                ps = P[ch % 2]
                odd_fm_chunk(self, O, w, cc, 512, ps)
                pr = pre[ch % 2]
                acc = O["acc"][ch % 2]
                self.dve(lambda: nc.vector.tensor_copy(pr[:, 0:3], carry[:, ch, :]), r=[carry], w=[pr])
                self.act(lambda: nc.scalar.copy(pr[:, 3:515], ps[:, 0:512]), r=[ps], w=[pr])
                self.dve(lambda: nc.vector.tensor_copy(carry[:, ch, :], pr[:, 512:515]), r=[pr], w=[carry])
                self.dve(lambda: nc.vector.tensor_scalar_mul(acc[:], pr[:, 0:512], cw[:, ch, 0:1]), r=[pr, cw], w=[acc])
                for i in range(1, 4):
                    self.dve(lambda: nc.vector.scalar_tensor_tensor(out=acc[:], in0=pr[:, i:i + 512], scalar=cw[:, ch, i:i + 1],
                                                                    in1=acc[:], op0=ALU.mult, op1=ALU.add), r=[pr, cw, acc], w=[acc])
                if ch < 16 or ch < 20:
                    dst = O["xc"][ch % 2] if ch < 16 else None
                    tgt = dst[:, 0:512] if ch < 16 else O["BT"][:, ch - 16, 0:512]
                    tb = dst if ch < 16 else O["BT"]
                    self.act(lambda: nc.scalar.activation(tgt, acc[:], AF.Silu, bias=fc[:, ch:ch + 1], scale=1.0),
                             r=[acc, fc], w=[tb])
                    tp = P[4 + ch % 2]
                    tv = tp[:, :].bitcast(BF16)

                    def tr():
                        ins = None
                        for st in range(4):
                            ins = nc.tensor.transpose(tv[:, st * 128:(st + 1) * 128], tgt[:, st * 128:(st + 1) * 128], self.ident_bf[:])
                        return ins
                    self.pe(tr, r=[tb, self.ident_bf], w=[tp])
                    src = tv[:, 0:512].rearrange("p (s c) -> p s c", s=4)
                    if ch < 16:
                        self.dve(lambda: nc.vector.tensor_copy(xst[:, :, ch * 128:(ch + 1) * 128], src), r=[tp], w=[xst])
                    else:
                        self.dve(lambda: nc.vector.tensor_copy(Btok[:, :, (ch - 16) * 128:(ch - 15) * 128], src), r=[tp], w=[Btok])
                else:
                    self.act(lambda: nc.scalar.activation(O["CT"][:, ch - 20, 0:512], acc[:], AF.Silu, bias=fc[:, ch:ch + 1], scale=1.0),
                             r=[acc, fc], w=[O["CT"]])
        for st in range(4):
            cs = st * 128
            acs, acl, dte, cd, eac, dtd = [sm[:, i, :] for i in range(6)]
            self.pe(lambda: nc.tensor.matmul(P[2][:, 0:32], mk[:, 6, :], av[:, st, :], start=True, stop=True), r=[mk, av], w=[P[2]])
            self.pe(lambda: nc.tensor.matmul(P[2][:, 32:64], self.ones_f[:], av[:, st, :], start=True, stop=True),
                    r=[self.ones_f, av], w=[P[2]])
            self.dve(lambda: nc.vector.tensor_copy(sm[:, 0:2, :], P[2][:, 0:64].rearrange("p (a b) -> p a b", a=2)), r=[P[2]], w=[sm])
            self.dve(lambda: nc.vector.tensor_tensor(out=dte, in0=acl, in1=acs, op=ALU.subtract), r=[sm], w=[sm])
            self.act(lambda: nc.scalar.activation(dte, dte, AF.Exp), r=[sm], w=[sm])
            self.act(lambda: nc.scalar.activation(cd, acl, AF.Exp), r=[sm], w=[sm])
            self.act(lambda: nc.scalar.activation(eac, acs, AF.Exp), r=[sm], w=[sm])
            self.dve(lambda: nc.vector.tensor_tensor(out=dtd, in0=dte, in1=dtv[:, st, :], op=ALU.mult), r=[sm, dtv], w=[sm])
            xv = xst[:, st, :].rearrange("p (h q) -> p h q", q=P_C)
            for q4 in range(4):
                hs = slice(q4 * 8, (q4 + 1) * 8)
                self.dve(lambda: nc.vector.tensor_tensor(out=xdt[:, q4 * 512:(q4 + 1) * 512].rearrange("p (h q) -> p h q", q=P_C),
                                                         in0=xv[:, hs, :], in1=dtv[:, st, hs].unsqueeze(2).to_broadcast([128, 8, P_C]),
                                                         op=ALU.mult), r=[xst, dtv], w=[xdt])
                self.dve(lambda: nc.vector.tensor_tensor(out=xdd[:, q4 * 512:(q4 + 1) * 512].rearrange("p (h q) -> p h q", q=P_C),
                                                         in0=xv[:, hs, :], in1=dtd[:, hs].unsqueeze(2).to_broadcast([128, 8, P_C]),
                                                         op=ALU.mult), r=[xst, sm], w=[xdd])
            for g in range(4):
                self.pe(lambda: nc.tensor.matmul(P[3][:, 0:128], O["BT"][:, g, cs:cs + 128], O["CT"][:, g, cs:cs + 128], start=True, stop=True),
                        r=[O["BT"], O["CT"]], w=[P[3]])
                yps = P[4]
                for hf in range(2):
                    i2 = (g * 2 + hf) % 2
                    h0 = g * 8 + hf * 4
                    r4, lt, wb = R4[i2], Lt[i2], Wb[i2]
                    for hh in range(4):
                        self.dve(lambda: nc.vector.tensor_scalar_mul(r4[:, hh, :], mk[:, 6, :], av[:, st, h0 + hh:h0 + hh + 1]),
                                 r=[mk, av], w=[r4])
                    bc = P[hf]
                    self.pe(lambda: nc.tensor.matmul(bc[:, 0:512], self.ones_f[:], r4[:].rearrange("p a b -> p (a b)"), start=True, stop=True),
                            r=[self.ones_f, r4], w=[bc])
                    self.dve(lambda: nc.vector.tensor_tensor(out=lt[:], in0=bc[:, 0:512].rearrange("p (a b) -> p a b", a=4),
                                                             in1=acs[:, h0:h0 + 4].unsqueeze(2).to_broadcast([128, 4, 128]),
                                                             op=ALU.subtract), r=[bc, sm], w=[lt])
                    self.dve(lambda: nc.vector.tensor_tensor(out=lt[:], in0=lt[:], in1=mk[:, 7, :].unsqueeze(1).to_broadcast([128, 4, 128]),
                                                             op=ALU.min), r=[lt, mk], w=[lt])
                    self.act(lambda: nc.scalar.activation(lt[:], lt[:], AF.Exp), r=[lt], w=[lt])
                    self.dve(lambda: nc.vector.tensor_tensor(out=wb[:], in0=lt[:], in1=P[3][:, 0:128].unsqueeze(1).to_broadcast([128, 4, 128]),
                                                             op=ALU.mult), r=[lt, P[3]], w=[wb])

                    def ymm():
                        ins = None
                        for hh in range(4):
                            h = h0 + hh
                            ins = nc.tensor.matmul(yps[:, (hf * 4 + hh) * 64:(hf * 4 + hh + 1) * 64], wb[:, hh, :], xdt[:, h * 64:(h + 1) * 64],
                                                   start=True, stop=True)
                        return ins
                    self.pe(ymm, r=[wb, xdt], w=[yps])
                self.pe(lambda: nc.tensor.matmul(P[5][:, 0:512], O["CT"][:, g, cs:cs + 128], STb[:, g * 512:(g + 1) * 512], start=True, stop=True),
                        r=[O["CT"], STb], w=[P[5]])
                t = tt[g % 2]
                gsl = slice(g * 512, (g + 1) * 512)
                self.dve(lambda: nc.vector.tensor_tensor(out=t[:].rearrange("p (h q) -> p h q", q=P_C),
                                                         in0=P[5][:, 0:512].rearrange("p (h q) -> p h q", q=P_C),
                                                         in1=eac[:, g * 8:(g + 1) * 8].unsqueeze(2).to_broadcast([128, 8, P_C]), op=ALU.mult),
                         r=[P[5], sm], w=[t])
                self.dve(lambda: nc.vector.tensor_tensor(out=t[:], in0=t[:], in1=yps[:, 0:512], op=ALU.add), r=[t, yps], w=[t])
                t2 = O["acc"][g % 2]
                self.dve(lambda: nc.vector.tensor_tensor(out=t2[:].rearrange("p (h q) -> p h q", q=P_C),
                                                         in0=xst[:, st, gsl].rearrange("p (h q) -> p h q", q=P_C),
                                                         in1=hb[:, 64 + g * 8:64 + (g + 1) * 8].unsqueeze(2).to_broadcast([128, 8, P_C]), op=ALU.mult),
                         r=[xst, hb], w=[t2])
                self.dve(lambda: nc.vector.tensor_tensor(out=ytk[:, gsl], in0=t[:], in1=t2[:], op=ALU.add), r=[t, t2], w=[ytk])
                self.pe(lambda: nc.tensor.matmul(P[6][:, 0:512], Btok[:, st, g * 128:(g + 1) * 128], xdd[:, gsl], start=True, stop=True),
                        r=[Btok, xdd], w=[P[6]])
                self.dve(lambda: nc.vector.tensor_tensor(out=ST[:, gsl].rearrange("p (h q) -> p h q", q=P_C),
                                                         in0=ST[:, gsl].rearrange("p (h q) -> p h q", q=P_C),
                                                         in1=cd[:, g * 8:(g + 1) * 8].unsqueeze(2).to_broadcast([128, 8, P_C]), op=ALU.mult),
                         r=[ST, sm], w=[ST])
                self.dve(lambda: nc.vector.tensor_tensor(out=ST[:, gsl], in0=ST[:, gsl], in1=P[6][:, 0:512], op=ALU.add), r=[ST, P[6]], w=[ST])
                self.act(lambda: nc.scalar.copy(STb[:, gsl], ST[:, gsl]), r=[ST], w=[STb])
            for hf in range(2):
                tp = P[hf]
                tv = tp[:, :].bitcast(BF16)

                def tr2():
                    ins = None
                    for j in range(8):
                        ch = hf * 8 + j
                        ins = nc.tensor.transpose(tv[:, j * 128:(j + 1) * 128], ytk[:, ch * 128:(ch + 1) * 128], self.ident_bf[:])
                    return ins
                self.pe(tr2, r=[ytk, self.ident_bf], w=[tp])
                self.act(lambda: nc.scalar.copy(O["yT"][:, hf * 8:(hf + 1) * 8, cs:cs + 128], tv[:, 0:1024].rearrange("p (j c) -> p j c", j=8)),
                         r=[tp], w=[O["yT"]])
        zw = [None]

        def zfn(j, consume):
            if j % 4 == 0:
                zw[0] = self.ws.get(4096)
            ps = P[2 + j % 2]
            odd_fm_chunk(self, O, zw[0], j % 4, 512, ps)
            consume(ps)
        odd_gate_norm_out(self, O, xb, c0, 512, zfn)
    for ch in range(24):
        self.out_dma(self.o_psc[e][:, ch * 128:(ch + 1) * 128].rearrange("i p -> p i"), carry[:, ch, :], r=[carry])
    for c in range(16):
        ps = P[c % 2]
        self.pe(lambda: nc.tensor.transpose(ps[:, 0:128], ST[:, c * 128:(c + 1) * 128], ident[:]), r=[ST, ident], w=[ps])
        self.dve(lambda: nc.vector.tensor_copy(sto[:, c, :], ps[:, 0:128]), r=[ps], w=[sto])
    self.out_dma(self.o_pss[e].rearrange("(c q) n -> q c n", q=128), sto[:, 0:16, :], r=[sto])
    self.end()
    odd_decode(self, l)


def odd_decode(self, l):
    nc = self.nc
    e = l // 2
    n = NS
    c0 = self.seq
    P = self.P
    ident, mk = self.ident, self.mk
    self.begin()
    T = self.T
    O = odd_alloc(self, n)
    xx = T("oxx", [128, 24, n, 4], F32)
    hs = T("ohs", [48, 3072], F32)
    gnew = T("ognew", [n, 3072], F32)
    xsT = T("xsT", [128, 16, n], F32)
    BCs = T("BCs", [128, 8, n], F32)
    dts = T("dts", [n, 64], F32)
    BCt = T("BCt", [n, 2, 512], F32)
    BCm = [T("BCm%d" % i, [n, 2, 512], F32) for i in range(2)]
    R5 = T("R5", [n, 2, 2, 16, n], F32)
    onesH = T("onesH", [n, 2, 128], F32)
    cols = T("ocols", [128, 2, 16, n], F32)
    xdc = T("xdc", [128, 16, n], F32)
    Sb = [T("Sb%d" % i, [128, 16, 128], F32) for i in range(2)]
    tS = T("otS", [128, 16, 128], F32)
    ysum = T("ysum", [128, 16, n], F32)
    odd_consts(self, O, e)
    cw, fc, hb = O["cw"], O["fc"], O["hb"]
    self.dma(hs[:], self.s_sc[e].rearrange("b i c -> (b i) c"), r=[self.Bin], w=[hs])
    self.dve(lambda: nc.vector.memset(onesH[:], 0.0), w=[onesH])
    self.dve(lambda: nc.vector.memset(onesH[:, 0, 0:64], 1.0), r=[onesH], w=[onesH])
    self.dve(lambda: nc.vector.memset(onesH[:, 1, 64:128], 1.0), r=[onesH], w=[onesH])
    for ch in range(24):
        ps = P[2 + ch % 2]
        self.pe(lambda: nc.tensor.transpose(ps[:, 0:48], hs[:, ch * 128:(ch + 1) * 128], ident[0:48, 0:48]), r=[hs, ident], w=[ps])
        self.dve(lambda: nc.vector.tensor_copy(xx[:, ch, :, 0:3], ps[:, 0:48].rearrange("p (b i) -> p b i", i=3)), r=[ps], w=[xx])
    xb = [self.xTb[self.nt]]
    self.rmsnorm_cols(xb, c0, n, 4 + l, O["hn"], 0, O["sqm"], O["rstd"])
    w = self.ws.get(OW_DT)
    wv = w[:, 0:OW_DT].rearrange("p (k c) -> p k c", k=KC)
    ps = P[2]

    def mm():
        ins = None
        for kc in range(KC):
            ins = nc.tensor.matmul(ps[0:n, 0:32], O["hn"][:, kc, 0:n], wv[:, kc, :], start=(kc == 0), stop=(kc == KC - 1))
        return ins
    self.pe(mm, r=[w, O["hn"]], w=[ps])
    odd_dt(self, O, ps[0:n, 0:32], n, dts[:, 0:32], dts[:, 32:64], [ps], [dts])
    self.act(lambda: nc.scalar.activation(dts[:, 32:64], dts[:, 32:64], AF.Exp), r=[dts], w=[dts])
    for blk in range(6):
        w = self.ws.get(4096)
        for cc in range(4):
            ch = blk * 4 + cc
            ps = P[ch % 2]
            odd_fm_chunk(self, O, w, cc, n, ps)
            acc = O["acc"][ch % 2]
            self.act(lambda: nc.scalar.copy(xx[:, ch, :, 3], ps[:, 0:n]), r=[ps], w=[xx])
            self.dve(lambda: nc.vector.tensor_scalar_mul(acc[:, 0:n], xx[:, ch, :, 0], cw[:, ch, 0:1]), r=[xx, cw], w=[acc])
            for i in range(1, 4):
                self.dve(lambda: nc.vector.scalar_tensor_tensor(out=acc[:, 0:n], in0=xx[:, ch, :, i], scalar=cw[:, ch, i:i + 1],
                                                                in1=acc[:, 0:n], op0=ALU.mult, op1=ALU.add), r=[xx, cw, acc], w=[acc])
            if ch < 16:
                dst, db = xsT[:, ch, :], xsT
            else:
                dst, db = BCs[:, ch - 16, :], BCs
            self.act(lambda: nc.scalar.activation(dst, acc[:, 0:n], AF.Silu, bias=fc[:, ch:ch + 1], scale=1.0), r=[acc, fc], w=[db])
            pt = P[6]
            self.pe(lambda: nc.tensor.transpose(pt[0:n, (ch % 4) * 128:(ch % 4 + 1) * 128], xx[:, ch, :, 3], ident[:]), r=[xx, ident], w=[pt])
            self.dve(lambda: nc.vector.tensor_copy(gnew[:, ch * 128:(ch + 1) * 128], pt[0:n, (ch % 4) * 128:(ch % 4 + 1) * 128]), r=[pt], w=[gnew])
    for t in range(2):
        pt = P[4 + t]
        for g in range(4):
            self.pe(lambda: nc.tensor.transpose(pt[0:n, g * 128:(g + 1) * 128], BCs[:, t * 4 + g, :], ident[:]), r=[BCs, ident], w=[pt])
        self.dve(lambda: nc.vector.tensor_copy(BCt[:, t, :], pt[0:n, 0:512]), r=[pt], w=[BCt])
    for t in range(2):
        src = dts[:, t * 32:(t + 1) * 32].rearrange("p (hp h2) -> p h2 hp", h2=2)
        self.dve(lambda: nc.vector.tensor_tensor(out=R5[:, t], in0=src.unsqueeze(3).to_broadcast([n, 2, 16, n]),
                                                 in1=ident[0:n, 0:n].unsqueeze(1).unsqueeze(1).to_broadcast([n, 2, 16, n]), op=ALU.mult),
                 r=[dts, ident], w=[R5])

        def cmm():
            ins = None
            for h2 in range(2):
                ins = nc.tensor.matmul(P[7][:, t * 256:(t + 1) * 256], onesH[:, h2, :], R5[:, t, h2].rearrange("p a b -> p (a b)"),
                                       start=(h2 == 0), stop=(h2 == 1))
            return ins
        self.pe(cmm, r=[onesH, R5], w=[P[7]])
    self.dve(lambda: nc.vector.tensor_copy(cols[:].rearrange("p t a b -> p (t a b)"), P[7][:, 0:512]), r=[P[7]], w=[cols])
    self.dve(lambda: nc.vector.tensor_tensor(out=xdc[:], in0=cols[:, 0], in1=xsT[:], op=ALU.mult), r=[cols, xsT], w=[xdc])
    sview = self.s_ss[e].rearrange("b (c q) n -> b q c n", q=128)
    oview = self.o_sss[e].rearrange("b (c q) n -> b q c n", q=128)
    self.dma(Sb[0][:], sview[0], r=[self.Bin], w=[Sb[0]])
    for b in range(n):
        S_ = Sb[b % 2]
        if b + 1 < n:
            self.dma(Sb[(b + 1) % 2][:], sview[b + 1], r=[self.Bin], w=[Sb[(b + 1) % 2]])
        bm = BCm[b % 2]
        self.dve(lambda: nc.vector.tensor_scalar_mul(bm[:].rearrange("p a b -> p (a b)"), BCt[:].rearrange("p a b -> p (a b)"),
                                                     ident[0:n, b:b + 1]), r=[BCt, ident], w=[bm])
        pB, pC = P[(b % 2) * 2], P[(b % 2) * 2 + 1]
        self.pe(lambda: nc.tensor.matmul(pB[:, 0:512], self.ones_f[0:n, :], bm[:, 0, :], start=True, stop=True), r=[self.ones_f, bm], w=[pB])
        self.pe(lambda: nc.tensor.matmul(pC[:, 0:512], self.ones_f[0:n, :], bm[:, 1, :], start=True, stop=True), r=[self.ones_f, bm], w=[pC])
        s4 = S_[:].rearrange("p (g r) n -> p g r n", r=4)
        t4 = tS[:].rearrange("p (g r) n -> p g r n", r=4)
        self.dve(lambda: nc.vector.tensor_tensor(out=tS[:], in0=S_[:], in1=cols[:, 1, :, b:b + 1].to_broadcast([128, 16, 128]), op=ALU.mult),
                 r=[S_, cols], w=[tS])
        self.dve(lambda: nc.vector.tensor_tensor(out=s4, in0=pB[:, 0:512].rearrange("p (g n) -> p g n", g=4).unsqueeze(2).to_broadcast([128, 4, 4, 128]),
                                                 in1=xdc[:, :, b].rearrange("p (g r) -> p g r", r=4).unsqueeze(3).to_broadcast([128, 4, 4, 128]),
                                                 op=ALU.mult), r=[pB, xdc], w=[S_])
        self.dve(lambda: nc.vector.tensor_tensor(out=S_[:], in0=S_[:], in1=tS[:], op=ALU.add), r=[S_, tS], w=[S_])
        self.dve(lambda: nc.vector.tensor_tensor(out=t4, in0=s4, in1=pC[:, 0:512].rearrange("p (g n) -> p g n", g=4).unsqueeze(2).to_broadcast([128, 4, 4, 128]),
                                                 op=ALU.mult), r=[S_, pC], w=[tS])
        self.dve(lambda: nc.vector.reduce_sum(out=ysum[:, :, b], in_=tS[:], axis=AX.X), r=[tS], w=[ysum])
        self.dma(oview[b], S_[:], r=[S_], w=[self.Bout])
    self.dve(lambda: nc.vector.tensor_tensor(out=xdc[:], in0=xsT[:], in1=fc[:, 40:56].unsqueeze(2).to_broadcast([128, 16, n]), op=ALU.mult),
             r=[xsT, fc], w=[xdc])
    self.dve(lambda: nc.vector.tensor_tensor(out=O["yT"][:, :, 0:n], in0=ysum[:], in1=xdc[:], op=ALU.add), r=[ysum, xdc], w=[O["yT"]])
    zw = [None]

    def zfn(j, consume):
        if j % 4 == 0:
            zw[0] = self.ws.get(4096)
        ps = P[2 + j % 2]
        odd_fm_chunk(self, O, zw[0], j % 4, n, ps)
        consume(ps)
    odd_gate_norm_out(self, O, xb, c0, n, zfn)
    self.out_dma(self.o_ssc[e][:, 0:2, :], self.s_sc[e][:, 1:3, :], r=[self.Bin])
    self.out_dma(self.o_ssc[e][:, 2, :], gnew[:], r=[gnew])
    self.end()


Prog.odd_mixer = odd_mixer


def tile_k(w, cb):
    K, N = w.shape
    return np.ascontiguousarray(w.reshape(K // 128, 128, N // cb, cb).transpose(2, 1, 0, 3))


def t5_bucket_np(dist):
    max_exact = 16
    df = np.maximum(dist, max_exact).astype(np.float32)
    large = max_exact + (np.log(df / max_exact) / math.log(128 / max_exact) * (32 - max_exact)).astype(np.int32)
    return np.where(dist < max_exact, dist, np.minimum(large, 31))


def static_masks():
    i = np.arange(128)[:, None]
    j = np.arange(128)[None, :]
    same = (i // 64) == (j // 64)
    m = np.zeros((128, 8, 128), np.float32)
    m[:, 0] = ((i <= j) & same)
    m[:, 1] = np.where((j < i) & same, 0.0, 1e30)
    m[:, 2] = np.where((j >= i) & same, 0.0, -1e30)
    m[:, 3] = ((j > i) & same)
    m[:, 4] = same
    m[:, 5, 0:16] = (np.arange(128)[:, None] // 8) == np.arange(16)[None, :]
    m[:, 6] = (i <= j)
    m[:, 7] = np.where(j >= i, 0.0, -1e30)
    return m.reshape(128, 8 * 128)


def prep_shared(inp):
    sh = {}
    g = np.zeros((128, 13, KC), np.float32)
    for l in range(4):
        g[:, l] = inp["norm_ff1"][l].reshape(KC, 128).T
        g[:, 4 + l] = inp["norm_mix"][l].reshape(KC, 128).T
        g[:, 8 + l] = inp["norm_ff2"][l].reshape(KC, 128).T
    g[:, 12] = inp["norm_final"].reshape(KC, 128).T
    sh["gains"] = g.reshape(128, 13 * KC)
    sh["masks"] = static_masks()
    wgu = np.empty((DEPTH, 2, 11, 128, 2, KC, 256), np.float32)
    wd = np.empty((DEPTH, 2, 8, 128, FC, 128), np.float32)
    for l in range(DEPTH):
        for f, (kg, ku, kd) in enumerate((("ff1_gate", "ff1_up", "ff1_down"), ("ff2_gate", "ff2_up", "ff2_down"))):
            wgu[l, f, :, :, 0] = tile_k(inp[kg][l], 256)
            wgu[l, f, :, :, 1] = tile_k(inp[ku][l], 256)
            wd[l, f] = tile_k(inp[kd][l], 128)
    sh["wgu"] = wgu.reshape(DEPTH, 2, 11, 128, 4096)
    sh["wd"] = wd.reshape(DEPTH, 2, 8, 128, 2816)
    ewin = np.zeros((2, 128, EW_TOT), np.float32)
    ewout = np.zeros((2, 2, 128, 4096), np.float32)
    econv = np.zeros((2, 128, 12, 4), np.float32)
    esm = np.zeros((2, 128, 17), np.float32)
    esk = np.zeros((2, 128, 1), np.float32)
    for e in range(2):
        W = inp["even_w_in"][e]
        qa, ka, va = W[:, 0:512], W[:, 512:640], W[:, 640:768]
        qkvb, zb, bb = W[:, 768:2304], W[:, 2304:2816], W[:, 2816:2824]
        cols = []
        for c in range(4):
            cols.append(np.concatenate([qa[:, c * 64:(c + 1) * 64], qa[:, 256 + c * 64:256 + (c + 1) * 64]], 1))
        cols.append(ka)
        cols.append(qkvb)
        cols.append(zb)
        fm = np.concatenate(cols, 1)
        assert fm.shape[1] == EV_FM * 128
        zero = np.zeros((1024, 64), np.float32)
        tok = np.concatenate([va[:, 0:64], zero, zero, va[:, 64:128], ka, bb], 1)
        assert tok.shape[1] == EV_TOK
        for b in range(5):
            ewin[e, :, EW_OFF[b]:EW_OFF[b] + 4096] = tile_k(fm[:, b * 512:(b + 1) * 512], 512)[0].reshape(128, 4096)
        ewin[e, :, EW_OFF[5]:EW_OFF[5] + 1024] = tile_k(fm[:, 2560:2688], 128)[0].reshape(128, 1024)
        ewin[e, :, EW_OFF[6]:] = tile_k(tok, EV_TOK)[0].reshape(128, 8 * EV_TOK)
        Wo = inp["even_w_out"][e]
        rows = []
        for c in range(4):
            rows.append(Wo[c * 64:(c + 1) * 64])
            rows.append(Wo[256 + c * 64:256 + (c + 1) * 64])
        rows.append(Wo[512:])
        Wp = np.concatenate(rows, 0)
        ewout[e] = tile_k(Wp, 512).reshape(2, 128, 4096)
        econv[e] = inp["gdn_conv_w"][e].reshape(4, 12, 128).transpose(2, 1, 0)
        esm[e, :, 0:4] = inp["gdn_A_log"][e][None, :]
        esm[e, :, 4:8] = inp["gdn_dt_bias"][e][None, :]
        esm[e, :, 8:16] = inp["swa_sinks"][e][None, :]
        esm[e, :, 16] = inp["gdn_norm"][e]
        esk[e, :, 0] = np.tile(inp["swa_sinks"][e], 16)
    sh["ewin"], sh["ewout"] = ewin, ewout
    sh["econv"] = econv.reshape(2, 128, 48)
    sh["esm"], sh["esk"] = esm, esk
    owin = np.zeros((2, 128, OW_TOT), np.float32)
    owout = np.zeros((2, 4, 128, 4096), np.float32)
    oconv = np.zeros((2, 128, 24, 4), np.float32)
    ohead = np.zeros((2, 128, 96), np.float32)
    ofeat = np.zeros((2, 128, 56), np.float32)
    for e in range(2):
        W = inp["ssd_w_in"][e]
        z, xbc, dtw = W[:, 0:2048], W[:, 2048:5120], W[:, 5120:5152]
        owin[e, :, 0:256] = tile_k(dtw, 32)[0].reshape(128, 256)
        fm = np.concatenate([xbc, z], 1)
        for b in range(10):
            owin[e, :, 256 + b * 4096:256 + (b + 1) * 4096] = tile_k(fm[:, b * 512:(b + 1) * 512], 512)[0].reshape(128, 4096)
        Wo = inp["ssd_w_out"][e]
        owout[e] = np.ascontiguousarray(Wo.reshape(16, 128, 4, 256).transpose(2, 1, 0, 3)).reshape(4, 128, 4096)
        oconv[e] = inp["ssd_conv_w"][e].reshape(4, 24, 128).transpose(2, 1, 0)
        ohead[e, :, 0:32] = inp["ssd_dt_bias"][e][None, :]
        ohead[e, :, 32:64] = inp["ssd_A_log"][e][None, :]
        ohead[e, :, 64:96] = inp["ssd_D"][e][None, :]
        ofeat[e, :, 0:24] = inp["ssd_conv_b"][e].reshape(24, 128).T
        ofeat[e, :, 24:40] = inp["ssd_norm"][e].reshape(16, 128).T
        ofeat[e, :, 40:56] = np.repeat(inp["ssd_D"][e].reshape(16, 2), 64, axis=1).T
    sh["owin"], sh["owout"] = owin, owout
    sh["oconv"] = oconv.reshape(2, 128, 96)
    sh["ohead"], sh["ofeat"] = ohead, ofeat
    rb = inp["rel_bias"]
    i = np.arange(128)[:, None]
    j = np.arange(256)[None, :]
    d = 128 + i - j
    valid = (d >= 0) & (d <= 128)
    bk = t5_bucket_np(np.clip(d, 0, 128))
    sb = np.where(valid[:, None, :], rb[bk].transpose(0, 2, 1), np.float32(NEG)).astype(np.float32)
    sh["swabias"] = np.ascontiguousarray(sb).reshape(128, 8 * 256)
    dd = 128 - np.arange(129)
    db = rb[t5_bucket_np(dd)]
    sh["decbias"] = np.ascontiguousarray(np.tile(db.T, (16, 1))).astype(np.float32)
    return sh


def core_inputs(inp, c, seq):
    m = {}
    m["x_p"] = np.ascontiguousarray(inp["x_prompt"][c][:seq])
    sl = slice(c * NS, (c + 1) * NS)
    m["x_s"] = np.ascontiguousarray(inp["x_sample"][sl, 0])
    m["c_k"] = np.ascontiguousarray(inp["cache_swa_k"][:, sl]).reshape(2, NS, 128, 128)
    m["c_v"] = np.ascontiguousarray(inp["cache_swa_v"][:, sl]).reshape(2, NS, 128, 128)
    m["s_gc"] = np.ascontiguousarray(inp["state_gdn_conv"][:, sl])
    m["s_gs"] = np.ascontiguousarray(inp["state_gdn_ssm"][:, sl])
    m["s_sc"] = np.ascontiguousarray(inp["state_ssd_conv"][:, sl])
    m["s_ss"] = np.ascontiguousarray(inp["state_ssd_ssm"][:, sl]).reshape(2, NS, 2048, 128)
    return m


_PROG_CACHE = {}


def get_prog(cfg):
    key = tuple(sorted((k, str(v)) for k, v in cfg.items()))
    if key not in _PROG_CACHE:
        _PROG_CACHE[key] = Prog(dict(cfg))
    return _PROG_CACHE[key]


def kernel(**inp):
    cfg = {"ntiles": 4, "depth": DEPTH}
    prog = get_prog(cfg)
    inp = {k: np.asarray(v) for k, v in inp.items()}
    sh = prep_shared(inp)
    in_maps = []
    for c in range(N_CORES):
        m = dict(sh)
        m.update(core_inputs(inp, c, SEQ))
        in_maps.append(m)
    res = run_bass_kernel_spmd(prog.nc, in_maps, core_ids=list(range(N_CORES)))
    R = res.results
    st1 = lambda k: np.stack([r[k] for r in R], 1)
    ct1 = lambda k: np.concatenate([r[k] for r in R], 1)
    y_p = np.stack([r["y_p"] for r in R], 0)
    y_s = np.concatenate([r["y_s"] for r in R], 0)[:, None, :]
    p_k = st1("o_pk").reshape(2, N_CORES, 128, 2, 64)
    p_v = st1("o_pv").reshape(2, N_CORES, 128, 2, 64)
    p_gc = st1("o_pgc")
    p_gs = st1("o_pgs")
    p_sc = st1("o_psc")
    p_ss = st1("o_pss").reshape(2, N_CORES, 32, 64, 128)
    s_k = ct1("o_sk").reshape(2, N_CORES * NS, 128, 2, 64)
    s_v = ct1("o_sv").reshape(2, N_CORES * NS, 128, 2, 64)
    s_gc = ct1("o_sgc")
    s_gs = ct1("o_sgs")
    s_sc = ct1("o_ssc")
    s_ss = ct1("o_sss").reshape(2, N_CORES * NS, 32, 64, 128)
    return (y_p, y_s, p_k, p_v, p_gc, p_gs, p_sc, p_ss, s_k, s_v, s_gc, s_gs, s_sc, s_ss)
```

```python
import math
import numpy as np
import concourse.bass as bass
import concourse.mybir as mybir
from concourse.bass_utils import run_bass_kernel_spmd

F32 = mybir.dt.float32
BF16 = mybir.dt.bfloat16
ALU = mybir.AluOpType
AF = mybir.ActivationFunctionType
AX = mybir.AxisListType

D = 1024
KC = 8
SEQ = 2048
NS = 16
DFF = 2816
FC = 22
EPS = 1e-6
N_CORES = 8
DEPTH = 4


class Buf:
    __slots__ = ("name", "w", "r", "excl")

    def __init__(self, name, excl=False):
        self.name = name
        self.w = None
        self.r = []
        self.excl = excl


class Sched:
    def __init__(self, nc, n_dma_sems=48):
        self.nc = nc
        self.E = {"pe": nc.tensor, "dve": nc.vector, "act": nc.scalar, "pool": nc.gpsimd, "sp": nc.sync}
        self.sems = {}
        self.cnt = {}
        for e in self.E:
            self.sems[e] = nc.alloc_semaphore("s_" + e)
            self.cnt[e] = 0
        self.dsems = []
        for i in range(n_dma_sems):
            k = "d%d" % i
            self.sems[k] = nc.alloc_semaphore("s_" + k)
            self.cnt[k] = 0
            self.dsems.append(k)
        self.dnext = 0
        self.seen = {e: {} for e in self.E}
        self.n_wait = 0
        self.n_ops = 0

    def _wait(self, eng, tick):
        if tick is None:
            return
        k, v = tick
        if self.seen[eng].get(k, 0) >= v:
            return
        self.E[eng].wait_ge(self.sems[k], v)
        self.seen[eng][k] = v
        self.n_wait += 1

    def _deps(self, eng, reads, writes):
        need = {}
        for b in reads:
            if b.w is not None:
                k, v = b.w
                if need.get(k, 0) < v:
                    need[k] = v
            if b.excl:
                for (k, v) in b.r:
                    if k != eng and need.get(k, 0) < v:
                        need[k] = v
        for b in writes:
            if b.w is not None:
                k, v = b.w
                if need.get(k, 0) < v:
                    need[k] = v
            for (k, v) in b.r:
                if need.get(k, 0) < v:
                    need[k] = v
        for k, v in need.items():
            self._wait(eng, (k, v))

    def _commit(self, tick, reads, writes):
        for b in reads:
            b.r.append(tick)
            if len(b.r) > 16:
                m = {}
                for (k, v) in b.r:
                    if m.get(k, 0) < v:
                        m[k] = v
                b.r = list(m.items())
        for b in writes:
            b.w = tick
            b.r = []

    def op(self, eng, fn, reads=(), writes=()):
        self._deps(eng, reads, writes)
        ins = fn()
        self.cnt[eng] += 1
        ins.then_inc(self.sems[eng], 1)
        tick = (eng, self.cnt[eng])
        self._commit(tick, reads, writes)
        self.n_ops += 1
        return tick

    def new_sem(self, k):
        self.sems[k] = self.nc.alloc_semaphore("s_" + k)
        self.cnt[k] = 0

    def barrier(self):
        for e in ("pe", "dve", "act", "sp"):
            for o in ("pe", "dve", "act", "pool"):
                if o != e and self.cnt[o] > 0:
                    self._wait(e, (o, self.cnt[o]))
            for k in self.dsems:
                if self.cnt[k] > 0:
                    self._wait(e, (k, self.cnt[k]))

    def dma(self, q, out=None, in_=None, reads=(), writes=(), multi=None, sem=None):
        pairs = multi if multi is not None else [(out, in_)]
        if sem is not None:
            k = sem
        else:
            k = self.dsems[self.dnext]
            self.dnext = (self.dnext + 1) % len(self.dsems)
        if self.cnt[k] > 0:
            self._wait(q, (k, self.cnt[k]))
        self._deps(q, reads, writes)
        for (o, i) in pairs:
            self.E[q].dma_start(out=o, in_=i, allow_slow_non_contiguous=True).then_inc(self.sems[k], 16)
            self.cnt[k] += 16
        tick = (k, self.cnt[k])
        self._commit(tick, reads, writes)
        return tick

    def finish(self):
        for e in ("pe", "dve", "act", "pool"):
            if self.cnt[e] > 0:
                self._wait("sp", (e, self.cnt[e]))
        for k in list(self.sems):
            if k not in self.E and self.cnt[k] > 0:
                self._wait("sp", (k, self.cnt[k]))


class Tile:
    def __init__(self, nc, name, shape, dtype, psum=False, handle=None):
        if handle is not None:
            self.t = handle
        elif psum:
            self.t = nc.alloc_psum_tensor(name, list(shape), dtype)
        else:
            self.t = nc.alloc_sbuf_tensor(name, list(shape), dtype)
        self.b = Buf(name, excl=psum)

    def __getitem__(self, k):
        return self.t[k]


SLOT_ELEMS = 4096
N_SLOTS = 3


class WStream:
    def __init__(self, nc, S, plan):
        self.nc, self.S = nc, S
        self.plan = plan
        self.slots = [Tile(nc, "wslot%d" % i, [128, SLOT_ELEMS], BF16) for i in range(N_SLOTS)]
        self.dram_buf = Buf("wdram")
        for i in range(N_SLOTS):
            S.new_sem("w%d" % i)
        self.issued = 0
        self.used = 0

    def _issue(self):
        i = self.issued
        ap, n = self.plan[i]
        sl = self.slots[i % N_SLOTS]
        self.S.dma("pool", out=sl[:, 0:n], in_=ap, reads=[self.dram_buf], writes=[sl.b], sem="w%d" % (i % N_SLOTS))
        self.issued += 1

    def get(self, expect_n):
        i = self.used
        while self.issued <= min(i + N_SLOTS - 2, len(self.plan) - 1):
            self._issue()
        assert self.plan[i][1] == expect_n, (i, self.plan[i][1], expect_n)
        self.used += 1
        return self.slots[i % N_SLOTS]


import contextlib

H_A, KV_A, HD_A = 8, 2, 64
H_B = 4
NEG = -1e30
EV_FM = 21
EV_TOK = 392
EW_OFF = [0, 4096, 8192, 12288, 16384, 20480, 21504]
EW_N = [4096, 4096, 4096, 4096, 4096, 1024, 8 * EV_TOK]
EW_TOT = 21504 + 8 * EV_TOK
OW_TOT = 256 + 10 * 4096


class Prog:
    def __init__(self, cfg):
        self.cfg = cfg
        self.nt = cfg.get("ntiles", 4)
        self.seq = self.nt * 512
        self.ntok = self.seq + NS
        self.depth = cfg.get("depth", DEPTH)
        self.layers = cfg.get("layers", None) or list(range(self.depth))
        nc = bass.Bass("TRN2", target_bir_lowering=False)
        self.nc = nc
        self.S = Sched(nc)
        self.stack = None
        self.declare_io()
        self.alloc()
        self.plan_weights()
        self.emit()

    def din(self, name, shape, dtype=F32):
        return self.nc.dram_tensor(name, list(shape), dtype, kind="ExternalInput").ap()

    def dout(self, name, shape, dtype=F32):
        return self.nc.dram_tensor(name, list(shape), dtype, kind="ExternalOutput").ap()

    def declare_io(self):
        sq = self.seq
        self.x_p = self.din("x_p", [sq, D])
        self.x_s = self.din("x_s", [NS, D])
        self.gains = self.din("gains", [128, 13 * KC])
        self.masks = self.din("masks", [128, 8 * 128])
        self.wgu = self.din("wgu", [DEPTH, 2, 11, 128, 4096])
        self.wd = self.din("wd", [DEPTH, 2, 8, 128, 2816])
        self.ewin = self.din("ewin", [2, 128, EW_TOT])
        self.ewout = self.din("ewout", [2, 2, 128, 4096])
        self.econv = self.din("econv", [2, 128, 48])
        self.esm = self.din("esm", [2, 128, 17])
        self.esk = self.din("esk", [2, 128, 1])
        self.swabias = self.din("swabias", [128, 8 * 256])
        self.decbias = self.din("decbias", [128, 129])
        self.c_k = self.din("c_k", [2, NS, 128, 128])
        self.c_v = self.din("c_v", [2, NS, 128, 128])
        self.s_gc = self.din("s_gc", [2, NS, 3, 1536])
        self.s_gs = self.din("s_gs", [2, NS, 4, 128, 128])
        self.owin = self.din("owin", [2, 128, OW_TOT])
        self.owout = self.din("owout", [2, 4, 128, 4096])
        self.oconv = self.din("oconv", [2, 128, 96])
        self.ohead = self.din("ohead", [2, 128, 96])
        self.ofeat = self.din("ofeat", [2, 128, 56])
        self.s_sc = self.din("s_sc", [2, NS, 3, 3072])
        self.s_ss = self.din("s_ss", [2, NS, 2048, 128])
        self.o_psc = self.dout("o_psc", [2, 3, 3072])
        self.o_pss = self.dout("o_pss", [2, 2048, 128])
        self.o_ssc = self.dout("o_ssc", [2, NS, 3, 3072])
        self.o_sss = self.dout("o_sss", [2, NS, 2048, 128])
        self.y_p = self.dout("y_p", [sq, D])
        self.y_s = self.dout("y_s", [NS, D])
        self.o_pk = self.dout("o_pk", [2, 128, 128])
        self.o_pv = self.dout("o_pv", [2, 128, 128])
        self.o_pgc = self.dout("o_pgc", [2, 3, 1536])
        self.o_pgs = self.dout("o_pgs", [2, 4, 128, 128])
        self.o_sk = self.dout("o_sk", [2, NS, 128, 128])
        self.o_sv = self.dout("o_sv", [2, NS, 128, 128])
        self.o_sgc = self.dout("o_sgc", [2, NS, 3, 1536])
        self.o_sgs = self.dout("o_sgs", [2, NS, 4, 128, 128])
        self.Bin = Buf("dram_in")
        self.Bout = Buf("dram_out")

    def alloc(self):
        nc = self.nc
        self.xT = Tile(nc, "xT", [128, KC, self.ntok], F32)
        self.xTb = [Buf("xT_t%d" % i) for i in range(self.nt)] + [Buf("xT_s")]
        self.ident = Tile(nc, "ident", [128, 128], F32)
        self.ident_bf = Tile(nc, "ident_bf", [128, 128], BF16)
        self.ones_bf = Tile(nc, "ones_bf", [128, 128], BF16)
        self.ones_f = Tile(nc, "ones_f", [128, 128], F32)
        self.gn = Tile(nc, "gn", [128, 13 * KC], F32)
        self.mk = Tile(nc, "mk", [128, 8, 128], F32)
        self.P = [Tile(nc, "ps%d" % i, [128, 512], F32, psum=True) for i in range(8)]

    def begin(self):
        assert self.stack is None
        self.stack = contextlib.ExitStack()
        self.nalloc = 0
        self.deferred = []

    def out_dma(self, dst, src, r=()):
        self.deferred.append((dst, src, list(r)))

    def end(self):
        for (dst, src, r) in self.deferred:
            self.dma(dst, src, r=r, w=[self.Bout])
        self.deferred = []
        self.S.barrier()
        self.stack.close()
        self.stack = None

    def T(self, name, shape, dtype):
        self.nalloc += 1
        h = self.stack.enter_context(self.nc.sbuf_tensor("%s_%d" % (name, self.S.n_ops), list(shape), dtype))
        return Tile(self.nc, name, shape, dtype, handle=h)

    def ffn_blocks(self, l, f):
        out = []
        for b in range(11):
            out.append((self.wgu[l, f, b], 4096))
        for b in range(8):
            out.append((self.wd[l, f, b], 2816))
        return out

    def tiles(self):
        ts = []
        for t in range(self.nt):
            segs = [(t * 512, 512, 0)]
            if t == self.nt - 1:
                segs.append((self.seq, NS, 512))
            ts.append(segs)
        return ts

    def even_blocks(self, e):
        out = []
        for _ in range(self.nt + 1):
            for b in range(7):
                out.append((self.ewin[e, :, EW_OFF[b]:EW_OFF[b] + EW_N[b]], EW_N[b]))
            for b in range(2):
                out.append((self.ewout[e, b], 4096))
        return out

    def odd_blocks(self, e):
        out = []
        for _ in range(self.nt + 1):
            out.append((self.owin[e, :, 0:256], 256))
            for b in range(10):
                out.append((self.owin[e, :, 256 + b * 4096:256 + (b + 1) * 4096], 4096))
            for b in range(4):
                out.append((self.owout[e, b], 4096))
        return out

    def mixer_blocks(self, l):
        if l % 2 == 0:
            return self.even_blocks(l // 2)
        return self.odd_blocks(l // 2)

    def plan_weights(self):
        plan = []
        ffn = self.cfg.get("ffn", True)
        for l in self.layers:
            if ffn:
                for _ in self.tiles():
                    plan += self.ffn_blocks(l, 0)
            if not self.cfg.get("nomix"):
                plan += self.mixer_blocks(l)
            if ffn:
                for _ in self.tiles():
                    plan += self.ffn_blocks(l, 1)
        self.ws = WStream(self.nc, self.S, plan)

    def bl(self, xs):
        return [getattr(x, 'b', x) for x in xs]

    def dve(self, fn, r=(), w=()):
        return self.S.op("dve", fn, self.bl(r), self.bl(w))

    def act(self, fn, r=(), w=()):
        return self.S.op("act", fn, self.bl(r), self.bl(w))

    def pe(self, fn, r=(), w=()):
        return self.S.op("pe", fn, self.bl(r), self.bl(w))

    def pool(self, fn, r=(), w=()):
        return self.S.op("pool", fn, self.bl(r), self.bl(w))

    def dma(self, out, in_, r=(), w=(), q="sp"):
        return self.S.dma(q, out=out, in_=in_, reads=self.bl(r), writes=self.bl(w))

    def gcol(self, idx, kc):
        return self.gn[:, idx * KC + kc: idx * KC + kc + 1]

    def tile_bufs(self, ti):
        b = [self.xTb[ti]]
        if ti == self.nt - 1:
            b.append(self.xTb[self.nt])
        return b

    def col_stats(self, src_fn, nchunk, n, ps, sq, ones, scale, bias_ln, out_ap, exp_bias=0.0, rd=()):
        nc = self.nc
        for c in range(nchunk):
            self.act(lambda c=c: nc.scalar.activation(sq[:, c, 0:n], src_fn(c), AF.Square), r=rd, w=[sq])

        def mm():
            ins = None
            for c in range(nchunk):
                ins = nc.tensor.matmul(ps[:, 0:n], ones[:], sq[:, c, 0:n], start=(c == 0), stop=(c == nchunk - 1))
            return ins
        self.pe(mm, r=[sq, ones], w=[ps])
        return ps

    def rmsnorm_cols(self, xbufs, c0, n, gidx, hn, l0, sq, rstd):
        nc = self.nc
        ps = self.P[7]
        self.act(lambda: nc.scalar.activation(sq[:, :, l0:l0 + n], self.xT[:, :, c0:c0 + n], AF.Square),
                 r=xbufs, w=[sq])

        def mm():
            ins = None
            for kc in range(KC):
                ins = nc.tensor.matmul(ps[:, 0:n], self.ones_bf[:], sq[:, kc, l0:l0 + n],
                                       start=(kc == 0), stop=(kc == KC - 1))
            return ins
        self.pe(mm, r=[sq, self.ones_bf], w=[ps])
        self.act(lambda: nc.scalar.activation(rstd[:, l0:l0 + n], ps[:, 0:n], AF.Ln, bias=EPS, scale=1.0 / D),
                 r=[ps], w=[rstd])
        self.act(lambda: nc.scalar.activation(rstd[:, l0:l0 + n], rstd[:, l0:l0 + n], AF.Exp, scale=-0.5),
                 r=[rstd], w=[rstd])
        for kc in range(KC):
            self.dve(lambda kc=kc: nc.vector.scalar_tensor_tensor(
                out=hn[:, kc, l0:l0 + n], in0=self.xT[:, kc, c0:c0 + n], scalar=self.gcol(gidx, kc),
                in1=rstd[:, l0:l0 + n], op0=ALU.mult, op1=ALU.mult),
                r=xbufs + [rstd, self.gn], w=[hn])

    def ffn_tile(self, l, f, ti, segs, hn, hT, sgs):
        nc = self.nc
        xb = self.tile_bufs(ti)
        it = 0
        for blk in range(11):
            w = self.ws.get(4096)
            wv = w[:, 0:4096].rearrange("p (a k c) -> p a k c", a=2, k=KC)
            for fc in range(2):
                fch = blk * 2 + fc
                for si, (g0, n, l0) in enumerate(segs):
                    if si == 0:
                        pg, pu = self.P[(it % 2)], self.P[2 + (it % 2)]
                    else:
                        pg, pu = self.P[4], self.P[5]

                    def mm(pt, a):
                        ins = None
                        for kc in range(KC):
                            ins = nc.tensor.matmul(pt[:, 0:n], wv[:, a, kc, fc * 128:(fc + 1) * 128],
                                                   hn[:, kc, l0:l0 + n], start=(kc == 0), stop=(kc == KC - 1))
                        return ins
                    self.pe(lambda: mm(pg, 0), r=[w, hn], w=[pg])
                    self.pe(lambda: mm(pu, 1), r=[w, hn], w=[pu])
                    sg = sgs[it % 2]
                    self.act(lambda: nc.scalar.activation(sg[:, 0:n], pg[:, 0:n], AF.Silu), r=[pg], w=[sg])
                    self.dve(lambda: nc.vector.tensor_tensor(out=hT[:, fch, l0:l0 + n], in0=sg[:, 0:n],
                                                             in1=pu[:, 0:n], op=ALU.mult), r=[sg, pu], w=[hT])
                it += 1
        it = 0
        for blk in range(8):
            w = self.ws.get(2816)
            wv = w[:, 0:2816].rearrange("p (k c) -> p k c", k=FC)
            dch = blk
            for si, (g0, n, l0) in enumerate(segs):
                po = self.P[6 + (it % 2)] if si == 0 else self.P[4 + (it % 2)]

                def mm():
                    ins = None
                    for k in range(FC):
                        ins = nc.tensor.matmul(po[:, 0:n], wv[:, k, :], hT[:, k, l0:l0 + n],
                                               start=(k == 0), stop=(k == FC - 1))
                    return ins
                self.pe(mm, r=[w, hT], w=[po])
                xs = self.xT[:, dch, g0:g0 + n]
                self.dve(lambda: nc.vector.scalar_tensor_tensor(out=xs, in0=po[:, 0:n], scalar=0.5, in1=xs,
                                                                op0=ALU.mult, op1=ALU.add), r=[po] + xb, w=xb)
            it += 1

    def ffn(self, l, f):
        self.begin()
        hns = [self.T("hn%d" % i, [128, KC, 528], BF16) for i in range(2)]
        hT = self.T("hT", [128, FC, 528], BF16)
        sq = self.T("sq", [128, KC, 528], BF16)
        rstd = self.T("rstd", [128, 528], F32)
        sgs = [self.T("sg%d" % i, [128, 512], F32) for i in range(2)]
        gidx = l if f == 0 else 8 + l
        for ti, segs in enumerate(self.tiles()):
            hn = hns[ti % 2]
            for (g0, n, l0) in segs:
                self.rmsnorm_cols(self.tile_bufs(ti), g0, n, gidx, hn, l0, sq, rstd)
            self.ffn_tile(l, f, ti, segs, hn, hT, sgs)
        self.end()

    def consts(self):
        nc = self.nc
        self.pool(lambda: nc.gpsimd.memset(self.ident[:], 0.0), w=[self.ident])
        self.pool(lambda: nc.gpsimd.affine_select(self.ident[:], self.ident[:], pattern=[[-1, 128]],
                                                  compare_op=ALU.not_equal, fill=1.0, base=0, channel_multiplier=1),
                  r=[self.ident], w=[self.ident])
        self.pool(lambda: nc.gpsimd.memset(self.ones_bf[:], 1.0), w=[self.ones_bf])
        self.pool(lambda: nc.gpsimd.memset(self.ones_f[:], 1.0), w=[self.ones_f])
        self.dve(lambda: nc.vector.tensor_copy(self.ident_bf[:], self.ident[:]), r=[self.ident], w=[self.ident_bf])
        self.dma(self.gn[:], self.gains, r=[self.Bin], w=[self.gn])
        self.dma(self.mk[:].rearrange("p a b -> p (a b)"), self.masks, r=[self.Bin], w=[self.mk])

    def load_x(self):
        nc = self.nc
        self.begin()
        xins = [self.T("xin%d" % i, [128, D], F32) for i in range(2)]
        ntt = self.seq // 128
        for tt in range(ntt + 1):
            xin = xins[tt % 2]
            if tt < ntt:
                rows, src, c0 = 128, self.x_p[tt * 128:(tt + 1) * 128, :], tt * 128
                xb = self.xTb[tt // 4]
            else:
                rows, src, c0 = NS, self.x_s, self.seq
                xb = self.xTb[self.nt]
            self.dma(xin[0:rows, :], src, r=[self.Bin], w=[xin])
            for half in range(2):
                ps = self.P[(tt % 2) * 2 + half]

                def tr():
                    ins = None
                    for j in range(4):
                        kc = half * 4 + j
                        ins = nc.tensor.transpose(ps[:, j * 128:j * 128 + rows], xin[0:rows, kc * 128:(kc + 1) * 128],
                                                  self.ident[0:rows, 0:rows])
                    return ins
                self.pe(tr, r=[xin, self.ident], w=[ps])
                pv = ps[:, :].rearrange("p (j c) -> p j c", j=4)[:, :, 0:rows]
                dst = self.xT[:, half * 4:half * 4 + 4, c0:c0 + rows]
                if half == 0:
                    self.dve(lambda: nc.vector.tensor_copy(dst, pv), r=[ps], w=[xb])
                else:
                    self.act(lambda: nc.scalar.copy(dst, pv), r=[ps], w=[xb])
        self.end()

    def final(self):
        nc = self.nc
        self.begin()
        sq = self.T("sq", [128, KC, 528], BF16)
        rstd = self.T("rstd", [128, 528], F32)
        yos = [self.T("yo%d" % i, [128, D], F32) for i in range(2)]
        for ti, segs in enumerate(self.tiles()):
            xb = self.tile_bufs(ti)
            for (c0, n, l0) in segs:
                ps = self.P[7]
                self.act(lambda: nc.scalar.activation(sq[:, :, l0:l0 + n], self.xT[:, :, c0:c0 + n], AF.Square),
                         r=xb, w=[sq])

                def mm():
                    ins = None
                    for kc in range(KC):
                        ins = nc.tensor.matmul(ps[:, 0:n], self.ones_bf[:], sq[:, kc, l0:l0 + n],
                                               start=(kc == 0), stop=(kc == KC - 1))
                    return ins
                self.pe(mm, r=[sq, self.ones_bf], w=[ps])
                self.act(lambda: nc.scalar.activation(rstd[:, l0:l0 + n], ps[:, 0:n], AF.Ln, bias=EPS, scale=1.0 / D),
                         r=[ps], w=[rstd])
                self.act(lambda: nc.scalar.activation(rstd[:, l0:l0 + n], rstd[:, l0:l0 + n], AF.Exp, scale=-0.5),
                         r=[rstd], w=[rstd])
                for kc in range(KC):
                    self.dve(lambda kc=kc: nc.vector.scalar_tensor_tensor(
                        out=self.xT[:, kc, c0:c0 + n], in0=self.xT[:, kc, c0:c0 + n], scalar=self.gcol(12, kc),
                        in1=rstd[:, l0:l0 + n], op0=ALU.mult, op1=ALU.mult), r=xb + [rstd, self.gn], w=xb)
        ntt = self.seq // 128
        for tt in range(ntt + 1):
            yo = yos[tt % 2]
            if tt < ntt:
                rows, dst, c0 = 128, self.y_p[tt * 128:(tt + 1) * 128, :], tt * 128
                xb = self.xTb[tt // 4]
            else:
                rows, dst, c0 = NS, self.y_s, self.seq
                xb = self.xTb[self.nt]
            for half in range(2):
                ps = self.P[(tt % 2) * 2 + half]

                def tr():
                    ins = None
                    for j in range(4):
                        kc = half * 4 + j
                        ins = nc.tensor.transpose(ps[0:rows, j * 128:(j + 1) * 128], self.xT[:, kc, c0:c0 + rows],
                                                  self.ident[:])
                    return ins
                self.pe(tr, r=[xb, self.ident], w=[ps])
                if half == 0:
                    self.dve(lambda: nc.vector.tensor_copy(yo[0:rows, 0:512], ps[0:rows, :]), r=[ps], w=[yo])
                else:
                    self.act(lambda: nc.scalar.copy(yo[0:rows, 512:1024], ps[0:rows, :]), r=[ps], w=[yo])
            self.dma(dst, yo[0:rows, :], r=[yo], w=[self.Bout])
        self.end()

    def mixer(self, l):
        if l % 2 == 0:
            self.even_mixer(l)
        else:
            self.odd_mixer(l)

    def emit(self):
        self.consts()
        self.load_x()
        for l in self.layers:
            if self.cfg.get("ffn", True):
                self.ffn(l, 0)
            if not self.cfg.get("nomix"):
                self.mixer(l)
            if self.cfg.get("ffn", True):
                self.ffn(l, 1)
        self.final()
        self.S.finish()


def even_alloc(self, nc_):
    G = {}
    T = self.T
    G["hn"] = T("hn", [128, KC, nc_], BF16)
    G["sqm"] = T("sqm", [128, KC, nc_], BF16)
    G["rstd"] = T("rstd", [128, nc_], F32)
    G["qT"] = T("qT", [128, 4, nc_], BF16)
    G["acc"] = [T("acc%d" % i, [128, nc_], F32) for i in range(2)]
    G["cq"] = T("cq", [128, 12, nc_], F32)
    G["zs"] = T("zs", [128, 4, nc_], BF16)
    oT = Tile(self.nc, "oT", [128, 4, nc_], F32, handle=G["hn"].t.bitcast(F32).reshape([128, 4, nc_]))
    oT.b = G["hn"].b
    G["oT"] = oT
    G["cw"] = T("cw", [128, 12, 4], F32)
    G["sm"] = T("sm", [128, 17], F32)
    G["eA"] = T("eA", [128, 4], F32)
    G["sq4"] = T("sq4", [128, 2, nc_], BF16)
    return G


def even_common(self, G, e):
    nc = self.nc
    self.dma(G["cw"][:].rearrange("p a b -> p (a b)"), self.econv[e], r=[self.Bin], w=[G["cw"]])
    self.dma(G["sm"][:], self.esm[e], r=[self.Bin], w=[G["sm"]])
    self.act(lambda: nc.scalar.activation(G["eA"][:], G["sm"][:, 0:4], AF.Exp), r=[G["sm"]], w=[G["eA"]])
    self.dve(lambda: nc.vector.tensor_scalar_mul(G["eA"][:], G["eA"][:], -1.0), r=[G["eA"]], w=[G["eA"]])


def even_inproj_fm(self, G, n, conv_fn, k_dst):
    nc = self.nc
    hn = G["hn"]
    j = 0
    for blk in range(6):
        nb = EW_N[blk]
        w = self.ws.get(nb)
        ncols = nb // KC
        wv = w[:, 0:nb].rearrange("p (k c) -> p k c", k=KC)
        for cc in range(ncols // 128):
            ps = self.P[j % 2]

            def mm():
                ins = None
                for kc in range(KC):
                    ins = nc.tensor.matmul(ps[:, 0:n], wv[:, kc, cc * 128:(cc + 1) * 128], hn[:, kc, 0:n],
                                           start=(kc == 0), stop=(kc == KC - 1))
                return ins
            self.pe(mm, r=[w, hn], w=[ps])
            if j < 4:
                self.act(lambda: nc.scalar.copy(G["qT"][:, j, 0:n], ps[:, 0:n]), r=[ps], w=[G["qT"]])
            elif j == 4:
                self.act(lambda: nc.scalar.copy(k_dst, ps[:, 0:n]), r=[ps], w=[G["kTa"]])
            elif j < 17:
                conv_fn(j - 5, ps)
            else:
                self.act(lambda: nc.scalar.activation(G["zs"][:, j - 17, 0:n], ps[:, 0:n], AF.Silu),
                         r=[ps], w=[G["zs"]])
            j += 1
    assert j == EV_FM


def gates_from_ba(self, G, ba_ps_ap, rows, beta_dst, g_dst, tmp, rd, wr):
    nc = self.nc
    sm = G["sm"]
    self.act(lambda: nc.scalar.activation(beta_dst, ba_ps_ap[:, 0:4], AF.Sigmoid), r=rd, w=wr)
    self.dve(lambda: nc.vector.tensor_tensor(out=tmp[0:rows, 0:4], in0=ba_ps_ap[:, 4:8], in1=sm[0:rows, 4:8], op=ALU.add),
             r=rd + [sm], w=[tmp])
    self.act(lambda: nc.scalar.activation(tmp[0:rows, 0:4], tmp[0:rows, 0:4], AF.Exp), r=[tmp], w=[tmp])
    self.act(lambda: nc.scalar.activation(tmp[0:rows, 0:4], tmp[0:rows, 0:4], AF.Ln, bias=1.0, scale=1.0), r=[tmp], w=[tmp])
    self.dve(lambda: nc.vector.tensor_tensor(out=g_dst, in0=tmp[0:rows, 0:4], in1=G["eA"][0:rows, :], op=ALU.mult),
             r=[tmp, G["eA"]], w=wr)


def l2norm_heads(self, G, n):
    nc = self.nc
    cq, sq4 = G["cq"], G["sq4"]
    for c in range(8):
        ps = self.P[c % 2]
        self.act(lambda: nc.scalar.activation(sq4[:, c % 2, 0:n], cq[:, c, 0:n], AF.Square), r=[cq], w=[sq4])
        self.pe(lambda: nc.tensor.matmul(ps[:, 0:n], self.ones_bf[:], sq4[:, c % 2, 0:n], start=True, stop=True),
                r=[sq4, self.ones_bf], w=[ps])
        rs = G["acc"][c % 2]
        self.act(lambda: nc.scalar.activation(rs[:, 0:n], ps[:, 0:n], AF.Ln, bias=EPS, scale=1.0), r=[ps], w=[rs])
        self.act(lambda: nc.scalar.activation(rs[:, 0:n], rs[:, 0:n], AF.Exp, scale=-0.5,
                                              bias=(math.log(128.0 ** -0.5) if c < 4 else 0.0)), r=[rs], w=[rs])
        self.dve(lambda: nc.vector.tensor_tensor(out=cq[:, c, 0:n], in0=cq[:, c, 0:n], in1=rs[:, 0:n], op=ALU.mult),
                 r=[cq, rs], w=[cq])


def gdn_post(self, G, n, e):
    nc = self.nc
    oT, sq4, mix = G["oT"], G["sq4"], G["sqm"]
    for h in range(4):
        ps = self.P[h % 2]
        self.act(lambda: nc.scalar.activation(sq4[:, h % 2, 0:n], oT[:, h, 0:n], AF.Square), r=[oT], w=[sq4])
        self.pe(lambda: nc.tensor.matmul(ps[:, 0:n], self.ones_bf[:], sq4[:, h % 2, 0:n], start=True, stop=True),
                r=[sq4, self.ones_bf], w=[ps])
        rs = G["acc"][h % 2]
        self.act(lambda: nc.scalar.activation(rs[:, 0:n], ps[:, 0:n], AF.Ln, bias=EPS, scale=1.0 / 128.0), r=[ps], w=[rs])
        self.act(lambda: nc.scalar.activation(rs[:, 0:n], rs[:, 0:n], AF.Exp, scale=-0.5), r=[rs], w=[rs])
        self.dve(lambda: nc.vector.scalar_tensor_tensor(out=rs[:, 0:n], in0=oT[:, h, 0:n], scalar=G["sm"][:, 16:17],
                                                        in1=rs[:, 0:n], op0=ALU.mult, op1=ALU.mult),
                 r=[oT, rs, G["sm"]], w=[rs])
        self.dve(lambda: nc.vector.tensor_tensor(out=mix[:, 4 + h, 0:n], in0=rs[:, 0:n], in1=G["zs"][:, h, 0:n], op=ALU.mult),
                 r=[rs, G["zs"]], w=[mix])


def even_outproj(self, G, segs_bufs, c0, n):
    nc = self.nc
    mix = G["sqm"]
    for blk in range(2):
        w = self.ws.get(4096)
        wv = w[:, 0:4096].rearrange("p (k c) -> p k c", k=KC)
        for dc in range(4):
            dch = blk * 4 + dc
            ps = self.P[dch % 2]

            def mm():
                ins = None
                for kc in range(KC):
                    ins = nc.tensor.matmul(ps[:, 0:n], wv[:, kc, dc * 128:(dc + 1) * 128], mix[:, kc, 0:n],
                                           start=(kc == 0), stop=(kc == KC - 1))
                return ins
            self.pe(mm, r=[w, mix], w=[ps])
            xs = self.xT[:, dch, c0:c0 + n]
            self.dve(lambda: nc.vector.tensor_tensor(out=xs, in0=xs, in1=ps[:, 0:n], op=ALU.add),
                     r=[ps] + segs_bufs, w=segs_bufs)


def swa_block(self, G, W, qb, ql):
    nc = self.nc
    mix = G["sqm"]
    sm = G["sm"]
    k0 = (qb - 1) * 128 if qb > 0 else 0
    nk = 256 if qb > 0 else 128
    nkt = nk // 128
    it = 0
    for c in range(4):
        o_ps = self.P[5]
        for kv in range(2):
            h = kv * 4 + c
            rows = slice(kv * 64, (kv + 1) * 64)
            s_ps = self.P[2 + it % 2]
            sb = W["sb"][it % 2]
            pn = W["pn"][it % 2]
            PT = W["PT"][it % 2]
            col = W["col"][it % 2]
            self.pe(lambda: nc.tensor.matmul(s_ps[:, 0:nk], G["qT"][rows, c, ql:ql + 128], G["kTa"][rows, k0:k0 + nk],
                                             start=True, stop=True), r=[G["qT"], G["kTa"]], w=[s_ps])
            self.dve(lambda: nc.vector.scalar_tensor_tensor(out=sb[:, 0:nk], in0=s_ps[:, 0:nk], scalar=0.125,
                                                            in1=G["bias"][:, h, 256 - nk:256], op0=ALU.mult, op1=ALU.add),
                     r=[s_ps, G["bias"]], w=[sb])
            self.dve(lambda: nc.vector.reduce_max(out=col[:, 0:1], in_=sb[:, 0:nk], axis=AX.X), r=[sb], w=[col])
            self.dve(lambda: nc.vector.tensor_tensor(out=col[:, 0:1], in0=col[:, 0:1], in1=sm[:, 8 + h:9 + h], op=ALU.max),
                     r=[col, sm], w=[col])
            self.dve(lambda: nc.vector.tensor_scalar_mul(col[:, 1:2], col[:, 0:1], -1.0), r=[col], w=[col])
            self.act(lambda: nc.scalar.activation(sb[:, 0:nk], sb[:, 0:nk], AF.Exp, bias=col[:, 1:2], scale=1.0),
                     r=[sb, col], w=[sb])
            self.dve(lambda: nc.vector.reduce_sum(out=col[:, 2:3], in_=sb[:, 0:nk], axis=AX.X), r=[sb], w=[col])
            self.act(lambda: nc.scalar.activation(col[:, 3:4], sm[:, 8 + h:9 + h], AF.Exp, bias=col[:, 1:2], scale=1.0),
                     r=[sm, col], w=[col])
            self.dve(lambda: nc.vector.tensor_tensor(out=col[:, 2:3], in0=col[:, 2:3], in1=col[:, 3:4], op=ALU.add),
                     r=[col], w=[col])
            self.dve(lambda: nc.vector.reciprocal(col[:, 4:5], col[:, 2:3]), r=[col], w=[col])
            self.dve(lambda: nc.vector.tensor_scalar_mul(pn[:, 0:nk], sb[:, 0:nk], col[:, 4:5]), r=[sb, col], w=[pn])
            t_ps = self.P[4]
            tv = t_ps[:, :].bitcast(BF16)

            def tr():
                ins = None
                for kt in range(nkt):
                    ins = nc.tensor.transpose(tv[:, kt * 128:(kt + 1) * 128], pn[:, kt * 128:(kt + 1) * 128],
                                              self.ident_bf[:])
                return ins
            self.pe(tr, r=[pn, self.ident_bf], w=[t_ps])
            self.act(lambda: nc.scalar.copy(PT[:, 0:nk], tv[:, 0:nk]), r=[t_ps], w=[PT])

            def pv():
                ins = None
                for kt in range(nkt):
                    ins = nc.tensor.matmul(o_ps[:, 0:128], G["Va"][:, k0 // 128 + kt, kv * 128:(kv + 1) * 128],
                                           PT[:, kt * 128:(kt + 1) * 128],
                                           start=(kv == 0 and kt == 0), stop=(kv == 1 and kt == nkt - 1))
                return ins
            self.pe(pv, r=[G["Va"], PT], w=[o_ps])
            it += 1
        self.act(lambda: nc.scalar.copy(mix[:, c, ql:ql + 128], o_ps[:, 0:128]), r=[o_ps], w=[mix])


def gdn_subtile(self, G, W, st, cs):
    nc = self.nc
    cq = G["cq"]
    mk = self.mk
    ident = self.ident
    beta_t, g_t = G["beta_t"], G["g_t"]
    sc = W["sc"]
    R = W["R"]
    P = self.P
    Sst, Sbf, oT = G["Sst"], G["Sbf"], G["oT"]
    self.pe(lambda: nc.tensor.matmul(P[4][:, 0:4], mk[:, 0, :], g_t[:, st, :], start=True, stop=True), r=[mk, g_t], w=[P[4]])
    self.pe(lambda: nc.tensor.matmul(P[4][:, 4:8], mk[:, 4, :], g_t[:, st, :], start=True, stop=True), r=[mk, g_t], w=[P[4]])
    self.dve(lambda: nc.vector.tensor_copy(sc[:, 0:8], P[4][:, 0:8]), r=[P[4]], w=[sc])
    self.dve(lambda: nc.vector.tensor_tensor(out=sc[:, 8:12], in0=sc[:, 4:8], in1=sc[:, 0:4], op=ALU.subtract), r=[sc], w=[sc])
    self.act(lambda: nc.scalar.activation(sc[:, 8:12], sc[:, 8:12], AF.Exp), r=[sc], w=[sc])
    self.act(lambda: nc.scalar.activation(sc[:, 12:16], sc[:, 0:4], AF.Exp), r=[sc], w=[sc])
    self.dve(lambda: nc.vector.tensor_tensor(out=sc[:, 16:20], in0=sc[:, 12:16], in1=beta_t[:, st, :], op=ALU.mult),
             r=[sc, beta_t], w=[sc])
    self.dve(lambda: nc.vector.tensor_tensor(out=R[:, :, 0:128], in0=mk[:, 0, :].unsqueeze(1).to_broadcast([128, 4, 128]),
                                             in1=g_t[:, st, :].unsqueeze(2).to_broadcast([128, 4, 128]), op=ALU.mult),
             r=[mk, g_t], w=[R])
    self.dve(lambda: nc.vector.tensor_tensor(out=R[:, :, 128:256], in0=ident[:].unsqueeze(1).to_broadcast([128, 4, 128]),
                                             in1=beta_t[:, st, :].unsqueeze(2).to_broadcast([128, 4, 128]), op=ALU.mult),
             r=[ident, beta_t], w=[R])
    for hh in range(2):
        self.pe(lambda: nc.tensor.matmul(P[hh][:, 0:512], self.ones_f[:], R[:, 2 * hh:2 * hh + 2, :], start=True, stop=True),
                r=[self.ones_f, R], w=[P[hh]])
    bcS = W["R"]
    self.dve(lambda: nc.vector.tensor_copy(bcS[:, 0:2, :], P[0][:, 0:512].rearrange("p (a b) -> p a b", a=2)), r=[P[0]], w=[bcS])
    self.act(lambda: nc.scalar.copy(bcS[:, 2:4, :], P[1][:, 0:512].rearrange("p (a b) -> p a b", a=2)), r=[P[1]], w=[bcS])

    def gcbc(h):
        return bcS[:, h, 0:128]

    def betabc(h):
        return bcS[:, h, 128:256]
    for h in range(4):
        self.act(lambda: nc.scalar.activation(sc[:, 20 + 2 * h:22 + 2 * h], gcbc(h)[:, 63:128:64], AF.Exp), r=[bcS], w=[sc])

    def head(h):
        q = h % 2
        tw = W["set"][q]
        PA, PN, PT_ = P[2 + q], P[4 + q], P[6 + q]
        kn = cq[:, 4 + h, cs:cs + 128]
        qn = cq[:, h, cs:cs + 128]
        vv = cq[:, 8 + h, cs:cs + 128]
        bc = bcS
        yield self.pe(lambda: nc.tensor.transpose(PA[:, 0:128], kn, ident[:]), r=[cq, ident], w=[PA])
        yield self.pe(lambda: nc.tensor.transpose(PA[:, 128:256], vv, ident[:]), r=[cq, ident], w=[PA])
        yield self.pe(lambda: nc.tensor.matmul(PA[:, 256:384], kn, kn, start=True, stop=True), r=[cq], w=[PA])
        yield self.pe(lambda: nc.tensor.matmul(PA[:, 384:512], kn, qn, start=True, stop=True), r=[cq], w=[PA])
        yield self.dve(lambda: nc.vector.tensor_scalar_mul(tw["kbg"][:], PA[:, 0:128], sc[:, 16 + h:17 + h]), r=[PA, sc], w=[tw["kbg"]])
        yield self.act(lambda: nc.scalar.mul(tw["kd"][:], PA[:, 0:128], sc[:, 8 + h:9 + h]), r=[PA, sc], w=[tw["kd"]])
        yield self.dve(lambda: nc.vector.tensor_scalar_mul(tw["vb"][:], PA[:, 128:256], beta_t[:, st, h:h + 1]),
                       r=[PA, beta_t], w=[tw["vb"]])
        yield self.dve(lambda: nc.vector.scalar_tensor_tensor(out=tw["tA"][:], in0=gcbc(h), scalar=sc[:, h:h + 1], in1=mk[:, 1, :],
                                                              op0=ALU.subtract, op1=ALU.max), r=[bc, sc, mk], w=[tw["tA"]])
        yield self.act(lambda: nc.scalar.activation(tw["Es"][:], tw["tA"][:], AF.Exp, scale=-1.0), r=[tw["tA"]], w=[tw["Es"]])
        yield self.dve(lambda: nc.vector.scalar_tensor_tensor(out=tw["tB"][:], in0=gcbc(h), scalar=sc[:, h:h + 1], in1=mk[:, 2, :],
                                                              op0=ALU.subtract, op1=ALU.min), r=[bc, sc, mk], w=[tw["tB"]])
        yield self.act(lambda: nc.scalar.activation(tw["ETd"][:], tw["tB"][:], AF.Exp), r=[tw["tB"]], w=[tw["ETd"]])
        yield self.dve(lambda: nc.vector.tensor_tensor(out=tw["ETs"][:], in0=tw["ETd"][:], in1=mk[:, 3, :], op=ALU.mult),
                       r=[tw["ETd"], mk], w=[tw["ETs"]])
        Pc, Qc, X, Y = tw["Pm"][0], tw["Qm"][0], tw["X"][0], tw["Y"][0]
        yield self.dve(lambda: nc.vector.scalar_tensor_tensor(out=Qc[:], in0=PA[:, 256:384], scalar=beta_t[:, st, h:h + 1],
                                                              in1=tw["Es"][:], op0=ALU.mult, op1=ALU.mult),
                       r=[PA, beta_t, tw["Es"]], w=[Qc])
        yield self.dve(lambda: nc.vector.tensor_tensor(out=tw["tA"][:], in0=PA[:, 256:384], in1=tw["ETs"][:], op=ALU.mult),
                       r=[PA, tw["ETs"]], w=[tw["tA"]])
        yield self.dve(lambda: nc.vector.tensor_tensor(out=Pc[:], in0=tw["tA"][:], in1=betabc(h), op=ALU.mult),
                       r=[tw["tA"], bc], w=[Pc])
        yield self.dve(lambda: nc.vector.tensor_tensor(out=tw["AqkT"][:], in0=PA[:, 384:512], in1=tw["ETd"][:], op=ALU.mult),
                       r=[PA, tw["ETd"]], w=[tw["AqkT"]])
        yield self.act(lambda: nc.scalar.activation(tw["tB"][:], gcbc(h), AF.Exp), r=[bc], w=[tw["tB"]])
        yield self.dve(lambda: nc.vector.tensor_tensor(out=tw["qgT"][:], in0=qn, in1=tw["tB"][:], op=ALU.mult),
                       r=[cq, tw["tB"]], w=[tw["qgT"]])
        yield self.dve(lambda: nc.vector.tensor_tensor(out=X[:], in0=ident[:], in1=Pc[:], op=ALU.subtract), r=[ident, Pc], w=[X])
        yield self.dve(lambda: nc.vector.tensor_tensor(out=Y[:], in0=ident[:], in1=Qc[:], op=ALU.subtract), r=[ident, Qc], w=[Y])
        for k in range(1, 6):
            bk = PN
            last = (k == 5)
            Pn, Qn, Xn, Yn = tw["Pm"][k % 2], tw["Qm"][k % 2], tw["X"][k % 2], tw["Y"][k % 2]

            def sqr():
                ins = nc.tensor.matmul(bk[:, 0:128], Qc[:], Pc[:], start=True, stop=True)
                if not last:
                    ins = nc.tensor.matmul(bk[:, 128:256], Pc[:], Qc[:], start=True, stop=True)
                return ins
            yield self.pe(sqr, r=[Pc, Qc], w=[bk])
            yield self.act(lambda: nc.scalar.copy(Pn[:], bk[:, 0:128]), r=[bk], w=[Pn])
            if not last:
                yield self.dve(lambda: nc.vector.tensor_copy(Qn[:], bk[:, 128:256]), r=[bk], w=[Qn])

            def upd():
                ins = nc.tensor.matmul(bk[:, 256:384], Y[:], Pn[:], start=True, stop=True)
                if not last:
                    ins = nc.tensor.matmul(bk[:, 384:512], X[:], Qn[:], start=True, stop=True)
                return ins
            yield self.pe(upd, r=[X, Y, Pn] + ([] if last else [Qn]), w=[bk])
            yield self.dve(lambda: nc.vector.tensor_tensor(out=Xn[:], in0=X[:], in1=bk[:, 256:384], op=ALU.add), r=[X, bk], w=[Xn])
            if not last:
                yield self.dve(lambda: nc.vector.tensor_tensor(out=Yn[:], in0=Y[:], in1=bk[:, 384:512], op=ALU.add), r=[Y, bk], w=[Yn])
            Pc, Qc, X, Y = Pn, Qn, Xn, Yn
        yield self.pe(lambda: nc.tensor.matmul(PT_[:, 0:128], X[:], tw["vb"][:], start=True, stop=True), r=[X, tw["vb"]], w=[PT_])
        yield self.pe(lambda: nc.tensor.matmul(PT_[:, 128:256], tw["kbg"][:], X[:], start=True, stop=True), r=[X, tw["kbg"]], w=[PT_])
        yield self.act(lambda: nc.scalar.copy(tw["u"][:], PT_[:, 0:128]), r=[PT_], w=[tw["u"]])
        yield self.dve(lambda: nc.vector.tensor_copy(tw["wTa"][:, 0:64], PT_[:, 128:192]), r=[PT_], w=[tw["wTa"]])
        yield self.dve(lambda: nc.vector.tensor_copy(tw["wTz"][:, 64:128], PT_[:, 192:256]), r=[PT_], w=[tw["wTz"]])
        for ck in range(2):
            rr = slice(ck * 64, (ck + 1) * 64)
            if ck == 0:
                yield self.pe(lambda: nc.tensor.matmul(PT_[0:64, 256:384], tw["wTa"][:, 0:64], Sbf[:, h, :], start=True, stop=True),
                              r=[tw["wTa"], Sbf], w=[PT_])
            else:
                yield self.pe(lambda: nc.tensor.matmul(PT_[:, 256:384], tw["wTz"][:], Sbf[:, h, :], start=True, stop=True),
                              r=[tw["wTz"], Sbf], w=[PT_])
            yield self.dve(lambda: nc.vector.tensor_tensor(out=tw["vnew"][rr, :], in0=tw["u"][rr, :], in1=PT_[rr, 256:384],
                                                           op=ALU.subtract), r=[tw["u"], PT_], w=[tw["vnew"]])

            def omm():
                nc.tensor.matmul(PA[:, ck * 64:(ck + 1) * 64], Sbf[:, h, :], tw["qgT"][:, rr], start=True, stop=False)
                return nc.tensor.matmul(PA[:, ck * 64:(ck + 1) * 64], tw["vnew"][rr, :], tw["AqkT"][rr, rr],
                                        start=False, stop=True)
            yield self.pe(omm, r=[Sbf, tw["qgT"], tw["vnew"], tw["AqkT"]], w=[PA])
            yield self.act(lambda: nc.scalar.copy(oT[:, h, cs + ck * 64:cs + (ck + 1) * 64], PA[:, ck * 64:(ck + 1) * 64]),
                           r=[PA], w=[oT])
            yield self.pe(lambda: nc.tensor.matmul(PT_[:, 384:512], tw["kd"][rr, :], tw["vnew"][rr, :], start=True, stop=True),
                          r=[tw["kd"], tw["vnew"]], w=[PT_])
            yield self.dve(lambda: nc.vector.scalar_tensor_tensor(out=Sst[:, h, :], in0=Sst[:, h, :],
                                                                  scalar=sc[:, 20 + 2 * h + ck:21 + 2 * h + ck],
                                                                  in1=PT_[:, 384:512], op0=ALU.mult, op1=ALU.add),
                           r=[Sst, sc, PT_], w=[Sst])
            yield self.act(lambda: nc.scalar.copy(Sbf[:, h, :], Sst[:, h, :]), r=[Sst], w=[Sbf])

    for pair in range(2):
        gens = [head(2 * pair), head(2 * pair + 1)]
        while gens:
            for g in list(gens):
                try:
                    next(g)
                except StopIteration:
                    gens.remove(g)


def even_mixer(self, l):
    nc = self.nc
    e = l // 2
    nsub = self.seq // 128
    self.begin()
    G = even_alloc(self, 512)
    T = self.T
    G["pre"] = [T("pre%d" % i, [128, 515], F32) for i in range(2)]
    G["kTa"] = T("kTa", [128, self.seq], BF16)
    G["Va"] = T("Va", [128, nsub, 256], BF16)
    G["carry"] = T("carry", [128, 12, 3], F32)
    U = T("U", [128, 2432], F32)
    G["bias"] = Tile(nc, "bias", [128, 8, 256], F32, handle=U.t[:, 0:2048].rearrange("p (a b) -> p a b", a=8))
    G["Sst"] = T("Sst", [128, 4, 128], F32)
    G["Sbf"] = T("Sbf", [128, 4, 128], BF16)
    G["beta_t"] = T("beta_t", [128, 4, 4], F32)
    G["g_t"] = T("g_t", [128, 4, 4], F32)
    gtmp = T("gtmp", [128, 4], F32)
    kvo = T("kvo", [128, 256], F32)
    gco = T("gco", [128, 1536], F32)
    W = {"sb": [T("sb%d" % i, [128, 256], F32) for i in range(2)],
         "pn": [T("pn%d" % i, [128, 256], BF16) for i in range(2)],
         "PT": [T("PT%d" % i, [128, 256], BF16) for i in range(2)],
         "col": [T("col%d" % i, [128, 8], F32) for i in range(2)],
         "sc": T("sc", [128, 40], F32),
         "R": T("R", [128, 4, 256], F32),
         "set": []}
    d = {}
    for nm in ("kbg", "vb", "tA", "Es", "tB", "ETd", "ETs", "u"):
        d[nm] = T("%s0" % nm, [128, 128], F32)
    for nm in ("kd", "AqkT", "qgT", "wTa", "wTz", "vnew"):
        d[nm] = T("%s0" % nm, [128, 128], BF16)
    for nm in ("Pm", "Qm", "X", "Y"):
        d[nm] = [T("%s0_%d" % (nm, j), [128, 128], F32) for j in range(2)]
    W["set"].append(d)
    self.dve(lambda: nc.vector.memset(d["wTz"][:], 0.0), w=[d["wTz"]])
    d2 = {}
    off = [0]

    def uview(name, dt):
        if dt == F32:
            v = U.t[:, off[0]:off[0] + 128]
            off[0] += 128
        else:
            v = U.t[:, off[0]:off[0] + 64].bitcast(BF16)
            off[0] += 64
        return Tile(nc, name, [128, 128], dt, handle=v)
    for nm in ("kbg", "vb", "tA", "Es", "tB", "ETd", "ETs", "u"):
        d2[nm] = uview(nm + "1", F32)
    for nm in ("Pm", "Qm", "X", "Y"):
        d2[nm] = [uview("%s1_%d" % (nm, j), F32) for j in range(2)]
    for nm in ("kd", "AqkT", "qgT", "wTa", "wTz", "vnew"):
        d2[nm] = uview(nm + "1", BF16)
    assert off[0] == 2432
    W["set"].append(d2)
    even_common(self, G, e)
    self.dve(lambda: nc.vector.memset(G["carry"][:], 0.0), w=[G["carry"]])
    self.dve(lambda: nc.vector.memset(G["Sst"][:], 0.0), w=[G["Sst"]])
    self.dve(lambda: nc.vector.memset(G["Sbf"][:], 0.0), w=[G["Sbf"]])
    cw = G["cw"]
    for ti in range(self.nt):
        c0 = ti * 512
        xb = [self.xTb[ti]]
        self.S.barrier()
        self.dma(G["bias"][:].rearrange("p a b -> p (a b)"), self.swabias, r=[self.Bin], w=[G["bias"]])
        self.rmsnorm_cols(xb, c0, 512, 4 + l, G["hn"], 0, G["sqm"], G["rstd"])
        cnt = [0]

        def conv_fn(ch, ps):
            pre = G["pre"][cnt[0] % 2]
            acc = G["acc"][cnt[0] % 2]
            cnt[0] += 1
            self.dve(lambda: nc.vector.tensor_copy(pre[:, 0:3], G["carry"][:, ch, :]), r=[G["carry"]], w=[pre])
            self.act(lambda: nc.scalar.copy(pre[:, 3:515], ps[:, 0:512]), r=[ps], w=[pre])
            self.dve(lambda: nc.vector.tensor_copy(G["carry"][:, ch, :], pre[:, 512:515]), r=[pre], w=[G["carry"]])
            self.dve(lambda: nc.vector.tensor_scalar_mul(acc[:], pre[:, 0:512], cw[:, ch, 0:1]), r=[pre, cw], w=[acc])
            for i in range(1, 4):
                self.dve(lambda: nc.vector.scalar_tensor_tensor(out=acc[:], in0=pre[:, i:i + 512], scalar=cw[:, ch, i:i + 1],
                                                                 in1=acc[:], op0=ALU.mult, op1=ALU.add), r=[pre, cw, acc], w=[acc])
            self.act(lambda: nc.scalar.activation(G["cq"][:, ch, :], acc[:], AF.Silu), r=[acc], w=[G["cq"]])
        even_inproj_fm(self, G, 512, conv_fn, G["kTa"][:, c0:c0 + 512])
        w = self.ws.get(EW_N[6])
        wv = w[:, 0:EW_N[6]].rearrange("p (k c) -> p k c", k=KC)
        for st in range(4):
            gs = ti * 4 + st
            ps = self.P[2 + st % 2]

            def mm():
                ins = None
                for kc in range(KC):
                    ins = nc.tensor.matmul(ps[:, 0:EV_TOK], G["hn"][:, kc, st * 128:(st + 1) * 128], wv[:, kc, :],
                                           start=(kc == 0), stop=(kc == KC - 1))
                return ins
            self.pe(mm, r=[w, G["hn"]], w=[ps])
            self.act(lambda: nc.scalar.copy(G["Va"][:, gs, :], ps[:, 0:256]), r=[ps], w=[G["Va"]])
            gates_from_ba(self, G, ps[:, 384:392], 128, G["beta_t"][:, st, :], G["g_t"][:, st, :], gtmp, [ps],
                          [G["beta_t"], G["g_t"]])
            if gs == nsub - 1 and not self.cfg.get("skip_kvo"):
                self.act(lambda: nc.scalar.copy(kvo[:, 0:128], ps[:, 256:384]), r=[ps], w=[kvo])
                self.act(lambda: nc.scalar.copy(kvo[:, 128:256], ps[:, 0:128]), r=[ps], w=[kvo])
                self.dve(lambda: nc.vector.tensor_tensor(out=kvo[:, 128:256], in0=kvo[:, 128:256], in1=ps[:, 128:256], op=ALU.add),
                         r=[ps, kvo], w=[kvo])
                pass
        if self.cfg.get("skip_swa") or self.cfg.get("skip_gdn") or self.cfg.get("gdn_stop", 6) < 6:
            self.dve(lambda: nc.vector.memset(G["sqm"][:], 0.0), w=[G["sqm"]])
            self.dve(lambda: nc.vector.memset(G["oT"][:], 0.0), w=[G["oT"]])
        if not self.cfg.get("skip_swa"):
            for qi in range(4):
                swa_block(self, G, W, ti * 4 + qi, qi * 128)
        self.S.barrier()
        self.dve(lambda: nc.vector.memset(W["set"][1]["wTz"][:], 0.0), w=[W["set"][1]["wTz"]])
        l2norm_heads(self, G, 512)
        if not self.cfg.get("skip_gdn"):
            for st in range(4):
                gdn_subtile(self, G, W, st, st * 128)
        gdn_post(self, G, 512, e)
        even_outproj(self, G, xb, c0, 512)
    self.out_dma(self.o_pk[e], kvo[:, 0:128], r=[kvo])
    self.out_dma(self.o_pv[e], kvo[:, 128:256], r=[kvo])
    for ch in range(12):
        self.out_dma(self.o_pgc[e][:, ch * 128:(ch + 1) * 128].rearrange("i p -> p i"), G["carry"][:, ch, :], r=[G["carry"]])
    self.out_dma(self.o_pgs[e].rearrange("h k v -> k h v"), G["Sst"][:], r=[G["Sst"]])
    self.end()
    if self.cfg.get("skip_dec"):
        for b in range(7):
            self.ws.get(EW_N[b])
        for b in range(2):
            self.ws.get(4096)
    else:
        even_decode(self, l)


def even_decode(self, l):
    nc = self.nc
    e = l // 2
    n = NS
    c0 = self.seq
    P = self.P
    ident, mk = self.ident, self.mk
    self.begin()
    T = self.T
    G = even_alloc(self, n)
    G["kTa"] = T("kTs", [128, n], BF16)
    xx = T("xx", [128, 12, n, 4], F32)
    hs = T("hs", [48, 1536], F32)
    gnew = T("gnew", [n, 1536], F32)
    knv = T("knv", [n, 256], F32)
    gts = T("gts", [n, 16], F32)
    Kc = T("Kc", [128, n, 128], F32)
    Vc = T("Vc", [128, n, 128], F32)
    Qb = T("Qb", [128, n, 8], F32)
    KTb = [T("KTb%d" % i, [128, 128], F32) for i in range(2)]
    sT = T("sT", [128, 128], F32)
    kTn = T("kTn", [128, n], F32)
    vTn = T("vTn", [128, n], F32)
    sf = T("sf", [128, 132], F32)
    pnf = T("pnf", [128, 132], F32)
    col = T("dcol", [128, 8], F32)
    PTs = T("PTs", [128, 128], F32)
    R2 = T("R2", [128, 128], F32)
    ov = T("ov", [128, n, 8], F32)
    dbias = T("dbias", [128, 129], F32)
    skc = T("skc", [128, 1], F32)
    R3 = T("R3", [n, 2, 4, n], F32)
    bcs = T("bcs", [128, 2, 4, n], F32)
    Sd = T("Sd", [128, n, 4, 128], F32)
    vnT = T("vnT", [128, 4, n], F32)
    t1 = T("t1", [128, 4, n], F32)
    krow = T("krow", [n, 512], F32)
    vrow = T("vrow", [n, 512], F32)
    Km = [T("Km%d" % i, [n, 512], F32) for i in range(2)]
    tS = T("tS", [128, 4, 128], F32)
    mix = G["sqm"]
    even_common(self, G, e)
    self.dma(dbias[:], self.decbias, r=[self.Bin], w=[dbias])
    self.dma(skc[:], self.esk[e], r=[self.Bin], w=[skc])
    self.dma(hs[:], self.s_gc[e].rearrange("b i c -> (b i) c"), r=[self.Bin], w=[hs])
    self.dma(Kc[:], self.c_k[e].rearrange("b w f -> w b f"), r=[self.Bin], w=[Kc])
    self.dma(Vc[:], self.c_v[e].rearrange("b w f -> w b f"), r=[self.Bin], w=[Vc])
    self.dma(Sd[:], self.s_gs[e].rearrange("b h k v -> k b h v"), r=[self.Bin], w=[Sd])
    for ch in range(12):
        ps = P[2 + ch % 2]
        self.pe(lambda: nc.tensor.transpose(ps[:, 0:48], hs[:, ch * 128:(ch + 1) * 128], ident[0:48, 0:48]),
                r=[hs, ident], w=[ps])
        self.dve(lambda: nc.vector.tensor_copy(xx[:, ch, :, 0:3], ps[:, 0:48].rearrange("p (b i) -> p b i", i=3)),
                 r=[ps], w=[xx])
    xb = [self.xTb[self.nt]]
    self.rmsnorm_cols(xb, c0, n, 4 + l, G["hn"], 0, G["sqm"], G["rstd"])
    cw = G["cw"]
    cnt = [0]

    def conv_fn(ch, ps):
        acc = G["acc"][cnt[0] % 2]
        cnt[0] += 1
        self.act(lambda: nc.scalar.copy(xx[:, ch, :, 3], ps[:, 0:n]), r=[ps], w=[xx])
        self.dve(lambda: nc.vector.tensor_scalar_mul(acc[:, 0:n], xx[:, ch, :, 0], cw[:, ch, 0:1]), r=[xx, cw], w=[acc])
        for i in range(1, 4):
            self.dve(lambda: nc.vector.scalar_tensor_tensor(out=acc[:, 0:n], in0=xx[:, ch, :, i], scalar=cw[:, ch, i:i + 1],
                                                            in1=acc[:, 0:n], op0=ALU.mult, op1=ALU.add), r=[xx, cw, acc], w=[acc])
        self.act(lambda: nc.scalar.activation(G["cq"][:, ch, 0:n], acc[:, 0:n], AF.Silu), r=[acc], w=[G["cq"]])
        pt = P[6]
        self.pe(lambda: nc.tensor.transpose(pt[0:n, (ch % 4) * 128:(ch % 4 + 1) * 128], xx[:, ch, :, 3], ident[:]),
                r=[xx, ident], w=[pt])
        self.dve(lambda: nc.vector.tensor_copy(gnew[:, ch * 128:(ch + 1) * 128], pt[0:n, (ch % 4) * 128:(ch % 4 + 1) * 128]),
                 r=[pt], w=[gnew])
    even_inproj_fm(self, G, n, conv_fn, G["kTa"][:, 0:n])
    w = self.ws.get(EW_N[6])
    wv = w[:, 0:EW_N[6]].rearrange("p (k c) -> p k c", k=KC)
    ps = P[2]

    def mm():
        ins = None
        for kc in range(KC):
            ins = nc.tensor.matmul(ps[0:n, 0:EV_TOK], G["hn"][:, kc, 0:n], wv[:, kc, :], start=(kc == 0), stop=(kc == KC - 1))
        return ins
    self.pe(mm, r=[w, G["hn"]], w=[ps])
    self.act(lambda: nc.scalar.copy(knv[:, 0:128], ps[0:n, 256:384]), r=[ps], w=[knv])
    self.act(lambda: nc.scalar.copy(knv[:, 128:256], ps[0:n, 0:128]), r=[ps], w=[knv])
    self.dve(lambda: nc.vector.tensor_tensor(out=knv[:, 128:256], in0=knv[:, 128:256], in1=ps[0:n, 128:256], op=ALU.add),
             r=[ps, knv], w=[knv])
    gates_from_ba(self, G, ps[0:n, 384:392], n, gts[:, 0:4], gts[:, 4:8], gts_tmp(gts), [ps], [gts])
    self.act(lambda: nc.scalar.activation(gts[:, 8:12], gts[:, 4:8], AF.Exp), r=[gts], w=[gts])
    self.dve(lambda: nc.vector.memset(Qb[:], 0.0), w=[Qb])
    for c in range(4):
        self.dve(lambda: nc.vector.tensor_copy(Qb[0:64, :, c], G["qT"][0:64, c, 0:n]), r=[G["qT"]], w=[Qb])
        self.dve(lambda: nc.vector.tensor_copy(Qb[64:128, :, 4 + c], G["qT"][64:128, c, 0:n]), r=[G["qT"]], w=[Qb])
    self.dve(lambda: nc.vector.tensor_copy(kTn[:], G["kTa"][:, 0:n]), r=[G["kTa"]], w=[kTn])
    for b in range(n):
        kp = P[b % 2]
        kt = KTb[b % 2]
        self.pe(lambda: nc.tensor.transpose(kp[:, 0:128], Kc[:, b, :], ident[:]), r=[Kc, ident], w=[kp])
        self.act(lambda: nc.scalar.copy(kt[:], kp[:, 0:128]), r=[kp], w=[kt])
        self.pe(lambda: nc.tensor.matmul(P[4][:, b * 8:(b + 1) * 8], kt[:], Qb[:, b, :], start=True, stop=True),
                r=[kt, Qb], w=[P[4]])
    self.dve(lambda: nc.vector.tensor_copy(sT[:], P[4][:, 0:128]), r=[P[4]], w=[sT])
    self.pe(lambda: nc.tensor.transpose(P[5][:, 0:128], sT[:], ident[:]), r=[sT, ident], w=[P[5]])
    self.pe(lambda: nc.tensor.matmul(P[5][:, 128:128 + n], Qb[:].rearrange("p b h -> p (b h)"), kTn[:], start=True, stop=True),
            r=[Qb, kTn], w=[P[5]])
    self.dve(lambda: nc.vector.tensor_tensor(out=sf[:, 0:n], in0=P[5][:, 128:128 + n], in1=mk[:, 5, 0:n], op=ALU.mult),
             r=[P[5], mk], w=[sf])
    self.dve(lambda: nc.vector.reduce_sum(out=col[:, 5:6], in_=sf[:, 0:n], axis=AX.X), r=[sf], w=[col])
    self.dve(lambda: nc.vector.scalar_tensor_tensor(out=sf[:, 0:128], in0=P[5][:, 0:128], scalar=0.125, in1=dbias[:, 0:128],
                                                    op0=ALU.mult, op1=ALU.add), r=[P[5], dbias], w=[sf])
    self.dve(lambda: nc.vector.scalar_tensor_tensor(out=sf[:, 128:129], in0=col[:, 5:6], scalar=0.125, in1=dbias[:, 128:129],
                                                    op0=ALU.mult, op1=ALU.add), r=[col, dbias], w=[sf])
    self.dve(lambda: nc.vector.reduce_max(out=col[:, 0:1], in_=sf[:, 0:129], axis=AX.X), r=[sf], w=[col])
    self.dve(lambda: nc.vector.tensor_tensor(out=col[:, 0:1], in0=col[:, 0:1], in1=skc[:, 0:1], op=ALU.max), r=[col, skc], w=[col])
    self.dve(lambda: nc.vector.tensor_scalar_mul(col[:, 1:2], col[:, 0:1], -1.0), r=[col], w=[col])
    self.act(lambda: nc.scalar.activation(sf[:, 0:129], sf[:, 0:129], AF.Exp, bias=col[:, 1:2], scale=1.0), r=[sf, col], w=[sf])
    self.dve(lambda: nc.vector.reduce_sum(out=col[:, 2:3], in_=sf[:, 0:129], axis=AX.X), r=[sf], w=[col])
    self.act(lambda: nc.scalar.activation(col[:, 3:4], skc[:, 0:1], AF.Exp, bias=col[:, 1:2], scale=1.0), r=[skc, col], w=[col])
    self.dve(lambda: nc.vector.tensor_tensor(out=col[:, 2:3], in0=col[:, 2:3], in1=col[:, 3:4], op=ALU.add), r=[col], w=[col])
    self.dve(lambda: nc.vector.reciprocal(col[:, 4:5], col[:, 2:3]), r=[col], w=[col])
    self.dve(lambda: nc.vector.tensor_scalar_mul(pnf[:, 0:129], sf[:, 0:129], col[:, 4:5]), r=[sf, col], w=[pnf])
    self.pe(lambda: nc.tensor.transpose(P[4][:, 128:256], pnf[:, 0:128], ident[:]), r=[pnf, ident], w=[P[4]])
    self.act(lambda: nc.scalar.copy(PTs[:], P[4][:, 128:256]), r=[P[4]], w=[PTs])
    for b in range(n):
        self.pe(lambda: nc.tensor.matmul(P[6][:, 256 + b * 8:256 + (b + 1) * 8], Vc[:, b, :], PTs[:, b * 8:(b + 1) * 8],
                                         start=True, stop=True), r=[Vc, PTs], w=[P[6]])
    self.pe(lambda: nc.tensor.transpose(P[7][:, 0:n], knv[:, 128:256], ident[0:n, 0:n]), r=[knv, ident], w=[P[7]])
    self.dve(lambda: nc.vector.tensor_copy(vTn[:], P[7][:, 0:n]), r=[P[7]], w=[vTn])
    self.dve(lambda: nc.vector.tensor_scalar_mul(R2[:], ident[:], pnf[:, 128:129]), r=[ident, pnf], w=[R2])
    self.pe(lambda: nc.tensor.matmul(P[7][:, 128:256], self.ones_f[:], R2[:], start=True, stop=True), r=[self.ones_f, R2], w=[P[7]])
    self.dve(lambda: nc.vector.tensor_tensor(out=ov[:], in0=P[7][:, 128:256].rearrange("p (b h) -> p b h", h=8),
                                             in1=vTn[:].unsqueeze(2).to_broadcast([128, n, 8]), op=ALU.mult),
             r=[P[7], vTn], w=[ov])
    self.dve(lambda: nc.vector.tensor_tensor(out=ov[:], in0=ov[:], in1=P[6][:, 256:384].rearrange("p (b h) -> p b h", h=8), op=ALU.add),
             r=[ov, P[6]], w=[ov])
    for c in range(4):
        self.dve(lambda: nc.vector.tensor_copy(mix[0:64, c, 0:n], ov[0:64, :, c]), r=[ov], w=[mix])
        self.dve(lambda: nc.vector.tensor_copy(mix[64:128, c, 0:n], ov[64:128, :, 4 + c]), r=[ov], w=[mix])
    l2norm_heads(self, G, n)
    cq = G["cq"]
    for t in range(2):
        src = gts[:, 0:4] if t == 0 else gts[:, 8:12]
        self.dve(lambda: nc.vector.tensor_tensor(out=R3[:, t, :, :], in0=src.unsqueeze(2).to_broadcast([n, 4, n]),
                                                 in1=ident[0:n, 0:n].unsqueeze(1).to_broadcast([n, 4, n]), op=ALU.mult),
                 r=[gts, ident], w=[R3])
    self.pe(lambda: nc.tensor.matmul(P[0][:, 0:128], self.ones_f[0:n, :], R3[:].rearrange("p t h b -> p (t h b)"),
                                     start=True, stop=True), r=[self.ones_f, R3], w=[P[0]])
    self.dve(lambda: nc.vector.tensor_copy(bcs[:].rearrange("p t h b -> p (t h b)"), P[0][:, 0:128]), r=[P[0]], w=[bcs])
    for b in range(n):
        for h in range(4):
            self.pe(lambda: nc.tensor.matmul(P[1][:, h * n + b:h * n + b + 1], Sd[:, b, h, :], cq[:, 4 + h, b:b + 1],
                                             start=True, stop=True), r=[Sd, cq], w=[P[1]])
    kSv = P[1][:, 0:4 * n].rearrange("p (h b) -> p h b", h=4)
    self.dve(lambda: nc.vector.tensor_tensor(out=t1[:], in0=kSv, in1=bcs[:, 1, :, :], op=ALU.mult), r=[P[1], bcs], w=[t1])
    self.dve(lambda: nc.vector.tensor_tensor(out=t1[:], in0=cq[:, 8:12, 0:n], in1=t1[:], op=ALU.subtract), r=[cq, t1], w=[t1])
    self.dve(lambda: nc.vector.tensor_tensor(out=vnT[:], in0=t1[:], in1=bcs[:, 0, :, :], op=ALU.mult), r=[t1, bcs], w=[vnT])
    for h in range(4):
        self.pe(lambda: nc.tensor.transpose(P[2][0:n, h * 128:(h + 1) * 128], cq[:, 4 + h, 0:n], ident[:]), r=[cq, ident], w=[P[2]])
        self.pe(lambda: nc.tensor.transpose(P[3][0:n, h * 128:(h + 1) * 128], vnT[:, h, :], ident[:]), r=[vnT, ident], w=[P[3]])
    self.act(lambda: nc.scalar.copy(krow[:], P[2][0:n, :]), r=[P[2]], w=[krow])
    self.dve(lambda: nc.vector.tensor_copy(vrow[:], P[3][0:n, :]), r=[P[3]], w=[vrow])
    for b in range(n):
        km = Km[b % 2]
        pp = P[4 + b % 2]
        self.dve(lambda: nc.vector.tensor_scalar_mul(km[:], krow[:], ident[0:n, b:b + 1]), r=[krow, ident], w=[km])

        def mm4():
            ins = None
            for h in range(4):
                ins = nc.tensor.matmul(pp[:, h * 128:(h + 1) * 128], km[:, h * 128:(h + 1) * 128], vrow[:, h * 128:(h + 1) * 128],
                                       start=True, stop=True)
            return ins
        self.pe(mm4, r=[km, vrow], w=[pp])
        self.dve(lambda: nc.vector.tensor_tensor(out=tS[:], in0=Sd[:, b, :, :],
                                                 in1=bcs[:, 1, :, b:b + 1].to_broadcast([128, 4, 128]), op=ALU.mult),
                 r=[Sd, bcs], w=[tS])
        self.dve(lambda: nc.vector.tensor_tensor(out=Sd[:, b, :, :], in0=tS[:], in1=pp[:, :].rearrange("p (h v) -> p h v", h=4), op=ALU.add),
                 r=[tS, pp], w=[Sd])
    for b in range(n):
        for h in range(4):
            self.pe(lambda: nc.tensor.matmul(P[1][:, 64 + h * n + b:64 + h * n + b + 1], Sd[:, b, h, :], cq[:, h, b:b + 1],
                                             start=True, stop=True), r=[Sd, cq], w=[P[1]])
    self.act(lambda: nc.scalar.copy(G["oT"][:, :, 0:n], P[1][:, 64:64 + 4 * n].rearrange("p (h b) -> p h b", h=4)), r=[P[1]], w=[G["oT"]])
    gdn_post(self, G, n, e)
    even_outproj(self, G, xb, c0, n)
    self.out_dma(self.o_sk[e][:, 0:127, :], self.c_k[e][:, 1:128, :], r=[self.Bin])
    self.out_dma(self.o_sv[e][:, 0:127, :], self.c_v[e][:, 1:128, :], r=[self.Bin])
    self.out_dma(self.o_sk[e][:, 127, :], knv[:, 0:128], r=[knv])
    self.out_dma(self.o_sv[e][:, 127, :], knv[:, 128:256], r=[knv])
    self.out_dma(self.o_sgc[e][:, 0:2, :], self.s_gc[e][:, 1:3, :], r=[self.Bin])
    self.out_dma(self.o_sgc[e][:, 2, :], gnew[:], r=[gnew])
    self.out_dma(self.o_sgs[e].rearrange("b h k v -> k b h v"), Sd[:], r=[Sd])
    self.end()


def gts_tmp(gts):
    class _V:
        b = gts.b

        def __getitem__(self, k):
            rows, cols = k
            return gts.t[rows, 12 + cols.start:12 + cols.stop]
    return _V()


Prog.even_mixer = even_mixer


OW_DT = 256
H_C, P_C, N_C, G_C = 32, 64, 128, 4


def odd_consts(self, O, e):
    nc = self.nc
    self.dma(O["cw"][:].rearrange("p a b -> p (a b)"), self.oconv[e], r=[self.Bin], w=[O["cw"]])
    self.dma(O["hb"][:], self.ohead[e], r=[self.Bin], w=[O["hb"]])
    self.dma(O["fc"][:], self.ofeat[e], r=[self.Bin], w=[O["fc"]])
    self.act(lambda: nc.scalar.activation(O["hb"][:, 32:64], O["hb"][:, 32:64], AF.Exp), r=[O["hb"]], w=[O["hb"]])
    self.dve(lambda: nc.vector.tensor_scalar_mul(O["hb"][:, 32:64], O["hb"][:, 32:64], -1.0), r=[O["hb"]], w=[O["hb"]])


def odd_alloc(self, n):
    T = self.T
    O = {}
    O["hn"] = T("hn", [128, KC, n], BF16)
    O["sqm"] = T("sqm", [128, KC, n], BF16)
    O["rstd"] = T("rstd", [128, n], F32)
    O["cw"] = T("ocw", [128, 24, 4], F32)
    O["hb"] = T("ohb", [128, 96], F32)
    O["fc"] = T("ofc", [128, 56], F32)
    O["acc"] = [T("oacc%d" % i, [128, n], F32) for i in range(2)]
    O["yT"] = T("yT", [128, 16, n], BF16)
    O["BT"] = T("BT", [128, 4, n], BF16)
    O["CT"] = T("CT", [128, 4, n], BF16)
    O["xc"] = [T("xc%d" % i, [128, n], BF16) for i in range(2)]
    return O


def odd_dt(self, O, ps_ap, rows, dt_dst, a_dst, rd, wr):
    nc = self.nc
    hb = O["hb"]
    self.dve(lambda: nc.vector.tensor_tensor(out=dt_dst, in0=ps_ap, in1=hb[0:rows, 0:32], op=ALU.add), r=rd + [hb], w=wr)
    self.act(lambda: nc.scalar.activation(dt_dst, dt_dst, AF.Exp), r=wr, w=wr)
    self.act(lambda: nc.scalar.activation(dt_dst, dt_dst, AF.Ln, bias=1.0, scale=1.0), r=wr, w=wr)
    self.dve(lambda: nc.vector.tensor_tensor(out=a_dst, in0=dt_dst, in1=hb[0:rows, 32:64], op=ALU.mult), r=wr + [hb], w=wr)


def odd_gate_norm_out(self, O, xbufs, c0, n, zfn):
    nc = self.nc
    yT, fc = O["yT"], O["fc"]
    sq = O["sqm"]
    for j in range(16):
        def consume(ps, j=j):
            zs = O["acc"][j % 2]
            self.act(lambda: nc.scalar.activation(zs[:, 0:n], ps[:, 0:n], AF.Silu), r=[ps], w=[zs])
            self.dve(lambda: nc.vector.tensor_tensor(out=yT[:, j, 0:n], in0=yT[:, j, 0:n], in1=zs[:, 0:n], op=ALU.mult),
                     r=[yT, zs], w=[yT])
        zfn(j, consume)
    for g in range(4):
        ps = self.P[6 + g % 2]
        for jj in range(4):
            j = g * 4 + jj
            self.act(lambda: nc.scalar.activation(sq[:, jj, 0:n], yT[:, j, 0:n], AF.Square), r=[yT], w=[sq])

        def mm():
            ins = None
            for jj in range(4):
                ins = nc.tensor.matmul(ps[:, 0:n], self.ones_bf[:], sq[:, jj, 0:n], start=(jj == 0), stop=(jj == 3))
            return ins
        self.pe(mm, r=[sq, self.ones_bf], w=[ps])
        rs = O["rstd"]
        self.act(lambda: nc.scalar.activation(rs[:, 0:n], ps[:, 0:n], AF.Ln, bias=EPS, scale=1.0 / 512.0), r=[ps], w=[rs])
        self.act(lambda: nc.scalar.activation(rs[:, 0:n], rs[:, 0:n], AF.Exp, scale=-0.5), r=[rs], w=[rs])
        for jj in range(4):
            j = g * 4 + jj
            self.dve(lambda: nc.vector.scalar_tensor_tensor(out=yT[:, j, 0:n], in0=yT[:, j, 0:n], scalar=fc[:, 24 + j:25 + j],
                                                            in1=rs[:, 0:n], op0=ALU.mult, op1=ALU.mult), r=[yT, fc, rs], w=[yT])
    for blk in range(4):
        w = self.ws.get(4096)
        wv = w[:, 0:4096].rearrange("p (k c) -> p k c", k=16)
        for dc in range(2):
            dch = blk * 2 + dc
            ps = self.P[dch % 2]

            def mm2():
                ins = None
                for k in range(16):
                    ins = nc.tensor.matmul(ps[:, 0:n], wv[:, k, dc * 128:(dc + 1) * 128], yT[:, k, 0:n],
                                           start=(k == 0), stop=(k == 15))
                return ins
            self.pe(mm2, r=[w, yT], w=[ps])
            xs = self.xT[:, dch, c0:c0 + n]
            self.dve(lambda: nc.vector.tensor_tensor(out=xs, in0=xs, in1=ps[:, 0:n], op=ALU.add), r=[ps] + xbufs, w=xbufs)


def odd_fm_chunk(self, O, w, cc, n, ps):
    nc = self.nc
    hn = O["hn"]
    wv = w[:, 0:4096].rearrange("p (k c) -> p k c", k=KC)

    def mm():
        ins = None
        for kc in range(KC):
            ins = nc.tensor.matmul(ps[:, 0:n], wv[:, kc, cc * 128:(cc + 1) * 128], hn[:, kc, 0:n],
                                   start=(kc == 0), stop=(kc == KC - 1))
        return ins
    self.pe(mm, r=[w, hn], w=[ps])


def odd_mixer(self, l):
    nc = self.nc
    e = l // 2
    P = self.P
    ident, mk = self.ident, self.mk
    self.begin()
    T = self.T
    O = odd_alloc(self, 512)
    pre = [T("opre%d" % i, [128, 515], F32) for i in range(2)]
    carry = T("ocarry", [128, 24, 3], F32)
    xst = T("xst", [128, 4, 2048], BF16)
    Btok = T("Btok", [128, 4, 512], BF16)
    dtv = T("dtv", [128, 4, 32], F32)
    av = T("av", [128, 4, 32], F32)
    sm = T("osm", [128, 6, 32], F32)
    R4 = [T("R4_%d" % i, [128, 4, 128], F32) for i in range(2)]
    Lt = [T("Lt%d" % i, [128, 4, 128], F32) for i in range(2)]
    Wb = [T("Wb%d" % i, [128, 4, 128], BF16) for i in range(2)]
    xdt = T("xdt", [128, 2048], BF16)
    xdd = T("xdd", [128, 2048], BF16)
    ytk = T("ytk", [128, 2048], BF16)
    tt = [T("ott%d" % i, [128, 512], F32) for i in range(2)]
    ST = T("ST", [128, 2048], F32)
    STb = T("STb", [128, 2048], BF16)
    sto = Tile(self.nc, "sto", [128, 32, 128], F32, handle=xst.t.bitcast(F32).reshape([128, 32, 128]))
    sto.b = xst.b
    odd_consts(self, O, e)
    self.dve(lambda: nc.vector.memset(carry[:], 0.0), w=[carry])
    self.dve(lambda: nc.vector.memset(ST[:], 0.0), w=[ST])
    self.dve(lambda: nc.vector.memset(STb[:], 0.0), w=[STb])
    cw, fc, hb = O["cw"], O["fc"], O["hb"]
    for ti in range(self.nt):
        c0 = ti * 512
        xb = [self.xTb[ti]]
        self.rmsnorm_cols(xb, c0, 512, 4 + l, O["hn"], 0, O["sqm"], O["rstd"])
        w = self.ws.get(OW_DT)
        wv = w[:, 0:OW_DT].rearrange("p (k c) -> p k c", k=KC)
        for st in range(4):
            ps = P[2 + st % 2]

            def mm():
                ins = None
                for kc in range(KC):
                    ins = nc.tensor.matmul(ps[:, 0:32], O["hn"][:, kc, st * 128:(st + 1) * 128], wv[:, kc, :],
                                           start=(kc == 0), stop=(kc == KC - 1))
                return ins
            self.pe(mm, r=[w, O["hn"]], w=[ps])
            odd_dt(self, O, ps[:, 0:32], 128, dtv[:, st, :], av[:, st, :], [ps], [dtv, av])
        for blk in range(6):
            w = self.ws.get(4096)
            for cc in range(4):
                ch = blk * 4 + cc
                ps = P[ch % 2]
                odd_fm_chunk(self, O, w, cc, 512, ps)
                pr = pre[ch % 2]
                acc = O["acc"][ch % 2]
                self.dve(lambda: nc.vector.tensor_copy(pr[:, 0:3], carry[:, ch, :]), r=[carry], w=[pr])
                self.act(lambda: nc.scalar.copy(pr[:, 3:515], ps[:, 0:512]), r=[ps], w=[pr])
                self.dve(lambda: nc.vector.tensor_copy(carry[:, ch, :], pr[:, 512:515]), r=[pr], w=[carry])
                self.dve(lambda: nc.vector.tensor_scalar_mul(acc[:], pr[:, 0:512], cw[:, ch, 0:1]), r=[pr, cw], w=[acc])
                for i in range(1, 4):
                    self.dve(lambda: nc.vector.scalar_tensor_tensor(out=acc[:], in0=pr[:, i:i + 512], scalar=cw[:, ch, i:i + 1],
                                                                     in1=acc[:], op0=ALU.mult, op1=ALU.add), r=[pr, cw, acc], w=[acc])
                if ch < 16 or ch < 20:
                    dst = O["xc"][ch % 2] if ch < 16 else None
                    tgt = dst[:, 0:512] if ch < 16 else O["BT"][:, ch - 16, 0:512]
                    tb = dst if ch < 16 else O["BT"]
                    self.act(lambda: nc.scalar.activation(tgt, acc[:], AF.Silu, bias=fc[:, ch:ch + 1], scale=1.0),
                             r=[acc, fc], w=[tb])
                    tp = P[4 + ch % 2]
                    tv = tp[:, :].bitcast(BF16)

                    def tr():
                        ins = None
                        for st in range(4):
                            ins = nc.tensor.transpose(tv[:, st * 128:(st + 1) * 128], tgt[:, st * 128:(st + 1) * 128], self.ident_bf[:])
                        return ins
                    self.pe(tr, r=[tb, self.ident_bf], w=[tp])
                    src = tv[:, 0:512].rearrange("p (s c) -> p s c", s=4)
                    if ch < 16:
                        self.dve(lambda: nc.vector.tensor_copy(xst[:, :, ch * 128:(ch + 1) * 128], src), r=[tp], w=[xst])
                    else:
                        self.dve(lambda: nc.vector.tensor_copy(Btok[:, :, (ch - 16) * 128:(ch - 15) * 128], src), r=[tp], w=[Btok])
                else:
                    self.act(lambda: nc.scalar.activation(O["CT"][:, ch - 20, 0:512], acc[:], AF.Silu, bias=fc[:, ch:ch + 1], scale=1.0),
                             r=[acc, fc], w=[O["CT"]])
        for st in range(4):
            cs = st * 128
            acs, acl, dte, cd, eac, dtd = [sm[:, i, :] for i in range(6)]
            self.pe(lambda: nc.tensor.matmul(P[2][:, 0:32], mk[:, 6, :], av[:, st, :], start=True, stop=True), r=[mk, av], w=[P[2]])
            self.pe(lambda: nc.tensor.matmul(P[2][:, 32:64], self.ones_f[:], av[:, st, :], start=True, stop=True),
                    r=[self.ones_f, av], w=[P[2]])
            self.dve(lambda: nc.vector.tensor_copy(sm[:, 0:2, :], P[2][:, 0:64].rearrange("p (a b) -> p a b", a=2)), r=[P[2]], w=[sm])
            self.dve(lambda: nc.vector.tensor_tensor(out=dte, in0=acl, in1=acs, op=ALU.subtract), r=[sm], w=[sm])
            self.act(lambda: nc.scalar.activation(dte, dte, AF.Exp), r=[sm], w=[sm])
            self.act(lambda: nc.scalar.activation(cd, acl, AF.Exp), r=[sm], w=[sm])
            self.act(lambda: nc.scalar.activation(eac, acs, AF.Exp), r=[sm], w=[sm])
            self.dve(lambda: nc.vector.tensor_tensor(out=dtd, in0=dte, in1=dtv[:, st, :], op=ALU.mult), r=[sm, dtv], w=[sm])
            xv = xst[:, st, :].rearrange("p (h q) -> p h q", q=P_C)
            for q4 in range(4):
                hs = slice(q4 * 8, (q4 + 1) * 8)
                self.dve(lambda: nc.vector.tensor_tensor(out=xdt[:, q4 * 512:(q4 + 1) * 512].rearrange("p (h q) -> p h q", q=P_C),
                                                          in0=xv[:, hs, :], in1=dtv[:, st, hs].unsqueeze(2).to_broadcast([128, 8, P_C]),
                                                          op=ALU.mult), r=[xst, dtv], w=[xdt])
                self.dve(lambda: nc.vector.tensor_tensor(out=xdd[:, q4 * 512:(q4 + 1) * 512].rearrange("p (h q) -> p h q", q=P_C),
                                                          in0=xv[:, hs, :], in1=dtd[:, hs].unsqueeze(2).to_broadcast([128, 8, P_C]),
                                                          op=ALU.mult), r=[xst, sm], w=[xdd])
            for g in range(4):
                self.pe(lambda: nc.tensor.matmul(P[3][:, 0:128], O["BT"][:, g, cs:cs + 128], O["CT"][:, g, cs:cs + 128], start=True, stop=True),
                        r=[O["BT"], O["CT"]], w=[P[3]])
                yps = P[4]
                for hf in range(2):
                    i2 = (g * 2 + hf) % 2
                    h0 = g * 8 + hf * 4
                    r4, lt, wb = R4[i2], Lt[i2], Wb[i2]
                    self.dve(lambda: nc.vector.tensor_tensor(out=r4[:], in0=mk[:, 6, :].unsqueeze(1).to_broadcast([128, 4, 128]),
                                                              in1=av[:, st, h0:h0 + 4].unsqueeze(2).to_broadcast([128, 4, 128]), op=ALU.mult),
                              r=[mk, av], w=[r4])
                    bc = P[hf]
                    self.pe(lambda: nc.tensor.matmul(bc[:, 0:512], self.ones_f[:], r4[:].rearrange("p a b -> p (a b)"), start=True, stop=True),
                            r=[self.ones_f, r4], w=[bc])
                    self.dve(lambda: nc.vector.tensor_tensor(out=lt[:], in0=bc[:, 0:512].rearrange("p (a b) -> p a b", a=4),
                                                             in1=acs[:, h0:h0 + 4].unsqueeze(2).to_broadcast([128, 4, 128]),
                                                             op=ALU.subtract), r=[bc, sm], w=[lt])
                    self.dve(lambda: nc.vector.tensor_tensor(out=lt[:], in0=lt[:], in1=mk[:, 7, :].unsqueeze(1).to_broadcast([128, 4, 128]),
                                                             op=ALU.min), r=[lt, mk], w=[lt])
                    self.act(lambda: nc.scalar.activation(lt[:], lt[:], AF.Exp), r=[lt], w=[lt])
                    self.dve(lambda: nc.vector.tensor_tensor(out=wb[:], in0=lt[:], in1=P[3][:, 0:128].unsqueeze(1).to_broadcast([128, 4, 128]),
                                                             op=ALU.mult), r=[lt, P[3]], w=[wb])

                    def ymm():
                        ins = None
                        for hh in range(4):
                            h = h0 + hh
                            ins = nc.tensor.matmul(yps[:, (hf * 4 + hh) * 64:(hf * 4 + hh + 1) * 64], wb[:, hh, :], xdt[:, h * 64:(h + 1) * 64],
                                                   start=True, stop=True)
                        return ins
                    self.pe(ymm, r=[wb, xdt], w=[yps])
                self.pe(lambda: nc.tensor.matmul(P[5][:, 0:512], O["CT"][:, g, cs:cs + 128], STb[:, g * 512:(g + 1) * 512], start=True, stop=True),
                        r=[O["CT"], STb], w=[P[5]])
                t = tt[g % 2]
                gsl = slice(g * 512, (g + 1) * 512)
                self.dve(lambda: nc.vector.tensor_tensor(out=t[:].rearrange("p (h q) -> p h q", q=P_C),
                                                         in0=P[5][:, 0:512].rearrange("p (h q) -> p h q", q=P_C),
                                                         in1=eac[:, g * 8:(g + 1) * 8].unsqueeze(2).to_broadcast([128, 8, P_C]), op=ALU.mult),
                         r=[P[5], sm], w=[t])
                self.dve(lambda: nc.vector.tensor_tensor(out=t[:], in0=t[:], in1=yps[:, 0:512], op=ALU.add), r=[t, yps], w=[t])
                t2 = O["acc"][g % 2]
                self.dve(lambda: nc.vector.tensor_tensor(out=t2[:].rearrange("p (h q) -> p h q", q=P_C),
                                                         in0=xst[:, st, gsl].rearrange("p (h q) -> p h q", q=P_C),
                                                         in1=hb[:, 64 + g * 8:64 + (g + 1) * 8].unsqueeze(2).to_broadcast([128, 8, P_C]), op=ALU.mult),
                         r=[xst, hb], w=[t2])
                self.dve(lambda: nc.vector.tensor_tensor(out=ytk[:, gsl], in0=t[:], in1=t2[:], op=ALU.add), r=[t, t2], w=[ytk])
                self.pe(lambda: nc.tensor.matmul(P[6][:, 0:512], Btok[:, st, g * 128:(g + 1) * 128], xdd[:, gsl], start=True, stop=True),
                        r=[Btok, xdd], w=[P[6]])
                self.dve(lambda: nc.vector.tensor_tensor(out=ST[:, gsl].rearrange("p (h q) -> p h q", q=P_C),
                                                         in0=ST[:, gsl].rearrange("p (h q) -> p h q", q=P_C),
                                                         in1=cd[:, g * 8:(g + 1) * 8].unsqueeze(2).to_broadcast([128, 8, P_C]), op=ALU.mult),
                         r=[ST, sm], w=[ST])
                self.dve(lambda: nc.vector.tensor_tensor(out=ST[:, gsl], in0=ST[:, gsl], in1=P[6][:, 0:512], op=ALU.add), r=[ST, P[6]], w=[ST])
                self.act(lambda: nc.scalar.copy(STb[:, gsl], ST[:, gsl]), r=[ST], w=[STb])
            for hf in range(2):
                tp = P[hf]
                tv = tp[:, :].bitcast(BF16)

                def tr2():
                    ins = None
                    for j in range(8):
                        ch = hf * 8 + j
                        ins = nc.tensor.transpose(tv[:, j * 128:(j + 1) * 128], ytk[:, ch * 128:(ch + 1) * 128], self.ident_bf[:])
                    return ins
                self.pe(tr2, r=[ytk, self.ident_bf], w=[tp])
                self.act(lambda: nc.scalar.copy(O["yT"][:, hf * 8:(hf + 1) * 8, cs:cs + 128], tv[:, 0:1024].rearrange("p (j c) -> p j c", j=8)),
                         r=[tp], w=[O["yT"]])
        zw = [None]

        def zfn(j, consume):
            if j % 4 == 0:
                zw[0] = self.ws.get(4096)
            ps = P[2 + j % 2]
            odd_fm_chunk(self, O, zw[0], j % 4, 512, ps)
            consume(ps)
        odd_gate_norm_out(self, O, xb, c0, 512, zfn)
    for ch in range(24):
        self.out_dma(self.o_psc[e][:, ch * 128:(ch + 1) * 128].rearrange("i p -> p i"), carry[:, ch, :], r=[carry])
    for c in range(16):
        ps = P[c % 2]
        self.pe(lambda: nc.tensor.transpose(ps[:, 0:128], ST[:, c * 128:(c + 1) * 128], ident[:]), r=[ST, ident], w=[ps])
        self.dve(lambda: nc.vector.tensor_copy(sto[:, c, :], ps[:, 0:128]), r=[ps], w=[sto])
    self.out_dma(self.o_pss[e].rearrange("(c q) n -> q c n", q=128), sto[:, 0:16, :], r=[sto])
    self.end()
    odd_decode(self, l)


def odd_decode(self, l):
    nc = self.nc
    e = l // 2
    n = NS
    c0 = self.seq
    P = self.P
    ident, mk = self.ident, self.mk
    self.begin()
    T = self.T
    O = odd_alloc(self, n)
    xx = T("oxx", [128, 24, n, 4], F32)
    hs = T("ohs", [48, 3072], F32)
    gnew = T("ognew", [n, 3072], F32)
    xsT = T("xsT", [128, 16, n], F32)
    BCs = T("BCs", [128, 8, n], F32)
    dts = T("dts", [n, 64], F32)
    BCt = T("BCt", [n, 2, 512], F32)
    BCm = [T("BCm%d" % i, [n, 2, 512], F32) for i in range(2)]
    R5 = T("R5", [n, 2, 2, 16, n], F32)
    onesH = T("onesH", [n, 2, 128], F32)
    cols = T("ocols", [128, 2, 16, n], F32)
    xdc = T("xdc", [128, 16, n], F32)
    Sb = [T("Sb%d" % i, [128, 16, 128], F32) for i in range(2)]
    tS = T("otS", [128, 16, 128], F32)
    ysum = T("ysum", [128, 16, n], F32)
    odd_consts(self, O, e)
    cw, fc, hb = O["cw"], O["fc"], O["hb"]
    self.dma(hs[:], self.s_sc[e].rearrange("b i c -> (b i) c"), r=[self.Bin], w=[hs])
    self.dve(lambda: nc.vector.memset(onesH[:], 0.0), w=[onesH])
    self.dve(lambda: nc.vector.memset(onesH[:, 0, 0:64], 1.0), r=[onesH], w=[onesH])
    self.dve(lambda: nc.vector.memset(onesH[:, 1, 64:128], 1.0), r=[onesH], w=[onesH])
    for ch in range(24):
        ps = P[2 + ch % 2]
        self.pe(lambda: nc.tensor.transpose(ps[:, 0:48], hs[:, ch * 128:(ch + 1) * 128], ident[0:48, 0:48]), r=[hs, ident], w=[ps])
        self.dve(lambda: nc.vector.tensor_copy(xx[:, ch, :, 0:3], ps[:, 0:48].rearrange("p (b i) -> p b i", i=3)), r=[ps], w=[xx])
    xb = [self.xTb[self.nt]]
    self.rmsnorm_cols(xb, c0, n, 4 + l, O["hn"], 0, O["sqm"], O["rstd"])
    w = self.ws.get(OW_DT)
    wv = w[:, 0:OW_DT].rearrange("p (k c) -> p k c", k=KC)
    ps = P[2]

    def mm():
        ins = None
        for kc in range(KC):
            ins = nc.tensor.matmul(ps[0:n, 0:32], O["hn"][:, kc, 0:n], wv[:, kc, :], start=(kc == 0), stop=(kc == KC - 1))
        return ins
    self.pe(mm, r=[w, O["hn"]], w=[ps])
    odd_dt(self, O, ps[0:n, 0:32], n, dts[:, 0:32], dts[:, 32:64], [ps], [dts])
    self.act(lambda: nc.scalar.activation(dts[:, 32:64], dts[:, 32:64], AF.Exp), r=[dts], w=[dts])
    for blk in range(6):
        w = self.ws.get(4096)
        for cc in range(4):
            ch = blk * 4 + cc
            ps = P[ch % 2]
            odd_fm_chunk(self, O, w, cc, n, ps)
            acc = O["acc"][ch % 2]
            self.act(lambda: nc.scalar.copy(xx[:, ch, :, 3], ps[:, 0:n]), r=[ps], w=[xx])
            self.dve(lambda: nc.vector.tensor_scalar_mul(acc[:, 0:n], xx[:, ch, :, 0], cw[:, ch, 0:1]), r=[xx, cw], w=[acc])
            for i in range(1, 4):
                self.dve(lambda: nc.vector.scalar_tensor_tensor(out=acc[:, 0:n], in0=xx[:, ch, :, i], scalar=cw[:, ch, i:i + 1],
                                                                in1=acc[:, 0:n], op0=ALU.mult, op1=ALU.add), r=[xx, cw, acc], w=[acc])
            if ch < 16:
                dst, db = xsT[:, ch, :], xsT
            else:
                dst, db = BCs[:, ch - 16, :], BCs
            self.act(lambda: nc.scalar.activation(dst, acc[:, 0:n], AF.Silu, bias=fc[:, ch:ch + 1], scale=1.0), r=[acc, fc], w=[db])
            pt = P[6]
            self.pe(lambda: nc.tensor.transpose(pt[0:n, (ch % 4) * 128:(ch % 4 + 1) * 128], xx[:, ch, :, 3], ident[:]), r=[xx, ident], w=[pt])
            self.dve(lambda: nc.vector.tensor_copy(gnew[:, ch * 128:(ch + 1) * 128], pt[0:n, (ch % 4) * 128:(ch % 4 + 1) * 128]), r=[pt], w=[gnew])
    for t in range(2):
        pt = P[4 + t]
        for g in range(4):
            self.pe(lambda: nc.tensor.transpose(pt[0:n, g * 128:(g + 1) * 128], BCs[:, t * 4 + g, :], ident[:]), r=[BCs, ident], w=[pt])
        self.dve(lambda: nc.vector.tensor_copy(BCt[:, t, :], pt[0:n, 0:512]), r=[pt], w=[BCt])
    for t in range(2):
        src = dts[:, t * 32:(t + 1) * 32].rearrange("p (hp h2) -> p h2 hp", h2=2)
        self.dve(lambda: nc.vector.tensor_tensor(out=R5[:, t], in0=src.unsqueeze(3).to_broadcast([n, 2, 16, n]),
                                                 in1=ident[0:n, 0:n].unsqueeze(1).unsqueeze(1).to_broadcast([n, 2, 16, n]), op=ALU.mult),
                 r=[dts, ident], w=[R5])

        def cmm():
            ins = None
            for h2 in range(2):
                ins = nc.tensor.matmul(P[7][:, t * 256:(t + 1) * 256], onesH[:, h2, :], R5[:, t, h2].rearrange("p a b -> p (a b)"),
                                       start=(h2 == 0), stop=(h2 == 1))
            return ins
        self.pe(cmm, r=[onesH, R5], w=[P[7]])
    self.dve(lambda: nc.vector.tensor_copy(cols[:].rearrange("p t a b -> p (t a b)"), P[7][:, 0:512]), r=[P[7]], w=[cols])
    self.dve(lambda: nc.vector.tensor_tensor(out=xdc[:], in0=cols[:, 0], in1=xsT[:], op=ALU.mult), r=[cols, xsT], w=[xdc])
    sview = self.s_ss[e].rearrange("b (c q) n -> b q c n", q=128)
    oview = self.o_sss[e].rearrange("b (c q) n -> b q c n", q=128)
    self.dma(Sb[0][:], sview[0], r=[self.Bin], w=[Sb[0]])
    for b in range(n):
        S_ = Sb[b % 2]
        if b + 1 < n:
            self.dma(Sb[(b + 1) % 2][:], sview[b + 1], r=[self.Bin], w=[Sb[(b + 1) % 2]])
        bm = BCm[b % 2]
        self.dve(lambda: nc.vector.tensor_scalar_mul(bm[:].rearrange("p a b -> p (a b)"), BCt[:].rearrange("p a b -> p (a b)"),
                                                     ident[0:n, b:b + 1]), r=[BCt, ident], w=[bm])
        pB, pC = P[(b % 2) * 2], P[(b % 2) * 2 + 1]
        self.pe(lambda: nc.tensor.matmul(pB[:, 0:512], self.ones_f[0:n, :], bm[:, 0, :], start=True, stop=True), r=[self.ones_f, bm], w=[pB])
        self.pe(lambda: nc.tensor.matmul(pC[:, 0:512], self.ones_f[0:n, :], bm[:, 1, :], start=True, stop=True), r=[self.ones_f, bm], w=[pC])
        s4 = S_[:].rearrange("p (g r) n -> p g r n", r=4)
        t4 = tS[:].rearrange("p (g r) n -> p g r n", r=4)
        self.dve(lambda: nc.vector.tensor_tensor(out=tS[:], in0=S_[:], in1=cols[:, 1, :, b:b + 1].to_broadcast([128, 16, 128]), op=ALU.mult),
                 r=[S_, cols], w=[tS])
        self.dve(lambda: nc.vector.tensor_tensor(out=s4, in0=pB[:, 0:512].rearrange("p (g n) -> p g n", g=4).unsqueeze(2).to_broadcast([128, 4, 4, 128]),
                                                 in1=xdc[:, :, b].rearrange("p (g r) -> p g r", r=4).unsqueeze(3).to_broadcast([128, 4, 4, 128]),
                                                 op=ALU.mult), r=[pB, xdc], w=[S_])
        self.dve(lambda: nc.vector.tensor_tensor(out=S_[:], in0=S_[:], in1=tS[:], op=ALU.add), r=[S_, tS], w=[S_])
        self.dve(lambda: nc.vector.tensor_tensor(out=t4, in0=s4, in1=pC[:, 0:512].rearrange("p (g n) -> p g n", g=4).unsqueeze(2).to_broadcast([128, 4, 4, 128]),
                                                 op=ALU.mult), r=[S_, pC], w=[tS])
        self.dve(lambda: nc.vector.reduce_sum(out=ysum[:, :, b], in_=tS[:], axis=AX.X), r=[tS], w=[ysum])
        self.dma(oview[b], S_[:], r=[S_], w=[self.Bout])
    self.dve(lambda: nc.vector.tensor_tensor(out=xdc[:], in0=xsT[:], in1=fc[:, 40:56].unsqueeze(2).to_broadcast([128, 16, n]), op=ALU.mult),
             r=[xsT, fc], w=[xdc])
    self.dve(lambda: nc.vector.tensor_tensor(out=O["yT"][:, :, 0:n], in0=ysum[:], in1=xdc[:], op=ALU.add), r=[ysum, xdc], w=[O["yT"]])
    zw = [None]

    def zfn(j, consume):
        if j % 4 == 0:
            zw[0] = self.ws.get(4096)
        ps = P[2 + j % 2]
        odd_fm_chunk(self, O, zw[0], j % 4, n, ps)
        consume(ps)
    odd_gate_norm_out(self, O, xb, c0, n, zfn)
    self.out_dma(self.o_ssc[e][:, 0:2, :], self.s_sc[e][:, 1:3, :], r=[self.Bin])
    self.out_dma(self.o_ssc[e][:, 2, :], gnew[:], r=[gnew])
    self.end()


Prog.odd_mixer = odd_mixer


def tile_k(w, cb):
    K, N = w.shape
    return np.ascontiguousarray(w.reshape(K // 128, 128, N // cb, cb).transpose(2, 1, 0, 3))


def t5_bucket_np(dist):
    max_exact = 16
    df = np.maximum(dist, max_exact).astype(np.float32)
    large = max_exact + (np.log(df / max_exact) / math.log(128 / max_exact) * (32 - max_exact)).astype(np.int32)
    return np.where(dist < max_exact, dist, np.minimum(large, 31))


def static_masks():
    i = np.arange(128)[:, None]
    j = np.arange(128)[None, :]
    same = (i // 64) == (j // 64)
    m = np.zeros((128, 8, 128), np.float32)
    m[:, 0] = ((i <= j) & same)
    m[:, 1] = np.where((j < i) & same, 0.0, 1e30)
    m[:, 2] = np.where((j >= i) & same, 0.0, -1e30)
    m[:, 3] = ((j > i) & same)
    m[:, 4] = same
    m[:, 5, 0:16] = (np.arange(128)[:, None] // 8) == np.arange(16)[None, :]
    m[:, 6] = (i <= j)
    m[:, 7] = np.where(j >= i, 0.0, -1e30)
    return m.reshape(128, 8 * 128)


def prep_shared(inp):
    sh = {}
    g = np.zeros((128, 13, KC), np.float32)
    for l in range(4):
        g[:, l] = inp["norm_ff1"][l].reshape(KC, 128).T
        g[:, 4 + l] = inp["norm_mix"][l].reshape(KC, 128).T
        g[:, 8 + l] = inp["norm_ff2"][l].reshape(KC, 128).T
    g[:, 12] = inp["norm_final"].reshape(KC, 128).T
    sh["gains"] = g.reshape(128, 13 * KC)
    sh["masks"] = static_masks()
    wgu = np.empty((DEPTH, 2, 11, 128, 2, KC, 256), np.float32)
    wd = np.empty((DEPTH, 2, 8, 128, FC, 128), np.float32)
    for l in range(DEPTH):
        for f, (kg, ku, kd) in enumerate((("ff1_gate", "ff1_up", "ff1_down"), ("ff2_gate", "ff2_up", "ff2_down"))):
            wgu[l, f, :, :, 0] = tile_k(inp[kg][l], 256)
            wgu[l, f, :, :, 1] = tile_k(inp[ku][l], 256)
            wd[l, f] = tile_k(inp[kd][l], 128)
    sh["wgu"] = wgu.reshape(DEPTH, 2, 11, 128, 4096)
    sh["wd"] = wd.reshape(DEPTH, 2, 8, 128, 2816)
    ewin = np.zeros((2, 128, EW_TOT), np.float32)
    ewout = np.zeros((2, 2, 128, 4096), np.float32)
    econv = np.zeros((2, 128, 12, 4), np.float32)
    esm = np.zeros((2, 128, 17), np.float32)
    esk = np.zeros((2, 128, 1), np.float32)
    for e in range(2):
        W = inp["even_w_in"][e]
        qa, ka, va = W[:, 0:512], W[:, 512:640], W[:, 640:768]
        qkvb, zb, bb = W[:, 768:2304], W[:, 2304:2816], W[:, 2816:2824]
        cols = []
        for c in range(4):
            cols.append(np.concatenate([qa[:, c * 64:(c + 1) * 64], qa[:, 256 + c * 64:256 + (c + 1) * 64]], 1))
        cols.append(ka)
        cols.append(qkvb)
        cols.append(zb)
        fm = np.concatenate(cols, 1)
        assert fm.shape[1] == EV_FM * 128
        zero = np.zeros((1024, 64), np.float32)
        tok = np.concatenate([va[:, 0:64], zero, zero, va[:, 64:128], ka, bb], 1)
        assert tok.shape[1] == EV_TOK
        for b in range(5):
            ewin[e, :, EW_OFF[b]:EW_OFF[b] + 4096] = tile_k(fm[:, b * 512:(b + 1) * 512], 512)[0].reshape(128, 4096)
        ewin[e, :, EW_OFF[5]:EW_OFF[5] + 1024] = tile_k(fm[:, 2560:2688], 128)[0].reshape(128, 1024)
        ewin[e, :, EW_OFF[6]:] = tile_k(tok, EV_TOK)[0].reshape(128, 8 * EV_TOK)
        Wo = inp["even_w_out"][e]
        rows = []
        for c in range(4):
            rows.append(Wo[c * 64:(c + 1) * 64])
            rows.append(Wo[256 + c * 64:256 + (c + 1) * 64])
        rows.append(Wo[512:])
        Wp = np.concatenate(rows, 0)
        ewout[e] = tile_k(Wp, 512).reshape(2, 128, 4096)
        econv[e] = inp["gdn_conv_w"][e].reshape(4, 12, 128).transpose(2, 1, 0)
        esm[e, :, 0:4] = inp["gdn_A_log"][e][None, :]
        esm[e, :, 4:8] = inp["gdn_dt_bias"][e][None, :]
        esm[e, :, 8:16] = inp["swa_sinks"][e][None, :]
        esm[e, :, 16] = inp["gdn_norm"][e]
        esk[e, :, 0] = np.tile(inp["swa_sinks"][e], 16)
    sh["ewin"], sh["ewout"] = ewin, ewout
    sh["econv"] = econv.reshape(2, 128, 48)
    sh["esm"], sh["esk"] = esm, esk
    owin = np.zeros((2, 128, OW_TOT), np.float32)
    owout = np.zeros((2, 4, 128, 4096), np.float32)
    oconv = np.zeros((2, 128, 24, 4), np.float32)
    ohead = np.zeros((2, 128, 96), np.float32)
    ofeat = np.zeros((2, 128, 56), np.float32)
    for e in range(2):
        W = inp["ssd_w_in"][e]
        z, xbc, dtw = W[:, 0:2048], W[:, 2048:5120], W[:, 5120:5152]
        owin[e, :, 0:256] = tile_k(dtw, 32)[0].reshape(128, 256)
        fm = np.concatenate([xbc, z], 1)
        for b in range(10):
            owin[e, :, 256 + b * 4096:256 + (b + 1) * 4096] = tile_k(fm[:, b * 512:(b + 1) * 512], 512)[0].reshape(128, 4096)
        Wo = inp["ssd_w_out"][e]
        owout[e] = np.ascontiguousarray(Wo.reshape(16, 128, 4, 256).transpose(2, 1, 0, 3)).reshape(4, 128, 4096)
        oconv[e] = inp["ssd_conv_w"][e].reshape(4, 24, 128).transpose(2, 1, 0)
        ohead[e, :, 0:32] = inp["ssd_dt_bias"][e][None, :]
        ohead[e, :, 32:64] = inp["ssd_A_log"][e][None, :]
        ohead[e, :, 64:96] = inp["ssd_D"][e][None, :]
        ofeat[e, :, 0:24] = inp["ssd_conv_b"][e].reshape(24, 128).T
        ofeat[e, :, 24:40] = inp["ssd_norm"][e].reshape(16, 128).T
        ofeat[e, :, 40:56] = np.repeat(inp["ssd_D"][e].reshape(16, 2), 64, axis=1).T
    sh["owin"], sh["owout"] = owin, owout
    sh["oconv"] = oconv.reshape(2, 128, 96)
    sh["ohead"], sh["ofeat"] = ohead, ofeat
    rb = inp["rel_bias"]
    i = np.arange(128)[:, None]
    j = np.arange(256)[None, :]
    d = 128 + i - j
    valid = (d >= 0) & (d <= 128)
    bk = t5_bucket_np(np.clip(d, 0, 128))
    sb = np.where(valid[:, None, :], rb[bk].transpose(0, 2, 1), np.float32(NEG)).astype(np.float32)
    sh["swabias"] = np.ascontiguousarray(sb).reshape(128, 8 * 256)
    dd = 128 - np.arange(129)
    db = rb[t5_bucket_np(dd)]
    sh["decbias"] = np.ascontiguousarray(np.tile(db.T, (16, 1))).astype(np.float32)
    return sh


def core_inputs(inp, c, seq):
    m = {}
    m["x_p"] = np.ascontiguousarray(inp["x_prompt"][c][:seq])
    sl = slice(c * NS, (c + 1) * NS)
    m["x_s"] = np.ascontiguousarray(inp["x_sample"][sl, 0])
    m["c_k"] = np.ascontiguousarray(inp["cache_swa_k"][:, sl]).reshape(2, NS, 128, 128)
    m["c_v"] = np.ascontiguousarray(inp["cache_swa_v"][:, sl]).reshape(2, NS, 128, 128)
    m["s_gc"] = np.ascontiguousarray(inp["state_gdn_conv"][:, sl])
    m["s_gs"] = np.ascontiguousarray(inp["state_gdn_ssm"][:, sl])
    m["s_sc"] = np.ascontiguousarray(inp["state_ssd_conv"][:, sl])
    m["s_ss"] = np.ascontiguousarray(inp["state_ssd_ssm"][:, sl]).reshape(2, NS, 2048, 128)
    return m


_PROG_CACHE = {}


def get_prog(cfg):
    key = tuple(sorted((k, str(v)) for k, v in cfg.items()))
    if key not in _PROG_CACHE:
        _PROG_CACHE[key] = Prog(dict(cfg))
    return _PROG_CACHE[key]


def kernel(**inp):
    cfg = {"ntiles": 4, "depth": DEPTH}
    prog = get_prog(cfg)
    inp = {k: np.asarray(v) for k, v in inp.items()}
    sh = prep_shared(inp)
    in_maps = []
    for c in range(N_CORES):
        m = dict(sh)
        m.update(core_inputs(inp, c, SEQ))
        in_maps.append(m)
    res = run_bass_kernel_spmd(prog.nc, in_maps, core_ids=list(range(N_CORES)))
    R = res.results
    st1 = lambda k: np.stack([r[k] for r in R], 1)
    ct1 = lambda k: np.concatenate([r[k] for r in R], 1)
    y_p = np.stack([r["y_p"] for r in R], 0)
    y_s = np.concatenate([r["y_s"] for r in R], 0)[:, None, :]
    p_k = st1("o_pk").reshape(2, N_CORES, 128, 2, 64)
    p_v = st1("o_pv").reshape(2, N_CORES, 128, 2, 64)
    p_gc = st1("o_pgc")
    p_gs = st1("o_pgs")
    p_sc = st1("o_psc")
    p_ss = st1("o_pss").reshape(2, N_CORES, 32, 64, 128)
    s_k = ct1("o_sk").reshape(2, N_CORES * NS, 128, 2, 64)
    s_v = ct1("o_sv").reshape(2, N_CORES * NS, 128, 2, 64)
    s_gc = ct1("o_sgc")
    s_gs = ct1("o_sgs")
    s_sc = ct1("o_ssc")
    s_ss = ct1("o_sss").reshape(2, N_CORES * NS, 32, 64, 128)
    return (y_p, y_s, p_k, p_v, p_gc, p_gs, p_sc, p_ss, s_k, s_v, s_gc, s_gs, s_sc, s_ss)
```

```python
import math
import numpy as np
import concourse.bass as bass
import concourse.mybir as mybir
from concourse.bass_utils import run_bass_kernel_spmd

F32 = mybir.dt.float32
BF16 = mybir.dt.bfloat16
ALU = mybir.AluOpType
AF = mybir.ActivationFunctionType
AX = mybir.AxisListType

D = 1024
KC = 8
SEQ = 2048
NS = 16
DFF = 2816
FC = 22
EPS = 1e-6
N_CORES = 8
DEPTH = 4


class Buf:
    __slots__ = ("name", "w", "r", "excl")

    def __init__(self, name, excl=False):
        self.name = name
        self.w = None
        self.r = []
        self.excl = excl


class Sched:
    def __init__(self, nc, n_dma_sems=48):
        self.nc = nc
        self.E = {"pe": nc.tensor, "dve": nc.vector, "act": nc.scalar, "pool": nc.gpsimd, "sp": nc.sync}
        self.sems = {}
        self.cnt = {}
        for e in self.E:
            self.sems[e] = nc.alloc_semaphore("s_" + e)
            self.cnt[e] = 0
        self.dsems = []
        for i in range(n_dma_sems):
            k = "d%d" % i
            self.sems[k] = nc.alloc_semaphore("s_" + k)
            self.cnt[k] = 0
            self.dsems.append(k)
        self.dnext = 0
        self.seen = {e: {} for e in self.E}
        self.n_wait = 0
        self.n_ops = 0

    def _wait(self, eng, tick):
        if tick is None:
            return
        k, v = tick
        if self.seen[eng].get(k, 0) >= v:
            return
        self.E[eng].wait_ge(self.sems[k], v)
        self.seen[eng][k] = v
        self.n_wait += 1

    def _deps(self, eng, reads, writes):
        need = {}
        for b in reads:
            if b.w is not None:
                k, v = b.w
                if need.get(k, 0) < v:
                    need[k] = v
            if b.excl:
                for (k, v) in b.r:
                    if k != eng and need.get(k, 0) < v:
                        need[k] = v
        for b in writes:
            if b.w is not None:
                k, v = b.w
                if need.get(k, 0) < v:
                    need[k] = v
            for (k, v) in b.r:
                if need.get(k, 0) < v:
                    need[k] = v
        for k, v in need.items():
            self._wait(eng, (k, v))

    def _commit(self, tick, reads, writes):
        for b in reads:
            b.r.append(tick)
            if len(b.r) > 16:
                m = {}
                for (k, v) in b.r:
                    if m.get(k, 0) < v:
                        m[k] = v
                b.r = list(m.items())
        for b in writes:
            b.w = tick
            b.r = []

    def op(self, eng, fn, reads=(), writes=()):
        self._deps(eng, reads, writes)
        ins = fn()
        self.cnt[eng] += 1
        ins.then_inc(self.sems[eng], 1)
        tick = (eng, self.cnt[eng])
        self._commit(tick, reads, writes)
        self.n_ops += 1
        return tick

    def new_sem(self, k):
        self.sems[k] = self.nc.alloc_semaphore("s_" + k)
        self.cnt[k] = 0

    def barrier(self):
        for e in ("pe", "dve", "act", "sp"):
            for o in ("pe", "dve", "act", "pool"):
                if o != e and self.cnt[o] > 0:
                    self._wait(e, (o, self.cnt[o]))
            for k in self.dsems:
                if self.cnt[k] > 0:
                    self._wait(e, (k, self.cnt[k]))

    def dma(self, q, out=None, in_=None, reads=(), writes=(), multi=None, sem=None):
        pairs = multi if multi is not None else [(out, in_)]
        if sem is not None:
            k = sem
        else:
            k = self.dsems[self.dnext]
            self.dnext = (self.dnext + 1) % len(self.dsems)
        if self.cnt[k] > 0:
            self._wait(q, (k, self.cnt[k]))
        self._deps(q, reads, writes)
        for (o, i) in pairs:
            self.E[q].dma_start(out=o, in_=i, allow_slow_non_contiguous=True).then_inc(self.sems[k], 16)
            self.cnt[k] += 16
        tick = (k, self.cnt[k])
        self._commit(tick, reads, writes)
        return tick

    def finish(self):
        for e in ("pe", "dve", "act", "pool"):
            if self.cnt[e] > 0:
                self._wait("sp", (e, self.cnt[e]))
        for k in list(self.sems):
            if k not in self.E and self.cnt[k] > 0:
                self._wait("sp", (k, self.cnt[k]))


class Tile:
    def __init__(self, nc, name, shape, dtype, psum=False, handle=None):
        if handle is not None:
            self.t = handle
        elif psum:
            self.t = nc.alloc_psum_tensor(name, list(shape), dtype)
        else:
            self.t = nc.alloc_sbuf_tensor(name, list(shape), dtype)
        self.b = Buf(name, excl=psum)

    def __getitem__(self, k):
        return self.t[k]


SLOT_ELEMS = 4096
N_SLOTS = 3


class WStream:
    def __init__(self, nc, S, plan):
        self.nc, self.S = nc, S
        self.plan = plan
        self.slots = [Tile(nc, "wslot%d" % i, [128, SLOT_ELEMS], BF16) for i in range(N_SLOTS)]
        self.dram_buf = Buf("wdram")
        for i in range(N_SLOTS):
            S.new_sem("w%d" % i)
        self.issued = 0
        self.used = 0

    def _issue(self):
        i = self.issued
        ap, n = self.plan[i]
        sl = self.slots[i % N_SLOTS]
        self.S.dma("pool", out=sl[:, 0:n], in_=ap, reads=[self.dram_buf], writes=[sl.b], sem="w%d" % (i % N_SLOTS))
        self.issued += 1

    def get(self, expect_n):
        i = self.used
        while self.issued <= min(i + N_SLOTS - 2, len(self.plan) - 1):
            self._issue()
        assert self.plan[i][1] == expect_n, (i, self.plan[i][1], expect_n)
        self.used += 1
        return self.slots[i % N_SLOTS]


import contextlib

H_A, KV_A, HD_A = 8, 2, 64
H_B = 4
NEG = -1e30
EV_FM = 21
EV_TOK = 392
EW_OFF = [0, 4096, 8192, 12288, 16384, 20480, 21504]
EW_N = [4096, 4096, 4096, 4096, 4096, 1024, 8 * EV_TOK]
EW_TOT = 21504 + 8 * EV_TOK
OW_TOT = 256 + 10 * 4096


class Prog:
    def __init__(self, cfg):
        self.cfg = cfg
        self.nt = cfg.get("ntiles", 4)
        self.seq = self.nt * 512
        self.ntok = self.seq + NS
        self.depth = cfg.get("depth", DEPTH)
        self.layers = cfg.get("layers", None) or list(range(self.depth))
        nc = bass.Bass("TRN2", target_bir_lowering=False)
        self.nc = nc
        self.S = Sched(nc)
        self.stack = None
        self.declare_io()
        self.alloc()
        self.plan_weights()
        self.emit()

    def din(self, name, shape, dtype=F32):
        return self.nc.dram_tensor(name, list(shape), dtype, kind="ExternalInput").ap()

    def dout(self, name, shape, dtype=F32):
        return self.nc.dram_tensor(name, list(shape), dtype, kind="ExternalOutput").ap()

    def declare_io(self):
        sq = self.seq
        self.x_p = self.din("x_p", [sq, D])
        self.x_s = self.din("x_s", [NS, D])
        self.gains = self.din("gains", [128, 13 * KC])
        self.masks = self.din("masks", [128, 8 * 128])
        self.wgu = self.din("wgu", [DEPTH, 2, 11, 128, 4096])
        self.wd = self.din("wd", [DEPTH, 2, 8, 128, 2816])
        self.ewin = self.din("ewin", [2, 128, EW_TOT])
        self.ewout = self.din("ewout", [2, 2, 128, 4096])
        self.econv = self.din("econv", [2, 128, 48])
        self.esm = self.din("esm", [2, 128, 17])
        self.esk = self.din("esk", [2, 128, 1])
        self.swabias = self.din("swabias", [128, 8 * 256])
        self.decbias = self.din("decbias", [128, 129])
        self.c_k = self.din("c_k", [2, NS, 128, 128])
        self.c_v = self.din("c_v", [2, NS, 128, 128])
        self.s_gc = self.din("s_gc", [2, NS, 3, 1536])
        self.s_gs = self.din("s_gs", [2, NS, 4, 128, 128])
        self.owin = self.din("owin", [2, 128, OW_TOT])
        self.owout = self.din("owout", [2, 4, 128, 4096])
        self.oconv = self.din("oconv", [2, 128, 96])
        self.ohead = self.din("ohead", [2, 128, 96])
        self.ofeat = self.din("ofeat", [2, 128, 56])
        self.s_sc = self.din("s_sc", [2, NS, 3, 3072])
        self.s_ss = self.din("s_ss", [2, NS, 2048, 128])
        self.o_psc = self.dout("o_psc", [2, 3, 3072])
        self.o_pss = self.dout("o_pss", [2, 2048, 128])
        self.o_ssc = self.dout("o_ssc", [2, NS, 3, 3072])
        self.o_sss = self.dout("o_sss", [2, NS, 2048, 128])
        self.y_p = self.dout("y_p", [sq, D])
        self.y_s = self.dout("y_s", [NS, D])
        self.o_pk = self.dout("o_pk", [2, 128, 128])
        self.o_pv = self.dout("o_pv", [2, 128, 128])
        self.o_pgc = self.dout("o_pgc", [2, 3, 1536])
        self.o_pgs = self.dout("o_pgs", [2, 4, 128, 128])
        self.o_sk = self.dout("o_sk", [2, NS, 128, 128])
        self.o_sv = self.dout("o_sv", [2, NS, 128, 128])
        self.o_sgc = self.dout("o_sgc", [2, NS, 3, 1536])
        self.o_sgs = self.dout("o_sgs", [2, NS, 4, 128, 128])
        self.Bin = Buf("dram_in")
        self.Bout = Buf("dram_out")

    def alloc(self):
        nc = self.nc
        self.xT = Tile(nc, "xT", [128, KC, self.ntok], F32)
        self.xTb = [Buf("xT_t%d" % i) for i in range(self.nt)] + [Buf("xT_s")]
        self.ident = Tile(nc, "ident", [128, 128], F32)
        self.ident_bf = Tile(nc, "ident_bf", [128, 128], BF16)
        self.ones_bf = Tile(nc, "ones_bf", [128, 128], BF16)
        self.ones_f = Tile(nc, "ones_f", [128, 128], F32)
        self.gn = Tile(nc, "gn", [128, 13 * KC], F32)
        self.mk = Tile(nc, "mk", [128, 8, 128], F32)
        self.P = [Tile(nc, "ps%d" % i, [128, 512], F32, psum=True) for i in range(8)]

    def begin(self):
        assert self.stack is None
        self.stack = contextlib.ExitStack()
        self.nalloc = 0
        self.deferred = []

    def out_dma(self, dst, src, r=()):
        self.deferred.append((dst, src, list(r)))

    def end(self):
        for (dst, src, r) in self.deferred:
            self.dma(dst, src, r=r, w=[self.Bout])
        self.deferred = []
        self.S.barrier()
        self.stack.close()
        self.stack = None

    def T(self, name, shape, dtype):
        self.nalloc += 1
        h = self.stack.enter_context(self.nc.sbuf_tensor("%s_%d" % (name, self.S.n_ops), list(shape), dtype))
        return Tile(self.nc, name, shape, dtype, handle=h)

    def ffn_blocks(self, l, f):
        out = []
        for b in range(11):
            out.append((self.wgu[l, f, b], 4096))
        for b in range(8):
            out.append((self.wd[l, f, b], 2816))
        return out

    def tiles(self):
        ts = []
        for t in range(self.nt):
            segs = [(t * 512, 512, 0)]
            if t == self.nt - 1:
                segs.append((self.seq, NS, 512))
            ts.append(segs)
        return ts

    def even_blocks(self, e):
        out = []
        for _ in range(self.nt + 1):
            for b in range(7):
                out.append((self.ewin[e, :, EW_OFF[b]:EW_OFF[b] + EW_N[b]], EW_N[b]))
            for b in range(2):
                out.append((self.ewout[e, b], 4096))
        return out

    def odd_blocks(self, e):
        out = []
        for _ in range(self.nt + 1):
            out.append((self.owin[e, :, 0:256], 256))
            for b in range(10):
                out.append((self.owin[e, :, 256 + b * 4096:256 + (b + 1) * 4096], 4096))
            for b in range(4):
                out.append((self.owout[e, b], 4096))
        return out

    def mixer_blocks(self, l):
        if l % 2 == 0:
            return self.even_blocks(l // 2)
        return self.odd_blocks(l // 2)

    def plan_weights(self):
        plan = []
        ffn = self.cfg.get("ffn", True)
        for l in self.layers:
            if ffn:
                for _ in self.tiles():
                    plan += self.ffn_blocks(l, 0)
            if not self.cfg.get("nomix"):
                plan += self.mixer_blocks(l)
            if ffn:
                for _ in self.tiles():
                    plan += self.ffn_blocks(l, 1)
        self.ws = WStream(self.nc, self.S, plan)

    def bl(self, xs):
        return [getattr(x, 'b', x) for x in xs]

    def dve(self, fn, r=(), w=()):
        return self.S.op("dve", fn, self.bl(r), self.bl(w))

    def act(self, fn, r=(), w=()):
        return self.S.op("act", fn, self.bl(r), self.bl(w))

    def pe(self, fn, r=(), w=()):
        return self.S.op("pe", fn, self.bl(r), self.bl(w))

    def pool(self, fn, r=(), w=()):
        return self.S.op("pool", fn, self.bl(r), self.bl(w))

    def dma(self, out, in_, r=(), w=(), q="sp"):
        return self.S.dma(q, out=out, in_=in_, reads=self.bl(r), writes=self.bl(w))

    def gcol(self, idx, kc):
        return self.gn[:, idx * KC + kc: idx * KC + kc + 1]

    def tile_bufs(self, ti):
        b = [self.xTb[ti]]
        if ti == self.nt - 1:
            b.append(self.xTb[self.nt])
        return b

    def col_stats(self, src_fn, nchunk, n, ps, sq, ones, scale, bias_ln, out_ap, exp_bias=0.0, rd=()):
        nc = self.nc
        for c in range(nchunk):
            self.act(lambda c=c: nc.scalar.activation(sq[:, c, 0:n], src_fn(c), AF.Square), r=rd, w=[sq])

        def mm():
            ins = None
            for c in range(nchunk):
                ins = nc.tensor.matmul(ps[:, 0:n], ones[:], sq[:, c, 0:n], start=(c == 0), stop=(c == nchunk - 1))
            return ins
        self.pe(mm, r=[sq, ones], w=[ps])
        return ps

    def rmsnorm_cols(self, xbufs, c0, n, gidx, hn, l0, sq, rstd):
        nc = self.nc
        ps = self.P[7]
        self.act(lambda: nc.scalar.activation(sq[:, :, l0:l0 + n], self.xT[:, :, c0:c0 + n], AF.Square),
                 r=xbufs, w=[sq])

        def mm():
            ins = None
            for kc in range(KC):
                ins = nc.tensor.matmul(ps[:, 0:n], self.ones_bf[:], sq[:, kc, l0:l0 + n],
                                       start=(kc == 0), stop=(kc == KC - 1))
            return ins
        self.pe(mm, r=[sq, self.ones_bf], w=[ps])
        self.act(lambda: nc.scalar.activation(rstd[:, l0:l0 + n], ps[:, 0:n], AF.Ln, bias=EPS, scale=1.0 / D),
                 r=[ps], w=[rstd])
        self.act(lambda: nc.scalar.activation(rstd[:, l0:l0 + n], rstd[:, l0:l0 + n], AF.Exp, scale=-0.5),
                 r=[rstd], w=[rstd])
        for kc in range(KC):
            self.dve(lambda kc=kc: nc.vector.scalar_tensor_tensor(
                out=hn[:, kc, l0:l0 + n], in0=self.xT[:, kc, c0:c0 + n], scalar=self.gcol(gidx, kc),
                in1=rstd[:, l0:l0 + n], op0=ALU.mult, op1=ALU.mult),
                r=xbufs + [rstd, self.gn], w=[hn])

    def ffn_tile(self, l, f, ti, segs, hn, hT, sgs):
        nc = self.nc
        xb = self.tile_bufs(ti)
        it = 0
        for blk in range(11):
            w = self.ws.get(4096)
            wv = w[:, 0:4096].rearrange("p (a k c) -> p a k c", a=2, k=KC)
            for fc in range(2):
                fch = blk * 2 + fc
                for si, (g0, n, l0) in enumerate(segs):
                    if si == 0:
                        pg, pu = self.P[(it % 2)], self.P[2 + (it % 2)]
                    else:
                        pg, pu = self.P[4], self.P[5]

                    def mm(pt, a):
                        ins = None
                        for kc in range(KC):
                            ins = nc.tensor.matmul(pt[:, 0:n], wv[:, a, kc, fc * 128:(fc + 1) * 128],
                                                   hn[:, kc, l0:l0 + n], start=(kc == 0), stop=(kc == KC - 1))
                        return ins
                    self.pe(lambda: mm(pg, 0), r=[w, hn], w=[pg])
                    self.pe(lambda: mm(pu, 1), r=[w, hn], w=[pu])
                    sg = sgs[it % 2]
                    self.act(lambda: nc.scalar.activation(sg[:, 0:n], pg[:, 0:n], AF.Silu), r=[pg], w=[sg])
                    self.dve(lambda: nc.vector.tensor_tensor(out=hT[:, fch, l0:l0 + n], in0=sg[:, 0:n],
                                                             in1=pu[:, 0:n], op=ALU.mult), r=[sg, pu], w=[hT])
                it += 1
        it = 0
        for blk in range(8):
            w = self.ws.get(2816)
            wv = w[:, 0:2816].rearrange("p (k c) -> p k c", k=FC)
            dch = blk
            for si, (g0, n, l0) in enumerate(segs):
                po = self.P[6 + (it % 2)] if si == 0 else self.P[4 + (it % 2)]

                def mm():
                    ins = None
                    for k in range(FC):
                        ins = nc.tensor.matmul(po[:, 0:n], wv[:, k, :], hT[:, k, l0:l0 + n],
                                               start=(k == 0), stop=(k == FC - 1))
                    return ins
                self.pe(mm, r=[w, hT], w=[po])
                xs = self.xT[:, dch, g0:g0 + n]
                self.dve(lambda: nc.vector.scalar_tensor_tensor(out=xs, in0=po[:, 0:n], scalar=0.5, in1=xs,
                                                                op0=ALU.mult, op1=ALU.add), r=[po] + xb, w=xb)
            it += 1

    def ffn(self, l, f):
        self.begin()
        hns = [self.T("hn%d" % i, [128, KC, 528], BF16) for i in range(2)]
        hT = self.T("hT", [128, FC, 528], BF16)
        sq = self.T("sq", [128, KC, 528], BF16)
        rstd = self.T("rstd", [128, 528], F32)
        sgs = [self.T("sg%d" % i, [128, 512], F32) for i in range(2)]
        gidx = l if f == 0 else 8 + l
        for ti, segs in enumerate(self.tiles()):
            hn = hns[ti % 2]
            for (g0, n, l0) in segs:
                self.rmsnorm_cols(self.tile_bufs(ti), g0, n, gidx, hn, l0, sq, rstd)
            self.ffn_tile(l, f, ti, segs, hn, hT, sgs)
        self.end()

    def consts(self):
        nc = self.nc
        self.pool(lambda: nc.gpsimd.memset(self.ident[:], 0.0), w=[self.ident])
        self.pool(lambda: nc.gpsimd.affine_select(self.ident[:], self.ident[:], pattern=[[-1, 128]],
                                                  compare_op=ALU.not_equal, fill=1.0, base=0, channel_multiplier=1),
                  r=[self.ident], w=[self.ident])
        self.pool(lambda: nc.gpsimd.memset(self.ones_bf[:], 1.0), w=[self.ones_bf])
        self.pool(lambda: nc.gpsimd.memset(self.ones_f[:], 1.0), w=[self.ones_f])
        self.dve(lambda: nc.vector.tensor_copy(self.ident_bf[:], self.ident[:]), r=[self.ident], w=[self.ident_bf])
        self.dma(self.gn[:], self.gains, r=[self.Bin], w=[self.gn])
        self.dma(self.mk[:].rearrange("p a b -> p (a b)"), self.masks, r=[self.Bin], w=[self.mk])

    def load_x(self):
        nc = self.nc
        self.begin()
        xins = [self.T("xin%d" % i, [128, D], F32) for i in range(2)]
        ntt = self.seq // 128
        for tt in range(ntt + 1):
            xin = xins[tt % 2]
            if tt < ntt:
                rows, src, c0 = 128, self.x_p[tt * 128:(tt + 1) * 128, :], tt * 128
                xb = self.xTb[tt // 4]
            else:
                rows, src, c0 = NS, self.x_s, self.seq
                xb = self.xTb[self.nt]
            self.dma(xin[0:rows, :], src, r=[self.Bin], w=[xin])
            for half in range(2):
                ps = self.P[(tt % 2) * 2 + half]

                def tr():
                    ins = None
                    for j in range(4):
                        kc = half * 4 + j
                        ins = nc.tensor.transpose(ps[:, j * 128:j * 128 + rows], xin[0:rows, kc * 128:(kc + 1) * 128],
                                                  self.ident[0:rows, 0:rows])
                    return ins
                self.pe(tr, r=[xin, self.ident], w=[ps])
                pv = ps[:, :].rearrange("p (j c) -> p j c", j=4)[:, :, 0:rows]
                dst = self.xT[:, half * 4:half * 4 + 4, c0:c0 + rows]
                if half == 0:
                    self.dve(lambda: nc.vector.tensor_copy(dst, pv), r=[ps], w=[xb])
                else:
                    self.act(lambda: nc.scalar.copy(dst, pv), r=[ps], w=[xb])
        self.end()

    def final(self):
        nc = self.nc
        self.begin()
        sq = self.T("sq", [128, KC, 528], BF16)
        rstd = self.T("rstd", [128, 528], F32)
        yos = [self.T("yo%d" % i, [128, D], F32) for i in range(2)]
        for ti, segs in enumerate(self.tiles()):
            xb = self.tile_bufs(ti)
            for (c0, n, l0) in segs:
                ps = self.P[7]
                self.act(lambda: nc.scalar.activation(sq[:, :, l0:l0 + n], self.xT[:, :, c0:c0 + n], AF.Square),
                         r=xb, w=[sq])

                def mm():
                    ins = None
                    for kc in range(KC):
                        ins = nc.tensor.matmul(ps[:, 0:n], self.ones_bf[:], sq[:, kc, l0:l0 + n],
                                               start=(kc == 0), stop=(kc == KC - 1))
                    return ins
                self.pe(mm, r=[sq, self.ones_bf], w=[ps])
                self.act(lambda: nc.scalar.activation(rstd[:, l0:l0 + n], ps[:, 0:n], AF.Ln, bias=EPS, scale=1.0 / D),
                         r=[ps], w=[rstd])
                self.act(lambda: nc.scalar.activation(rstd[:, l0:l0 + n], rstd[:, l0:l0 + n], AF.Exp, scale=-0.5),
                         r=[rstd], w=[rstd])
                for kc in range(KC):
                    self.dve(lambda kc=kc: nc.vector.scalar_tensor_tensor(
                        out=self.xT[:, kc, c0:c0 + n], in0=self.xT[:, kc, c0:c0 + n], scalar=self.gcol(12, kc),
                        in1=rstd[:, l0:l0 + n], op0=ALU.mult, op1=ALU.mult), r=xb + [rstd, self.gn], w=xb)
        ntt = self.seq // 128
        for tt in range(ntt + 1):
            yo = yos[tt % 2]
            if tt < ntt:
                rows, dst, c0 = 128, self.y_p[tt * 128:(tt + 1) * 128, :], tt * 128
                xb = self.xTb[tt // 4]
            else:
                rows, dst, c0 = NS, self.y_s, self.seq
                xb = self.xTb[self.nt]
            for half in range(2):
                ps = self.P[(tt % 2) * 2 + half]

                def tr():
                    ins = None
                    for j in range(4):
                        kc = half * 4 + j
                        ins = nc.tensor.transpose(ps[0:rows, j * 128:(j + 1) * 128], self.xT[:, kc, c0:c0 + rows],
                                                  self.ident[:])
                    return ins
                self.pe(tr, r=[xb, self.ident], w=[ps])
                if half == 0:
                    self.dve(lambda: nc.vector.tensor_copy(yo[0:rows, 0:512], ps[0:rows, :]), r=[ps], w=[yo])
                else:
                    self.act(lambda: nc.scalar.copy(yo[0:rows, 512:1024], ps[0:rows, :]), r=[ps], w=[yo])
            self.dma(dst, yo[0:rows, :], r=[yo], w=[self.Bout])
        self.end()

    def mixer(self, l):
        if l % 2 == 0:
            self.even_mixer(l)
        else:
            self.odd_mixer(l)

    def emit(self):
        self.consts()
        self.load_x()
        for l in self.layers:
            if self.cfg.get("ffn", True):
                self.ffn(l, 0)
            if not self.cfg.get("nomix"):
                self.mixer(l)
            if self.cfg.get("ffn", True):
                self.ffn(l, 1)
        self.final()
        self.S.finish()


def even_alloc(self, nc_):
    G = {}
    T = self.T
    G["hn"] = T("hn", [128, KC, nc_], BF16)
    G["sqm"] = T("sqm", [128, KC, nc_], BF16)
    G["rstd"] = T("rstd", [128, nc_], F32)
    G["qT"] = T("qT", [128, 4, nc_], BF16)
    G["acc"] = [T("acc%d" % i, [128, nc_], F32) for i in range(2)]
    G["cq"] = T("cq", [128, 12, nc_], F32)
    G["zs"] = T("zs", [128, 4, nc_], BF16)
    oT = Tile(self.nc, "oT", [128, 4, nc_], F32, handle=G["hn"].t.bitcast(F32).reshape([128, 4, nc_]))
    oT.b = G["hn"].b
    G["oT"] = oT
    G["cw"] = T("cw", [128, 12, 4], F32)
    G["sm"] = T("sm", [128, 17], F32)
    G["eA"] = T("eA", [128, 4], F32)
    G["sq4"] = T("sq4", [128, 2, nc_], BF16)
    return G


def even_common(self, G, e):
    nc = self.nc
    self.dma(G["cw"][:].rearrange("p a b -> p (a b)"), self.econv[e], r=[self.Bin], w=[G["cw"]])
    self.dma(G["sm"][:], self.esm[e], r=[self.Bin], w=[G["sm"]])
    self.act(lambda: nc.scalar.activation(G["eA"][:], G["sm"][:, 0:4], AF.Exp), r=[G["sm"]], w=[G["eA"]])
    self.dve(lambda: nc.vector.tensor_scalar_mul(G["eA"][:], G["eA"][:], -1.0), r=[G["eA"]], w=[G["eA"]])


def even_inproj_fm(self, G, n, conv_fn, k_dst):
    nc = self.nc
    hn = G["hn"]
    j = 0
    for blk in range(6):
        nb = EW_N[blk]
        w = self.ws.get(nb)
        ncols = nb // KC
        wv = w[:, 0:nb].rearrange("p (k c) -> p k c", k=KC)
        for cc in range(ncols // 128):
            ps = self.P[j % 2]

            def mm():
                ins = None
                for kc in range(KC):
                    ins = nc.tensor.matmul(ps[:, 0:n], wv[:, kc, cc * 128:(cc + 1) * 128], hn[:, kc, 0:n],
                                           start=(kc == 0), stop=(kc == KC - 1))
                return ins
            self.pe(mm, r=[w, hn], w=[ps])
            if j < 4:
                self.act(lambda: nc.scalar.copy(G["qT"][:, j, 0:n], ps[:, 0:n]), r=[ps], w=[G["qT"]])
            elif j == 4:
                self.act(lambda: nc.scalar.copy(k_dst, ps[:, 0:n]), r=[ps], w=[G["kTa"]])
            elif j < 17:
                conv_fn(j - 5, ps)
            else:
                self.act(lambda: nc.scalar.activation(G["zs"][:, j - 17, 0:n], ps[:, 0:n], AF.Silu),
                         r=[ps], w=[G["zs"]])
            j += 1
    assert j == EV_FM


def gates_from_ba(self, G, ba_ps_ap, rows, beta_dst, g_dst, tmp, rd, wr):
    nc = self.nc
    sm = G["sm"]
    self.act(lambda: nc.scalar.activation(beta_dst, ba_ps_ap[:, 0:4], AF.Sigmoid), r=rd, w=wr)
    self.dve(lambda: nc.vector.tensor_tensor(out=tmp[0:rows, 0:4], in0=ba_ps_ap[:, 4:8], in1=sm[0:rows, 4:8], op=ALU.add),
             r=rd + [sm], w=[tmp])
    self.act(lambda: nc.scalar.activation(tmp[0:rows, 0:4], tmp[0:rows, 0:4], AF.Exp), r=[tmp], w=[tmp])
    self.act(lambda: nc.scalar.activation(tmp[0:rows, 0:4], tmp[0:rows, 0:4], AF.Ln, bias=1.0, scale=1.0), r=[tmp], w=[tmp])
    self.dve(lambda: nc.vector.tensor_tensor(out=g_dst, in0=tmp[0:rows, 0:4], in1=G["eA"][0:rows, :], op=ALU.mult),
             r=[tmp, G["eA"]], w=wr)


def l2norm_heads(self, G, n):
    nc = self.nc
    cq, sq4 = G["cq"], G["sq4"]
    for c in range(8):
        ps = self.P[c % 2]
        self.act(lambda: nc.scalar.activation(sq4[:, c % 2, 0:n], cq[:, c, 0:n], AF.Square), r=[cq], w=[sq4])
        self.pe(lambda: nc.tensor.matmul(ps[:, 0:n], self.ones_bf[:], sq4[:, c % 2, 0:n], start=True, stop=True),
                r=[sq4, self.ones_bf], w=[ps])
        rs = G["acc"][c % 2]
        self.act(lambda: nc.scalar.activation(rs[:, 0:n], ps[:, 0:n], AF.Ln, bias=EPS, scale=1.0), r=[ps], w=[rs])
        self.act(lambda: nc.scalar.activation(rs[:, 0:n], rs[:, 0:n], AF.Exp, scale=-0.5,
                                              bias=(math.log(128.0 ** -0.5) if c < 4 else 0.0)), r=[rs], w=[rs])
        self.dve(lambda: nc.vector.tensor_tensor(out=cq[:, c, 0:n], in0=cq[:, c, 0:n], in1=rs[:, 0:n], op=ALU.mult),
                 r=[cq, rs], w=[cq])


def gdn_post(self, G, n, e):
    nc = self.nc
    oT, sq4, mix = G["oT"], G["sq4"], G["sqm"]
    for h in range(4):
        ps = self.P[h % 2]
        self.act(lambda: nc.scalar.activation(sq4[:, h % 2, 0:n], oT[:, h, 0:n], AF.Square), r=[oT], w=[sq4])
        self.pe(lambda: nc.tensor.matmul(ps[:, 0:n], self.ones_bf[:], sq4[:, h % 2, 0:n], start=True, stop=True),
                r=[sq4, self.ones_bf], w=[ps])
        rs = G["acc"][h % 2]
        self.act(lambda: nc.scalar.activation(rs[:, 0:n], ps[:, 0:n], AF.Ln, bias=EPS, scale=1.0 / 128.0), r=[ps], w=[rs])
        self.act(lambda: nc.scalar.activation(rs[:, 0:n], rs[:, 0:n], AF.Exp, scale=-0.5), r=[rs], w=[rs])
        self.dve(lambda: nc.vector.scalar_tensor_tensor(out=rs[:, 0:n], in0=oT[:, h, 0:n], scalar=G["sm"][:, 16:17],
                                                        in1=rs[:, 0:n], op0=ALU.mult, op1=ALU.mult),
                 r=[oT, rs, G["sm"]], w=[rs])
        self.dve(lambda: nc.vector.tensor_tensor(out=mix[:, 4 + h, 0:n], in0=rs[:, 0:n], in1=G["zs"][:, h, 0:n], op=ALU.mult),
                 r=[rs, G["zs"]], w=[mix])


def even_outproj(self, G, segs_bufs, c0, n):
    nc = self.nc
    mix = G["sqm"]
    for blk in range(2):
        w = self.ws.get(4096)
        wv = w[:, 0:4096].rearrange("p (k c) -> p k c", k=KC)
        for dc in range(4):
            dch = blk * 4 + dc
            ps = self.P[dch % 2]

            def mm():
                ins = None
                for kc in range(KC):
                    ins = nc.tensor.matmul(ps[:, 0:n], wv[:, kc, dc * 128:(dc + 1) * 128], mix[:, kc, 0:n],
                                           start=(kc == 0), stop=(kc == KC - 1))
                return ins
            self.pe(mm, r=[w, mix], w=[ps])
            xs = self.xT[:, dch, c0:c0 + n]
            self.dve(lambda: nc.vector.tensor_tensor(out=xs, in0=xs, in1=ps[:, 0:n], op=ALU.add),
                     r=[ps] + segs_bufs, w=segs_bufs)


def swa_block(self, G, W, qb, ql):
    nc = self.nc
    mix = G["sqm"]
    sm = G["sm"]
    k0 = (qb - 1) * 128 if qb > 0 else 0
    nk = 256 if qb > 0 else 128
    nkt = nk // 128

    def pairgen(c):
        q = c % 2
        s_ps, t_ps, o_ps = self.P[2 + q], self.P[4 + q], self.P[6 + q]
        sb, pn, PT, col = W["sb"][q], W["pn"][q], W["PT"][q], W["col"][q]
        tv = t_ps[:, :].bitcast(BF16)
        for kv in range(2):
            h = kv * 4 + c
            rows = slice(kv * 64, (kv + 1) * 64)
            yield self.pe(lambda: nc.tensor.matmul(s_ps[:, 0:nk], G["qT"][rows, c, ql:ql + 128], G["kTa"][rows, k0:k0 + nk],
                                                   start=True, stop=True), r=[G["qT"], G["kTa"]], w=[s_ps])
            yield self.dve(lambda: nc.vector.scalar_tensor_tensor(out=sb[:, 0:nk], in0=s_ps[:, 0:nk], scalar=0.125,
                                                                  in1=G["bias"][:, h, 256 - nk:256], op0=ALU.mult, op1=ALU.add),
                           r=[s_ps, G["bias"]], w=[sb])
            yield self.dve(lambda: nc.vector.reduce_max(out=col[:, 0:1], in_=sb[:, 0:nk], axis=AX.X), r=[sb], w=[col])
            yield self.dve(lambda: nc.vector.tensor_tensor(out=col[:, 0:1], in0=col[:, 0:1], in1=sm[:, 8 + h:9 + h], op=ALU.max),
                           r=[col, sm], w=[col])
            yield self.dve(lambda: nc.vector.tensor_scalar_mul(col[:, 1:2], col[:, 0:1], -1.0), r=[col], w=[col])
            yield self.act(lambda: nc.scalar.activation(sb[:, 0:nk], sb[:, 0:nk], AF.Exp, bias=col[:, 1:2], scale=1.0),
                           r=[sb, col], w=[sb])
            yield self.dve(lambda: nc.vector.reduce_sum(out=col[:, 2:3], in_=sb[:, 0:nk], axis=AX.X), r=[sb], w=[col])
            yield self.act(lambda: nc.scalar.activation(col[:, 3:4], sm[:, 8 + h:9 + h], AF.Exp, bias=col[:, 1:2], scale=1.0),
                           r=[sm, col], w=[col])
            yield self.dve(lambda: nc.vector.tensor_tensor(out=col[:, 2:3], in0=col[:, 2:3], in1=col[:, 3:4], op=ALU.add),
                           r=[col], w=[col])
            yield self.dve(lambda: nc.vector.reciprocal(col[:, 4:5], col[:, 2:3]), r=[col], w=[col])
            yield self.dve(lambda: nc.vector.tensor_scalar_mul(pn[:, 0:nk], sb[:, 0:nk], col[:, 4:5]), r=[sb, col], w=[pn])

            def tr():
                ins = None
                for kt in range(nkt):
                    ins = nc.tensor.transpose(tv[:, kt * 128:(kt + 1) * 128], pn[:, kt * 128:(kt + 1) * 128],
                                              self.ident_bf[:])
                return ins
            yield self.pe(tr, r=[pn, self.ident_bf], w=[t_ps])
            yield self.act(lambda: nc.scalar.copy(PT[:, 0:nk], tv[:, 0:nk]), r=[t_ps], w=[PT])

            def pv():
                ins = None
                for kt in range(nkt):
                    ins = nc.tensor.matmul(o_ps[:, 0:128], G["Va"][:, k0 // 128 + kt, kv * 128:(kv + 1) * 128],
                                           PT[:, kt * 128:(kt + 1) * 128],
                                           start=(kv == 0 and kt == 0), stop=(kv == 1 and kt == nkt - 1))
                return ins
            yield self.pe(pv, r=[G["Va"], PT], w=[o_ps])
        yield self.act(lambda: nc.scalar.copy(mix[:, c, ql:ql + 128], o_ps[:, 0:128]), r=[o_ps], w=[mix])

    for pr in range(2):
        gens = [pairgen(2 * pr), pairgen(2 * pr + 1)]
        while gens:
            for g in list(gens):
                try:
                    next(g)
                except StopIteration:
                    gens.remove(g)


def gdn_subtile(self, G, W, st, cs):
    nc = self.nc
    cq = G["cq"]
    mk = self.mk
    ident = self.ident
    beta_t, g_t = G["beta_t"], G["g_t"]
    sc = W["sc"]
    R = W["R"]
    P = self.P
    Sst, Sbf, oT = G["Sst"], G["Sbf"], G["oT"]
    self.pe(lambda: nc.tensor.matmul(P[4][:, 0:4], mk[:, 0, :], g_t[:, st, :], start=True, stop=True), r=[mk, g_t], w=[P[4]])
    self.pe(lambda: nc.tensor.matmul(P[4][:, 4:8], mk[:, 4, :], g_t[:, st, :], start=True, stop=True), r=[mk, g_t], w=[P[4]])
    self.dve(lambda: nc.vector.tensor_copy(sc[:, 0:8], P[4][:, 0:8]), r=[P[4]], w=[sc])
    self.dve(lambda: nc.vector.tensor_tensor(out=sc[:, 8:12], in0=sc[:, 4:8], in1=sc[:, 0:4], op=ALU.subtract), r=[sc], w=[sc])
    self.act(lambda: nc.scalar.activation(sc[:, 8:12], sc[:, 8:12], AF.Exp), r=[sc], w=[sc])
    self.act(lambda: nc.scalar.activation(sc[:, 12:16], sc[:, 0:4], AF.Exp), r=[sc], w=[sc])
    self.dve(lambda: nc.vector.tensor_tensor(out=sc[:, 16:20], in0=sc[:, 12:16], in1=beta_t[:, st, :], op=ALU.mult),
             r=[sc, beta_t], w=[sc])
    self.dve(lambda: nc.vector.tensor_tensor(out=R[:, :, 0:128], in0=mk[:, 0, :].unsqueeze(1).to_broadcast([128, 4, 128]),
                                             in1=g_t[:, st, :].unsqueeze(2).to_broadcast([128, 4, 128]), op=ALU.mult),
             r=[mk, g_t], w=[R])
    self.dve(lambda: nc.vector.tensor_tensor(out=R[:, :, 128:256], in0=ident[:].unsqueeze(1).to_broadcast([128, 4, 128]),
                                             in1=beta_t[:, st, :].unsqueeze(2).to_broadcast([128, 4, 128]), op=ALU.mult),
             r=[ident, beta_t], w=[R])
    for hh in range(2):
        self.pe(lambda: nc.tensor.matmul(P[hh][:, 0:512], self.ones_f[:], R[:, 2 * hh:2 * hh + 2, :], start=True, stop=True),
                r=[self.ones_f, R], w=[P[hh]])
    bcS = W["R"]
    self.dve(lambda: nc.vector.tensor_copy(bcS[:, 0:2, :], P[0][:, 0:512].rearrange("p (a b) -> p a b", a=2)), r=[P[0]], w=[bcS])
    self.act(lambda: nc.scalar.copy(bcS[:, 2:4, :], P[1][:, 0:512].rearrange("p (a b) -> p a b", a=2)), r=[P[1]], w=[bcS])

    def gcbc(h):
        return bcS[:, h, 0:128]

    def betabc(h):
        return bcS[:, h, 128:256]
    for h in range(4):
        self.act(lambda: nc.scalar.activation(sc[:, 20 + 2 * h:22 + 2 * h], gcbc(h)[:, 63:128:64], AF.Exp), r=[bcS], w=[sc])

    def head(h):
        q = h % 2
        tw = W["set"][q]
        PA, PN, PT_ = P[2 + q], P[4 + q], P[6 + q]
        kn = cq[:, 4 + h, cs:cs + 128]
        qn = cq[:, h, cs:cs + 128]
        vv = cq[:, 8 + h, cs:cs + 128]
        bc = bcS
        yield self.pe(lambda: nc.tensor.transpose(PA[:, 0:128], kn, ident[:]), r=[cq, ident], w=[PA])
        yield self.pe(lambda: nc.tensor.transpose(PA[:, 128:256], vv, ident[:]), r=[cq, ident], w=[PA])
        yield self.pe(lambda: nc.tensor.matmul(PA[:, 256:384], kn, kn, start=True, stop=True), r=[cq], w=[PA])
        yield self.pe(lambda: nc.tensor.matmul(PA[:, 384:512], kn, qn, start=True, stop=True), r=[cq], w=[PA])
        yield self.dve(lambda: nc.vector.tensor_scalar_mul(tw["kbg"][:], PA[:, 0:128], sc[:, 16 + h:17 + h]), r=[PA, sc], w=[tw["kbg"]])
        yield self.act(lambda: nc.scalar.mul(tw["kd"][:], PA[:, 0:128], sc[:, 8 + h:9 + h]), r=[PA, sc], w=[tw["kd"]])
        yield self.dve(lambda: nc.vector.tensor_scalar_mul(tw["vb"][:], PA[:, 128:256], beta_t[:, st, h:h + 1]),
                       r=[PA, beta_t], w=[tw["vb"]])
        yield self.dve(lambda: nc.vector.scalar_tensor_tensor(out=tw["tA"][:], in0=gcbc(h), scalar=sc[:, h:h + 1], in1=mk[:, 1, :],
                                                              op0=ALU.subtract, op1=ALU.max), r=[bc, sc, mk], w=[tw["tA"]])
        yield self.act(lambda: nc.scalar.activation(tw["Es"][:], tw["tA"][:], AF.Exp, scale=-1.0), r=[tw["tA"]], w=[tw["Es"]])
        yield self.dve(lambda: nc.vector.scalar_tensor_tensor(out=tw["tB"][:], in0=gcbc(h), scalar=sc[:, h:h + 1], in1=mk[:, 2, :],
                                                              op0=ALU.subtract, op1=ALU.min), r=[bc, sc, mk], w=[tw["tB"]])
        yield self.act(lambda: nc.scalar.activation(tw["ETd"][:], tw["tB"][:], AF.Exp), r=[tw["tB"]], w=[tw["ETd"]])
        yield self.dve(lambda: nc.vector.tensor_tensor(out=tw["ETs"][:], in0=tw["ETd"][:], in1=mk[:, 3, :], op=ALU.mult),
                       r=[tw["ETd"], mk], w=[tw["ETs"]])
        Pc, Qc, X, Y = tw["Pm"][0], tw["Qm"][0], tw["X"][0], tw["Y"][0]
        yield self.dve(lambda: nc.vector.scalar_tensor_tensor(out=Qc[:], in0=PA[:, 256:384], scalar=beta_t[:, st, h:h + 1],
                                                              in1=tw["Es"][:], op0=ALU.mult, op1=ALU.mult),
                       r=[PA, beta_t, tw["Es"]], w=[Qc])
        yield self.dve(lambda: nc.vector.tensor_tensor(out=tw["tA"][:], in0=PA[:, 256:384], in1=tw["ETs"][:], op=ALU.mult),
                       r=[PA, tw["ETs"]], w=[tw["tA"]])
        yield self.dve(lambda: nc.vector.tensor_tensor(out=Pc[:], in0=tw["tA"][:], in1=betabc(h), op=ALU.mult),
                       r=[tw["tA"], bc], w=[Pc])
        yield self.dve(lambda: nc.vector.tensor_tensor(out=tw["AqkT"][:], in0=PA[:, 384:512], in1=tw["ETd"][:], op=ALU.mult),
                       r=[PA, tw["ETd"]], w=[tw["AqkT"]])
        yield self.act(lambda: nc.scalar.activation(tw["tB"][:], gcbc(h), AF.Exp), r=[bc], w=[tw["tB"]])
        yield self.dve(lambda: nc.vector.tensor_tensor(out=tw["qgT"][:], in0=qn, in1=tw["tB"][:], op=ALU.mult),
                       r=[cq, tw["tB"]], w=[tw["qgT"]])
        yield self.dve(lambda: nc.vector.tensor_tensor(out=X[:], in0=ident[:], in1=Pc[:], op=ALU.subtract), r=[ident, Pc], w=[X])
        yield self.dve(lambda: nc.vector.tensor_tensor(out=Y[:], in0=ident[:], in1=Qc[:], op=ALU.subtract), r=[ident, Qc], w=[Y])
        for k in range(1, 6):
            bk = PN
            last = (k == 5)
            Pn, Qn, Xn, Yn = tw["Pm"][k % 2], tw["Qm"][k % 2], tw["X"][k % 2], tw["Y"][k % 2]

            def sqr():
                ins = nc.tensor.matmul(bk[:, 0:128], Qc[:], Pc[:], start=True, stop=True)
                if not last:
                    ins = nc.tensor.matmul(bk[:, 128:256], Pc[:], Qc[:], start=True, stop=True)
                return ins
            yield self.pe(sqr, r=[Pc, Qc], w=[bk])
            yield self.act(lambda: nc.scalar.copy(Pn[:], bk[:, 0:128]), r=[bk], w=[Pn])
            if not last:
                yield self.dve(lambda: nc.vector.tensor_copy(Qn[:], bk[:, 128:256]), r=[bk], w=[Qn])

            def upd():
                ins = nc.tensor.matmul(bk[:, 256:384], Y[:], Pn[:], start=True, stop=True)
                if not last:
                    ins = nc.tensor.matmul(bk[:, 384:512], X[:], Qn[:], start=True, stop=True)
                return ins
            yield self.pe(upd, r=[X, Y, Pn] + ([] if last else [Qn]), w=[bk])
            yield self.dve(lambda: nc.vector.tensor_tensor(out=Xn[:], in0=X[:], in1=bk[:, 256:384], op=ALU.add), r=[X, bk], w=[Xn])
            if not last:
                yield self.dve(lambda: nc.vector.tensor_tensor(out=Yn[:], in0=Y[:], in1=bk[:, 384:512], op=ALU.add), r=[Y, bk], w=[Yn])
            Pc, Qc, X, Y = Pn, Qn, Xn, Yn
        yield self.pe(lambda: nc.tensor.matmul(PT_[:, 0:128], X[:], tw["vb"][:], start=True, stop=True), r=[X, tw["vb"]], w=[PT_])
        yield self.pe(lambda: nc.tensor.matmul(PT_[:, 128:256], tw["kbg"][:], X[:], start=True, stop=True), r=[X, tw["kbg"]], w=[PT_])
        yield self.act(lambda: nc.scalar.copy(tw["u"][:], PT_[:, 0:128]), r=[PT_], w=[tw["u"]])
        yield self.dve(lambda: nc.vector.tensor_copy(tw["wTa"][:, 0:64], PT_[:, 128:192]), r=[PT_], w=[tw["wTa"]])
        yield self.dve(lambda: nc.vector.tensor_copy(tw["wTz"][:, 64:128], PT_[:, 192:256]), r=[PT_], w=[tw["wTz"]])
        for ck in range(2):
            rr = slice(ck * 64, (ck + 1) * 64)
            if ck == 0:
                yield self.pe(lambda: nc.tensor.matmul(PT_[0:64, 256:384], tw["wTa"][:, 0:64], Sbf[:, h, :], start=True, stop=True),
                              r=[tw["wTa"], Sbf], w=[PT_])
            else:
                yield self.pe(lambda: nc.tensor.matmul(PT_[:, 256:384], tw["wTz"][:], Sbf[:, h, :], start=True, stop=True),
                              r=[tw["wTz"], Sbf], w=[PT_])
            yield self.dve(lambda: nc.vector.tensor_tensor(out=tw["vnew"][rr, :], in0=tw["u"][rr, :], in1=PT_[rr, 256:384],
                                                           op=ALU.subtract), r=[tw["u"], PT_], w=[tw["vnew"]])

            def omm():
                nc.tensor.matmul(PA[:, ck * 64:(ck + 1) * 64], Sbf[:, h, :], tw["qgT"][:, rr], start=True, stop=False)
                return nc.tensor.matmul(PA[:, ck * 64:(ck + 1) * 64], tw["vnew"][rr, :], tw["AqkT"][rr, rr],
                                        start=False, stop=True)
            yield self.pe(omm, r=[Sbf, tw["qgT"], tw["vnew"], tw["AqkT"]], w=[PA])
            yield self.act(lambda: nc.scalar.copy(oT[:, h, cs + ck * 64:cs + (ck + 1) * 64], PA[:, ck * 64:(ck + 1) * 64]),
                           r=[PA], w=[oT])
            yield self.pe(lambda: nc.tensor.matmul(PT_[:, 384:512], tw["kd"][rr, :], tw["vnew"][rr, :], start=True, stop=True),
                          r=[tw["kd"], tw["vnew"]], w=[PT_])
            yield self.dve(lambda: nc.vector.scalar_tensor_tensor(out=Sst[:, h, :], in0=Sst[:, h, :],
                                                                  scalar=sc[:, 20 + 2 * h + ck:21 + 2 * h + ck],
                                                                  in1=PT_[:, 384:512], op0=ALU.mult, op1=ALU.add),
                           r=[Sst, sc, PT_], w=[Sst])
            yield self.act(lambda: nc.scalar.copy(Sbf[:, h, :], Sst[:, h, :]), r=[Sst], w=[Sbf])

    for pair in range(2):
        gens = [head(2 * pair), head(2 * pair + 1)]
        while gens:
            for g in list(gens):
                try:
                    next(g)
                except StopIteration:
                    gens.remove(g)


def even_mixer(self, l):
    nc = self.nc
    e = l // 2
    nsub = self.seq // 128
    self.begin()
    G = even_alloc(self, 512)
    T = self.T
    G["pre"] = [T("pre%d" % i, [128, 515], F32) for i in range(2)]
    G["kTa"] = T("kTa", [128, self.seq], BF16)
    G["Va"] = T("Va", [128, nsub, 256], BF16)
    G["carry"] = T("carry", [128, 12, 3], F32)
    U = T("U", [128, 2432], F32)
    G["bias"] = Tile(nc, "bias", [128, 8, 256], F32, handle=U.t[:, 0:2048].rearrange("p (a b) -> p a b", a=8))
    G["Sst"] = T("Sst", [128, 4, 128], F32)
    G["Sbf"] = T("Sbf", [128, 4, 128], BF16)
    G["beta_t"] = T("beta_t", [128, 4, 4], F32)
    G["g_t"] = T("g_t", [128, 4, 4], F32)
    gtmp = T("gtmp", [128, 4], F32)
    kvo = T("kvo", [128, 256], F32)
    gco = T("gco", [128, 1536], F32)
    W = {"sb": [T("sb%d" % i, [128, 256], F32) for i in range(2)],
         "pn": [T("pn%d" % i, [128, 256], BF16) for i in range(2)],
         "PT": [T("PT%d" % i, [128, 256], BF16) for i in range(2)],
         "col": [T("col%d" % i, [128, 8], F32) for i in range(2)],
         "sc": T("sc", [128, 40], F32),
         "R": T("R", [128, 4, 256], F32),
         "set": []}
    d = {}
    for nm in ("kbg", "vb", "tA", "Es", "tB", "ETd", "ETs", "u"):
        d[nm] = T("%s0" % nm, [128, 128], F32)
    for nm in ("kd", "AqkT", "qgT", "wTa", "wTz", "vnew"):
        d[nm] = T("%s0" % nm, [128, 128], BF16)
    for nm in ("Pm", "Qm", "X", "Y"):
        d[nm] = [T("%s0_%d" % (nm, j), [128, 128], F32) for j in range(2)]
    W["set"].append(d)
    self.dve(lambda: nc.vector.memset(d["wTz"][:], 0.0), w=[d["wTz"]])
    d2 = {}
    off = [0]

    def uview(name, dt):
        if dt == F32:
            v = U.t[:, off[0]:off[0] + 128]
            off[0] += 128
        else:
            v = U.t[:, off[0]:off[0] + 64].bitcast(BF16)
            off[0] += 64
        return Tile(nc, name, [128, 128], dt, handle=v)
    for nm in ("kbg", "vb", "tA", "Es", "tB", "ETd", "ETs", "u"):
        d2[nm] = uview(nm + "1", F32)
    for nm in ("Pm", "Qm", "X", "Y"):
        d2[nm] = [uview("%s1_%d" % (nm, j), F32) for j in range(2)]
    for nm in ("kd", "AqkT", "qgT", "wTa", "wTz", "vnew"):
        d2[nm] = uview(nm + "1", BF16)
    assert off[0] == 2432
    W["set"].append(d2)
    even_common(self, G, e)
    self.dve(lambda: nc.vector.memset(G["carry"][:], 0.0), w=[G["carry"]])
    self.dve(lambda: nc.vector.memset(G["Sst"][:], 0.0), w=[G["Sst"]])
    self.dve(lambda: nc.vector.memset(G["Sbf"][:], 0.0), w=[G["Sbf"]])
    cw = G["cw"]
    for ti in range(self.nt):
        c0 = ti * 512
        xb = [self.xTb[ti]]
        self.S.barrier()
        self.dma(G["bias"][:].rearrange("p a b -> p (a b)"), self.swabias, r=[self.Bin], w=[G["bias"]])
        self.rmsnorm_cols(xb, c0, 512, 4 + l, G["hn"], 0, G["sqm"], G["rstd"])
        cnt = [0]

        def conv_fn(ch, ps):
            pre = G["pre"][cnt[0] % 2]
            acc = G["acc"][cnt[0] % 2]
            cnt[0] += 1
            self.dve(lambda: nc.vector.tensor_copy(pre[:, 0:3], G["carry"][:, ch, :]), r=[G["carry"]], w=[pre])
            self.act(lambda: nc.scalar.copy(pre[:, 3:515], ps[:, 0:512]), r=[ps], w=[pre])
            self.dve(lambda: nc.vector.tensor_copy(G["carry"][:, ch, :], pre[:, 512:515]), r=[pre], w=[G["carry"]])
            self.dve(lambda: nc.vector.tensor_scalar_mul(acc[:], pre[:, 0:512], cw[:, ch, 0:1]), r=[pre, cw], w=[acc])
            for i in range(1, 4):
                self.dve(lambda: nc.vector.scalar_tensor_tensor(out=acc[:], in0=pre[:, i:i + 512], scalar=cw[:, ch, i:i + 1],
                                                                 in1=acc[:], op0=ALU.mult, op1=ALU.add), r=[pre, cw, acc], w=[acc])
            self.act(lambda: nc.scalar.activation(G["cq"][:, ch, :], acc[:], AF.Silu), r=[acc], w=[G["cq"]])
        even_inproj_fm(self, G, 512, conv_fn, G["kTa"][:, c0:c0 + 512])
        w = self.ws.get(EW_N[6])
        wv = w[:, 0:EW_N[6]].rearrange("p (k c) -> p k c", k=KC)
        for st in range(4):
            gs = ti * 4 + st
            ps = self.P[2 + st % 2]

            def mm():
                ins = None
                for kc in range(KC):
                    ins = nc.tensor.matmul(ps[:, 0:EV_TOK], G["hn"][:, kc, st * 128:(st + 1) * 128], wv[:, kc, :],
                                           start=(kc == 0), stop=(kc == KC - 1))
                return ins
            self.pe(mm, r=[w, G["hn"]], w=[ps])
            self.act(lambda: nc.scalar.copy(G["Va"][:, gs, :], ps[:, 0:256]), r=[ps], w=[G["Va"]])
            gates_from_ba(self, G, ps[:, 384:392], 128, G["beta_t"][:, st, :], G["g_t"][:, st, :], gtmp, [ps],
                          [G["beta_t"], G["g_t"]])
            if gs == nsub - 1 and not self.cfg.get("skip_kvo"):
                self.act(lambda: nc.scalar.copy(kvo[:, 0:128], ps[:, 256:384]), r=[ps], w=[kvo])
                self.act(lambda: nc.scalar.copy(kvo[:, 128:256], ps[:, 0:128]), r=[ps], w=[kvo])
                self.dve(lambda: nc.vector.tensor_tensor(out=kvo[:, 128:256], in0=kvo[:, 128:256], in1=ps[:, 128:256], op=ALU.add),
                         r=[ps, kvo], w=[kvo])
                pass
        if self.cfg.get("skip_swa") or self.cfg.get("skip_gdn") or self.cfg.get("gdn_stop", 6) < 6:
            self.dve(lambda: nc.vector.memset(G["sqm"][:], 0.0), w=[G["sqm"]])
            self.dve(lambda: nc.vector.memset(G["oT"][:], 0.0), w=[G["oT"]])
        if not self.cfg.get("skip_swa"):
            for qi in range(4):
                swa_block(self, G, W, ti * 4 + qi, qi * 128)
        self.S.barrier()
        self.dve(lambda: nc.vector.memset(W["set"][1]["wTz"][:], 0.0), w=[W["set"][1]["wTz"]])
        l2norm_heads(self, G, 512)
        if not self.cfg.get("skip_gdn"):
            for st in range(4):
                gdn_subtile(self, G, W, st, st * 128)
        gdn_post(self, G, 512, e)
        even_outproj(self, G, xb, c0, 512)
    self.out_dma(self.o_pk[e], kvo[:, 0:128], r=[kvo])
    self.out_dma(self.o_pv[e], kvo[:, 128:256], r=[kvo])
    for ch in range(12):
        self.out_dma(self.o_pgc[e][:, ch * 128:(ch + 1) * 128].rearrange("i p -> p i"), G["carry"][:, ch, :], r=[G["carry"]])
    self.out_dma(self.o_pgs[e].rearrange("h k v -> k h v"), G["Sst"][:], r=[G["Sst"]])
    self.end()
    if self.cfg.get("skip_dec"):
        for b in range(7):
            self.ws.get(EW_N[b])
        for b in range(2):
            self.ws.get(4096)
    else:
        even_decode(self, l)


def even_decode(self, l):
    nc = self.nc
    e = l // 2
    n = NS
    c0 = self.seq
    P = self.P
    ident, mk = self.ident, self.mk
    self.begin()
    T = self.T
    G = even_alloc(self, n)
    G["kTa"] = T("kTs", [128, n], BF16)
    xx = T("xx", [128, 12, n, 4], F32)
    hs = T("hs", [48, 1536], F32)
    gnew = T("gnew", [n, 1536], F32)
    knv = T("knv", [n, 256], F32)
    gts = T("gts", [n, 16], F32)
    Kc = T("Kc", [128, n, 128], F32)
    Vc = T("Vc", [128, n, 128], F32)
    Qb = T("Qb", [128, n, 8], F32)
    KTb = [T("KTb%d" % i, [128, 128], F32) for i in range(2)]
    sT = T("sT", [128, 128], F32)
    kTn = T("kTn", [128, n], F32)
    vTn = T("vTn", [128, n], F32)
    sf = T("sf", [128, 132], F32)
    pnf = T("pnf", [128, 132], F32)
    col = T("dcol", [128, 8], F32)
    PTs = T("PTs", [128, 128], F32)
    R2 = T("R2", [128, 128], F32)
    ov = T("ov", [128, n, 8], F32)
    dbias = T("dbias", [128, 129], F32)
    skc = T("skc", [128, 1], F32)
    R3 = T("R3", [n, 2, 4, n], F32)
    bcs = T("bcs", [128, 2, 4, n], F32)
    Sd = T("Sd", [128, n, 4, 128], F32)
    vnT = T("vnT", [128, 4, n], F32)
    t1 = T("t1", [128, 4, n], F32)
    krow = T("krow", [n, 512], F32)
    vrow = T("vrow", [n, 512], F32)
    Km = [T("Km%d" % i, [n, 512], F32) for i in range(2)]
    tS = T("tS", [128, 4, 128], F32)
    mix = G["sqm"]
    even_common(self, G, e)
    self.dma(dbias[:], self.decbias, r=[self.Bin], w=[dbias])
    self.dma(skc[:], self.esk[e], r=[self.Bin], w=[skc])
    self.dma(hs[:], self.s_gc[e].rearrange("b i c -> (b i) c"), r=[self.Bin], w=[hs])
    self.dma(Kc[:], self.c_k[e].rearrange("b w f -> w b f"), r=[self.Bin], w=[Kc])
    self.dma(Vc[:], self.c_v[e].rearrange("b w f -> w b f"), r=[self.Bin], w=[Vc])
    self.dma(Sd[:], self.s_gs[e].rearrange("b h k v -> k b h v"), r=[self.Bin], w=[Sd])
    for ch in range(12):
        ps = P[2 + ch % 2]
        self.pe(lambda: nc.tensor.transpose(ps[:, 0:48], hs[:, ch * 128:(ch + 1) * 128], ident[0:48, 0:48]),
                r=[hs, ident], w=[ps])
        self.dve(lambda: nc.vector.tensor_copy(xx[:, ch, :, 0:3], ps[:, 0:48].rearrange("p (b i) -> p b i", i=3)),
                 r=[ps], w=[xx])
    xb = [self.xTb[self.nt]]
    self.rmsnorm_cols(xb, c0, n, 4 + l, G["hn"], 0, G["sqm"], G["rstd"])
    cw = G["cw"]
    cnt = [0]

    def conv_fn(ch, ps):
        acc = G["acc"][cnt[0] % 2]
        cnt[0] += 1
        self.act(lambda: nc.scalar.copy(xx[:, ch, :, 3], ps[:, 0:n]), r=[ps], w=[xx])
        self.dve(lambda: nc.vector.tensor_scalar_mul(acc[:, 0:n], xx[:, ch, :, 0], cw[:, ch, 0:1]), r=[xx, cw], w=[acc])
        for i in range(1, 4):
            self.dve(lambda: nc.vector.scalar_tensor_tensor(out=acc[:, 0:n], in0=xx[:, ch, :, i], scalar=cw[:, ch, i:i + 1],
                                                            in1=acc[:, 0:n], op0=ALU.mult, op1=ALU.add), r=[xx, cw, acc], w=[acc])
        self.act(lambda: nc.scalar.activation(G["cq"][:, ch, 0:n], acc[:, 0:n], AF.Silu), r=[acc], w=[G["cq"]])
        pt = P[6]
        self.pe(lambda: nc.tensor.transpose(pt[0:n, (ch % 4) * 128:(ch % 4 + 1) * 128], xx[:, ch, :, 3], ident[:]),
                r=[xx, ident], w=[pt])
        self.dve(lambda: nc.vector.tensor_copy(gnew[:, ch * 128:(ch + 1) * 128], pt[0:n, (ch % 4) * 128:(ch % 4 + 1) * 128]),
                 r=[pt], w=[gnew])
    even_inproj_fm(self, G, n, conv_fn, G["kTa"][:, 0:n])
    w = self.ws.get(EW_N[6])
    wv = w[:, 0:EW_N[6]].rearrange("p (k c) -> p k c", k=KC)
    ps = P[2]

    def mm():
        ins = None
        for kc in range(KC):
            ins = nc.tensor.matmul(ps[0:n, 0:EV_TOK], G["hn"][:, kc, 0:n], wv[:, kc, :], start=(kc == 0), stop=(kc == KC - 1))
        return ins
    self.pe(mm, r=[w, G["hn"]], w=[ps])
    self.act(lambda: nc.scalar.copy(knv[:, 0:128], ps[0:n, 256:384]), r=[ps], w=[knv])
    self.act(lambda: nc.scalar.copy(knv[:, 128:256], ps[0:n, 0:128]), r=[ps], w=[knv])
    self.dve(lambda: nc.vector.tensor_tensor(out=knv[:, 128:256], in0=knv[:, 128:256], in1=ps[0:n, 128:256], op=ALU.add),
             r=[ps, knv], w=[knv])
    gates_from_ba(self, G, ps[0:n, 384:392], n, gts[:, 0:4], gts[:, 4:8], gts_tmp(gts), [ps], [gts])
    self.act(lambda: nc.scalar.activation(gts[:, 8:12], gts[:, 4:8], AF.Exp), r=[gts], w=[gts])
    self.dve(lambda: nc.vector.memset(Qb[:], 0.0), w=[Qb])
    for c in range(4):
        self.dve(lambda: nc.vector.tensor_copy(Qb[0:64, :, c], G["qT"][0:64, c, 0:n]), r=[G["qT"]], w=[Qb])
        self.dve(lambda: nc.vector.tensor_copy(Qb[64:128, :, 4 + c], G["qT"][64:128, c, 0:n]), r=[G["qT"]], w=[Qb])
    self.dve(lambda: nc.vector.tensor_copy(kTn[:], G["kTa"][:, 0:n]), r=[G["kTa"]], w=[kTn])
    for b in range(n):
        kp = P[b % 2]
        kt = KTb[b % 2]
        self.pe(lambda: nc.tensor.transpose(kp[:, 0:128], Kc[:, b, :], ident[:]), r=[Kc, ident], w=[kp])
        self.act(lambda: nc.scalar.copy(kt[:], kp[:, 0:128]), r=[kp], w=[kt])
        self.pe(lambda: nc.tensor.matmul(P[4][:, b * 8:(b + 1) * 8], kt[:], Qb[:, b, :], start=True, stop=True),
                r=[kt, Qb], w=[P[4]])
    self.dve(lambda: nc.vector.tensor_copy(sT[:], P[4][:, 0:128]), r=[P[4]], w=[sT])
    self.pe(lambda: nc.tensor.transpose(P[5][:, 0:128], sT[:], ident[:]), r=[sT, ident], w=[P[5]])
    self.pe(lambda: nc.tensor.matmul(P[5][:, 128:128 + n], Qb[:].rearrange("p b h -> p (b h)"), kTn[:], start=True, stop=True),
            r=[Qb, kTn], w=[P[5]])
    self.dve(lambda: nc.vector.tensor_tensor(out=sf[:, 0:n], in0=P[5][:, 128:128 + n], in1=mk[:, 5, 0:n], op=ALU.mult),
             r=[P[5], mk], w=[sf])
    self.dve(lambda: nc.vector.reduce_sum(out=col[:, 5:6], in_=sf[:, 0:n], axis=AX.X), r=[sf], w=[col])
    self.dve(lambda: nc.vector.scalar_tensor_tensor(out=sf[:, 0:128], in0=P[5][:, 0:128], scalar=0.125, in1=dbias[:, 0:128],
                                                    op0=ALU.mult, op1=ALU.add), r=[P[5], dbias], w=[sf])
    self.dve(lambda: nc.vector.scalar_tensor_tensor(out=sf[:, 128:129], in0=col[:, 5:6], scalar=0.125, in1=dbias[:, 128:129],
                                                    op0=ALU.mult, op1=ALU.add), r=[col, dbias], w=[sf])
    self.dve(lambda: nc.vector.reduce_max(out=col[:, 0:1], in_=sf[:, 0:129], axis=AX.X), r=[sf], w=[col])
    self.dve(lambda: nc.vector.tensor_tensor(out=col[:, 0:1], in0=col[:, 0:1], in1=skc[:, 0:1], op=ALU.max), r=[col, skc], w=[col])
    self.dve(lambda: nc.vector.tensor_scalar_mul(col[:, 1:2], col[:, 0:1], -1.0), r=[col], w=[col])
    self.act(lambda: nc.scalar.activation(sf[:, 0:129], sf[:, 0:129], AF.Exp, bias=col[:, 1:2], scale=1.0), r=[sf, col], w=[sf])
    self.dve(lambda: nc.vector.reduce_sum(out=col[:, 2:3], in_=sf[:, 0:129], axis=AX.X), r=[sf], w=[col])
    self.act(lambda: nc.scalar.activation(col[:, 3:4], skc[:, 0:1], AF.Exp, bias=col[:, 1:2], scale=1.0), r=[skc, col], w=[col])
    self.dve(lambda: nc.vector.tensor_tensor(out=col[:, 2:3], in0=col[:, 2:3], in1=col[:, 3:4], op=ALU.add), r=[col], w=[col])
    self.dve(lambda: nc.vector.reciprocal(col[:, 4:5], col[:, 2:3]), r=[col], w=[col])
    self.dve(lambda: nc.vector.tensor_scalar_mul(pnf[:, 0:129], sf[:, 0:129], col[:, 4:5]), r=[sf, col], w=[pnf])
    self.pe(lambda: nc.tensor.transpose(P[4][:, 128:256], pnf[:, 0:128], ident[:]), r=[pnf, ident], w=[P[4]])
    self.act(lambda: nc.scalar.copy(PTs[:], P[4][:, 128:256]), r=[P[4]], w=[PTs])
    for b in range(n):
        self.pe(lambda: nc.tensor.matmul(P[6][:, 256 + b * 8:256 + (b + 1) * 8], Vc[:, b, :], PTs[:, b * 8:(b + 1) * 8],
                                         start=True, stop=True), r=[Vc, PTs], w=[P[6]])
    self.pe(lambda: nc.tensor.transpose(P[7][:, 0:n], knv[:, 128:256], ident[0:n, 0:n]), r=[knv, ident], w=[P[7]])
    self.dve(lambda: nc.vector.tensor_copy(vTn[:], P[7][:, 0:n]), r=[P[7]], w=[vTn])
    self.dve(lambda: nc.vector.tensor_scalar_mul(R2[:], ident[:], pnf[:, 128:129]), r=[ident, pnf], w=[R2])
    self.pe(lambda: nc.tensor.matmul(P[7][:, 128:256], self.ones_f[:], R2[:], start=True, stop=True), r=[self.ones_f, R2], w=[P[7]])
    self.dve(lambda: nc.vector.tensor_tensor(out=ov[:], in0=P[7][:, 128:256].rearrange("p (b h) -> p b h", h=8),
                                             in1=vTn[:].unsqueeze(2).to_broadcast([128, n, 8]), op=ALU.mult),
             r=[P[7], vTn], w=[ov])
    self.dve(lambda: nc.vector.tensor_tensor(out=ov[:], in0=ov[:], in1=P[6][:, 256:384].rearrange("p (b h) -> p b h", h=8), op=ALU.add),
             r=[ov, P[6]], w=[ov])
    for c in range(4):
        self.dve(lambda: nc.vector.tensor_copy(mix[0:64, c, 0:n], ov[0:64, :, c]), r=[ov], w=[mix])
        self.dve(lambda: nc.vector.tensor_copy(mix[64:128, c, 0:n], ov[64:128, :, 4 + c]), r=[ov], w=[mix])
    l2norm_heads(self, G, n)
    cq = G["cq"]
    for t in range(2):
        src = gts[:, 0:4] if t == 0 else gts[:, 8:12]
        self.dve(lambda: nc.vector.tensor_tensor(out=R3[:, t, :, :], in0=src.unsqueeze(2).to_broadcast([n, 4, n]),
                                                 in1=ident[0:n, 0:n].unsqueeze(1).to_broadcast([n, 4, n]), op=ALU.mult),
                 r=[gts, ident], w=[R3])
    self.pe(lambda: nc.tensor.matmul(P[0][:, 0:128], self.ones_f[0:n, :], R3[:].rearrange("p t h b -> p (t h b)"),
                                     start=True, stop=True), r=[self.ones_f, R3], w=[P[0]])
    self.dve(lambda: nc.vector.tensor_copy(bcs[:].rearrange("p t h b -> p (t h b)"), P[0][:, 0:128]), r=[P[0]], w=[bcs])
    for b in range(n):
        for h in range(4):
            self.pe(lambda: nc.tensor.matmul(P[1][:, h * n + b:h * n + b + 1], Sd[:, b, h, :], cq[:, 4 + h, b:b + 1],
                                             start=True, stop=True), r=[Sd, cq], w=[P[1]])
    kSv = P[1][:, 0:4 * n].rearrange("p (h b) -> p h b", h=4)
    self.dve(lambda: nc.vector.tensor_tensor(out=t1[:], in0=kSv, in1=bcs[:, 1, :, :], op=ALU.mult), r=[P[1], bcs], w=[t1])
    self.dve(lambda: nc.vector.tensor_tensor(out=t1[:], in0=cq[:, 8:12, 0:n], in1=t1[:], op=ALU.subtract), r=[cq, t1], w=[t1])
    self.dve(lambda: nc.vector.tensor_tensor(out=vnT[:], in0=t1[:], in1=bcs[:, 0, :, :], op=ALU.mult), r=[t1, bcs], w=[vnT])
    for h in range(4):
        self.pe(lambda: nc.tensor.transpose(P[2][0:n, h * 128:(h + 1) * 128], cq[:, 4 + h, 0:n], ident[:]), r=[cq, ident], w=[P[2]])
        self.pe(lambda: nc.tensor.transpose(P[3][0:n, h * 128:(h + 1) * 128], vnT[:, h, :], ident[:]), r=[vnT, ident], w=[P[3]])
    self.act(lambda: nc.scalar.copy(krow[:], P[2][0:n, :]), r=[P[2]], w=[krow])
    self.dve(lambda: nc.vector.tensor_copy(vrow[:], P[3][0:n, :]), r=[P[3]], w=[vrow])
    for b in range(n):
        km = Km[b % 2]
        pp = P[4 + b % 2]
        self.dve(lambda: nc.vector.tensor_scalar_mul(km[:], krow[:], ident[0:n, b:b + 1]), r=[krow, ident], w=[km])

        def mm4():
            ins = None
            for h in range(4):
                ins = nc.tensor.matmul(pp[:, h * 128:(h + 1) * 128], km[:, h * 128:(h + 1) * 128], vrow[:, h * 128:(h + 1) * 128],
                                       start=True, stop=True)
            return ins
        self.pe(mm4, r=[km, vrow], w=[pp])
        self.dve(lambda: nc.vector.tensor_tensor(out=tS[:], in0=Sd[:, b, :, :],
                                                 in1=bcs[:, 1, :, b:b + 1].to_broadcast([128, 4, 128]), op=ALU.mult),
                 r=[Sd, bcs], w=[tS])
        self.dve(lambda: nc.vector.tensor_tensor(out=Sd[:, b, :, :], in0=tS[:], in1=pp[:, :].rearrange("p (h v) -> p h v", h=4), op=ALU.add),
                 r=[tS, pp], w=[Sd])
    for b in range(n):
        for h in range(4):
            self.pe(lambda: nc.tensor.matmul(P[1][:, 64 + h * n + b:64 + h * n + b + 1], Sd[:, b, h, :], cq[:, h, b:b + 1],
                                             start=True, stop=True), r=[Sd, cq], w=[P[1]])
    self.act(lambda: nc.scalar.copy(G["oT"][:, :, 0:n], P[1][:, 64:64 + 4 * n].rearrange("p (h b) -> p h b", h=4)), r=[P[1]], w=[G["oT"]])
    gdn_post(self, G, n, e)
    even_outproj(self, G, xb, c0, n)
    self.out_dma(self.o_sk[e][:, 0:127, :], self.c_k[e][:, 1:128, :], r=[self.Bin])
    self.out_dma(self.o_sv[e][:, 0:127, :], self.c_v[e][:, 1:128, :], r=[self.Bin])
    self.out_dma(self.o_sk[e][:, 127, :], knv[:, 0:128], r=[knv])
    self.out_dma(self.o_sv[e][:, 127, :], knv[:, 128:256], r=[knv])
    self.out_dma(self.o_sgc[e][:, 0:2, :], self.s_gc[e][:, 1:3, :], r=[self.Bin])
    self.out_dma(self.o_sgc[e][:, 2, :], gnew[:], r=[gnew])
    self.out_dma(self.o_sgs[e].rearrange("b h k v -> k b h v"), Sd[:], r=[Sd])
    self.end()


def gts_tmp(gts):
    class _V:
        b = gts.b

        def __getitem__(self, k):
            rows, cols = k
            return gts.t[rows, 12 + cols.start:12 + cols.stop]
    return _V()


Prog.even_mixer = even_mixer


OW_DT = 256
H_C, P_C, N_C, G_C = 32, 64, 128, 4


def odd_consts(self, O, e):
    nc = self.nc
    self.dma(O["cw"][:].rearrange("p a b -> p (a b)"), self.oconv[e], r=[self.Bin], w=[O["cw"]])
    self.dma(O["hb"][:], self.ohead[e], r=[self.Bin], w=[O["hb"]])
    self.dma(O["fc"][:], self.ofeat[e], r=[self.Bin], w=[O["fc"]])
    self.act(lambda: nc.scalar.activation(O["hb"][:, 32:64], O["hb"][:, 32:64], AF.Exp), r=[O["hb"]], w=[O["hb"]])
    self.dve(lambda: nc.vector.tensor_scalar_mul(O["hb"][:, 32:64], O["hb"][:, 32:64], -1.0), r=[O["hb"]], w=[O["hb"]])


def odd_alloc(self, n):
    T = self.T
    O = {}
    O["hn"] = T("hn", [128, KC, n], BF16)
    O["sqm"] = T("sqm", [128, KC, n], BF16)
    O["rstd"] = T("rstd", [128, n], F32)
    O["cw"] = T("ocw", [128, 24, 4], F32)
    O["hb"] = T("ohb", [128, 96], F32)
    O["fc"] = T("ofc", [128, 56], F32)
    O["acc"] = [T("oacc%d" % i, [128, n], F32) for i in range(2)]
    O["yT"] = T("yT", [128, 16, n], BF16)
    O["BT"] = T("BT", [128, 4, n], BF16)
    O["CT"] = T("CT", [128, 4, n], BF16)
    O["xc"] = [T("xc%d" % i, [128, n], BF16) for i in range(2)]
    return O


def odd_dt(self, O, ps_ap, rows, dt_dst, a_dst, rd, wr):
    nc = self.nc
    hb = O["hb"]
    self.dve(lambda: nc.vector.tensor_tensor(out=dt_dst, in0=ps_ap, in1=hb[0:rows, 0:32], op=ALU.add), r=rd + [hb], w=wr)
    self.act(lambda: nc.scalar.activation(dt_dst, dt_dst, AF.Exp), r=wr, w=wr)
    self.act(lambda: nc.scalar.activation(dt_dst, dt_dst, AF.Ln, bias=1.0, scale=1.0), r=wr, w=wr)
    self.dve(lambda: nc.vector.tensor_tensor(out=a_dst, in0=dt_dst, in1=hb[0:rows, 32:64], op=ALU.mult), r=wr + [hb], w=wr)


def odd_gate_norm_out(self, O, xbufs, c0, n, zfn):
    nc = self.nc
    yT, fc = O["yT"], O["fc"]
    sq = O["sqm"]
    for j in range(16):
        def consume(ps, j=j):
            zs = O["acc"][j % 2]
            self.act(lambda: nc.scalar.activation(zs[:, 0:n], ps[:, 0:n], AF.Silu), r=[ps], w=[zs])
            self.dve(lambda: nc.vector.tensor_tensor(out=yT[:, j, 0:n], in0=yT[:, j, 0:n], in1=zs[:, 0:n], op=ALU.mult),
                     r=[yT, zs], w=[yT])
        zfn(j, consume)
    for g in range(4):
        ps = self.P[6 + g % 2]
        for jj in range(4):
            j = g * 4 + jj
            self.act(lambda: nc.scalar.activation(sq[:, jj, 0:n], yT[:, j, 0:n], AF.Square), r=[yT], w=[sq])

        def mm():
            ins = None
            for jj in range(4):
                ins = nc.tensor.matmul(ps[:, 0:n], self.ones_bf[:], sq[:, jj, 0:n], start=(jj == 0), stop=(jj == 3))
            return ins
        self.pe(mm, r=[sq, self.ones_bf], w=[ps])
        rs = O["rstd"]
        self.act(lambda: nc.scalar.activation(rs[:, 0:n], ps[:, 0:n], AF.Ln, bias=EPS, scale=1.0 / 512.0), r=[ps], w=[rs])
        self.act(lambda: nc.scalar.activation(rs[:, 0:n], rs[:, 0:n], AF.Exp, scale=-0.5), r=[rs], w=[rs])
        for jj in range(4):
            j = g * 4 + jj
            self.dve(lambda: nc.vector.scalar_tensor_tensor(out=yT[:, j, 0:n], in0=yT[:, j, 0:n], scalar=fc[:, 24 + j:25 + j],
                                                            in1=rs[:, 0:n], op0=ALU.mult, op1=ALU.mult), r=[yT, fc, rs], w=[yT])
    for blk in range(4):
        w = self.ws.get(4096)
        wv = w[:, 0:4096].rearrange("p (k c) -> p k c", k=16)
        for dc in range(2):
            dch = blk * 2 + dc
            ps = self.P[dch % 2]

            def mm2():
                ins = None
                for k in range(16):
                    ins = nc.tensor.matmul(ps[:, 0:n], wv[:, k, dc * 128:(dc + 1) * 128], yT[:, k, 0:n],
                                           start=(k == 0), stop=(k == 15))
                return ins
            self.pe(mm2, r=[w, yT], w=[ps])
            xs = self.xT[:, dch, c0:c0 + n]
            self.dve(lambda: nc.vector.tensor_tensor(out=xs, in0=xs, in1=ps[:, 0:n], op=ALU.add), r=[ps] + xbufs, w=xbufs)


def odd_fm_chunk(self, O, w, cc, n, ps):
    nc = self.nc
    hn = O["hn"]
    wv = w[:, 0:4096].rearrange("p (k c) -> p k c", k=KC)

    def mm():
        ins = None
        for kc in range(KC):
            ins = nc.tensor.matmul(ps[:, 0:n], wv[:, kc, cc * 128:(cc + 1) * 128], hn[:, kc, 0:n],
                                   start=(kc == 0), stop=(kc == KC - 1))
        return ins
    self.pe(mm, r=[w, hn], w=[ps])


def odd_mixer(self, l):
    nc = self.nc
    e = l // 2
    P = self.P
    ident, mk = self.ident, self.mk
    self.begin()
    T = self.T
    O = odd_alloc(self, 512)
    pre = [T("opre%d" % i, [128, 515], F32) for i in range(2)]
    carry = T("ocarry", [128, 24, 3], F32)
    xst = T("xst", [128, 4, 2048], BF16)
    Btok = T("Btok", [128, 4, 512], BF16)
    dtv = T("dtv", [128, 4, 32], F32)
    av = T("av", [128, 4, 32], F32)
    sm = T("osm", [128, 6, 32], F32)
    R4 = [T("R4_%d" % i, [128, 4, 128], F32) for i in range(2)]
    Lt = [T("Lt%d" % i, [128, 4, 128], F32) for i in range(2)]
    Wb = [T("Wb%d" % i, [128, 4, 128], BF16) for i in range(2)]
    xdt = T("xdt", [128, 2048], BF16)
    xdd = T("xdd", [128, 2048], BF16)
    ytk = T("ytk", [128, 2048], BF16)
    tt = [T("ott%d" % i, [128, 512], F32) for i in range(2)]
    ST = T("ST", [128, 2048], F32)
    STb = T("STb", [128, 2048], BF16)
    sto = Tile(self.nc, "sto", [128, 32, 128], F32, handle=xst.t.bitcast(F32).reshape([128, 32, 128]))
    sto.b = xst.b
    odd_consts(self, O, e)
    self.dve(lambda: nc.vector.memset(carry[:], 0.0), w=[carry])
    self.dve(lambda: nc.vector.memset(ST[:], 0.0), w=[ST])
    self.dve(lambda: nc.vector.memset(STb[:], 0.0), w=[STb])
    cw, fc, hb = O["cw"], O["fc"], O["hb"]
    for ti in range(self.nt):
        c0 = ti * 512
        xb = [self.xTb[ti]]
        self.rmsnorm_cols(xb, c0, 512, 4 + l, O["hn"], 0, O["sqm"], O["rstd"])
        w = self.ws.get(OW_DT)
        wv = w[:, 0:OW_DT].rearrange("p (k c) -> p k c", k=KC)
        for st in range(4):
            ps = P[2 + st % 2]

            def mm():
                ins = None
                for kc in range(KC):
                    ins = nc.tensor.matmul(ps[:, 0:32], O["hn"][:, kc, st * 128:(st + 1) * 128], wv[:, kc, :],
                                           start=(kc == 0), stop=(kc == KC - 1))
                return ins
            self.pe(mm, r=[w, O["hn"]], w=[ps])
            odd_dt(self, O, ps[:, 0:32], 128, dtv[:, st, :], av[:, st, :], [ps], [dtv, av])
        for blk in range(6):
            w = self.ws.get(4096)
            for cc in range(4):
                ch = blk * 4 + cc
                ps = P[ch % 2]
                odd_fm_chunk(self, O, w, cc, 512, ps)
                pr = pre[ch % 2]
                acc = O["acc"][ch % 2]
                self.dve(lambda: nc.vector.tensor_copy(pr[:, 0:3], carry[:, ch, :]), r=[carry], w=[pr])
                self.act(lambda: nc.scalar.copy(pr[:, 3:515], ps[:, 0:512]), r=[ps], w=[pr])
                self.dve(lambda: nc.vector.tensor_copy(carry[:, ch, :], pr[:, 512:515]), r=[pr], w=[carry])
                self.dve(lambda: nc.vector.tensor_scalar_mul(acc[:], pr[:, 0:512], cw[:, ch, 0:1]), r=[pr, cw], w=[acc])
                for i in range(1, 4):
                    self.dve(lambda: nc.vector.scalar_tensor_tensor(out=acc[:], in0=pr[:, i:i + 512], scalar=cw[:, ch, i:i + 1],
                                                                     in1=acc[:], op0=ALU.mult, op1=ALU.add), r=[pr, cw, acc], w=[acc])
                if ch < 16 or ch < 20:
                    dst = O["xc"][ch % 2] if ch < 16 else None
                    tgt = dst[:, 0:512] if ch < 16 else O["BT"][:, ch - 16, 0:512]
                    tb = dst if ch < 16 else O["BT"]
                    self.act(lambda: nc.scalar.activation(tgt, acc[:], AF.Silu, bias=fc[:, ch:ch + 1], scale=1.0),
                             r=[acc, fc], w=[tb])
                    tp = P[4 + ch % 2]
                    tv = tp[:, :].bitcast(BF16)

                    def tr():
                        ins = None
                        for st in range(4):
                            ins = nc.tensor.transpose(tv[:, st * 128:(st + 1) * 128], tgt[:, st * 128:(st + 1) * 128], self.ident_bf[:])
                        return ins
                    self.pe(tr, r=[tb, self.ident_bf], w=[tp])
                    src = tv[:, 0:512].rearrange("p (s c) -> p s c", s=4)
                    if ch < 16:
                        self.dve(lambda: nc.vector.tensor_copy(xst[:, :, ch * 128:(ch + 1) * 128], src), r=[tp], w=[xst])
                    else:
                        self.dve(lambda: nc.vector.tensor_copy(Btok[:, :, (ch - 16) * 128:(ch - 15) * 128], src), r=[tp], w=[Btok])
                else:
                    self.act(lambda: nc.scalar.activation(O["CT"][:, ch - 20, 0:512], acc[:], AF.Silu, bias=fc[:, ch:ch + 1], scale=1.0),
                             r=[acc, fc], w=[O["CT"]])
        for st in range(4):
            cs = st * 128
            acs, acl, dte, cd, eac, dtd = [sm[:, i, :] for i in range(6)]
            self.pe(lambda: nc.tensor.matmul(P[6][:, 256:288], mk[:, 6, :], av[:, st, :], start=True, stop=True), r=[mk, av], w=[P[6]])
            self.pe(lambda: nc.tensor.matmul(P[6][:, 288:320], self.ones_f[:], av[:, st, :], start=True, stop=True),
                    r=[self.ones_f, av], w=[P[6]])
            self.dve(lambda: nc.vector.tensor_copy(sm[:, 0:2, :], P[6][:, 256:320].rearrange("p (a b) -> p a b", a=2)), r=[P[6]], w=[sm])
            self.dve(lambda: nc.vector.tensor_tensor(out=dte, in0=acl, in1=acs, op=ALU.subtract), r=[sm], w=[sm])
            self.act(lambda: nc.scalar.activation(dte, dte, AF.Exp), r=[sm], w=[sm])
            self.act(lambda: nc.scalar.activation(cd, acl, AF.Exp), r=[sm], w=[sm])
            self.act(lambda: nc.scalar.activation(eac, acs, AF.Exp), r=[sm], w=[sm])
            self.dve(lambda: nc.vector.tensor_tensor(out=dtd, in0=dte, in1=dtv[:, st, :], op=ALU.mult), r=[sm, dtv], w=[sm])
            xv = xst[:, st, :].rearrange("p (h q) -> p h q", q=P_C)
            for q4 in range(4):
                hs = slice(q4 * 8, (q4 + 1) * 8)
                self.dve(lambda: nc.vector.tensor_tensor(out=xdt[:, q4 * 512:(q4 + 1) * 512].rearrange("p (h q) -> p h q", q=P_C),
                                                          in0=xv[:, hs, :], in1=dtv[:, st, hs].unsqueeze(2).to_broadcast([128, 8, P_C]),
                                                          op=ALU.mult), r=[xst, dtv], w=[xdt])
                self.dve(lambda: nc.vector.tensor_tensor(out=xdd[:, q4 * 512:(q4 + 1) * 512].rearrange("p (h q) -> p h q", q=P_C),
                                                          in0=xv[:, hs, :], in1=dtd[:, hs].unsqueeze(2).to_broadcast([128, 8, P_C]),
                                                          op=ALU.mult), r=[xst, sm], w=[xdd])
            def group(g):
                q = g % 2
                bcB, yoffB, yps, scB = P[q], P[2 + q], P[4 + q], P[6 + q]
                r4, lt, wb = R4[q], Lt[q], Wb[q]
                yield self.pe(lambda: nc.tensor.matmul(scB[:, 0:128], O["BT"][:, g, cs:cs + 128], O["CT"][:, g, cs:cs + 128], start=True, stop=True),
                              r=[O["BT"], O["CT"]], w=[scB])
                for hf in range(2):
                    h0 = g * 8 + hf * 4
                    yield self.dve(lambda: nc.vector.tensor_tensor(out=r4[:], in0=mk[:, 6, :].unsqueeze(1).to_broadcast([128, 4, 128]),
                                                                   in1=av[:, st, h0:h0 + 4].unsqueeze(2).to_broadcast([128, 4, 128]), op=ALU.mult),
                                   r=[mk, av], w=[r4])
                    yield self.pe(lambda: nc.tensor.matmul(bcB[:, 0:512], self.ones_f[:], r4[:].rearrange("p a b -> p (a b)"), start=True, stop=True),
                                  r=[self.ones_f, r4], w=[bcB])
                    yield self.dve(lambda: nc.vector.tensor_tensor(out=lt[:], in0=bcB[:, 0:512].rearrange("p (a b) -> p a b", a=4),
                                                                   in1=acs[:, h0:h0 + 4].unsqueeze(2).to_broadcast([128, 4, 128]),
                                                                   op=ALU.subtract), r=[bcB, sm], w=[lt])
                    yield self.dve(lambda: nc.vector.tensor_tensor(out=lt[:], in0=lt[:], in1=mk[:, 7, :].unsqueeze(1).to_broadcast([128, 4, 128]),
                                                                   op=ALU.min), r=[lt, mk], w=[lt])
                    yield self.act(lambda: nc.scalar.activation(lt[:], lt[:], AF.Exp), r=[lt], w=[lt])
                    yield self.dve(lambda: nc.vector.tensor_tensor(out=wb[:], in0=lt[:], in1=scB[:, 0:128].unsqueeze(1).to_broadcast([128, 4, 128]),
                                                                   op=ALU.mult), r=[lt, scB], w=[wb])

                    def ymm():
                        ins = None
                        for hh in range(4):
                            h = h0 + hh
                            ins = nc.tensor.matmul(yps[:, (hf * 4 + hh) * 64:(hf * 4 + hh + 1) * 64], wb[:, hh, :], xdt[:, h * 64:(h + 1) * 64],
                                                   start=True, stop=True)
                        return ins
                    yield self.pe(ymm, r=[wb, xdt], w=[yps])
                yield self.pe(lambda: nc.tensor.matmul(yoffB[:, 0:512], O["CT"][:, g, cs:cs + 128], STb[:, g * 512:(g + 1) * 512], start=True, stop=True),
                              r=[O["CT"], STb], w=[yoffB])
                t = tt[q]
                gsl = slice(g * 512, (g + 1) * 512)
                yield self.dve(lambda: nc.vector.tensor_tensor(out=t[:].rearrange("p (h q) -> p h q", q=P_C),
                                                               in0=yoffB[:, 0:512].rearrange("p (h q) -> p h q", q=P_C),
                                                               in1=eac[:, g * 8:(g + 1) * 8].unsqueeze(2).to_broadcast([128, 8, P_C]), op=ALU.mult),
                               r=[yoffB, sm], w=[t])
                yield self.dve(lambda: nc.vector.tensor_tensor(out=t[:], in0=t[:], in1=yps[:, 0:512], op=ALU.add), r=[t, yps], w=[t])
                t2 = O["acc"][q]
                yield self.dve(lambda: nc.vector.tensor_tensor(out=t2[:].rearrange("p (h q) -> p h q", q=P_C),
                                                               in0=xst[:, st, gsl].rearrange("p (h q) -> p h q", q=P_C),
                                                               in1=hb[:, 64 + g * 8:64 + (g + 1) * 8].unsqueeze(2).to_broadcast([128, 8, P_C]), op=ALU.mult),
                               r=[xst, hb], w=[t2])
                yield self.dve(lambda: nc.vector.tensor_tensor(out=ytk[:, gsl], in0=t[:], in1=t2[:], op=ALU.add), r=[t, t2], w=[ytk])
                yield self.pe(lambda: nc.tensor.matmul(bcB[:, 0:512], Btok[:, st, g * 128:(g + 1) * 128], xdd[:, gsl], start=True, stop=True),
                              r=[Btok, xdd], w=[bcB])
                yield self.dve(lambda: nc.vector.tensor_tensor(out=ST[:, gsl].rearrange("p (h q) -> p h q", q=P_C),
                                                               in0=ST[:, gsl].rearrange("p (h q) -> p h q", q=P_C),
                                                               in1=cd[:, g * 8:(g + 1) * 8].unsqueeze(2).to_broadcast([128, 8, P_C]), op=ALU.mult),
                               r=[ST, sm], w=[ST])
                yield self.dve(lambda: nc.vector.tensor_tensor(out=ST[:, gsl], in0=ST[:, gsl], in1=bcB[:, 0:512], op=ALU.add), r=[ST, bcB], w=[ST])
                yield self.act(lambda: nc.scalar.copy(STb[:, gsl], ST[:, gsl]), r=[ST], w=[STb])

            for pr in range(2):
                gens = [group(2 * pr), group(2 * pr + 1)]
                while gens:
                    for gg in list(gens):
                        try:
                            next(gg)
                        except StopIteration:
                            gens.remove(gg)
            for hf in range(2):
                tp = P[hf]
                tv = tp[:, :].bitcast(BF16)

                def tr2():
                    ins = None
                    for j in range(8):
                        ch = hf * 8 + j
                        ins = nc.tensor.transpose(tv[:, j * 128:(j + 1) * 128], ytk[:, ch * 128:(ch + 1) * 128], self.ident_bf[:])
                    return ins
                self.pe(tr2, r=[ytk, self.ident_bf], w=[tp])
                self.act(lambda: nc.scalar.copy(O["yT"][:, hf * 8:(hf + 1) * 8, cs:cs + 128], tv[:, 0:1024].rearrange("p (j c) -> p j c", j=8)),
                         r=[tp], w=[O["yT"]])
        zw = [None]

        def zfn(j, consume):
            if j % 4 == 0:
                zw[0] = self.ws.get(4096)
            ps = P[2 + j % 2]
            odd_fm_chunk(self, O, zw[0], j % 4, 512, ps)
            consume(ps)
        odd_gate_norm_out(self, O, xb, c0, 512, zfn)
    for ch in range(24):
        self.out_dma(self.o_psc[e][:, ch * 128:(ch + 1) * 128].rearrange("i p -> p i"), carry[:, ch, :], r=[carry])
    for c in range(16):
        ps = P[c % 2]
        self.pe(lambda: nc.tensor.transpose(ps[:, 0:128], ST[:, c * 128:(c + 1) * 128], ident[:]), r=[ST, ident], w=[ps])
        self.dve(lambda: nc.vector.tensor_copy(sto[:, c, :], ps[:, 0:128]), r=[ps], w=[sto])
    self.out_dma(self.o_pss[e].rearrange("(c q) n -> q c n", q=128), sto[:, 0:16, :], r=[sto])
    self.end()
    odd_decode(self, l)


def odd_decode(self, l):
    nc = self.nc
    e = l // 2
    n = NS
    c0 = self.seq
    P = self.P
    ident, mk = self.ident, self.mk
    self.begin()
    T = self.T
    O = odd_alloc(self, n)
    xx = T("oxx", [128, 24, n, 4], F32)
    hs = T("ohs", [48, 3072], F32)
    gnew = T("ognew", [n, 3072], F32)
    xsT = T("xsT", [128, 16, n], F32)
    BCs = T("BCs", [128, 8, n], F32)
    dts = T("dts", [n, 64], F32)
    BCt = T("BCt", [n, 2, 512], F32)
    BCm = [T("BCm%d" % i, [n, 2, 512], F32) for i in range(2)]
    R5 = T("R5", [n, 2, 2, 16, n], F32)
    onesH = T("onesH", [n, 2, 128], F32)
    cols = T("ocols", [128, 2, 16, n], F32)
    xdc = T("xdc", [128, 16, n], F32)
    Sb = [T("Sb%d" % i, [128, 16, 128], F32) for i in range(2)]
    tS = T("otS", [128, 16, 128], F32)
    ysum = T("ysum", [128, 16, n], F32)
    odd_consts(self, O, e)
    cw, fc, hb = O["cw"], O["fc"], O["hb"]
    self.dma(hs[:], self.s_sc[e].rearrange("b i c -> (b i) c"), r=[self.Bin], w=[hs])
    self.dve(lambda: nc.vector.memset(onesH[:], 0.0), w=[onesH])
    self.dve(lambda: nc.vector.memset(onesH[:, 0, 0:64], 1.0), r=[onesH], w=[onesH])
    self.dve(lambda: nc.vector.memset(onesH[:, 1, 64:128], 1.0), r=[onesH], w=[onesH])
    for ch in range(24):
        ps = P[2 + ch % 2]
        self.pe(lambda: nc.tensor.transpose(ps[:, 0:48], hs[:, ch * 128:(ch + 1) * 128], ident[0:48, 0:48]), r=[hs, ident], w=[ps])
        self.dve(lambda: nc.vector.tensor_copy(xx[:, ch, :, 0:3], ps[:, 0:48].rearrange("p (b i) -> p b i", i=3)), r=[ps], w=[xx])
    xb = [self.xTb[self.nt]]
    self.rmsnorm_cols(xb, c0, n, 4 + l, O["hn"], 0, O["sqm"], O["rstd"])
    w = self.ws.get(OW_DT)
    wv = w[:, 0:OW_DT].rearrange("p (k c) -> p k c", k=KC)
    ps = P[2]

    def mm():
        ins = None
        for kc in range(KC):
            ins = nc.tensor.matmul(ps[0:n, 0:32], O["hn"][:, kc, 0:n], wv[:, kc, :], start=(kc == 0), stop=(kc == KC - 1))
        return ins
    self.pe(mm, r=[w, O["hn"]], w=[ps])
    odd_dt(self, O, ps[0:n, 0:32], n, dts[:, 0:32], dts[:, 32:64], [ps], [dts])
    self.act(lambda: nc.scalar.activation(dts[:, 32:64], dts[:, 32:64], AF.Exp), r=[dts], w=[dts])
    for blk in range(6):
        w = self.ws.get(4096)
        for cc in range(4):
            ch = blk * 4 + cc
            ps = P[ch % 2]
            odd_fm_chunk(self, O, w, cc, n, ps)
            acc = O["acc"][ch % 2]
            self.act(lambda: nc.scalar.copy(xx[:, ch, :, 3], ps[:, 0:n]), r=[ps], w=[xx])
            self.dve(lambda: nc.vector.tensor_scalar_mul(acc[:, 0:n], xx[:, ch, :, 0], cw[:, ch, 0:1]), r=[xx, cw], w=[acc])
            for i in range(1, 4):
                self.dve(lambda: nc.vector.scalar_tensor_tensor(out=acc[:, 0:n], in0=xx[:, ch, :, i], scalar=cw[:, ch, i:i + 1],
                                                                in1=acc[:, 0:n], op0=ALU.mult, op1=ALU.add), r=[xx, cw, acc], w=[acc])
            if ch < 16:
                dst, db = xsT[:, ch, :], xsT
            else:
                dst, db = BCs[:, ch - 16, :], BCs
            self.act(lambda: nc.scalar.activation(dst, acc[:, 0:n], AF.Silu, bias=fc[:, ch:ch + 1], scale=1.0), r=[acc, fc], w=[db])
            pt = P[6]
            self.pe(lambda: nc.tensor.transpose(pt[0:n, (ch % 4) * 128:(ch % 4 + 1) * 128], xx[:, ch, :, 3], ident[:]), r=[xx, ident], w=[pt])
            self.dve(lambda: nc.vector.tensor_copy(gnew[:, ch * 128:(ch + 1) * 128], pt[0:n, (ch % 4) * 128:(ch % 4 + 1) * 128]), r=[pt], w=[gnew])
    for t in range(2):
        pt = P[4 + t]
        for g in range(4):
            self.pe(lambda: nc.tensor.transpose(pt[0:n, g * 128:(g + 1) * 128], BCs[:, t * 4 + g, :], ident[:]), r=[BCs, ident], w=[pt])
        self.dve(lambda: nc.vector.tensor_copy(BCt[:, t, :], pt[0:n, 0:512]), r=[pt], w=[BCt])
    for t in range(2):
        src = dts[:, t * 32:(t + 1) * 32].rearrange("p (hp h2) -> p h2 hp", h2=2)
        self.dve(lambda: nc.vector.tensor_tensor(out=R5[:, t], in0=src.unsqueeze(3).to_broadcast([n, 2, 16, n]),
                                                 in1=ident[0:n, 0:n].unsqueeze(1).unsqueeze(1).to_broadcast([n, 2, 16, n]), op=ALU.mult),
                 r=[dts, ident], w=[R5])

        def cmm():
            ins = None
            for h2 in range(2):
                ins = nc.tensor.matmul(P[7][:, t * 256:(t + 1) * 256], onesH[:, h2, :], R5[:, t, h2].rearrange("p a b -> p (a b)"),
                                       start=(h2 == 0), stop=(h2 == 1))
            return ins
        self.pe(cmm, r=[onesH, R5], w=[P[7]])
    self.dve(lambda: nc.vector.tensor_copy(cols[:].rearrange("p t a b -> p (t a b)"), P[7][:, 0:512]), r=[P[7]], w=[cols])
    self.dve(lambda: nc.vector.tensor_tensor(out=xdc[:], in0=cols[:, 0], in1=xsT[:], op=ALU.mult), r=[cols, xsT], w=[xdc])
    sview = self.s_ss[e].rearrange("b (c q) n -> b q c n", q=128)
    oview = self.o_sss[e].rearrange("b (c q) n -> b q c n", q=128)
    self.dma(Sb[0][:], sview[0], r=[self.Bin], w=[Sb[0]])
    for b in range(n):
        S_ = Sb[b % 2]
        if b + 1 < n:
            self.dma(Sb[(b + 1) % 2][:], sview[b + 1], r=[self.Bin], w=[Sb[(b + 1) % 2]])
        bm = BCm[b % 2]
        self.dve(lambda: nc.vector.tensor_scalar_mul(bm[:].rearrange("p a b -> p (a b)"), BCt[:].rearrange("p a b -> p (a b)"),
                                                     ident[0:n, b:b + 1]), r=[BCt, ident], w=[bm])
        pB, pC = P[(b % 2) * 2], P[(b % 2) * 2 + 1]
        self.pe(lambda: nc.tensor.matmul(pB[:, 0:512], self.ones_f[0:n, :], bm[:, 0, :], start=True, stop=True), r=[self.ones_f, bm], w=[pB])
        self.pe(lambda: nc.tensor.matmul(pC[:, 0:512], self.ones_f[0:n, :], bm[:, 1, :], start=True, stop=True), r=[self.ones_f, bm], w=[pC])
        s4 = S_[:].rearrange("p (g r) n -> p g r n", r=4)
        t4 = tS[:].rearrange("p (g r) n -> p g r n", r=4)
        self.dve(lambda: nc.vector.tensor_tensor(out=tS[:], in0=S_[:], in1=cols[:, 1, :, b:b + 1].to_broadcast([128, 16, 128]), op=ALU.mult),
                 r=[S_, cols], w=[tS])
        self.dve(lambda: nc.vector.tensor_tensor(out=s4, in0=pB[:, 0:512].rearrange("p (g n) -> p g n", g=4).unsqueeze(2).to_broadcast([128, 4, 4, 128]),
                                                 in1=xdc[:, :, b].rearrange("p (g r) -> p g r", r=4).unsqueeze(3).to_broadcast([128, 4, 4, 128]),
                                                 op=ALU.mult), r=[pB, xdc], w=[S_])
        self.dve(lambda: nc.vector.tensor_tensor(out=S_[:], in0=S_[:], in1=tS[:], op=ALU.add), r=[S_, tS], w=[S_])
        self.dve(lambda: nc.vector.tensor_tensor(out=t4, in0=s4, in1=pC[:, 0:512].rearrange("p (g n) -> p g n", g=4).unsqueeze(2).to_broadcast([128, 4, 4, 128]),
                                                 op=ALU.mult), r=[S_, pC], w=[tS])
        self.dve(lambda: nc.vector.reduce_sum(out=ysum[:, :, b], in_=tS[:], axis=AX.X), r=[tS], w=[ysum])
        self.dma(oview[b], S_[:], r=[S_], w=[self.Bout])
    self.dve(lambda: nc.vector.tensor_tensor(out=xdc[:], in0=xsT[:], in1=fc[:, 40:56].unsqueeze(2).to_broadcast([128, 16, n]), op=ALU.mult),
             r=[xsT, fc], w=[xdc])
    self.dve(lambda: nc.vector.tensor_tensor(out=O["yT"][:, :, 0:n], in0=ysum[:], in1=xdc[:], op=ALU.add), r=[ysum, xdc], w=[O["yT"]])
    zw = [None]

    def zfn(j, consume):
        if j % 4 == 0:
            zw[0] = self.ws.get(4096)
        ps = P[2 + j % 2]
        odd_fm_chunk(self, O, zw[0], j % 4, n, ps)
        consume(ps)
    odd_gate_norm_out(self, O, xb, c0, n, zfn)
    self.out_dma(self.o_ssc[e][:, 0:2, :], self.s_sc[e][:, 1:3, :], r=[self.Bin])
    self.out_dma(self.o_ssc[e][:, 2, :], gnew[:], r=[gnew])
    self.end()


Prog.odd_mixer = odd_mixer


def tile_k(w, cb):
    K, N = w.shape
    return np.ascontiguousarray(w.reshape(K // 128, 128, N // cb, cb).transpose(2, 1, 0, 3))


def t5_bucket_np(dist):
    max_exact = 16
    df = np.maximum(dist, max_exact).astype(np.float32)
    large = max_exact + (np.log(df / max_exact) / math.log(128 / max_exact) * (32 - max_exact)).astype(np.int32)
    return np.where(dist < max_exact, dist, np.minimum(large, 31))


def static_masks():
    i = np.arange(128)[:, None]
    j = np.arange(128)[None, :]
    same = (i // 64) == (j // 64)
    m = np.zeros((128, 8, 128), np.float32)
    m[:, 0] = ((i <= j) & same)
    m[:, 1] = np.where((j < i) & same, 0.0, 1e30)
    m[:, 2] = np.where((j >= i) & same, 0.0, -1e30)
    m[:, 3] = ((j > i) & same)
    m[:, 4] = same
    m[:, 5, 0:16] = (np.arange(128)[:, None] // 8) == np.arange(16)[None, :]
    m[:, 6] = (i <= j)
    m[:, 7] = np.where(j >= i, 0.0, -1e30)
    return m.reshape(128, 8 * 128)


def prep_shared(inp):
    sh = {}
    g = np.zeros((128, 13, KC), np.float32)
    for l in range(4):
        g[:, l] = inp["norm_ff1"][l].reshape(KC, 128).T
        g[:, 4 + l] = inp["norm_mix"][l].reshape(KC, 128).T
        g[:, 8 + l] = inp["norm_ff2"][l].reshape(KC, 128).T
    g[:, 12] = inp["norm_final"].reshape(KC, 128).T
    sh["gains"] = g.reshape(128, 13 * KC)
    sh["masks"] = static_masks()
    wgu = np.empty((DEPTH, 2, 11, 128, 2, KC, 256), np.float32)
    wd = np.empty((DEPTH, 2, 8, 128, FC, 128), np.float32)
    for l in range(DEPTH):
        for f, (kg, ku, kd) in enumerate((("ff1_gate", "ff1_up", "ff1_down"), ("ff2_gate", "ff2_up", "ff2_down"))):
            wgu[l, f, :, :, 0] = tile_k(inp[kg][l], 256)
            wgu[l, f, :, :, 1] = tile_k(inp[ku][l], 256)
            wd[l, f] = tile_k(inp[kd][l], 128)
    sh["wgu"] = wgu.reshape(DEPTH, 2, 11, 128, 4096)
    sh["wd"] = wd.reshape(DEPTH, 2, 8, 128, 2816)
    ewin = np.zeros((2, 128, EW_TOT), np.float32)
    ewout = np.zeros((2, 2, 128, 4096), np.float32)
    econv = np.zeros((2, 128, 12, 4), np.float32)
    esm = np.zeros((2, 128, 17), np.float32)
    esk = np.zeros((2, 128, 1), np.float32)
    for e in range(2):
        W = inp["even_w_in"][e]
        qa, ka, va = W[:, 0:512], W[:, 512:640], W[:, 640:768]
        qkvb, zb, bb = W[:, 768:2304], W[:, 2304:2816], W[:, 2816:2824]
        cols = []
        for c in range(4):
            cols.append(np.concatenate([qa[:, c * 64:(c + 1) * 64], qa[:, 256 + c * 64:256 + (c + 1) * 64]], 1))
        cols.append(ka)
        cols.append(qkvb)
        cols.append(zb)
        fm = np.concatenate(cols, 1)
        assert fm.shape[1] == EV_FM * 128
        zero = np.zeros((1024, 64), np.float32)
        tok = np.concatenate([va[:, 0:64], zero, zero, va[:, 64:128], ka, bb], 1)
        assert tok.shape[1] == EV_TOK
        for b in range(5):
            ewin[e, :, EW_OFF[b]:EW_OFF[b] + 4096] = tile_k(fm[:, b * 512:(b + 1) * 512], 512)[0].reshape(128, 4096)
        ewin[e, :, EW_OFF[5]:EW_OFF[5] + 1024] = tile_k(fm[:, 2560:2688], 128)[0].reshape(128, 1024)
        ewin[e, :, EW_OFF[6]:] = tile_k(tok, EV_TOK)[0].reshape(128, 8 * EV_TOK)
        Wo = inp["even_w_out"][e]
        rows = []
        for c in range(4):
            rows.append(Wo[c * 64:(c + 1) * 64])
            rows.append(Wo[256 + c * 64:256 + (c + 1) * 64])
        rows.append(Wo[512:])
        Wp = np.concatenate(rows, 0)
        ewout[e] = tile_k(Wp, 512).reshape(2, 128, 4096)
        econv[e] = inp["gdn_conv_w"][e].reshape(4, 12, 128).transpose(2, 1, 0)
        esm[e, :, 0:4] = inp["gdn_A_log"][e][None, :]
        esm[e, :, 4:8] = inp["gdn_dt_bias"][e][None, :]
        esm[e, :, 8:16] = inp["swa_sinks"][e][None, :]
        esm[e, :, 16] = inp["gdn_norm"][e]
        esk[e, :, 0] = np.tile(inp["swa_sinks"][e], 16)
    sh["ewin"], sh["ewout"] = ewin, ewout
    sh["econv"] = econv.reshape(2, 128, 48)
    sh["esm"], sh["esk"] = esm, esk
    owin = np.zeros((2, 128, OW_TOT), np.float32)
    owout = np.zeros((2, 4, 128, 4096), np.float32)
    oconv = np.zeros((2, 128, 24, 4), np.float32)
    ohead = np.zeros((2, 128, 96), np.float32)
    ofeat = np.zeros((2, 128, 56), np.float32)
    for e in range(2):
        W = inp["ssd_w_in"][e]
        z, xbc, dtw = W[:, 0:2048], W[:, 2048:5120], W[:, 5120:5152]
        owin[e, :, 0:256] = tile_k(dtw, 32)[0].reshape(128, 256)
        fm = np.concatenate([xbc, z], 1)
        for b in range(10):
            owin[e, :, 256 + b * 4096:256 + (b + 1) * 4096] = tile_k(fm[:, b * 512:(b + 1) * 512], 512)[0].reshape(128, 4096)
        Wo = inp["ssd_w_out"][e]
        owout[e] = np.ascontiguousarray(Wo.reshape(16, 128, 4, 256).transpose(2, 1, 0, 3)).reshape(4, 128, 4096)
        oconv[e] = inp["ssd_conv_w"][e].reshape(4, 24, 128).transpose(2, 1, 0)
        ohead[e, :, 0:32] = inp["ssd_dt_bias"][e][None, :]
        ohead[e, :, 32:64] = inp["ssd_A_log"][e][None, :]
        ohead[e, :, 64:96] = inp["ssd_D"][e][None, :]
        ofeat[e, :, 0:24] = inp["ssd_conv_b"][e].reshape(24, 128).T
        ofeat[e, :, 24:40] = inp["ssd_norm"][e].reshape(16, 128).T
        ofeat[e, :, 40:56] = np.repeat(inp["ssd_D"][e].reshape(16, 2), 64, axis=1).T
    sh["owin"], sh["owout"] = owin, owout
    sh["oconv"] = oconv.reshape(2, 128, 96)
    sh["ohead"], sh["ofeat"] = ohead, ofeat
    rb = inp["rel_bias"]
    i = np.arange(128)[:, None]
    j = np.arange(256)[None, :]
    d = 128 + i - j
    valid = (d >= 0) & (d <= 128)
    bk = t5_bucket_np(np.clip(d, 0, 128))
    sb = np.where(valid[:, None, :], rb[bk].transpose(0, 2, 1), np.float32(NEG)).astype(np.float32)
    sh["swabias"] = np.ascontiguousarray(sb).reshape(128, 8 * 256)
    dd = 128 - np.arange(129)
    db = rb[t5_bucket_np(dd)]
    sh["decbias"] = np.ascontiguousarray(np.tile(db.T, (16, 1))).astype(np.float32)
    return sh


def core_inputs(inp, c, seq):
    m = {}
    m["x_p"] = np.ascontiguousarray(inp["x_prompt"][c][:seq])
    sl = slice(c * NS, (c + 1) * NS)
    m["x_s"] = np.ascontiguousarray(inp["x_sample"][sl, 0])
    m["c_k"] = np.ascontiguousarray(inp["cache_swa_k"][:, sl]).reshape(2, NS, 128, 128)
    m["c_v"] = np.ascontiguousarray(inp["cache_swa_v"][:, sl]).reshape(2, NS, 128, 128)
    m["s_gc"] = np.ascontiguousarray(inp["state_gdn_conv"][:, sl])
    m["s_gs"] = np.ascontiguousarray(inp["state_gdn_ssm"][:, sl])
    m["s_sc"] = np.ascontiguousarray(inp["state_ssd_conv"][:, sl])
    m["s_ss"] = np.ascontiguousarray(inp["state_ssd_ssm"][:, sl]).reshape(2, NS, 2048, 128)
    return m


_PROG_CACHE = {}


def get_prog(cfg):
    key = tuple(sorted((k, str(v)) for k, v in cfg.items()))
    if key not in _PROG_CACHE:
        _PROG_CACHE[key] = Prog(dict(cfg))
    return _PROG_CACHE[key]


def kernel(**inp):
    cfg = {"ntiles": 4, "depth": DEPTH}
    prog = get_prog(cfg)
    inp = {k: np.asarray(v) for k, v in inp.items()}
    sh = prep_shared(inp)
    in_maps = []
    for c in range(N_CORES):
        m = dict(sh)
        m.update(core_inputs(inp, c, SEQ))
        in_maps.append(m)
    res = run_bass_kernel_spmd(prog.nc, in_maps, core_ids=list(range(N_CORES)))
    R = res.results
    st1 = lambda k: np.stack([r[k] for r in R], 1)
    ct1 = lambda k: np.concatenate([r[k] for r in R], 1)
    y_p = np.stack([r["y_p"] for r in R], 0)
    y_s = np.concatenate([r["y_s"] for r in R], 0)[:, None, :]
    p_k = st1("o_pk").reshape(2, N_CORES, 128, 2, 64)
    p_v = st1("o_pv").reshape(2, N_CORES, 128, 2, 64)
    p_gc = st1("o_pgc")
    p_gs = st1("o_pgs")
    p_sc = st1("o_psc")
    p_ss = st1("o_pss").reshape(2, N_CORES, 32, 64, 128)
    s_k = ct1("o_sk").reshape(2, N_CORES * NS, 128, 2, 64)
    s_v = ct1("o_sv").reshape(2, N_CORES * NS, 128, 2, 64)
    s_gc = ct1("o_sgc")
    s_gs = ct1("o_sgs")
    s_sc = ct1("o_ssc")
    s_ss = ct1("o_sss").reshape(2, N_CORES * NS, 32, 64, 128)
    return (y_p, y_s, p_k, p_v, p_gc, p_gs, p_sc, p_ss, s_k, s_v, s_gc, s_gs, s_sc, s_ss)
```

```python
import math
import numpy as np
import concourse.bass as bass
import concourse.mybir as mybir
from concourse.bass_utils import run_bass_kernel_spmd

F32 = mybir.dt.float32
BF16 = mybir.dt.bfloat16
ALU = mybir.AluOpType
AF = mybir.ActivationFunctionType
AX = mybir.AxisListType

D = 1024
KC = 8
SEQ = 2048
NS = 16
DFF = 2816
FC = 22
EPS = 1e-6
N_CORES = 8
DEPTH = 4


class Buf:
    __slots__ = ("name", "w", "r", "excl")

    def __init__(self, name, excl=False):
        self.name = name
        self.w = None
        self.r = []
        self.excl = excl


class Sched:
    def __init__(self, nc, n_dma_sems=48):
        self.nc = nc
        self.E = {"pe": nc.tensor, "dve": nc.vector, "act": nc.scalar, "pool": nc.gpsimd, "sp": nc.sync}
        self.sems = {}
        self.cnt = {}
        for e in self.E:
            self.sems[e] = nc.alloc_semaphore("s_" + e)
            self.cnt[e] = 0
        self.dsems = []
        for i in range(n_dma_sems):
            k = "d%d" % i
            self.sems[k] = nc.alloc_semaphore("s_" + k)
            self.cnt[k] = 0
            self.dsems.append(k)
        self.dnext = 0
        self.seen = {e: {} for e in self.E}
        self.n_wait = 0
        self.n_ops = 0

    def _wait(self, eng, tick):
        if tick is None:
            return
        k, v = tick
        if self.seen[eng].get(k, 0) >= v:
            return
        self.E[eng].wait_ge(self.sems[k], v)
        self.seen[eng][k] = v
        self.n_wait += 1

    def _deps(self, eng, reads, writes):
        need = {}
        for b in reads:
            if b.w is not None:
                k, v = b.w
                if need.get(k, 0) < v:
                    need[k] = v
            if b.excl:
                for (k, v) in b.r:
                    if k != eng and need.get(k, 0) < v:
                        need[k] = v
        for b in writes:
            if b.w is not None:
                k, v = b.w
                if need.get(k, 0) < v:
                    need[k] = v
            for (k, v) in b.r:
                if need.get(k, 0) < v:
                    need[k] = v
        for k, v in need.items():
            self._wait(eng, (k, v))

    def _commit(self, tick, reads, writes):
        for b in reads:
            b.r.append(tick)
            if len(b.r) > 16:
                m = {}
                for (k, v) in b.r:
                    if m.get(k, 0) < v:
                        m[k] = v
                b.r = list(m.items())
        for b in writes:
            b.w = tick
            b.r = []

    def op(self, eng, fn, reads=(), writes=()):
        self._deps(eng, reads, writes)
        ins = fn()
        self.cnt[eng] += 1
        ins.then_inc(self.sems[eng], 1)
        tick = (eng, self.cnt[eng])
        self._commit(tick, reads, writes)
        self.n_ops += 1
        return tick

    def new_sem(self, k):
        self.sems[k] = self.nc.alloc_semaphore("s_" + k)
        self.cnt[k] = 0

    def barrier(self):
        for e in ("pe", "dve", "act", "sp"):
            for o in ("pe", "dve", "act", "pool"):
                if o != e and self.cnt[o] > 0:
                    self._wait(e, (o, self.cnt[o]))
            for k in self.dsems:
                if self.cnt[k] > 0:
                    self._wait(e, (k, self.cnt[k]))

    def dma(self, q, out=None, in_=None, reads=(), writes=(), multi=None, sem=None):
        pairs = multi if multi is not None else [(out, in_)]
        if sem is not None:
            k = sem
        else:
            k = self.dsems[self.dnext]
            self.dnext = (self.dnext + 1) % len(self.dsems)
        if self.cnt[k] > 0:
            self._wait(q, (k, self.cnt[k]))
        self._deps(q, reads, writes)
        for (o, i) in pairs:
            self.E[q].dma_start(out=o, in_=i, allow_slow_non_contiguous=True).then_inc(self.sems[k], 16)
            self.cnt[k] += 16
        tick = (k, self.cnt[k])
        self._commit(tick, reads, writes)
        return tick

    def finish(self):
        for e in ("pe", "dve", "act", "pool"):
            if self.cnt[e] > 0:
                self._wait("sp", (e, self.cnt[e]))
        for k in list(self.sems):
            if k not in self.E and self.cnt[k] > 0:
                self._wait("sp", (k, self.cnt[k]))


class Tile:
    def __init__(self, nc, name, shape, dtype, psum=False, handle=None):
        if handle is not None:
            self.t = handle
        elif psum:
            self.t = nc.alloc_psum_tensor(name, list(shape), dtype)
        else:
            self.t = nc.alloc_sbuf_tensor(name, list(shape), dtype)
        self.b = Buf(name, excl=psum)

    def __getitem__(self, k):
        return self.t[k]


SLOT_ELEMS = 4096
N_SLOTS = 3


class WStream:
    def __init__(self, nc, S, plan):
        self.nc, self.S = nc, S
        self.plan = plan
        self.slots = [Tile(nc, "wslot%d" % i, [128, SLOT_ELEMS], BF16) for i in range(N_SLOTS)]
        self.dram_buf = Buf("wdram")
        for i in range(N_SLOTS):
            S.new_sem("w%d" % i)
        self.issued = 0
        self.used = 0

    def _issue(self):
        i = self.issued
        ap, n = self.plan[i]
        sl = self.slots[i % N_SLOTS]
        self.S.dma("pool", out=sl[:, 0:n], in_=ap, reads=[self.dram_buf], writes=[sl.b], sem="w%d" % (i % N_SLOTS))
        self.issued += 1

    def get(self, expect_n):
        i = self.used
        while self.issued <= min(i + N_SLOTS - 2, len(self.plan) - 1):
            self._issue()
        assert self.plan[i][1] == expect_n, (i, self.plan[i][1], expect_n)
        self.used += 1
        return self.slots[i % N_SLOTS]


import contextlib

H_A, KV_A, HD_A = 8, 2, 64
H_B = 4
NEG = -1e30
EV_FM = 21
EV_TOK = 392
EW_OFF = [0, 4096, 8192, 12288, 16384, 20480, 21504]
EW_N = [4096, 4096, 4096, 4096, 4096, 1024, 8 * EV_TOK]
EW_TOT = 21504 + 8 * EV_TOK
OW_TOT = 256 + 10 * 4096


class Prog:
    def __init__(self, cfg):
        self.cfg = cfg
        self.nt = cfg.get("ntiles", 4)
        self.seq = self.nt * 512
        self.ntok = self.seq + NS
        self.depth = cfg.get("depth", DEPTH)
        self.layers = cfg.get("layers", None) or list(range(self.depth))
        nc = bass.Bass("TRN2", target_bir_lowering=False)
        self.nc = nc
        self.S = Sched(nc)
        self.stack = None
        self.declare_io()
        self.alloc()
        self.plan_weights()
        self.emit()

    def din(self, name, shape, dtype=F32):
        return self.nc.dram_tensor(name, list(shape), dtype, kind="ExternalInput").ap()

    def dout(self, name, shape, dtype=F32):
        return self.nc.dram_tensor(name, list(shape), dtype, kind="ExternalOutput").ap()

    def declare_io(self):
        sq = self.seq
        self.x_p = self.din("x_p", [sq, D])
        self.x_s = self.din("x_s", [NS, D])
        self.gains = self.din("gains", [128, 13 * KC])
        self.masks = self.din("masks", [128, 8 * 128])
        self.wgu = self.din("wgu", [DEPTH, 2, 11, 128, 4096])
        self.wd = self.din("wd", [DEPTH, 2, 8, 128, 2816])
        self.ewin = self.din("ewin", [2, 128, EW_TOT])
        self.ewout = self.din("ewout", [2, 2, 128, 4096])
        self.econv = self.din("econv", [2, 128, 48])
        self.esm = self.din("esm", [2, 128, 17])
        self.esk = self.din("esk", [2, 128, 1])
        self.swabias = self.din("swabias", [128, 8 * 256])
        self.decbias = self.din("decbias", [128, 129])
        self.c_k = self.din("c_k", [2, NS, 128, 128])
        self.c_v = self.din("c_v", [2, NS, 128, 128])
        self.s_gc = self.din("s_gc", [2, NS, 3, 1536])
        self.s_gs = self.din("s_gs", [2, NS, 4, 128, 128])
        self.owin = self.din("owin", [2, 128, OW_TOT])
        self.owout = self.din("owout", [2, 4, 128, 4096])
        self.oconv = self.din("oconv", [2, 128, 96])
        self.ohead = self.din("ohead", [2, 128, 96])
        self.ofeat = self.din("ofeat", [2, 128, 56])
        self.s_sc = self.din("s_sc", [2, NS, 3, 3072])
        self.s_ss = self.din("s_ss", [2, NS, 2048, 128])
        self.o_psc = self.dout("o_psc", [2, 3, 3072])
        self.o_pss = self.dout("o_pss", [2, 2048, 128])
        self.o_ssc = self.dout("o_ssc", [2, NS, 3, 3072])
        self.o_sss = self.dout("o_sss", [2, NS, 2048, 128])
        self.y_p = self.dout("y_p", [sq, D])
        self.y_s = self.dout("y_s", [NS, D])
        self.o_pk = self.dout("o_pk", [2, 128, 128])
        self.o_pv = self.dout("o_pv", [2, 128, 128])
        self.o_pgc = self.dout("o_pgc", [2, 3, 1536])
        self.o_pgs = self.dout("o_pgs", [2, 4, 128, 128])
        self.o_sk = self.dout("o_sk", [2, NS, 128, 128])
        self.o_sv = self.dout("o_sv", [2, NS, 128, 128])
        self.o_sgc = self.dout("o_sgc", [2, NS, 3, 1536])
        self.o_sgs = self.dout("o_sgs", [2, NS, 4, 128, 128])
        self.Bin = Buf("dram_in")
        self.Bout = Buf("dram_out")

    def alloc(self):
        nc = self.nc
        self.xT = Tile(nc, "xT", [128, KC, self.ntok], F32)
        self.xTb = [Buf("xT_t%d" % i) for i in range(self.nt)] + [Buf("xT_s")]
        self.ident = Tile(nc, "ident", [128, 128], F32)
        self.ident_bf = Tile(nc, "ident_bf", [128, 128], BF16)
        self.ones_bf = Tile(nc, "ones_bf", [128, 128], BF16)
        self.ones_f = Tile(nc, "ones_f", [128, 128], F32)
        self.gn = Tile(nc, "gn", [128, 13 * KC], F32)
        self.mk = Tile(nc, "mk", [128, 8, 128], F32)
        self.P = [Tile(nc, "ps%d" % i, [128, 512], F32, psum=True) for i in range(8)]

    def begin(self):
        assert self.stack is None
        self.stack = contextlib.ExitStack()
        self.nalloc = 0
        self.deferred = []

    def out_dma(self, dst, src, r=()):
        self.deferred.append((dst, src, list(r)))

    def end(self):
        for (dst, src, r) in self.deferred:
            self.dma(dst, src, r=r, w=[self.Bout])
        self.deferred = []
        self.S.barrier()
        self.stack.close()
        self.stack = None

    def T(self, name, shape, dtype):
        self.nalloc += 1
        h = self.stack.enter_context(self.nc.sbuf_tensor("%s_%d" % (name, self.S.n_ops), list(shape), dtype))
        return Tile(self.nc, name, shape, dtype, handle=h)

    def ffn_blocks(self, l, f):
        out = []
        for b in range(11):
            out.append((self.wgu[l, f, b], 4096))
        for b in range(8):
            out.append((self.wd[l, f, b], 2816))
        return out

    def tiles(self):
        ts = []
        for t in range(self.nt):
            segs = [(t * 512, 512, 0)]
            if t == self.nt - 1:
                segs.append((self.seq, NS, 512))
            ts.append(segs)
        return ts

    def even_blocks(self, e):
        out = []
        for _ in range(self.nt + 1):
            for b in range(7):
                out.append((self.ewin[e, :, EW_OFF[b]:EW_OFF[b] + EW_N[b]], EW_N[b]))
            for b in range(2):
                out.append((self.ewout[e, b], 4096))
        return out

    def odd_blocks(self, e):
        out = []
        for _ in range(self.nt + 1):
            out.append((self.owin[e, :, 0:256], 256))
            for b in range(10):
                out.append((self.owin[e, :, 256 + b * 4096:256 + (b + 1) * 4096], 4096))
            for b in range(4):
                out.append((self.owout[e, b], 4096))
        return out

    def mixer_blocks(self, l):
        if l % 2 == 0:
            return self.even_blocks(l // 2)
        return self.odd_blocks(l // 2)

    def plan_weights(self):
        plan = []
        ffn = self.cfg.get("ffn", True)
        for l in self.layers:
            if ffn:
                for _ in self.tiles():
                    plan += self.ffn_blocks(l, 0)
            if not self.cfg.get("nomix"):
                plan += self.mixer_blocks(l)
            if ffn:
                for _ in self.tiles():
                    plan += self.ffn_blocks(l, 1)
        self.ws = WStream(self.nc, self.S, plan)

    def bl(self, xs):
        return [getattr(x, 'b', x) for x in xs]

    def dve(self, fn, r=(), w=()):
        return self.S.op("dve", fn, self.bl(r), self.bl(w))

    def act(self, fn, r=(), w=()):
        return self.S.op("act", fn, self.bl(r), self.bl(w))

    def pe(self, fn, r=(), w=()):
        return self.S.op("pe", fn, self.bl(r), self.bl(w))

    def pool(self, fn, r=(), w=()):
        return self.S.op("pool", fn, self.bl(r), self.bl(w))

    def dma(self, out, in_, r=(), w=(), q="sp"):
        return self.S.dma(q, out=out, in_=in_, reads=self.bl(r), writes=self.bl(w))

    def gcol(self, idx, kc):
        return self.gn[:, idx * KC + kc: idx * KC + kc + 1]

    def tile_bufs(self, ti):
        b = [self.xTb[ti]]
        if ti == self.nt - 1:
            b.append(self.xTb[self.nt])
        return b

    def col_stats(self, src_fn, nchunk, n, ps, sq, ones, scale, bias_ln, out_ap, exp_bias=0.0, rd=()):
        nc = self.nc
        for c in range(nchunk):
            self.act(lambda c=c: nc.scalar.activation(sq[:, c, 0:n], src_fn(c), AF.Square), r=rd, w=[sq])

        def mm():
            ins = None
            for c in range(nchunk):
                ins = nc.tensor.matmul(ps[:, 0:n], ones[:], sq[:, c, 0:n], start=(c == 0), stop=(c == nchunk - 1))
            return ins
        self.pe(mm, r=[sq, ones], w=[ps])
        return ps

    def rmsnorm_cols(self, xbufs, c0, n, gidx, hn, l0, sq, rstd):
        nc = self.nc
        ps = self.P[7]
        self.act(lambda: nc.scalar.activation(sq[:, :, l0:l0 + n], self.xT[:, :, c0:c0 + n], AF.Square),
                 r=xbufs, w=[sq])

        def mm():
            ins = None
            for kc in range(KC):
                ins = nc.tensor.matmul(ps[:, 0:n], self.ones_bf[:], sq[:, kc, l0:l0 + n],
                                       start=(kc == 0), stop=(kc == KC - 1))
            return ins
        self.pe(mm, r=[sq, self.ones_bf], w=[ps])
        self.act(lambda: nc.scalar.activation(rstd[:, l0:l0 + n], ps[:, 0:n], AF.Ln, bias=EPS, scale=1.0 / D),
                 r=[ps], w=[rstd])
        self.act(lambda: nc.scalar.activation(rstd[:, l0:l0 + n], rstd[:, l0:l0 + n], AF.Exp, scale=-0.5),
                 r=[rstd], w=[rstd])
        for kc in range(KC):
            self.dve(lambda kc=kc: nc.vector.scalar_tensor_tensor(
                out=hn[:, kc, l0:l0 + n], in0=self.xT[:, kc, c0:c0 + n], scalar=self.gcol(gidx, kc),
                in1=rstd[:, l0:l0 + n], op0=ALU.mult, op1=ALU.mult),
                r=xbufs + [rstd, self.gn], w=[hn])

    def ffn_tile(self, l, f, ti, segs, hn, hT, sgs):
        nc = self.nc
        xb = self.tile_bufs(ti)
        it = 0
        for blk in range(11):
            w = self.ws.get(4096)
            wv = w[:, 0:4096].rearrange("p (a k c) -> p a k c", a=2, k=KC)
            for fc in range(2):
                fch = blk * 2 + fc
                for si, (g0, n, l0) in enumerate(segs):
                    if si == 0:
                        pg, pu = self.P[(it % 2)], self.P[2 + (it % 2)]
                    else:
                        pg, pu = self.P[4], self.P[5]

                    def mm(pt, a):
                        ins = None
                        for kc in range(KC):
                            ins = nc.tensor.matmul(pt[:, 0:n], wv[:, a, kc, fc * 128:(fc + 1) * 128],
                                                   hn[:, kc, l0:l0 + n], start=(kc == 0), stop=(kc == KC - 1))
                        return ins
                    self.pe(lambda: mm(pg, 0), r=[w, hn], w=[pg])
                    self.pe(lambda: mm(pu, 1), r=[w, hn], w=[pu])
                    sg = sgs[it % 2]
                    self.act(lambda: nc.scalar.activation(sg[:, 0:n], pg[:, 0:n], AF.Silu), r=[pg], w=[sg])
                    self.dve(lambda: nc.vector.tensor_tensor(out=hT[:, fch, l0:l0 + n], in0=sg[:, 0:n],
                                                             in1=pu[:, 0:n], op=ALU.mult), r=[sg, pu], w=[hT])
                it += 1
        it = 0
        for blk in range(8):
            w = self.ws.get(2816)
            wv = w[:, 0:2816].rearrange("p (k c) -> p k c", k=FC)
            dch = blk
            for si, (g0, n, l0) in enumerate(segs):
                po = self.P[6 + (it % 2)] if si == 0 else self.P[4 + (it % 2)]

                def mm():
                    ins = None
                    for k in range(FC):
                        ins = nc.tensor.matmul(po[:, 0:n], wv[:, k, :], hT[:, k, l0:l0 + n],
                                               start=(k == 0), stop=(k == FC - 1))
                    return ins
                self.pe(mm, r=[w, hT], w=[po])
                xs = self.xT[:, dch, g0:g0 + n]
                self.dve(lambda: nc.vector.scalar_tensor_tensor(out=xs, in0=po[:, 0:n], scalar=0.5, in1=xs,
                                                                op0=ALU.mult, op1=ALU.add), r=[po] + xb, w=xb)
            it += 1

    def ffn(self, l, f):
        self.begin()
        hns = [self.T("hn%d" % i, [128, KC, 528], BF16) for i in range(2)]
        hT = self.T("hT", [128, FC, 528], BF16)
        sq = self.T("sq", [128, KC, 528], BF16)
        rstd = self.T("rstd", [128, 528], F32)
        sgs = [self.T("sg%d" % i, [128, 512], F32) for i in range(2)]
        gidx = l if f == 0 else 8 + l
        for ti, segs in enumerate(self.tiles()):
            hn = hns[ti % 2]
            for (g0, n, l0) in segs:
                self.rmsnorm_cols(self.tile_bufs(ti), g0, n, gidx, hn, l0, sq, rstd)
            self.ffn_tile(l, f, ti, segs, hn, hT, sgs)
        self.end()

    def consts(self):
        nc = self.nc
        self.pool(lambda: nc.gpsimd.memset(self.ident[:], 0.0), w=[self.ident])
        self.pool(lambda: nc.gpsimd.affine_select(self.ident[:], self.ident[:], pattern=[[-1, 128]],
                                                  compare_op=ALU.not_equal, fill=1.0, base=0, channel_multiplier=1),
                  r=[self.ident], w=[self.ident])
        self.pool(lambda: nc.gpsimd.memset(self.ones_bf[:], 1.0), w=[self.ones_bf])
        self.pool(lambda: nc.gpsimd.memset(self.ones_f[:], 1.0), w=[self.ones_f])
        self.dve(lambda: nc.vector.tensor_copy(self.ident_bf[:], self.ident[:]), r=[self.ident], w=[self.ident_bf])
        self.dma(self.gn[:], self.gains, r=[self.Bin], w=[self.gn])
        self.dma(self.mk[:].rearrange("p a b -> p (a b)"), self.masks, r=[self.Bin], w=[self.mk])

    def load_x(self):
        nc = self.nc
        self.begin()
        xins = [self.T("xin%d" % i, [128, D], F32) for i in range(2)]
        ntt = self.seq // 128
        for tt in range(ntt + 1):
            xin = xins[tt % 2]
            if tt < ntt:
                rows, src, c0 = 128, self.x_p[tt * 128:(tt + 1) * 128, :], tt * 128
                xb = self.xTb[tt // 4]
            else:
                rows, src, c0 = NS, self.x_s, self.seq
                xb = self.xTb[self.nt]
            self.dma(xin[0:rows, :], src, r=[self.Bin], w=[xin])
            for half in range(2):
                ps = self.P[(tt % 2) * 2 + half]

                def tr():
                    ins = None
                    for j in range(4):
                        kc = half * 4 + j
                        ins = nc.tensor.transpose(ps[:, j * 128:j * 128 + rows], xin[0:rows, kc * 128:(kc + 1) * 128],
                                                  self.ident[0:rows, 0:rows])
                    return ins
                self.pe(tr, r=[xin, self.ident], w=[ps])
                pv = ps[:, :].rearrange("p (j c) -> p j c", j=4)[:, :, 0:rows]
                dst = self.xT[:, half * 4:half * 4 + 4, c0:c0 + rows]
                if half == 0:
                    self.dve(lambda: nc.vector.tensor_copy(dst, pv), r=[ps], w=[xb])
                else:
                    self.act(lambda: nc.scalar.copy(dst, pv), r=[ps], w=[xb])
        self.end()

    def final(self):
        nc = self.nc
        self.begin()
        sq = self.T("sq", [128, KC, 528], BF16)
        rstd = self.T("rstd", [128, 528], F32)
        yos = [self.T("yo%d" % i, [128, D], F32) for i in range(2)]
        for ti, segs in enumerate(self.tiles()):
            xb = self.tile_bufs(ti)
            for (c0, n, l0) in segs:
                ps = self.P[7]
                self.act(lambda: nc.scalar.activation(sq[:, :, l0:l0 + n], self.xT[:, :, c0:c0 + n], AF.Square),
                         r=xb, w=[sq])

                def mm():
                    ins = None
                    for kc in range(KC):
                        ins = nc.tensor.matmul(ps[:, 0:n], self.ones_bf[:], sq[:, kc, l0:l0 + n],
                                               start=(kc == 0), stop=(kc == KC - 1))
                    return ins
                self.pe(mm, r=[sq, self.ones_bf], w=[ps])
                self.act(lambda: nc.scalar.activation(rstd[:, l0:l0 + n], ps[:, 0:n], AF.Ln, bias=EPS, scale=1.0 / D),
                         r=[ps], w=[rstd])
                self.act(lambda: nc.scalar.activation(rstd[:, l0:l0 + n], rstd[:, l0:l0 + n], AF.Exp, scale=-0.5),
                         r=[rstd], w=[rstd])
                for kc in range(KC):
                    self.dve(lambda kc=kc: nc.vector.scalar_tensor_tensor(
                        out=self.xT[:, kc, c0:c0 + n], in0=self.xT[:, kc, c0:c0 + n], scalar=self.gcol(12, kc),
                        in1=rstd[:, l0:l0 + n], op0=ALU.mult, op1=ALU.mult), r=xb + [rstd, self.gn], w=xb)
        ntt = self.seq // 128
        for tt in range(ntt + 1):
            yo = yos[tt % 2]
            if tt < ntt:
                rows, dst, c0 = 128, self.y_p[tt * 128:(tt + 1) * 128, :], tt * 128
                xb = self.xTb[tt // 4]
            else:
                rows, dst, c0 = NS, self.y_s, self.seq
                xb = self.xTb[self.nt]
            for half in range(2):
                ps = self.P[(tt % 2) * 2 + half]

                def tr():
                    ins = None
                    for j in range(4):
                        kc = half * 4 + j
                        ins = nc.tensor.transpose(ps[0:rows, j * 128:(j + 1) * 128], self.xT[:, kc, c0:c0 + rows],
                                                  self.ident[:])
                    return ins
                self.pe(tr, r=[xb, self.ident], w=[ps])
                if half == 0:
                    self.dve(lambda: nc.vector.tensor_copy(yo[0:rows, 0:512], ps[0:rows, :]), r=[ps], w=[yo])
                else:
                    self.act(lambda: nc.scalar.copy(yo[0:rows, 512:1024], ps[0:rows, :]), r=[ps], w=[yo])
            self.dma(dst, yo[0:rows, :], r=[yo], w=[self.Bout])
        self.end()

    def mixer(self, l):
        if l % 2 == 0:
            self.even_mixer(l)
        else:
            self.odd_mixer(l)

    def emit(self):
        self.consts()
        self.load_x()
        for l in self.layers:
            if self.cfg.get("ffn", True):
                self.ffn(l, 0)
            if not self.cfg.get("nomix"):
                self.mixer(l)
            if self.cfg.get("ffn", True):
                self.ffn(l, 1)
        self.final()
        self.S.finish()


def even_alloc(self, nc_):
    G = {}
    T = self.T
    G["hn"] = T("hn", [128, KC, nc_], BF16)
    G["sqm"] = T("sqm", [128, KC, nc_], BF16)
    G["rstd"] = T("rstd", [128, nc_], F32)
    G["qT"] = T("qT", [128, 4, nc_], BF16)
    G["acc"] = [T("acc%d" % i, [128, nc_], F32) for i in range(2)]
    G["cq"] = T("cq", [128, 12, nc_], F32)
    G["zs"] = T("zs", [128, 4, nc_], BF16)
    oT = Tile(self.nc, "oT", [128, 4, nc_], F32, handle=G["hn"].t.bitcast(F32).reshape([128, 4, nc_]))
    oT.b = G["hn"].b
    G["oT"] = oT
    G["cw"] = T("cw", [128, 12, 4], F32)
    G["sm"] = T("sm", [128, 17], F32)
    G["eA"] = T("eA", [128, 4], F32)
    G["sq4"] = T("sq4", [128, 2, nc_], BF16)
    return G


def even_common(self, G, e):
    nc = self.nc
    self.dma(G["cw"][:].rearrange("p a b -> p (a b)"), self.econv[e], r=[self.Bin], w=[G["cw"]])
    self.dma(G["sm"][:], self.esm[e], r=[self.Bin], w=[G["sm"]])
    self.act(lambda: nc.scalar.activation(G["eA"][:], G["sm"][:, 0:4], AF.Exp), r=[G["sm"]], w=[G["eA"]])
    self.dve(lambda: nc.vector.tensor_scalar_mul(G["eA"][:], G["eA"][:], -1.0), r=[G["eA"]], w=[G["eA"]])


def even_inproj_fm(self, G, n, conv_fn, k_dst):
    nc = self.nc
    hn = G["hn"]
    j = 0
    for blk in range(6):
        nb = EW_N[blk]
        w = self.ws.get(nb)
        ncols = nb // KC
        wv = w[:, 0:nb].rearrange("p (k c) -> p k c", k=KC)
        for cc in range(ncols // 128):
            ps = self.P[j % 2]

            def mm():
                ins = None
                for kc in range(KC):
                    ins = nc.tensor.matmul(ps[:, 0:n], wv[:, kc, cc * 128:(cc + 1) * 128], hn[:, kc, 0:n],
                                           start=(kc == 0), stop=(kc == KC - 1))
                return ins
            self.pe(mm, r=[w, hn], w=[ps])
            if j < 4:
                self.act(lambda: nc.scalar.copy(G["qT"][:, j, 0:n], ps[:, 0:n]), r=[ps], w=[G["qT"]])
            elif j == 4:
                self.act(lambda: nc.scalar.copy(k_dst, ps[:, 0:n]), r=[ps], w=[G["kTa"]])
            elif j < 17:
                conv_fn(j - 5, ps)
            else:
                self.act(lambda: nc.scalar.activation(G["zs"][:, j - 17, 0:n], ps[:, 0:n], AF.Silu),
                         r=[ps], w=[G["zs"]])
            j += 1
    assert j == EV_FM


def gates_from_ba(self, G, ba_ps_ap, rows, beta_dst, g_dst, tmp, rd, wr):
    nc = self.nc
    sm = G["sm"]
    self.act(lambda: nc.scalar.activation(beta_dst, ba_ps_ap[:, 0:4], AF.Sigmoid), r=rd, w=wr)
    self.dve(lambda: nc.vector.tensor_tensor(out=tmp[0:rows, 0:4], in0=ba_ps_ap[:, 4:8], in1=sm[0:rows, 4:8], op=ALU.add),
             r=rd + [sm], w=[tmp])
    self.act(lambda: nc.scalar.activation(tmp[0:rows, 0:4], tmp[0:rows, 0:4], AF.Exp), r=[tmp], w=[tmp])
    self.act(lambda: nc.scalar.activation(tmp[0:rows, 0:4], tmp[0:rows, 0:4], AF.Ln, bias=1.0, scale=1.0), r=[tmp], w=[tmp])
    self.dve(lambda: nc.vector.tensor_tensor(out=g_dst, in0=tmp[0:rows, 0:4], in1=G["eA"][0:rows, :], op=ALU.mult),
             r=[tmp, G["eA"]], w=wr)


def l2norm_heads(self, G, n):
    nc = self.nc
    cq, sq4 = G["cq"], G["sq4"]
    for c in range(8):
        ps = self.P[c % 2]
        self.act(lambda: nc.scalar.activation(sq4[:, c % 2, 0:n], cq[:, c, 0:n], AF.Square), r=[cq], w=[sq4])
        self.pe(lambda: nc.tensor.matmul(ps[:, 0:n], self.ones_bf[:], sq4[:, c % 2, 0:n], start=True, stop=True),
                r=[sq4, self.ones_bf], w=[ps])
        rs = G["acc"][c % 2]
        self.act(lambda: nc.scalar.activation(rs[:, 0:n], ps[:, 0:n], AF.Ln, bias=EPS, scale=1.0), r=[ps], w=[rs])
        self.act(lambda: nc.scalar.activation(rs[:, 0:n], rs[:, 0:n], AF.Exp, scale=-0.5,
                                              bias=(math.log(128.0 ** -0.5) if c < 4 else 0.0)), r=[rs], w=[rs])
        self.dve(lambda: nc.vector.tensor_tensor(out=cq[:, c, 0:n], in0=cq[:, c, 0:n], in1=rs[:, 0:n], op=ALU.mult),
                 r=[cq, rs], w=[cq])


def gdn_post(self, G, n, e):
    nc = self.nc
    oT, sq4, mix = G["oT"], G["sq4"], G["sqm"]
    for h in range(4):
        ps = self.P[h % 2]
        self.act(lambda: nc.scalar.activation(sq4[:, h % 2, 0:n], oT[:, h, 0:n], AF.Square), r=[oT], w=[sq4])
        self.pe(lambda: nc.tensor.matmul(ps[:, 0:n], self.ones_bf[:], sq4[:, h % 2, 0:n], start=True, stop=True),
                r=[sq4, self.ones_bf], w=[ps])
        rs = G["acc"][h % 2]
        self.act(lambda: nc.scalar.activation(rs[:, 0:n], ps[:, 0:n], AF.Ln, bias=EPS, scale=1.0 / 128.0), r=[ps], w=[rs])
        self.act(lambda: nc.scalar.activation(rs[:, 0:n], rs[:, 0:n], AF.Exp, scale=-0.5), r=[rs], w=[rs])
        self.dve(lambda: nc.vector.scalar_tensor_tensor(out=rs[:, 0:n], in0=oT[:, h, 0:n], scalar=G["sm"][:, 16:17],
                                                        in1=rs[:, 0:n], op0=ALU.mult, op1=ALU.mult),
                 r=[oT, rs, G["sm"]], w=[rs])
        self.dve(lambda: nc.vector.tensor_tensor(out=mix[:, 4 + h, 0:n], in0=rs[:, 0:n], in1=G["zs"][:, h, 0:n], op=ALU.mult),
                 r=[rs, G["zs"]], w=[mix])


def even_outproj(self, G, segs_bufs, c0, n):
    nc = self.nc
    mix = G["sqm"]
    for blk in range(2):
        w = self.ws.get(4096)
        wv = w[:, 0:4096].rearrange("p (k c) -> p k c", k=KC)
        for dc in range(4):
            dch = blk * 4 + dc
            ps = self.P[dch % 2]

            def mm():
                ins = None
                for kc in range(KC):
                    ins = nc.tensor.matmul(ps[:, 0:n], wv[:, kc, dc * 128:(dc + 1) * 128], mix[:, kc, 0:n],
                                           start=(kc == 0), stop=(kc == KC - 1))
                return ins
            self.pe(mm, r=[w, mix], w=[ps])
            xs = self.xT[:, dch, c0:c0 + n]
            self.dve(lambda: nc.vector.tensor_tensor(out=xs, in0=xs, in1=ps[:, 0:n], op=ALU.add),
                     r=[ps] + segs_bufs, w=segs_bufs)


def swa_block(self, G, W, qb, ql):
    nc = self.nc
    mix = G["sqm"]
    sm = G["sm"]
    k0 = (qb - 1) * 128 if qb > 0 else 0
    nk = 256 if qb > 0 else 128
    nkt = nk // 128

    def pairgen(c):
        q = c % 2
        s_ps, t_ps, o_ps = self.P[2 + q], self.P[4 + q], self.P[6 + q]
        sb, pn, PT, col = W["sb"][q], W["pn"][q], W["PT"][q], W["col"][q]
        tv = t_ps[:, :].bitcast(BF16)
        for kv in range(2):
            h = kv * 4 + c
            rows = slice(kv * 64, (kv + 1) * 64)
            yield self.pe(lambda: nc.tensor.matmul(s_ps[:, 0:nk], G["qT"][rows, c, ql:ql + 128], G["kTa"][rows, k0:k0 + nk],
                                                   start=True, stop=True), r=[G["qT"], G["kTa"]], w=[s_ps])
            yield self.dve(lambda: nc.vector.scalar_tensor_tensor(out=sb[:, 0:nk], in0=s_ps[:, 0:nk], scalar=0.125,
                                                                  in1=G["bias"][:, h, 256 - nk:256], op0=ALU.mult, op1=ALU.add),
                           r=[s_ps, G["bias"]], w=[sb])
            yield self.dve(lambda: nc.vector.reduce_max(out=col[:, 0:1], in_=sb[:, 0:nk], axis=AX.X), r=[sb], w=[col])
            yield self.dve(lambda: nc.vector.tensor_tensor(out=col[:, 0:1], in0=col[:, 0:1], in1=sm[:, 8 + h:9 + h], op=ALU.max),
                           r=[col, sm], w=[col])
            yield self.dve(lambda: nc.vector.tensor_scalar_mul(col[:, 1:2], col[:, 0:1], -1.0), r=[col], w=[col])
            yield self.act(lambda: nc.scalar.activation(sb[:, 0:nk], sb[:, 0:nk], AF.Exp, bias=col[:, 1:2], scale=1.0),
                           r=[sb, col], w=[sb])
            yield self.dve(lambda: nc.vector.reduce_sum(out=col[:, 2:3], in_=sb[:, 0:nk], axis=AX.X), r=[sb], w=[col])
            yield self.act(lambda: nc.scalar.activation(col[:, 3:4], sm[:, 8 + h:9 + h], AF.Exp, bias=col[:, 1:2], scale=1.0),
                           r=[sm, col], w=[col])
            yield self.dve(lambda: nc.vector.tensor_tensor(out=col[:, 2:3], in0=col[:, 2:3], in1=col[:, 3:4], op=ALU.add),
                           r=[col], w=[col])
            yield self.dve(lambda: nc.vector.reciprocal(col[:, 4:5], col[:, 2:3]), r=[col], w=[col])
            yield self.dve(lambda: nc.vector.tensor_scalar_mul(pn[:, 0:nk], sb[:, 0:nk], col[:, 4:5]), r=[sb, col], w=[pn])

            def tr():
                ins = None
                for kt in range(nkt):
                    ins = nc.tensor.transpose(tv[:, kt * 128:(kt + 1) * 128], pn[:, kt * 128:(kt + 1) * 128],
                                              self.ident_bf[:])
                return ins
            yield self.pe(tr, r=[pn, self.ident_bf], w=[t_ps])
            yield self.act(lambda: nc.scalar.copy(PT[:, 0:nk], tv[:, 0:nk]), r=[t_ps], w=[PT])

            def pv():
                ins = None
                for kt in range(nkt):
                    ins = nc.tensor.matmul(o_ps[:, 0:128], G["Va"][:, k0 // 128 + kt, kv * 128:(kv + 1) * 128],
                                           PT[:, kt * 128:(kt + 1) * 128],
                                           start=(kv == 0 and kt == 0), stop=(kv == 1 and kt == nkt - 1))
                return ins
            yield self.pe(pv, r=[G["Va"], PT], w=[o_ps])
        yield self.act(lambda: nc.scalar.copy(mix[:, c, ql:ql + 128], o_ps[:, 0:128]), r=[o_ps], w=[mix])

    for pr in range(2):
        gens = [pairgen(2 * pr), pairgen(2 * pr + 1)]
        while gens:
            for g in list(gens):
                try:
                    next(g)
                except StopIteration:
                    gens.remove(g)


def gdn_subtile(self, G, W, st, cs):
    nc = self.nc
    cq = G["cq"]
    mk = self.mk
    ident = self.ident
    beta_t, g_t = G["beta_t"], G["g_t"]
    sc = W["sc"]
    R = W["R"]
    P = self.P
    Sst, Sbf, oT = G["Sst"], G["Sbf"], G["oT"]
    self.pe(lambda: nc.tensor.matmul(P[4][:, 0:4], mk[:, 0, :], g_t[:, st, :], start=True, stop=True), r=[mk, g_t], w=[P[4]])
    self.pe(lambda: nc.tensor.matmul(P[4][:, 4:8], mk[:, 4, :], g_t[:, st, :], start=True, stop=True), r=[mk, g_t], w=[P[4]])
    self.dve(lambda: nc.vector.tensor_copy(sc[:, 0:8], P[4][:, 0:8]), r=[P[4]], w=[sc])
    self.dve(lambda: nc.vector.tensor_tensor(out=sc[:, 8:12], in0=sc[:, 4:8], in1=sc[:, 0:4], op=ALU.subtract), r=[sc], w=[sc])
    self.act(lambda: nc.scalar.activation(sc[:, 8:12], sc[:, 8:12], AF.Exp), r=[sc], w=[sc])
    self.act(lambda: nc.scalar.activation(sc[:, 12:16], sc[:, 0:4], AF.Exp), r=[sc], w=[sc])
    self.dve(lambda: nc.vector.tensor_tensor(out=sc[:, 16:20], in0=sc[:, 12:16], in1=beta_t[:, st, :], op=ALU.mult),
             r=[sc, beta_t], w=[sc])
    self.dve(lambda: nc.vector.tensor_tensor(out=R[:, :, 0:128], in0=mk[:, 0, :].unsqueeze(1).to_broadcast([128, 4, 128]),
                                             in1=g_t[:, st, :].unsqueeze(2).to_broadcast([128, 4, 128]), op=ALU.mult),
             r=[mk, g_t], w=[R])
    self.dve(lambda: nc.vector.tensor_tensor(out=R[:, :, 128:256], in0=ident[:].unsqueeze(1).to_broadcast([128, 4, 128]),
                                             in1=beta_t[:, st, :].unsqueeze(2).to_broadcast([128, 4, 128]), op=ALU.mult),
             r=[ident, beta_t], w=[R])
    for hh in range(2):
        self.pe(lambda: nc.tensor.matmul(P[hh][:, 0:512], self.ones_f[:], R[:, 2 * hh:2 * hh + 2, :], start=True, stop=True),
                r=[self.ones_f, R], w=[P[hh]])
    bcS = W["R"]
    self.dve(lambda: nc.vector.tensor_copy(bcS[:, 0:2, :], P[0][:, 0:512].rearrange("p (a b) -> p a b", a=2)), r=[P[0]], w=[bcS])
    self.act(lambda: nc.scalar.copy(bcS[:, 2:4, :], P[1][:, 0:512].rearrange("p (a b) -> p a b", a=2)), r=[P[1]], w=[bcS])

    def gcbc(h):
        return bcS[:, h, 0:128]

    def betabc(h):
        return bcS[:, h, 128:256]
    for h in range(4):
        self.act(lambda: nc.scalar.activation(sc[:, 20 + 2 * h:22 + 2 * h], gcbc(h)[:, 63:128:64], AF.Exp), r=[bcS], w=[sc])

    def head(h):
        q = h % 2
        tw = W["set"][q]
        PA, PN, PT_ = P[2 + q], P[4 + q], P[6 + q]
        kn = cq[:, 4 + h, cs:cs + 128]
        qn = cq[:, h, cs:cs + 128]
        vv = cq[:, 8 + h, cs:cs + 128]
        bc = bcS
        yield self.pe(lambda: nc.tensor.transpose(PA[:, 0:128], kn, ident[:]), r=[cq, ident], w=[PA])
        yield self.pe(lambda: nc.tensor.transpose(PA[:, 128:256], vv, ident[:]), r=[cq, ident], w=[PA])
        yield self.pe(lambda: nc.tensor.matmul(PA[:, 256:384], kn, kn, start=True, stop=True), r=[cq], w=[PA])
        yield self.pe(lambda: nc.tensor.matmul(PA[:, 384:512], kn, qn, start=True, stop=True), r=[cq], w=[PA])
        yield self.dve(lambda: nc.vector.tensor_scalar_mul(tw["kbg"][:], PA[:, 0:128], sc[:, 16 + h:17 + h]), r=[PA, sc], w=[tw["kbg"]])
        yield self.act(lambda: nc.scalar.mul(tw["kd"][:], PA[:, 0:128], sc[:, 8 + h:9 + h]), r=[PA, sc], w=[tw["kd"]])
        yield self.dve(lambda: nc.vector.tensor_scalar_mul(tw["vb"][:], PA[:, 128:256], beta_t[:, st, h:h + 1]),
                       r=[PA, beta_t], w=[tw["vb"]])
        yield self.dve(lambda: nc.vector.scalar_tensor_tensor(out=tw["tA"][:], in0=gcbc(h), scalar=sc[:, h:h + 1], in1=mk[:, 1, :],
                                                              op0=ALU.subtract, op1=ALU.max), r=[bc, sc, mk], w=[tw["tA"]])
        yield self.act(lambda: nc.scalar.activation(tw["Es"][:], tw["tA"][:], AF.Exp, scale=-1.0), r=[tw["tA"]], w=[tw["Es"]])
        yield self.dve(lambda: nc.vector.scalar_tensor_tensor(out=tw["tB"][:], in0=gcbc(h), scalar=sc[:, h:h + 1], in1=mk[:, 2, :],
                                                              op0=ALU.subtract, op1=ALU.min), r=[bc, sc, mk], w=[tw["tB"]])
        yield self.act(lambda: nc.scalar.activation(tw["ETd"][:], tw["tB"][:], AF.Exp), r=[tw["tB"]], w=[tw["ETd"]])
        yield self.dve(lambda: nc.vector.tensor_tensor(out=tw["ETs"][:], in0=tw["ETd"][:], in1=mk[:, 3, :], op=ALU.mult),
                       r=[tw["ETd"], mk], w=[tw["ETs"]])
        Pc, Qc, X, Y = tw["Pm"][0], tw["Qm"][0], tw["X"][0], tw["Y"][0]
        yield self.dve(lambda: nc.vector.scalar_tensor_tensor(out=Qc[:], in0=PA[:, 256:384], scalar=beta_t[:, st, h:h + 1],
                                                              in1=tw["Es"][:], op0=ALU.mult, op1=ALU.mult),
                       r=[PA, beta_t, tw["Es"]], w=[Qc])
        yield self.dve(lambda: nc.vector.tensor_tensor(out=tw["tA"][:], in0=PA[:, 256:384], in1=tw["ETs"][:], op=ALU.mult),
                       r=[PA, tw["ETs"]], w=[tw["tA"]])
        yield self.dve(lambda: nc.vector.tensor_tensor(out=Pc[:], in0=tw["tA"][:], in1=betabc(h), op=ALU.mult),
                       r=[tw["tA"], bc], w=[Pc])
        yield self.dve(lambda: nc.vector.tensor_tensor(out=tw["AqkT"][:], in0=PA[:, 384:512], in1=tw["ETd"][:], op=ALU.mult),
                       r=[PA, tw["ETd"]], w=[tw["AqkT"]])
        yield self.act(lambda: nc.scalar.activation(tw["tB"][:], gcbc(h), AF.Exp), r=[bc], w=[tw["tB"]])
        yield self.dve(lambda: nc.vector.tensor_tensor(out=tw["qgT"][:], in0=qn, in1=tw["tB"][:], op=ALU.mult),
                       r=[cq, tw["tB"]], w=[tw["qgT"]])
        yield self.dve(lambda: nc.vector.tensor_tensor(out=X[:], in0=ident[:], in1=Pc[:], op=ALU.subtract), r=[ident, Pc], w=[X])
        for k in range(1, 6):
            bk = PN
            last = (k == 5)
            Pn, Qn, Xn = tw["Pm"][k % 2], tw["Qm"][k % 2], tw["X"][k % 2]

            def sqr():
                ins = nc.tensor.matmul(bk[:, 128:256], Pc[:], Qc[:], start=True, stop=True)
                if not last:
                    ins = nc.tensor.matmul(bk[:, 0:128], Qc[:], Pc[:], start=True, stop=True)
                return ins
            yield self.pe(sqr, r=[Pc, Qc], w=[bk])
            yield self.dve(lambda: nc.vector.tensor_copy(Qn[:], bk[:, 128:256]), r=[bk], w=[Qn])
            if not last:
                yield self.act(lambda: nc.scalar.copy(Pn[:], bk[:, 0:128]), r=[bk], w=[Pn])
            yield self.pe(lambda: nc.tensor.matmul(bk[:, 256:384], Qn[:], X[:], start=True, stop=True), r=[X, Qn], w=[bk])
            yield self.dve(lambda: nc.vector.tensor_tensor(out=Xn[:], in0=X[:], in1=bk[:, 256:384], op=ALU.add), r=[X, bk], w=[Xn])
            Pc, Qc, X = Pn, Qn, Xn
        yield self.pe(lambda: nc.tensor.matmul(PT_[:, 0:128], X[:], tw["vb"][:], start=True, stop=True), r=[X, tw["vb"]], w=[PT_])
        yield self.pe(lambda: nc.tensor.matmul(PT_[:, 128:256], tw["kbg"][:], X[:], start=True, stop=True), r=[X, tw["kbg"]], w=[PT_])
        yield self.act(lambda: nc.scalar.copy(tw["u"][:], PT_[:, 0:128]), r=[PT_], w=[tw["u"]])
        yield self.dve(lambda: nc.vector.tensor_copy(tw["wTa"][:, 0:64], PT_[:, 128:192]), r=[PT_], w=[tw["wTa"]])
        yield self.dve(lambda: nc.vector.tensor_copy(tw["wTz"][:, 64:128], PT_[:, 192:256]), r=[PT_], w=[tw["wTz"]])
        for ck in range(2):
            rr = slice(ck * 64, (ck + 1) * 64)
            if ck == 0:
                yield self.pe(lambda: nc.tensor.matmul(PT_[0:64, 256:384], tw["wTa"][:, 0:64], Sbf[:, h, :], start=True, stop=True),
                              r=[tw["wTa"], Sbf], w=[PT_])
            else:
                yield self.pe(lambda: nc.tensor.matmul(PT_[:, 256:384], tw["wTz"][:], Sbf[:, h, :], start=True, stop=True),
                              r=[tw["wTz"], Sbf], w=[PT_])
            yield self.dve(lambda: nc.vector.tensor_tensor(out=tw["vnew"][rr, :], in0=tw["u"][rr, :], in1=PT_[rr, 256:384],
                                                           op=ALU.subtract), r=[tw["u"], PT_], w=[tw["vnew"]])

            def omm():
                nc.tensor.matmul(PA[:, ck * 64:(ck + 1) * 64], Sbf[:, h, :], tw["qgT"][:, rr], start=True, stop=False)
                return nc.tensor.matmul(PA[:, ck * 64:(ck + 1) * 64], tw["vnew"][rr, :], tw["AqkT"][rr, rr],
                                        start=False, stop=True)
            yield self.pe(omm, r=[Sbf, tw["qgT"], tw["vnew"], tw["AqkT"]], w=[PA])
            yield self.act(lambda: nc.scalar.copy(oT[:, h, cs + ck * 64:cs + (ck + 1) * 64], PA[:, ck * 64:(ck + 1) * 64]),
                           r=[PA], w=[oT])
            yield self.pe(lambda: nc.tensor.matmul(PT_[:, 384:512], tw["kd"][rr, :], tw["vnew"][rr, :], start=True, stop=True),
                          r=[tw["kd"], tw["vnew"]], w=[PT_])
            yield self.dve(lambda: nc.vector.scalar_tensor_tensor(out=Sst[:, h, :], in0=Sst[:, h, :],
                                                                  scalar=sc[:, 20 + 2 * h + ck:21 + 2 * h + ck],
                                                                  in1=PT_[:, 384:512], op0=ALU.mult, op1=ALU.add),
                           r=[Sst, sc, PT_], w=[Sst])
            yield self.act(lambda: nc.scalar.copy(Sbf[:, h, :], Sst[:, h, :]), r=[Sst], w=[Sbf])

    for pair in range(2):
        gens = [head(2 * pair), head(2 * pair + 1)]
        while gens:
            for g in list(gens):
                try:
                    next(g)
                except StopIteration:
                    gens.remove(g)


def even_mixer(self, l):
    nc = self.nc
    e = l // 2
    nsub = self.seq // 128
    self.begin()
    G = even_alloc(self, 512)
    T = self.T
    G["pre"] = [T("pre%d" % i, [128, 515], F32) for i in range(2)]
    G["kTa"] = T("kTa", [128, self.seq], BF16)
    G["Va"] = T("Va", [128, nsub, 256], BF16)
    G["carry"] = T("carry", [128, 12, 3], F32)
    U = T("U", [128, 2432], F32)
    G["bias"] = Tile(nc, "bias", [128, 8, 256], F32, handle=U.t[:, 0:2048].rearrange("p (a b) -> p a b", a=8))
    G["Sst"] = T("Sst", [128, 4, 128], F32)
    G["Sbf"] = T("Sbf", [128, 4, 128], BF16)
    G["beta_t"] = T("beta_t", [128, 4, 4], F32)
    G["g_t"] = T("g_t", [128, 4, 4], F32)
    gtmp = T("gtmp", [128, 4], F32)
    kvo = T("kvo", [128, 256], F32)
    gco = T("gco", [128, 1536], F32)
    W = {"sb": [T("sb%d" % i, [128, 256], F32) for i in range(2)],
         "pn": [T("pn%d" % i, [128, 256], BF16) for i in range(2)],
         "PT": [T("PT%d" % i, [128, 256], BF16) for i in range(2)],
         "col": [T("col%d" % i, [128, 8], F32) for i in range(2)],
         "sc": T("sc", [128, 40], F32),
         "R": T("R", [128, 4, 256], F32),
         "set": []}
    d = {}
    for nm in ("kbg", "vb", "tA", "Es", "tB", "ETd", "ETs", "u"):
        d[nm] = T("%s0" % nm, [128, 128], F32)
    for nm in ("kd", "AqkT", "qgT", "wTa", "wTz", "vnew"):
        d[nm] = T("%s0" % nm, [128, 128], BF16)
    for nm in ("Pm", "Qm", "X", "Y"):
        d[nm] = [T("%s0_%d" % (nm, j), [128, 128], F32) for j in range(2)]
    W["set"].append(d)
    self.dve(lambda: nc.vector.memset(d["wTz"][:], 0.0), w=[d["wTz"]])
    d2 = {}
    off = [0]

    def uview(name, dt):
        if dt == F32:
            v = U.t[:, off[0]:off[0] + 128]
            off[0] += 128
        else:
            v = U.t[:, off[0]:off[0] + 64].bitcast(BF16)
            off[0] += 64
        return Tile(nc, name, [128, 128], dt, handle=v)
    for nm in ("kbg", "vb", "tA", "Es", "tB", "ETd", "ETs", "u"):
        d2[nm] = uview(nm + "1", F32)
    for nm in ("Pm", "Qm", "X", "Y"):
        d2[nm] = [uview("%s1_%d" % (nm, j), F32) for j in range(2)]
    for nm in ("kd", "AqkT", "qgT", "wTa", "wTz", "vnew"):
        d2[nm] = uview(nm + "1", BF16)
    assert off[0] == 2432
    W["set"].append(d2)
    even_common(self, G, e)
    self.dve(lambda: nc.vector.memset(G["carry"][:], 0.0), w=[G["carry"]])
    self.dve(lambda: nc.vector.memset(G["Sst"][:], 0.0), w=[G["Sst"]])
    self.dve(lambda: nc.vector.memset(G["Sbf"][:], 0.0), w=[G["Sbf"]])
    cw = G["cw"]
    for ti in range(self.nt):
        c0 = ti * 512
        xb = [self.xTb[ti]]
        self.S.barrier()
        self.dma(G["bias"][:].rearrange("p a b -> p (a b)"), self.swabias, r=[self.Bin], w=[G["bias"]])
        self.rmsnorm_cols(xb, c0, 512, 4 + l, G["hn"], 0, G["sqm"], G["rstd"])
        cnt = [0]

        def conv_fn(ch, ps):
            pre = G["pre"][cnt[0] % 2]
            acc = G["acc"][cnt[0] % 2]
            cnt[0] += 1
            self.dve(lambda: nc.vector.tensor_copy(pre[:, 0:3], G["carry"][:, ch, :]), r=[G["carry"]], w=[pre])
            self.act(lambda: nc.scalar.copy(pre[:, 3:515], ps[:, 0:512]), r=[ps], w=[pre])
            self.dve(lambda: nc.vector.tensor_copy(G["carry"][:, ch, :], pre[:, 512:515]), r=[pre], w=[G["carry"]])
            self.dve(lambda: nc.vector.tensor_scalar_mul(acc[:], pre[:, 0:512], cw[:, ch, 0:1]), r=[pre, cw], w=[acc])
            for i in range(1, 4):
                self.dve(lambda: nc.vector.scalar_tensor_tensor(out=acc[:], in0=pre[:, i:i + 512], scalar=cw[:, ch, i:i + 1],
                                                                 in1=acc[:], op0=ALU.mult, op1=ALU.add), r=[pre, cw, acc], w=[acc])
            self.act(lambda: nc.scalar.activation(G["cq"][:, ch, :], acc[:], AF.Silu), r=[acc], w=[G["cq"]])
        even_inproj_fm(self, G, 512, conv_fn, G["kTa"][:, c0:c0 + 512])
        w = self.ws.get(EW_N[6])
        wv = w[:, 0:EW_N[6]].rearrange("p (k c) -> p k c", k=KC)
        for st in range(4):
            gs = ti * 4 + st
            ps = self.P[2 + st % 2]

            def mm():
                ins = None
                for kc in range(KC):
                    ins = nc.tensor.matmul(ps[:, 0:EV_TOK], G["hn"][:, kc, st * 128:(st + 1) * 128], wv[:, kc, :],
                                           start=(kc == 0), stop=(kc == KC - 1))
                return ins
            self.pe(mm, r=[w, G["hn"]], w=[ps])
            self.act(lambda: nc.scalar.copy(G["Va"][:, gs, :], ps[:, 0:256]), r=[ps], w=[G["Va"]])
            gates_from_ba(self, G, ps[:, 384:392], 128, G["beta_t"][:, st, :], G["g_t"][:, st, :], gtmp, [ps],
                          [G["beta_t"], G["g_t"]])
            if gs == nsub - 1 and not self.cfg.get("skip_kvo"):
                self.act(lambda: nc.scalar.copy(kvo[:, 0:128], ps[:, 256:384]), r=[ps], w=[kvo])
                self.act(lambda: nc.scalar.copy(kvo[:, 128:256], ps[:, 0:128]), r=[ps], w=[kvo])
                self.dve(lambda: nc.vector.tensor_tensor(out=kvo[:, 128:256], in0=kvo[:, 128:256], in1=ps[:, 128:256], op=ALU.add),
                         r=[ps, kvo], w=[kvo])
                pass
        if self.cfg.get("skip_swa") or self.cfg.get("skip_gdn") or self.cfg.get("gdn_stop", 6) < 6:
            self.dve(lambda: nc.vector.memset(G["sqm"][:], 0.0), w=[G["sqm"]])
            self.dve(lambda: nc.vector.memset(G["oT"][:], 0.0), w=[G["oT"]])
        if not self.cfg.get("skip_swa"):
            for qi in range(4):
                swa_block(self, G, W, ti * 4 + qi, qi * 128)
        self.S.barrier()
        self.dve(lambda: nc.vector.memset(W["set"][1]["wTz"][:], 0.0), w=[W["set"][1]["wTz"]])
        l2norm_heads(self, G, 512)
        if not self.cfg.get("skip_gdn"):
            for st in range(4):
                gdn_subtile(self, G, W, st, st * 128)
        gdn_post(self, G, 512, e)
        even_outproj(self, G, xb, c0, 512)
    self.out_dma(self.o_pk[e], kvo[:, 0:128], r=[kvo])
    self.out_dma(self.o_pv[e], kvo[:, 128:256], r=[kvo])
    for ch in range(12):
        self.out_dma(self.o_pgc[e][:, ch * 128:(ch + 1) * 128].rearrange("i p -> p i"), G["carry"][:, ch, :], r=[G["carry"]])
    self.out_dma(self.o_pgs[e].rearrange("h k v -> k h v"), G["Sst"][:], r=[G["Sst"]])
    self.end()
    if self.cfg.get("skip_dec"):
        for b in range(7):
            self.ws.get(EW_N[b])
        for b in range(2):
            self.ws.get(4096)
    else:
        even_decode(self, l)


def even_decode(self, l):
    nc = self.nc
    e = l // 2
    n = NS
    c0 = self.seq
    P = self.P
    ident, mk = self.ident, self.mk
    self.begin()
    T = self.T
    G = even_alloc(self, n)
    G["kTa"] = T("kTs", [128, n], BF16)
    xx = T("xx", [128, 12, n, 4], F32)
    hs = T("hs", [48, 1536], F32)
    gnew = T("gnew", [n, 1536], F32)
    knv = T("knv", [n, 256], F32)
    gts = T("gts", [n, 16], F32)
    Kc = T("Kc", [128, n, 128], F32)
    Vc = T("Vc", [128, n, 128], F32)
    Qb = T("Qb", [128, n, 8], F32)
    KTb = [T("KTb%d" % i, [128, 128], F32) for i in range(2)]
    sT = T("sT", [128, 128], F32)
    kTn = T("kTn", [128, n], F32)
    vTn = T("vTn", [128, n], F32)
    sf = T("sf", [128, 132], F32)
    pnf = T("pnf", [128, 132], F32)
    col = T("dcol", [128, 8], F32)
    PTs = T("PTs", [128, 128], F32)
    R2 = T("R2", [128, 128], F32)
    ov = T("ov", [128, n, 8], F32)
    dbias = T("dbias", [128, 129], F32)
    skc = T("skc", [128, 1], F32)
    R3 = T("R3", [n, 2, 4, n], F32)
    bcs = T("bcs", [128, 2, 4, n], F32)
    Sd = T("Sd", [128, n, 4, 128], F32)
    vnT = T("vnT", [128, 4, n], F32)
    t1 = T("t1", [128, 4, n], F32)
    krow = T("krow", [n, 512], F32)
    vrow = T("vrow", [n, 512], F32)
    Km = [T("Km%d" % i, [n, 512], F32) for i in range(2)]
    tS = T("tS", [128, 4, 128], F32)
    mix = G["sqm"]
    even_common(self, G, e)
    self.dma(dbias[:], self.decbias, r=[self.Bin], w=[dbias])
    self.dma(skc[:], self.esk[e], r=[self.Bin], w=[skc])
    self.dma(hs[:], self.s_gc[e].rearrange("b i c -> (b i) c"), r=[self.Bin], w=[hs])
    self.dma(Kc[:], self.c_k[e].rearrange("b w f -> w b f"), r=[self.Bin], w=[Kc])
    self.dma(Vc[:], self.c_v[e].rearrange("b w f -> w b f"), r=[self.Bin], w=[Vc])
    self.dma(Sd[:], self.s_gs[e].rearrange("b h k v -> k b h v"), r=[self.Bin], w=[Sd])
    for ch in range(12):
        ps = P[2 + ch % 2]
        self.pe(lambda: nc.tensor.transpose(ps[:, 0:48], hs[:, ch * 128:(ch + 1) * 128], ident[0:48, 0:48]),
                r=[hs, ident], w=[ps])
        self.dve(lambda: nc.vector.tensor_copy(xx[:, ch, :, 0:3], ps[:, 0:48].rearrange("p (b i) -> p b i", i=3)),
                 r=[ps], w=[xx])
    xb = [self.xTb[self.nt]]
    self.rmsnorm_cols(xb, c0, n, 4 + l, G["hn"], 0, G["sqm"], G["rstd"])
    cw = G["cw"]
    cnt = [0]

    def conv_fn(ch, ps):
        acc = G["acc"][cnt[0] % 2]
        cnt[0] += 1
        self.act(lambda: nc.scalar.copy(xx[:, ch, :, 3], ps[:, 0:n]), r=[ps], w=[xx])
        self.dve(lambda: nc.vector.tensor_scalar_mul(acc[:, 0:n], xx[:, ch, :, 0], cw[:, ch, 0:1]), r=[xx, cw], w=[acc])
        for i in range(1, 4):
            self.dve(lambda: nc.vector.scalar_tensor_tensor(out=acc[:, 0:n], in0=xx[:, ch, :, i], scalar=cw[:, ch, i:i + 1],
                                                            in1=acc[:, 0:n], op0=ALU.mult, op1=ALU.add), r=[xx, cw, acc], w=[acc])
        self.act(lambda: nc.scalar.activation(G["cq"][:, ch, 0:n], acc[:, 0:n], AF.Silu), r=[acc], w=[G["cq"]])
        pt = P[6]
        self.pe(lambda: nc.tensor.transpose(pt[0:n, (ch % 4) * 128:(ch % 4 + 1) * 128], xx[:, ch, :, 3], ident[:]),
                r=[xx, ident], w=[pt])
        self.dve(lambda: nc.vector.tensor_copy(gnew[:, ch * 128:(ch + 1) * 128], pt[0:n, (ch % 4) * 128:(ch % 4 + 1) * 128]),
                 r=[pt], w=[gnew])
    even_inproj_fm(self, G, n, conv_fn, G["kTa"][:, 0:n])
    w = self.ws.get(EW_N[6])
    wv = w[:, 0:EW_N[6]].rearrange("p (k c) -> p k c", k=KC)
    ps = P[2]

    def mm():
        ins = None
        for kc in range(KC):
            ins = nc.tensor.matmul(ps[0:n, 0:EV_TOK], G["hn"][:, kc, 0:n], wv[:, kc, :], start=(kc == 0), stop=(kc == KC - 1))
        return ins
    self.pe(mm, r=[w, G["hn"]], w=[ps])
    self.act(lambda: nc.scalar.copy(knv[:, 0:128], ps[0:n, 256:384]), r=[ps], w=[knv])
    self.act(lambda: nc.scalar.copy(knv[:, 128:256], ps[0:n, 0:128]), r=[ps], w=[knv])
    self.dve(lambda: nc.vector.tensor_tensor(out=knv[:, 128:256], in0=knv[:, 128:256], in1=ps[0:n, 128:256], op=ALU.add),
             r=[ps, knv], w=[knv])
    gates_from_ba(self, G, ps[0:n, 384:392], n, gts[:, 0:4], gts[:, 4:8], gts_tmp(gts), [ps], [gts])
    self.act(lambda: nc.scalar.activation(gts[:, 8:12], gts[:, 4:8], AF.Exp), r=[gts], w=[gts])
    self.dve(lambda: nc.vector.memset(Qb[:], 0.0), w=[Qb])
    for c in range(4):
        self.dve(lambda: nc.vector.tensor_copy(Qb[0:64, :, c], G["qT"][0:64, c, 0:n]), r=[G["qT"]], w=[Qb])
        self.dve(lambda: nc.vector.tensor_copy(Qb[64:128, :, 4 + c], G["qT"][64:128, c, 0:n]), r=[G["qT"]], w=[Qb])
    self.dve(lambda: nc.vector.tensor_copy(kTn[:], G["kTa"][:, 0:n]), r=[G["kTa"]], w=[kTn])
    for b in range(n):
        kp = P[b % 2]
        kt = KTb[b % 2]
        self.pe(lambda: nc.tensor.transpose(kp[:, 0:128], Kc[:, b, :], ident[:]), r=[Kc, ident], w=[kp])
        self.act(lambda: nc.scalar.copy(kt[:], kp[:, 0:128]), r=[kp], w=[kt])
        self.pe(lambda: nc.tensor.matmul(P[4][:, b * 8:(b + 1) * 8], kt[:], Qb[:, b, :], start=True, stop=True),
                r=[kt, Qb], w=[P[4]])
    self.dve(lambda: nc.vector.tensor_copy(sT[:], P[4][:, 0:128]), r=[P[4]], w=[sT])
    self.pe(lambda: nc.tensor.transpose(P[5][:, 0:128], sT[:], ident[:]), r=[sT, ident], w=[P[5]])
    self.pe(lambda: nc.tensor.matmul(P[5][:, 128:128 + n], Qb[:].rearrange("p b h -> p (b h)"), kTn[:], start=True, stop=True),
            r=[Qb, kTn], w=[P[5]])
    self.dve(lambda: nc.vector.tensor_tensor(out=sf[:, 0:n], in0=P[5][:, 128:128 + n], in1=mk[:, 5, 0:n], op=ALU.mult),
             r=[P[5], mk], w=[sf])
    self.dve(lambda: nc.vector.reduce_sum(out=col[:, 5:6], in_=sf[:, 0:n], axis=AX.X), r=[sf], w=[col])
    self.dve(lambda: nc.vector.scalar_tensor_tensor(out=sf[:, 0:128], in0=P[5][:, 0:128], scalar=0.125, in1=dbias[:, 0:128],
                                                    op0=ALU.mult, op1=ALU.add), r=[P[5], dbias], w=[sf])
    self.dve(lambda: nc.vector.scalar_tensor_tensor(out=sf[:, 128:129], in0=col[:, 5:6], scalar=0.125, in1=dbias[:, 128:129],
                                                    op0=ALU.mult, op1=ALU.add), r=[col, dbias], w=[sf])
    self.dve(lambda: nc.vector.reduce_max(out=col[:, 0:1], in_=sf[:, 0:129], axis=AX.X), r=[sf], w=[col])
    self.dve(lambda: nc.vector.tensor_tensor(out=col[:, 0:1], in0=col[:, 0:1], in1=skc[:, 0:1], op=ALU.max), r=[col, skc], w=[col])
    self.dve(lambda: nc.vector.tensor_scalar_mul(col[:, 1:2], col[:, 0:1], -1.0), r=[col], w=[col])
    self.act(lambda: nc.scalar.activation(sf[:, 0:129], sf[:, 0:129], AF.Exp, bias=col[:, 1:2], scale=1.0), r=[sf, col], w=[sf])
    self.dve(lambda: nc.vector.reduce_sum(out=col[:, 2:3], in_=sf[:, 0:129], axis=AX.X), r=[sf], w=[col])
    self.act(lambda: nc.scalar.activation(col[:, 3:4], skc[:, 0:1], AF.Exp, bias=col[:, 1:2], scale=1.0), r=[skc, col], w=[col])
    self.dve(lambda: nc.vector.tensor_tensor(out=col[:, 2:3], in0=col[:, 2:3], in1=col[:, 3:4], op=ALU.add), r=[col], w=[col])
    self.dve(lambda: nc.vector.reciprocal(col[:, 4:5], col[:, 2:3]), r=[col], w=[col])
    self.dve(lambda: nc.vector.tensor_scalar_mul(pnf[:, 0:129], sf[:, 0:129], col[:, 4:5]), r=[sf, col], w=[pnf])
    self.pe(lambda: nc.tensor.transpose(P[4][:, 128:256], pnf[:, 0:128], ident[:]), r=[pnf, ident], w=[P[4]])
    self.act(lambda: nc.scalar.copy(PTs[:], P[4][:, 128:256]), r=[P[4]], w=[PTs])
    for b in range(n):
        self.pe(lambda: nc.tensor.matmul(P[6][:, 256 + b * 8:256 + (b + 1) * 8], Vc[:, b, :], PTs[:, b * 8:(b + 1) * 8],
                                         start=True, stop=True), r=[Vc, PTs], w=[P[6]])
    self.pe(lambda: nc.tensor.transpose(P[7][:, 0:n], knv[:, 128:256], ident[0:n, 0:n]), r=[knv, ident], w=[P[7]])
    self.dve(lambda: nc.vector.tensor_copy(vTn[:], P[7][:, 0:n]), r=[P[7]], w=[vTn])
    self.dve(lambda: nc.vector.tensor_scalar_mul(R2[:], ident[:], pnf[:, 128:129]), r=[ident, pnf], w=[R2])
    self.pe(lambda: nc.tensor.matmul(P[7][:, 128:256], self.ones_f[:], R2[:], start=True, stop=True), r=[self.ones_f, R2], w=[P[7]])
    self.dve(lambda: nc.vector.tensor_tensor(out=ov[:], in0=P[7][:, 128:256].rearrange("p (b h) -> p b h", h=8),
                                             in1=vTn[:].unsqueeze(2).to_broadcast([128, n, 8]), op=ALU.mult),
             r=[P[7], vTn], w=[ov])
    self.dve(lambda: nc.vector.tensor_tensor(out=ov[:], in0=ov[:], in1=P[6][:, 256:384].rearrange("p (b h) -> p b h", h=8), op=ALU.add),
             r=[ov, P[6]], w=[ov])
    for c in range(4):
        self.dve(lambda: nc.vector.tensor_copy(mix[0:64, c, 0:n], ov[0:64, :, c]), r=[ov], w=[mix])
        self.dve(lambda: nc.vector.tensor_copy(mix[64:128, c, 0:n], ov[64:128, :, 4 + c]), r=[ov], w=[mix])
    l2norm_heads(self, G, n)
    cq = G["cq"]
    for t in range(2):
        src = gts[:, 0:4] if t == 0 else gts[:, 8:12]
        self.dve(lambda: nc.vector.tensor_tensor(out=R3[:, t, :, :], in0=src.unsqueeze(2).to_broadcast([n, 4, n]),
                                                 in1=ident[0:n, 0:n].unsqueeze(1).to_broadcast([n, 4, n]), op=ALU.mult),
                 r=[gts, ident], w=[R3])
    self.pe(lambda: nc.tensor.matmul(P[0][:, 0:128], self.ones_f[0:n, :], R3[:].rearrange("p t h b -> p (t h b)"),
                                     start=True, stop=True), r=[self.ones_f, R3], w=[P[0]])
    self.dve(lambda: nc.vector.tensor_copy(bcs[:].rearrange("p t h b -> p (t h b)"), P[0][:, 0:128]), r=[P[0]], w=[bcs])
    for b in range(n):
        for h in range(4):
            self.pe(lambda: nc.tensor.matmul(P[1][:, h * n + b:h * n + b + 1], Sd[:, b, h, :], cq[:, 4 + h, b:b + 1],
                                             start=True, stop=True), r=[Sd, cq], w=[P[1]])
    kSv = P[1][:, 0:4 * n].rearrange("p (h b) -> p h b", h=4)
    self.dve(lambda: nc.vector.tensor_tensor(out=t1[:], in0=kSv, in1=bcs[:, 1, :, :], op=ALU.mult), r=[P[1], bcs], w=[t1])
    self.dve(lambda: nc.vector.tensor_tensor(out=t1[:], in0=cq[:, 8:12, 0:n], in1=t1[:], op=ALU.subtract), r=[cq, t1], w=[t1])
    self.dve(lambda: nc.vector.tensor_tensor(out=vnT[:], in0=t1[:], in1=bcs[:, 0, :, :], op=ALU.mult), r=[t1, bcs], w=[vnT])
    for h in range(4):
        self.pe(lambda: nc.tensor.transpose(P[2][0:n, h * 128:(h + 1) * 128], cq[:, 4 + h, 0:n], ident[:]), r=[cq, ident], w=[P[2]])
        self.pe(lambda: nc.tensor.transpose(P[3][0:n, h * 128:(h + 1) * 128], vnT[:, h, :], ident[:]), r=[vnT, ident], w=[P[3]])
    self.act(lambda: nc.scalar.copy(krow[:], P[2][0:n, :]), r=[P[2]], w=[krow])
    self.dve(lambda: nc.vector.tensor_copy(vrow[:], P[3][0:n, :]), r=[P[3]], w=[vrow])
    for b in range(n):
        km = Km[b % 2]
        pp = P[4 + b % 2]
        self.dve(lambda: nc.vector.tensor_scalar_mul(km[:], krow[:], ident[0:n, b:b + 1]), r=[krow, ident], w=[km])

        def mm4():
            ins = None
            for h in range(4):
                ins = nc.tensor.matmul(pp[:, h * 128:(h + 1) * 128], km[:, h * 128:(h + 1) * 128], vrow[:, h * 128:(h + 1) * 128],
                                       start=True, stop=True)
            return ins
        self.pe(mm4, r=[km, vrow], w=[pp])
        self.dve(lambda: nc.vector.tensor_tensor(out=tS[:], in0=Sd[:, b, :, :],
                                                 in1=bcs[:, 1, :, b:b + 1].to_broadcast([128, 4, 128]), op=ALU.mult),
                 r=[Sd, bcs], w=[tS])
        self.dve(lambda: nc.vector.tensor_tensor(out=Sd[:, b, :, :], in0=tS[:], in1=pp[:, :].rearrange("p (h v) -> p h v", h=4), op=ALU.add),
                 r=[tS, pp], w=[Sd])
    for b in range(n):
        for h in range(4):
            self.pe(lambda: nc.tensor.matmul(P[1][:, 64 + h * n + b:64 + h * n + b + 1], Sd[:, b, h, :], cq[:, h, b:b + 1],
                                             start=True, stop=True), r=[Sd, cq], w=[P[1]])
    self.act(lambda: nc.scalar.copy(G["oT"][:, :, 0:n], P[1][:, 64:64 + 4 * n].rearrange("p (h b) -> p h b", h=4)), r=[P[1]], w=[G["oT"]])
    gdn_post(self, G, n, e)
    even_outproj(self, G, xb, c0, n)
    self.out_dma(self.o_sk[e][:, 0:127, :], self.c_k[e][:, 1:128, :], r=[self.Bin])
    self.out_dma(self.o_sv[e][:, 0:127, :], self.c_v[e][:, 1:128, :], r=[self.Bin])
    self.out_dma(self.o_sk[e][:, 127, :], knv[:, 0:128], r=[knv])
    self.out_dma(self.o_sv[e][:, 127, :], knv[:, 128:256], r=[knv])
    self.out_dma(self.o_sgc[e][:, 0:2, :], self.s_gc[e][:, 1:3, :], r=[self.Bin])
    self.out_dma(self.o_sgc[e][:, 2, :], gnew[:], r=[gnew])
    self.out_dma(self.o_sgs[e].rearrange("b h k v -> k b h v"), Sd[:], r=[Sd])
    self.end()


def gts_tmp(gts):
    class _V:
        b = gts.b

        def __getitem__(self, k):
            rows, cols = k
            return gts.t[rows, 12 + cols.start:12 + cols.stop]
    return _V()


Prog.even_mixer = even_mixer


OW_DT = 256
H_C, P_C, N_C, G_C = 32, 64, 128, 4


def odd_consts(self, O, e):
    nc = self.nc
    self.dma(O["cw"][:].rearrange("p a b -> p (a b)"), self.oconv[e], r=[self.Bin], w=[O["cw"]])
    self.dma(O["hb"][:], self.ohead[e], r=[self.Bin], w=[O["hb"]])
    self.dma(O["fc"][:], self.ofeat[e], r=[self.Bin], w=[O["fc"]])
    self.act(lambda: nc.scalar.activation(O["hb"][:, 32:64], O["hb"][:, 32:64], AF.Exp), r=[O["hb"]], w=[O["hb"]])
    self.dve(lambda: nc.vector.tensor_scalar_mul(O["hb"][:, 32:64], O["hb"][:, 32:64], -1.0), r=[O["hb"]], w=[O["hb"]])


def odd_alloc(self, n):
    T = self.T
    O = {}
    O["hn"] = T("hn", [128, KC, n], BF16)
    O["sqm"] = T("sqm", [128, KC, n], BF16)
    O["rstd"] = T("rstd", [128, n], F32)
    O["cw"] = T("ocw", [128, 24, 4], F32)
    O["hb"] = T("ohb", [128, 96], F32)
    O["fc"] = T("ofc", [128, 56], F32)
    O["acc"] = [T("oacc%d" % i, [128, n], F32) for i in range(2)]
    O["yT"] = T("yT", [128, 16, n], BF16)
    O["BT"] = T("BT", [128, 4, n], BF16)
    O["CT"] = T("CT", [128, 4, n], BF16)
    O["xc"] = [T("xc%d" % i, [128, n], BF16) for i in range(2)]
    return O


def odd_dt(self, O, ps_ap, rows, dt_dst, a_dst, rd, wr):
    nc = self.nc
    hb = O["hb"]
    self.dve(lambda: nc.vector.tensor_tensor(out=dt_dst, in0=ps_ap, in1=hb[0:rows, 0:32], op=ALU.add), r=rd + [hb], w=wr)
    self.act(lambda: nc.scalar.activation(dt_dst, dt_dst, AF.Exp), r=wr, w=wr)
    self.act(lambda: nc.scalar.activation(dt_dst, dt_dst, AF.Ln, bias=1.0, scale=1.0), r=wr, w=wr)
    self.dve(lambda: nc.vector.tensor_tensor(out=a_dst, in0=dt_dst, in1=hb[0:rows, 32:64], op=ALU.mult), r=wr + [hb], w=wr)


def odd_gate_norm_out(self, O, xbufs, c0, n, zfn):
    nc = self.nc
    yT, fc = O["yT"], O["fc"]
    sq = O["sqm"]
    for j in range(16):
        def consume(ps, j=j):
            zs = O["acc"][j % 2]
            self.act(lambda: nc.scalar.activation(zs[:, 0:n], ps[:, 0:n], AF.Silu), r=[ps], w=[zs])
            self.dve(lambda: nc.vector.tensor_tensor(out=yT[:, j, 0:n], in0=yT[:, j, 0:n], in1=zs[:, 0:n], op=ALU.mult),
                     r=[yT, zs], w=[yT])
        zfn(j, consume)
    for g in range(4):
        ps = self.P[6 + g % 2]
        for jj in range(4):
            j = g * 4 + jj
            self.act(lambda: nc.scalar.activation(sq[:, jj, 0:n], yT[:, j, 0:n], AF.Square), r=[yT], w=[sq])

        def mm():
            ins = None
            for jj in range(4):
                ins = nc.tensor.matmul(ps[:, 0:n], self.ones_bf[:], sq[:, jj, 0:n], start=(jj == 0), stop=(jj == 3))
            return ins
        self.pe(mm, r=[sq, self.ones_bf], w=[ps])
        rs = O["rstd"]
        self.act(lambda: nc.scalar.activation(rs[:, 0:n], ps[:, 0:n], AF.Ln, bias=EPS, scale=1.0 / 512.0), r=[ps], w=[rs])
        self.act(lambda: nc.scalar.activation(rs[:, 0:n], rs[:, 0:n], AF.Exp, scale=-0.5), r=[rs], w=[rs])
        for jj in range(4):
            j = g * 4 + jj
            self.dve(lambda: nc.vector.scalar_tensor_tensor(out=yT[:, j, 0:n], in0=yT[:, j, 0:n], scalar=fc[:, 24 + j:25 + j],
                                                            in1=rs[:, 0:n], op0=ALU.mult, op1=ALU.mult), r=[yT, fc, rs], w=[yT])
    for blk in range(4):
        w = self.ws.get(4096)
        wv = w[:, 0:4096].rearrange("p (k c) -> p k c", k=16)
        for dc in range(2):
            dch = blk * 2 + dc
            ps = self.P[dch % 2]

            def mm2():
                ins = None
                for k in range(16):
                    ins = nc.tensor.matmul(ps[:, 0:n], wv[:, k, dc * 128:(dc + 1) * 128], yT[:, k, 0:n],
                                           start=(k == 0), stop=(k == 15))
                return ins
            self.pe(mm2, r=[w, yT], w=[ps])
            xs = self.xT[:, dch, c0:c0 + n]
            self.dve(lambda: nc.vector.tensor_tensor(out=xs, in0=xs, in1=ps[:, 0:n], op=ALU.add), r=[ps] + xbufs, w=xbufs)


def odd_fm_chunk(self, O, w, cc, n, ps):
    nc = self.nc
    hn = O["hn"]
    wv = w[:, 0:4096].rearrange("p (k c) -> p k c", k=KC)

    def mm():
        ins = None
        for kc in range(KC):
            ins = nc.tensor.matmul(ps[:, 0:n], wv[:, kc, cc * 128:(cc + 1) * 128], hn[:, kc, 0:n],
                                   start=(kc == 0), stop=(kc == KC - 1))
        return ins
    self.pe(mm, r=[w, hn], w=[ps])


def odd_mixer(self, l):
    nc = self.nc
    e = l // 2
    P = self.P
    ident, mk = self.ident, self.mk
    self.begin()
    T = self.T
    O = odd_alloc(self, 512)
    pre = [T("opre%d" % i, [128, 515], F32) for i in range(2)]
    carry = T("ocarry", [128, 24, 3], F32)
    xst = T("xst", [128, 4, 2048], BF16)
    Btok = T("Btok", [128, 4, 512], BF16)
    dtv = T("dtv", [128, 4, 32], F32)
    av = T("av", [128, 4, 32], F32)
    sm = T("osm", [128, 6, 32], F32)
    R4 = [T("R4_%d" % i, [128, 4, 128], F32) for i in range(2)]
    Lt = [T("Lt%d" % i, [128, 4, 128], F32) for i in range(2)]
    Wb = [T("Wb%d" % i, [128, 4, 128], BF16) for i in range(2)]
    xdt = T("xdt", [128, 2048], BF16)
    xdd = T("xdd", [128, 2048], BF16)
    ytk = T("ytk", [128, 2048], BF16)
    tt = [T("ott%d" % i, [128, 512], F32) for i in range(2)]
    ST = T("ST", [128, 2048], F32)
    STb = T("STb", [128, 2048], BF16)
    sto = Tile(self.nc, "sto", [128, 32, 128], F32, handle=xst.t.bitcast(F32).reshape([128, 32, 128]))
    sto.b = xst.b
    odd_consts(self, O, e)
    self.dve(lambda: nc.vector.memset(carry[:], 0.0), w=[carry])
    self.dve(lambda: nc.vector.memset(ST[:], 0.0), w=[ST])
    self.dve(lambda: nc.vector.memset(STb[:], 0.0), w=[STb])
    cw, fc, hb = O["cw"], O["fc"], O["hb"]
    for ti in range(self.nt):
        c0 = ti * 512
        xb = [self.xTb[ti]]
        self.rmsnorm_cols(xb, c0, 512, 4 + l, O["hn"], 0, O["sqm"], O["rstd"])
        w = self.ws.get(OW_DT)
        wv = w[:, 0:OW_DT].rearrange("p (k c) -> p k c", k=KC)
        for st in range(4):
            ps = P[2 + st % 2]

            def mm():
                ins = None
                for kc in range(KC):
                    ins = nc.tensor.matmul(ps[:, 0:32], O["hn"][:, kc, st * 128:(st + 1) * 128], wv[:, kc, :],
                                           start=(kc == 0), stop=(kc == KC - 1))
                return ins
            self.pe(mm, r=[w, O["hn"]], w=[ps])
            odd_dt(self, O, ps[:, 0:32], 128, dtv[:, st, :], av[:, st, :], [ps], [dtv, av])
        for blk in range(6):
            w = self.ws.get(4096)
            for cc in range(4):
                ch = blk * 4 + cc
                ps = P[ch % 2]
                odd_fm_chunk(self, O, w, cc, 512, ps)
                pr = pre[ch % 2]
                acc = O["acc"][ch % 2]
                self.dve(lambda: nc.vector.tensor_copy(pr[:, 0:3], carry[:, ch, :]), r=[carry], w=[pr])
                self.act(lambda: nc.scalar.copy(pr[:, 3:515], ps[:, 0:512]), r=[ps], w=[pr])
                self.dve(lambda: nc.vector.tensor_copy(carry[:, ch, :], pr[:, 512:515]), r=[pr], w=[carry])
                self.dve(lambda: nc.vector.tensor_scalar_mul(acc[:], pr[:, 0:512], cw[:, ch, 0:1]), r=[pr, cw], w=[acc])
                for i in range(1, 4):
                    self.dve(lambda: nc.vector.scalar_tensor_tensor(out=acc[:], in0=pr[:, i:i + 512], scalar=cw[:, ch, i:i + 1],
                                                                     in1=acc[:], op0=ALU.mult, op1=ALU.add), r=[pr, cw, acc], w=[acc])
                if ch < 16 or ch < 20:
                    dst = O["xc"][ch % 2] if ch < 16 else None
                    tgt = dst[:, 0:512] if ch < 16 else O["BT"][:, ch - 16, 0:512]
                    tb = dst if ch < 16 else O["BT"]
                    self.act(lambda: nc.scalar.activation(tgt, acc[:], AF.Silu, bias=fc[:, ch:ch + 1], scale=1.0),
                             r=[acc, fc], w=[tb])
                    tp = P[4 + ch % 2]
                    tv = tp[:, :].bitcast(BF16)

                    def tr():
                        ins = None
                        for st in range(4):
                            ins = nc.tensor.transpose(tv[:, st * 128:(st + 1) * 128], tgt[:, st * 128:(st + 1) * 128], self.ident_bf[:])
                        return ins
                    self.pe(tr, r=[tb, self.ident_bf], w=[tp])
                    src = tv[:, 0:512].rearrange("p (s c) -> p s c", s=4)
                    if ch < 16:
                        self.dve(lambda: nc.vector.tensor_copy(xst[:, :, ch * 128:(ch + 1) * 128], src), r=[tp], w=[xst])
                    else:
                        self.dve(lambda: nc.vector.tensor_copy(Btok[:, :, (ch - 16) * 128:(ch - 15) * 128], src), r=[tp], w=[Btok])
                else:
                    self.act(lambda: nc.scalar.activation(O["CT"][:, ch - 20, 0:512], acc[:], AF.Silu, bias=fc[:, ch:ch + 1], scale=1.0),
                             r=[acc, fc], w=[O["CT"]])
        for st in range(4):
            cs = st * 128
            acs, acl, dte, cd, eac, dtd = [sm[:, i, :] for i in range(6)]
            self.pe(lambda: nc.tensor.matmul(P[6][:, 256:288], mk[:, 6, :], av[:, st, :], start=True, stop=True), r=[mk, av], w=[P[6]])
            self.pe(lambda: nc.tensor.matmul(P[6][:, 288:320], self.ones_f[:], av[:, st, :], start=True, stop=True),
                    r=[self.ones_f, av], w=[P[6]])
            self.dve(lambda: nc.vector.tensor_copy(sm[:, 0:2, :], P[6][:, 256:320].rearrange("p (a b) -> p a b", a=2)), r=[P[6]], w=[sm])
            self.dve(lambda: nc.vector.tensor_tensor(out=dte, in0=acl, in1=acs, op=ALU.subtract), r=[sm], w=[sm])
            self.act(lambda: nc.scalar.activation(dte, dte, AF.Exp), r=[sm], w=[sm])
            self.act(lambda: nc.scalar.activation(cd, acl, AF.Exp), r=[sm], w=[sm])
            self.act(lambda: nc.scalar.activation(eac, acs, AF.Exp), r=[sm], w=[sm])
            self.dve(lambda: nc.vector.tensor_tensor(out=dtd, in0=dte, in1=dtv[:, st, :], op=ALU.mult), r=[sm, dtv], w=[sm])
            xv = xst[:, st, :].rearrange("p (h q) -> p h q", q=P_C)
            for q4 in range(4):
                hs = slice(q4 * 8, (q4 + 1) * 8)
                self.dve(lambda: nc.vector.tensor_tensor(out=xdt[:, q4 * 512:(q4 + 1) * 512].rearrange("p (h q) -> p h q", q=P_C),
                                                          in0=xv[:, hs, :], in1=dtv[:, st, hs].unsqueeze(2).to_broadcast([128, 8, P_C]),
                                                          op=ALU.mult), r=[xst, dtv], w=[xdt])
                self.dve(lambda: nc.vector.tensor_tensor(out=xdd[:, q4 * 512:(q4 + 1) * 512].rearrange("p (h q) -> p h q", q=P_C),
                                                          in0=xv[:, hs, :], in1=dtd[:, hs].unsqueeze(2).to_broadcast([128, 8, P_C]),
                                                          op=ALU.mult), r=[xst, sm], w=[xdd])
            def group(g):
                q = g % 2
                bcB, yoffB, yps, scB = P[q], P[2 + q], P[4 + q], P[6 + q]
                r4, lt, wb = R4[q], Lt[q], Wb[q]
                yield self.pe(lambda: nc.tensor.matmul(scB[:, 0:128], O["BT"][:, g, cs:cs + 128], O["CT"][:, g, cs:cs + 128], start=True, stop=True),
                              r=[O["BT"], O["CT"]], w=[scB])
                for hf in range(2):
                    h0 = g * 8 + hf * 4
                    yield self.dve(lambda: nc.vector.tensor_tensor(out=r4[:], in0=mk[:, 6, :].unsqueeze(1).to_broadcast([128, 4, 128]),
                                                                   in1=av[:, st, h0:h0 + 4].unsqueeze(2).to_broadcast([128, 4, 128]), op=ALU.mult),
                                   r=[mk, av], w=[r4])
                    yield self.pe(lambda: nc.tensor.matmul(bcB[:, 0:512], self.ones_f[:], r4[:].rearrange("p a b -> p (a b)"), start=True, stop=True),
                                  r=[self.ones_f, r4], w=[bcB])
                    yield self.dve(lambda: nc.vector.tensor_tensor(out=lt[:], in0=bcB[:, 0:512].rearrange("p (a b) -> p a b", a=4),
                                                                   in1=acs[:, h0:h0 + 4].unsqueeze(2).to_broadcast([128, 4, 128]),
                                                                   op=ALU.subtract), r=[bcB, sm], w=[lt])
                    yield self.dve(lambda: nc.vector.tensor_tensor(out=lt[:], in0=lt[:], in1=mk[:, 7, :].unsqueeze(1).to_broadcast([128, 4, 128]),
                                                                   op=ALU.min), r=[lt, mk], w=[lt])
                    yield self.act(lambda: nc.scalar.activation(lt[:], lt[:], AF.Exp), r=[lt], w=[lt])
                    yield self.dve(lambda: nc.vector.tensor_tensor(out=wb[:], in0=lt[:], in1=scB[:, 0:128].unsqueeze(1).to_broadcast([128, 4, 128]),
                                                                   op=ALU.mult), r=[lt, scB], w=[wb])

                    def ymm():
                        ins = None
                        for hh in range(4):
                            h = h0 + hh
                            ins = nc.tensor.matmul(yps[:, (hf * 4 + hh) * 64:(hf * 4 + hh + 1) * 64], wb[:, hh, :], xdt[:, h * 64:(h + 1) * 64],
                                                   start=True, stop=True)
                        return ins
                    yield self.pe(ymm, r=[wb, xdt], w=[yps])
                yield self.pe(lambda: nc.tensor.matmul(yoffB[:, 0:512], O["CT"][:, g, cs:cs + 128], STb[:, g * 512:(g + 1) * 512], start=True, stop=True),
                              r=[O["CT"], STb], w=[yoffB])
                t = tt[q]
                gsl = slice(g * 512, (g + 1) * 512)
                yield self.dve(lambda: nc.vector.tensor_tensor(out=t[:].rearrange("p (h q) -> p h q", q=P_C),
                                                               in0=yoffB[:, 0:512].rearrange("p (h q) -> p h q", q=P_C),
                                                               in1=eac[:, g * 8:(g + 1) * 8].unsqueeze(2).to_broadcast([128, 8, P_C]), op=ALU.mult),
                               r=[yoffB, sm], w=[t])
                yield self.dve(lambda: nc.vector.tensor_tensor(out=t[:], in0=t[:], in1=yps[:, 0:512], op=ALU.add), r=[t, yps], w=[t])
                t2 = O["acc"][q]
                yield self.dve(lambda: nc.vector.tensor_tensor(out=t2[:].rearrange("p (h q) -> p h q", q=P_C),
                                                               in0=xst[:, st, gsl].rearrange("p (h q) -> p h q", q=P_C),
                                                               in1=hb[:, 64 + g * 8:64 + (g + 1) * 8].unsqueeze(2).to_broadcast([128, 8, P_C]), op=ALU.mult),
                               r=[xst, hb], w=[t2])
                yield self.dve(lambda: nc.vector.tensor_tensor(out=ytk[:, gsl], in0=t[:], in1=t2[:], op=ALU.add), r=[t, t2], w=[ytk])
                yield self.pe(lambda: nc.tensor.matmul(bcB[:, 0:512], Btok[:, st, g * 128:(g + 1) * 128], xdd[:, gsl], start=True, stop=True),
                              r=[Btok, xdd], w=[bcB])
                yield self.dve(lambda: nc.vector.tensor_tensor(out=ST[:, gsl].rearrange("p (h q) -> p h q", q=P_C),
                                                               in0=ST[:, gsl].rearrange("p (h q) -> p h q", q=P_C),
                                                               in1=cd[:, g * 8:(g + 1) * 8].unsqueeze(2).to_broadcast([128, 8, P_C]), op=ALU.mult),
                               r=[ST, sm], w=[ST])
                yield self.dve(lambda: nc.vector.tensor_tensor(out=ST[:, gsl], in0=ST[:, gsl], in1=bcB[:, 0:512], op=ALU.add), r=[ST, bcB], w=[ST])
                yield self.act(lambda: nc.scalar.copy(STb[:, gsl], ST[:, gsl]), r=[ST], w=[STb])

            for pr in range(2):
                gens = [group(2 * pr), group(2 * pr + 1)]
                while gens:
                    for gg in list(gens):
                        try:
                            next(gg)
                        except StopIteration:
                            gens.remove(gg)
            for hf in range(2):
                tp = P[hf]
                tv = tp[:, :].bitcast(BF16)

                def tr2():
                    ins = None
                    for j in range(8):
                        ch = hf * 8 + j
                        ins = nc.tensor.transpose(tv[:, j * 128:(j + 1) * 128], ytk[:, ch * 128:(ch + 1) * 128], self.ident_bf[:])
                    return ins
                self.pe(tr2, r=[ytk, self.ident_bf], w=[tp])
                self.act(lambda: nc.scalar.copy(O["yT"][:, hf * 8:(hf + 1) * 8, cs:cs + 128], tv[:, 0:1024].rearrange("p (j c) -> p j c", j=8)),
                         r=[tp], w=[O["yT"]])
        zw = [None]

        def zfn(j, consume):
            if j % 4 == 0:
                zw[0] = self.ws.get(4096)
            ps = P[2 + j % 2]
            odd_fm_chunk(self, O, zw[0], j % 4, 512, ps)
            consume(ps)
        odd_gate_norm_out(self, O, xb, c0, 512, zfn)
    for ch in range(24):
        self.out_dma(self.o_psc[e][:, ch * 128:(ch + 1) * 128].rearrange("i p -> p i"), carry[:, ch, :], r=[carry])
    for c in range(16):
        ps = P[c % 2]
        self.pe(lambda: nc.tensor.transpose(ps[:, 0:128], ST[:, c * 128:(c + 1) * 128], ident[:]), r=[ST, ident], w=[ps])
        self.dve(lambda: nc.vector.tensor_copy(sto[:, c, :], ps[:, 0:128]), r=[ps], w=[sto])
    self.out_dma(self.o_pss[e].rearrange("(c q) n -> q c n", q=128), sto[:, 0:16, :], r=[sto])
    self.end()
    odd_decode(self, l)


def odd_decode(self, l):
    nc = self.nc
    e = l // 2
    n = NS
    c0 = self.seq
    P = self.P
    ident, mk = self.ident, self.mk
    self.begin()
    T = self.T
    O = odd_alloc(self, n)
    xx = T("oxx", [128, 24, n, 4], F32)
    hs = T("ohs", [48, 3072], F32)
    gnew = T("ognew", [n, 3072], F32)
    xsT = T("xsT", [128, 16, n], F32)
    BCs = T("BCs", [128, 8, n], F32)
    dts = T("dts", [n, 64], F32)
    BCt = T("BCt", [n, 2, 512], F32)
    BCm = [T("BCm%d" % i, [n, 2, 512], F32) for i in range(2)]
    R5 = T("R5", [n, 2, 2, 16, n], F32)
    onesH = T("onesH", [n, 2, 128], F32)
    cols = T("ocols", [128, 2, 16, n], F32)
    xdc = T("xdc", [128, 16, n], F32)
    Sb = [T("Sb%d" % i, [128, 16, 128], F32) for i in range(2)]
    tS = T("otS", [128, 16, 128], F32)
    ysum = T("ysum", [128, 16, n], F32)
    odd_consts(self, O, e)
    cw, fc, hb = O["cw"], O["fc"], O["hb"]
    self.dma(hs[:], self.s_sc[e].rearrange("b i c -> (b i) c"), r=[self.Bin], w=[hs])
    self.dve(lambda: nc.vector.memset(onesH[:], 0.0), w=[onesH])
    self.dve(lambda: nc.vector.memset(onesH[:, 0, 0:64], 1.0), r=[onesH], w=[onesH])
    self.dve(lambda: nc.vector.memset(onesH[:, 1, 64:128], 1.0), r=[onesH], w=[onesH])
    for ch in range(24):
        ps = P[2 + ch % 2]
        self.pe(lambda: nc.tensor.transpose(ps[:, 0:48], hs[:, ch * 128:(ch + 1) * 128], ident[0:48, 0:48]), r=[hs, ident], w=[ps])
        self.dve(lambda: nc.vector.tensor_copy(xx[:, ch, :, 0:3], ps[:, 0:48].rearrange("p (b i) -> p b i", i=3)), r=[ps], w=[xx])
    xb = [self.xTb[self.nt]]
    self.rmsnorm_cols(xb, c0, n, 4 + l, O["hn"], 0, O["sqm"], O["rstd"])
    w = self.ws.get(OW_DT)
    wv = w[:, 0:OW_DT].rearrange("p (k c) -> p k c", k=KC)
    ps = P[2]

    def mm():
        ins = None
        for kc in range(KC):
            ins = nc.tensor.matmul(ps[0:n, 0:32], O["hn"][:, kc, 0:n], wv[:, kc, :], start=(kc == 0), stop=(kc == KC - 1))
        return ins
    self.pe(mm, r=[w, O["hn"]], w=[ps])
    odd_dt(self, O, ps[0:n, 0:32], n, dts[:, 0:32], dts[:, 32:64], [ps], [dts])
    self.act(lambda: nc.scalar.activation(dts[:, 32:64], dts[:, 32:64], AF.Exp), r=[dts], w=[dts])
    for blk in range(6):
        w = self.ws.get(4096)
        for cc in range(4):
            ch = blk * 4 + cc
            ps = P[ch % 2]
            odd_fm_chunk(self, O, w, cc, n, ps)
            acc = O["acc"][ch % 2]
            self.act(lambda: nc.scalar.copy(xx[:, ch, :, 3], ps[:, 0:n]), r=[ps], w=[xx])
            self.dve(lambda: nc.vector.tensor_scalar_mul(acc[:, 0:n], xx[:, ch, :, 0], cw[:, ch, 0:1]), r=[xx, cw], w=[acc])
            for i in range(1, 4):
                self.dve(lambda: nc.vector.scalar_tensor_tensor(out=acc[:, 0:n], in0=xx[:, ch, :, i], scalar=cw[:, ch, i:i + 1],
                                                                in1=acc[:, 0:n], op0=ALU.mult, op1=ALU.add), r=[xx, cw, acc], w=[acc])
            if ch < 16:
                dst, db = xsT[:, ch, :], xsT
            else:
                dst, db = BCs[:, ch - 16, :], BCs
            self.act(lambda: nc.scalar.activation(dst, acc[:, 0:n], AF.Silu, bias=fc[:, ch:ch + 1], scale=1.0), r=[acc, fc], w=[db])
            pt = P[6]
            self.pe(lambda: nc.tensor.transpose(pt[0:n, (ch % 4) * 128:(ch % 4 + 1) * 128], xx[:, ch, :, 3], ident[:]), r=[xx, ident], w=[pt])
            self.dve(lambda: nc.vector.tensor_copy(gnew[:, ch * 128:(ch + 1) * 128], pt[0:n, (ch % 4) * 128:(ch % 4 + 1) * 128]), r=[pt], w=[gnew])
    for t in range(2):
        pt = P[4 + t]
        for g in range(4):
            self.pe(lambda: nc.tensor.transpose(pt[0:n, g * 128:(g + 1) * 128], BCs[:, t * 4 + g, :], ident[:]), r=[BCs, ident], w=[pt])
        self.dve(lambda: nc.vector.tensor_copy(BCt[:, t, :], pt[0:n, 0:512]), r=[pt], w=[BCt])
    for t in range(2):
        src = dts[:, t * 32:(t + 1) * 32].rearrange("p (hp h2) -> p h2 hp", h2=2)
        self.dve(lambda: nc.vector.tensor_tensor(out=R5[:, t], in0=src.unsqueeze(3).to_broadcast([n, 2, 16, n]),
                                                 in1=ident[0:n, 0:n].unsqueeze(1).unsqueeze(1).to_broadcast([n, 2, 16, n]), op=ALU.mult),
                 r=[dts, ident], w=[R5])

        def cmm():
            ins = None
            for h2 in range(2):
                ins = nc.tensor.matmul(P[7][:, t * 256:(t + 1) * 256], onesH[:, h2, :], R5[:, t, h2].rearrange("p a b -> p (a b)"),
                                       start=(h2 == 0), stop=(h2 == 1))
            return ins
        self.pe(cmm, r=[onesH, R5], w=[P[7]])
    self.dve(lambda: nc.vector.tensor_copy(cols[:].rearrange("p t a b -> p (t a b)"), P[7][:, 0:512]), r=[P[7]], w=[cols])
    self.dve(lambda: nc.vector.tensor_tensor(out=xdc[:], in0=cols[:, 0], in1=xsT[:], op=ALU.mult), r=[cols, xsT], w=[xdc])
    sview = self.s_ss[e].rearrange("b (c q) n -> b q c n", q=128)
    oview = self.o_sss[e].rearrange("b (c q) n -> b q c n", q=128)
    self.dma(Sb[0][:], sview[0], r=[self.Bin], w=[Sb[0]])
    for b in range(n):
        S_ = Sb[b % 2]
        if b + 1 < n:
            self.dma(Sb[(b + 1) % 2][:], sview[b + 1], r=[self.Bin], w=[Sb[(b + 1) % 2]])
        bm = BCm[b % 2]
        self.dve(lambda: nc.vector.tensor_scalar_mul(bm[:].rearrange("p a b -> p (a b)"), BCt[:].rearrange("p a b -> p (a b)"),
                                                     ident[0:n, b:b + 1]), r=[BCt, ident], w=[bm])
        pB, pC = P[(b % 2) * 2], P[(b % 2) * 2 + 1]
        self.pe(lambda: nc.tensor.matmul(pB[:, 0:512], self.ones_f[0:n, :], bm[:, 0, :], start=True, stop=True), r=[self.ones_f, bm], w=[pB])
        self.pe(lambda: nc.tensor.matmul(pC[:, 0:512], self.ones_f[0:n, :], bm[:, 1, :], start=True, stop=True), r=[self.ones_f, bm], w=[pC])
        s4 = S_[:].rearrange("p (g r) n -> p g r n", r=4)
        t4 = tS[:].rearrange("p (g r) n -> p g r n", r=4)
        self.dve(lambda: nc.vector.tensor_tensor(out=tS[:], in0=S_[:], in1=cols[:, 1, :, b:b + 1].to_broadcast([128, 16, 128]), op=ALU.mult),
                 r=[S_, cols], w=[tS])
        self.dve(lambda: nc.vector.tensor_tensor(out=s4, in0=pB[:, 0:512].rearrange("p (g n) -> p g n", g=4).unsqueeze(2).to_broadcast([128, 4, 4, 128]),
                                                 in1=xdc[:, :, b].rearrange("p (g r) -> p g r", r=4).unsqueeze(3).to_broadcast([128, 4, 4, 128]),
                                                 op=ALU.mult), r=[pB, xdc], w=[S_])
        self.dve(lambda: nc.vector.tensor_tensor(out=S_[:], in0=S_[:], in1=tS[:], op=ALU.add), r=[S_, tS], w=[S_])
        self.dve(lambda: nc.vector.tensor_tensor(out=t4, in0=s4, in1=pC[:, 0:512].rearrange("p (g n) -> p g n", g=4).unsqueeze(2).to_broadcast([128, 4, 4, 128]),
                                                 op=ALU.mult), r=[S_, pC], w=[tS])
        self.dve(lambda: nc.vector.reduce_sum(out=ysum[:, :, b], in_=tS[:], axis=AX.X), r=[tS], w=[ysum])
        self.dma(oview[b], S_[:], r=[S_], w=[self.Bout])
    self.dve(lambda: nc.vector.tensor_tensor(out=xdc[:], in0=xsT[:], in1=fc[:, 40:56].unsqueeze(2).to_broadcast([128, 16, n]), op=ALU.mult),
             r=[xsT, fc], w=[xdc])
    self.dve(lambda: nc.vector.tensor_tensor(out=O["yT"][:, :, 0:n], in0=ysum[:], in1=xdc[:], op=ALU.add), r=[ysum, xdc], w=[O["yT"]])
    zw = [None]

    def zfn(j, consume):
        if j % 4 == 0:
            zw[0] = self.ws.get(4096)
        ps = P[2 + j % 2]
        odd_fm_chunk(self, O, zw[0], j % 4, n, ps)
        consume(ps)
    odd_gate_norm_out(self, O, xb, c0, n, zfn)
    self.out_dma(self.o_ssc[e][:, 0:2, :], self.s_sc[e][:, 1:3, :], r=[self.Bin])
    self.out_dma(self.o_ssc[e][:, 2, :], gnew[:], r=[gnew])
    self.end()


Prog.odd_mixer = odd_mixer


def tile_k(w, cb):
    K, N = w.shape
    return np.ascontiguousarray(w.reshape(K // 128, 128, N // cb, cb).transpose(2, 1, 0, 3))


def t5_bucket_np(dist):
    max_exact = 16
    df = np.maximum(dist, max_exact).astype(np.float32)
    large = max_exact + (np.log(df / max_exact) / math.log(128 / max_exact) * (32 - max_exact)).astype(np.int32)
    return np.where(dist < max_exact, dist, np.minimum(large, 31))


def static_masks():
    i = np.arange(128)[:, None]
    j = np.arange(128)[None, :]
    same = (i // 64) == (j // 64)
    m = np.zeros((128, 8, 128), np.float32)
    m[:, 0] = ((i <= j) & same)
    m[:, 1] = np.where((j < i) & same, 0.0, 1e30)
    m[:, 2] = np.where((j >= i) & same, 0.0, -1e30)
    m[:, 3] = ((j > i) & same)
    m[:, 4] = same
    m[:, 5, 0:16] = (np.arange(128)[:, None] // 8) == np.arange(16)[None, :]
    m[:, 6] = (i <= j)
    m[:, 7] = np.where(j >= i, 0.0, -1e30)
    return m.reshape(128, 8 * 128)


def prep_shared(inp):
    sh = {}
    g = np.zeros((128, 13, KC), np.float32)
    for l in range(4):
        g[:, l] = inp["norm_ff1"][l].reshape(KC, 128).T
        g[:, 4 + l] = inp["norm_mix"][l].reshape(KC, 128).T
        g[:, 8 + l] = inp["norm_ff2"][l].reshape(KC, 128).T
    g[:, 12] = inp["norm_final"].reshape(KC, 128).T
    sh["gains"] = g.reshape(128, 13 * KC)
    sh["masks"] = static_masks()
    wgu = np.empty((DEPTH, 2, 11, 128, 2, KC, 256), np.float32)
    wd = np.empty((DEPTH, 2, 8, 128, FC, 128), np.float32)
    for l in range(DEPTH):
        for f, (kg, ku, kd) in enumerate((("ff1_gate", "ff1_up", "ff1_down"), ("ff2_gate", "ff2_up", "ff2_down"))):
            wgu[l, f, :, :, 0] = tile_k(inp[kg][l], 256)
            wgu[l, f, :, :, 1] = tile_k(inp[ku][l], 256)
            wd[l, f] = tile_k(inp[kd][l], 128)
    sh["wgu"] = wgu.reshape(DEPTH, 2, 11, 128, 4096)
    sh["wd"] = wd.reshape(DEPTH, 2, 8, 128, 2816)
    ewin = np.zeros((2, 128, EW_TOT), np.float32)
    ewout = np.zeros((2, 2, 128, 4096), np.float32)
    econv = np.zeros((2, 128, 12, 4), np.float32)
    esm = np.zeros((2, 128, 17), np.float32)
    esk = np.zeros((2, 128, 1), np.float32)
    for e in range(2):
        W = inp["even_w_in"][e]
        qa, ka, va = W[:, 0:512], W[:, 512:640], W[:, 640:768]
        qkvb, zb, bb = W[:, 768:2304], W[:, 2304:2816], W[:, 2816:2824]
        cols = []
        for c in range(4):
            cols.append(np.concatenate([qa[:, c * 64:(c + 1) * 64], qa[:, 256 + c * 64:256 + (c + 1) * 64]], 1))
        cols.append(ka)
        cols.append(qkvb)
        cols.append(zb)
        fm = np.concatenate(cols, 1)
        assert fm.shape[1] == EV_FM * 128
        zero = np.zeros((1024, 64), np.float32)
        tok = np.concatenate([va[:, 0:64], zero, zero, va[:, 64:128], ka, bb], 1)
        assert tok.shape[1] == EV_TOK
        for b in range(5):
            ewin[e, :, EW_OFF[b]:EW_OFF[b] + 4096] = tile_k(fm[:, b * 512:(b + 1) * 512], 512)[0].reshape(128, 4096)
        ewin[e, :, EW_OFF[5]:EW_OFF[5] + 1024] = tile_k(fm[:, 2560:2688], 128)[0].reshape(128, 1024)
        ewin[e, :, EW_OFF[6]:] = tile_k(tok, EV_TOK)[0].reshape(128, 8 * EV_TOK)
        Wo = inp["even_w_out"][e]
        rows = []
        for c in range(4):
            rows.append(Wo[c * 64:(c + 1) * 64])
            rows.append(Wo[256 + c * 64:256 + (c + 1) * 64])
        rows.append(Wo[512:])
        Wp = np.concatenate(rows, 0)
        ewout[e] = tile_k(Wp, 512).reshape(2, 128, 4096)
        econv[e] = inp["gdn_conv_w"][e].reshape(4, 12, 128).transpose(2, 1, 0)
        esm[e, :, 0:4] = inp["gdn_A_log"][e][None, :]
        esm[e, :, 4:8] = inp["gdn_dt_bias"][e][None, :]
        esm[e, :, 8:16] = inp["swa_sinks"][e][None, :]
        esm[e, :, 16] = inp["gdn_norm"][e]
        esk[e, :, 0] = np.tile(inp["swa_sinks"][e], 16)
    sh["ewin"], sh["ewout"] = ewin, ewout
    sh["econv"] = econv.reshape(2, 128, 48)
    sh["esm"], sh["esk"] = esm, esk
    owin = np.zeros((2, 128, OW_TOT), np.float32)
    owout = np.zeros((2, 4, 128, 4096), np.float32)
    oconv = np.zeros((2, 128, 24, 4), np.float32)
    ohead = np.zeros((2, 128, 96), np.float32)
    ofeat = np.zeros((2, 128, 56), np.float32)
    for e in range(2):
        W = inp["ssd_w_in"][e]
        z, xbc, dtw = W[:, 0:2048], W[:, 2048:5120], W[:, 5120:5152]
        owin[e, :, 0:256] = tile_k(dtw, 32)[0].reshape(128, 256)
        fm = np.concatenate([xbc, z], 1)
        for b in range(10):
            owin[e, :, 256 + b * 4096:256 + (b + 1) * 4096] = tile_k(fm[:, b * 512:(b + 1) * 512], 512)[0].reshape(128, 4096)
        Wo = inp["ssd_w_out"][e]
        owout[e] = np.ascontiguousarray(Wo.reshape(16, 128, 4, 256).transpose(2, 1, 0, 3)).reshape(4, 128, 4096)
        oconv[e] = inp["ssd_conv_w"][e].reshape(4, 24, 128).transpose(2, 1, 0)
        ohead[e, :, 0:32] = inp["ssd_dt_bias"][e][None, :]
        ohead[e, :, 32:64] = inp["ssd_A_log"][e][None, :]
        ohead[e, :, 64:96] = inp["ssd_D"][e][None, :]
        ofeat[e, :, 0:24] = inp["ssd_conv_b"][e].reshape(24, 128).T
        ofeat[e, :, 24:40] = inp["ssd_norm"][e].reshape(16, 128).T
        ofeat[e, :, 40:56] = np.repeat(inp["ssd_D"][e].reshape(16, 2), 64, axis=1).T
    sh["owin"], sh["owout"] = owin, owout
    sh["oconv"] = oconv.reshape(2, 128, 96)
    sh["ohead"], sh["ofeat"] = ohead, ofeat
    rb = inp["rel_bias"]
    i = np.arange(128)[:, None]
    j = np.arange(256)[None, :]
    d = 128 + i - j
    valid = (d >= 0) & (d <= 128)
    bk = t5_bucket_np(np.clip(d, 0, 128))
    sb = np.where(valid[:, None, :], rb[bk].transpose(0, 2, 1), np.float32(NEG)).astype(np.float32)
    sh["swabias"] = np.ascontiguousarray(sb).reshape(128, 8 * 256)
    dd = 128 - np.arange(129)
    db = rb[t5_bucket_np(dd)]
    sh["decbias"] = np.ascontiguousarray(np.tile(db.T, (16, 1))).astype(np.float32)
    return sh


def core_inputs(inp, c, seq):
    m = {}
    m["x_p"] = np.ascontiguousarray(inp["x_prompt"][c][:seq])
    sl = slice(c * NS, (c + 1) * NS)
    m["x_s"] = np.ascontiguousarray(inp["x_sample"][sl, 0])
    m["c_k"] = np.ascontiguousarray(inp["cache_swa_k"][:, sl]).reshape(2, NS, 128, 128)
    m["c_v"] = np.ascontiguousarray(inp["cache_swa_v"][:, sl]).reshape(2, NS, 128, 128)
    m["s_gc"] = np.ascontiguousarray(inp["state_gdn_conv"][:, sl])
    m["s_gs"] = np.ascontiguousarray(inp["state_gdn_ssm"][:, sl])
    m["s_sc"] = np.ascontiguousarray(inp["state_ssd_conv"][:, sl])
    m["s_ss"] = np.ascontiguousarray(inp["state_ssd_ssm"][:, sl]).reshape(2, NS, 2048, 128)
    return m


_PROG_CACHE = {}


def get_prog(cfg):
    key = tuple(sorted((k, str(v)) for k, v in cfg.items()))
    if key not in _PROG_CACHE:
        _PROG_CACHE[key] = Prog(dict(cfg))
    return _PROG_CACHE[key]


def kernel(**inp):
    cfg = {"ntiles": 4, "depth": DEPTH}
    prog = get_prog(cfg)
    inp = {k: np.asarray(v) for k, v in inp.items()}
    sh = prep_shared(inp)
    in_maps = []
    for c in range(N_CORES):
        m = dict(sh)
        m.update(core_inputs(inp, c, SEQ))
        in_maps.append(m)
    res = run_bass_kernel_spmd(prog.nc, in_maps, core_ids=list(range(N_CORES)))
    R = res.results
    st1 = lambda k: np.stack([r[k] for r in R], 1)
    ct1 = lambda k: np.concatenate([r[k] for r in R], 1)
    y_p = np.stack([r["y_p"] for r in R], 0)
    y_s = np.concatenate([r["y_s"] for r in R], 0)[:, None, :]
    p_k = st1("o_pk").reshape(2, N_CORES, 128, 2, 64)
    p_v = st1("o_pv").reshape(2, N_CORES, 128, 2, 64)
    p_gc = st1("o_pgc")
    p_gs = st1("o_pgs")
    p_sc = st1("o_psc")
    p_ss = st1("o_pss").reshape(2, N_CORES, 32, 64, 128)
    s_k = ct1("o_sk").reshape(2, N_CORES * NS, 128, 2, 64)
    s_v = ct1("o_sv").reshape(2, N_CORES * NS, 128, 2, 64)
    s_gc = ct1("o_sgc")
    s_gs = ct1("o_sgs")
    s_sc = ct1("o_ssc")
    s_ss = ct1("o_sss").reshape(2, N_CORES * NS, 32, 64, 128)
    return (y_p, y_s, p_k, p_v, p_gc, p_gs, p_sc, p_ss, s_k, s_v, s_gc, s_gs, s_sc, s_ss)
```

```python
import math
import numpy as np
import concourse.bass as bass
import concourse.mybir as mybir
from concourse.bass_utils import run_bass_kernel_spmd

F32 = mybir.dt.float32
BF16 = mybir.dt.bfloat16
ALU = mybir.AluOpType
AF = mybir.ActivationFunctionType
AX = mybir.AxisListType

D = 1024
KC = 8
SEQ = 2048
NS = 16
DFF = 2816
FC = 22
EPS = 1e-6
N_CORES = 8
DEPTH = 4


class Buf:
    __slots__ = ("name", "w", "r", "excl")

    def __init__(self, name, excl=False):
        self.name = name
        self.w = None
        self.r = []
        self.excl = excl


class Sched:
    def __init__(self, nc, n_dma_sems=48):
        self.nc = nc
        self.E = {"pe": nc.tensor, "dve": nc.vector, "act": nc.scalar, "pool": nc.gpsimd, "sp": nc.sync}
        self.sems = {}
        self.cnt = {}
        for e in self.E:
            self.sems[e] = nc.alloc_semaphore("s_" + e)
            self.cnt[e] = 0
        self.dsems = []
        for i in range(n_dma_sems):
            k = "d%d" % i
            self.sems[k] = nc.alloc_semaphore("s_" + k)
            self.cnt[k] = 0
            self.dsems.append(k)
        self.dnext = 0
        self.seen = {e: {} for e in self.E}
        self.n_wait = 0
        self.n_ops = 0

    def _wait(self, eng, tick):
        if tick is None:
            return
        k, v = tick
        if self.seen[eng].get(k, 0) >= v:
            return
        self.E[eng].wait_ge(self.sems[k], v)
        self.seen[eng][k] = v
        self.n_wait += 1

    def _deps(self, eng, reads, writes):
        need = {}
        for b in reads:
            if b.w is not None:
                k, v = b.w
                if need.get(k, 0) < v:
                    need[k] = v
            if b.excl:
                for (k, v) in b.r:
                    if k != eng and need.get(k, 0) < v:
                        need[k] = v
        for b in writes:
            if b.w is not None:
                k, v = b.w
                if need.get(k, 0) < v:
                    need[k] = v
            for (k, v) in b.r:
                if need.get(k, 0) < v:
                    need[k] = v
        for k, v in need.items():
            self._wait(eng, (k, v))

    def _commit(self, tick, reads, writes):
        for b in reads:
            b.r.append(tick)
            if len(b.r) > 16:
                m = {}
                for (k, v) in b.r:
                    if m.get(k, 0) < v:
                        m[k] = v
                b.r = list(m.items())
        for b in writes:
            b.w = tick
            b.r = []

    def op(self, eng, fn, reads=(), writes=()):
        self._deps(eng, reads, writes)
        ins = fn()
        self.cnt[eng] += 1
        ins.then_inc(self.sems[eng], 1)
        tick = (eng, self.cnt[eng])
        self._commit(tick, reads, writes)
        self.n_ops += 1
        return tick

    def new_sem(self, k):
        self.sems[k] = self.nc.alloc_semaphore("s_" + k)
        self.cnt[k] = 0

    def barrier(self):
        for e in ("pe", "dve", "act", "sp"):
            for o in ("pe", "dve", "act", "pool"):
                if o != e and self.cnt[o] > 0:
                    self._wait(e, (o, self.cnt[o]))
            for k in self.dsems:
                if self.cnt[k] > 0:
                    self._wait(e, (k, self.cnt[k]))

    def dma(self, q, out=None, in_=None, reads=(), writes=(), multi=None, sem=None):
        pairs = multi if multi is not None else [(out, in_)]
        if sem is not None:
            k = sem
        else:
            k = self.dsems[self.dnext]
            self.dnext = (self.dnext + 1) % len(self.dsems)
        if self.cnt[k] > 0:
            self._wait(q, (k, self.cnt[k]))
        self._deps(q, reads, writes)
        for (o, i) in pairs:
            self.E[q].dma_start(out=o, in_=i, allow_slow_non_contiguous=True).then_inc(self.sems[k], 16)
            self.cnt[k] += 16
        tick = (k, self.cnt[k])
        self._commit(tick, reads, writes)
        return tick

    def finish(self):
        for e in ("pe", "dve", "act", "pool"):
            if self.cnt[e] > 0:
                self._wait("sp", (e, self.cnt[e]))
        for k in list(self.sems):
            if k not in self.E and self.cnt[k] > 0:
                self._wait("sp", (k, self.cnt[k]))


class Tile:
    def __init__(self, nc, name, shape, dtype, psum=False, handle=None):
        if handle is not None:
            self.t = handle
        elif psum:
            self.t = nc.alloc_psum_tensor(name, list(shape), dtype)
        else:
            self.t = nc.alloc_sbuf_tensor(name, list(shape), dtype)
        self.b = Buf(name, excl=psum)

    def __getitem__(self, k):
        return self.t[k]


SLOT_ELEMS = 4096
N_SLOTS = 3


class WStream:
    def __init__(self, nc, S, plan):
        self.nc, self.S = nc, S
        self.plan = plan
        self.slots = [Tile(nc, "wslot%d" % i, [128, SLOT_ELEMS], BF16) for i in range(N_SLOTS)]
        self.dram_buf = Buf("wdram")
        for i in range(N_SLOTS):
            S.new_sem("w%d" % i)
        self.issued = 0
        self.used = 0

    def _issue(self):
        i = self.issued
        ap, n = self.plan[i]
        sl = self.slots[i % N_SLOTS]
        self.S.dma("pool", out=sl[:, 0:n], in_=ap, reads=[self.dram_buf], writes=[sl.b], sem="w%d" % (i % N_SLOTS))
        self.issued += 1

    def get(self, expect_n):
        i = self.used
        while self.issued <= min(i + N_SLOTS - 2, len(self.plan) - 1):
            self._issue()
        assert self.plan[i][1] == expect_n, (i, self.plan[i][1], expect_n)
        self.used += 1
        return self.slots[i % N_SLOTS]


import contextlib

H_A, KV_A, HD_A = 8, 2, 64
H_B = 4
NEG = -1e30
EV_FM = 21
EV_TOK = 392
EW_OFF = [0, 4096, 8192, 12288, 16384, 20480, 21504]
EW_N = [4096, 4096, 4096, 4096, 4096, 1024, 8 * EV_TOK]
EW_TOT = 21504 + 8 * EV_TOK
OW_TOT = 256 + 10 * 4096


class Prog:
    def __init__(self, cfg):
        self.cfg = cfg
        self.nt = cfg.get("ntiles", 4)
        self.seq = self.nt * 512
        self.ntok = self.seq + NS
        self.depth = cfg.get("depth", DEPTH)
        self.layers = cfg.get("layers", None) or list(range(self.depth))
        nc = bass.Bass("TRN2", target_bir_lowering=False)
        self.nc = nc
        self.S = Sched(nc)
        self.stack = None
        self.declare_io()
        self.alloc()
        self.plan_weights()
        self.emit()

    def din(self, name, shape, dtype=F32):
        return self.nc.dram_tensor(name, list(shape), dtype, kind="ExternalInput").ap()

    def dout(self, name, shape, dtype=F32):
        return self.nc.dram_tensor(name, list(shape), dtype, kind="ExternalOutput").ap()

    def declare_io(self):
        sq = self.seq
        self.x_p = self.din("x_p", [sq, D])
        self.x_s = self.din("x_s", [NS, D])
        self.gains = self.din("gains", [128, 13 * KC])
        self.masks = self.din("masks", [128, 8 * 128])
        self.wgu = self.din("wgu", [DEPTH, 2, 11, 128, 4096])
        self.wd = self.din("wd", [DEPTH, 2, 8, 128, 2816])
        self.ewin = self.din("ewin", [2, 128, EW_TOT])
        self.ewout = self.din("ewout", [2, 2, 128, 4096])
        self.econv = self.din("econv", [2, 128, 48])
        self.esm = self.din("esm", [2, 128, 17])
        self.esk = self.din("esk", [2, 128, 1])
        self.swabias = self.din("swabias", [128, 8 * 256])
        self.decbias = self.din("decbias", [128, 129])
        self.c_k = self.din("c_k", [2, NS, 128, 128])
        self.c_v = self.din("c_v", [2, NS, 128, 128])
        self.s_gc = self.din("s_gc", [2, NS, 3, 1536])
        self.s_gs = self.din("s_gs", [2, NS, 4, 128, 128])
        self.owin = self.din("owin", [2, 128, OW_TOT])
        self.owout = self.din("owout", [2, 4, 128, 4096])
        self.oconv = self.din("oconv", [2, 128, 96])
        self.ohead = self.din("ohead", [2, 128, 96])
        self.ofeat = self.din("ofeat", [2, 128, 56])
        self.s_sc = self.din("s_sc", [2, NS, 3, 3072])
        self.s_ss = self.din("s_ss", [2, NS, 2048, 128])
        self.o_psc = self.dout("o_psc", [2, 3, 3072])
        self.o_pss = self.dout("o_pss", [2, 2048, 128])
        self.o_ssc = self.dout("o_ssc", [2, NS, 3, 3072])
        self.o_sss = self.dout("o_sss", [2, NS, 2048, 128])
        self.y_p = self.dout("y_p", [sq, D])
        self.y_s = self.dout("y_s", [NS, D])
        self.o_pk = self.dout("o_pk", [2, 128, 128])
        self.o_pv = self.dout("o_pv", [2, 128, 128])
        self.o_pgc = self.dout("o_pgc", [2, 3, 1536])
        self.o_pgs = self.dout("o_pgs", [2, 4, 128, 128])
        self.o_sk = self.dout("o_sk", [2, NS, 128, 128])
        self.o_sv = self.dout("o_sv", [2, NS, 128, 128])
        self.o_sgc = self.dout("o_sgc", [2, NS, 3, 1536])
        self.o_sgs = self.dout("o_sgs", [2, NS, 4, 128, 128])
        self.Bin = Buf("dram_in")
        self.Bout = Buf("dram_out")

    def alloc(self):
        nc = self.nc
        self.xT = Tile(nc, "xT", [128, KC, self.ntok], F32)
        self.xTb = [Buf("xT_t%d" % i) for i in range(self.nt)] + [Buf("xT_s")]
        self.ident = Tile(nc, "ident", [128, 128], F32)
        self.ident_bf = Tile(nc, "ident_bf", [128, 128], BF16)
        self.ones_bf = Tile(nc, "ones_bf", [128, 128], BF16)
        self.ones_f = Tile(nc, "ones_f", [128, 128], F32)
        self.gn = Tile(nc, "gn", [128, 13 * KC], F32)
        self.mk = Tile(nc, "mk", [128, 8, 128], F32)
        self.P = [Tile(nc, "ps%d" % i, [128, 512], F32, psum=True) for i in range(8)]

    def begin(self):
        assert self.stack is None
        self.stack = contextlib.ExitStack()
        self.nalloc = 0
        self.deferred = []

    def out_dma(self, dst, src, r=()):
        self.deferred.append((dst, src, list(r)))

    def end(self):
        for (dst, src, r) in self.deferred:
            self.dma(dst, src, r=r, w=[self.Bout])
        self.deferred = []
        self.S.barrier()
        self.stack.close()
        self.stack = None

    def T(self, name, shape, dtype):
        self.nalloc += 1
        h = self.stack.enter_context(self.nc.sbuf_tensor("%s_%d" % (name, self.S.n_ops), list(shape), dtype))
        return Tile(self.nc, name, shape, dtype, handle=h)

    def ffn_blocks(self, l, f):
        out = []
        for b in range(11):
            out.append((self.wgu[l, f, b], 4096))
        for b in range(8):
            out.append((self.wd[l, f, b], 2816))
        return out

    def tiles(self):
        ts = []
        for t in range(self.nt):
            segs = [(t * 512, 512, 0)]
            if t == self.nt - 1:
                segs.append((self.seq, NS, 512))
            ts.append(segs)
        return ts

    def even_blocks(self, e):
        out = []
        for _ in range(self.nt + 1):
            for b in range(7):
                out.append((self.ewin[e, :, EW_OFF[b]:EW_OFF[b] + EW_N[b]], EW_N[b]))
            for b in range(2):
                out.append((self.ewout[e, b], 4096))
        return out

    def odd_blocks(self, e):
        out = []
        for _ in range(self.nt + 1):
            out.append((self.owin[e, :, 0:256], 256))
            for b in range(10):
                out.append((self.owin[e, :, 256 + b * 4096:256 + (b + 1) * 4096], 4096))
            for b in range(4):
                out.append((self.owout[e, b], 4096))
        return out

    def mixer_blocks(self, l):
        if l % 2 == 0:
            return self.even_blocks(l // 2)
        return self.odd_blocks(l // 2)

    def plan_weights(self):
        plan = []
        ffn = self.cfg.get("ffn", True)
        for l in self.layers:
            if ffn:
                for _ in self.tiles():
                    plan += self.ffn_blocks(l, 0)
            if not self.cfg.get("nomix"):
                plan += self.mixer_blocks(l)
            if ffn:
                for _ in self.tiles():
                    plan += self.ffn_blocks(l, 1)
        self.ws = WStream(self.nc, self.S, plan)

    def bl(self, xs):
        return [getattr(x, 'b', x) for x in xs]

    def dve(self, fn, r=(), w=()):
        return self.S.op("dve", fn, self.bl(r), self.bl(w))

    def act(self, fn, r=(), w=()):
        return self.S.op("act", fn, self.bl(r), self.bl(w))

    def pe(self, fn, r=(), w=()):
        return self.S.op("pe", fn, self.bl(r), self.bl(w))

    def pool(self, fn, r=(), w=()):
        return self.S.op("pool", fn, self.bl(r), self.bl(w))

    def dma(self, out, in_, r=(), w=(), q="sp"):
        return self.S.dma(q, out=out, in_=in_, reads=self.bl(r), writes=self.bl(w))

    def gcol(self, idx, kc):
        return self.gn[:, idx * KC + kc: idx * KC + kc + 1]

    def tile_bufs(self, ti):
        b = [self.xTb[ti]]
        if ti == self.nt - 1:
            b.append(self.xTb[self.nt])
        return b

    def col_stats(self, src_fn, nchunk, n, ps, sq, ones, scale, bias_ln, out_ap, exp_bias=0.0, rd=()):
        nc = self.nc
        for c in range(nchunk):
            self.act(lambda c=c: nc.scalar.activation(sq[:, c, 0:n], src_fn(c), AF.Square), r=rd, w=[sq])

        def mm():
            ins = None
            for c in range(nchunk):
                ins = nc.tensor.matmul(ps[:, 0:n], ones[:], sq[:, c, 0:n], start=(c == 0), stop=(c == nchunk - 1))
            return ins
        self.pe(mm, r=[sq, ones], w=[ps])
        return ps

    def rmsnorm_cols(self, xbufs, c0, n, gidx, hn, l0, sq, rstd):
        nc = self.nc
        ps = self.P[7]
        self.act(lambda: nc.scalar.activation(sq[:, :, l0:l0 + n], self.xT[:, :, c0:c0 + n], AF.Square),
                 r=xbufs, w=[sq])

        def mm():
            ins = None
            for kc in range(KC):
                ins = nc.tensor.matmul(ps[:, 0:n], self.ones_bf[:], sq[:, kc, l0:l0 + n],
                                       start=(kc == 0), stop=(kc == KC - 1))
            return ins
        self.pe(mm, r=[sq, self.ones_bf], w=[ps])
        self.act(lambda: nc.scalar.activation(rstd[:, l0:l0 + n], ps[:, 0:n], AF.Ln, bias=EPS, scale=1.0 / D),
                 r=[ps], w=[rstd])
        self.act(lambda: nc.scalar.activation(rstd[:, l0:l0 + n], rstd[:, l0:l0 + n], AF.Exp, scale=-0.5),
                 r=[rstd], w=[rstd])
        for kc in range(KC):
            self.dve(lambda kc=kc: nc.vector.scalar_tensor_tensor(
                out=hn[:, kc, l0:l0 + n], in0=self.xT[:, kc, c0:c0 + n], scalar=self.gcol(gidx, kc),
                in1=rstd[:, l0:l0 + n], op0=ALU.mult, op1=ALU.mult),
                r=xbufs + [rstd, self.gn], w=[hn])

    def ffn_tile(self, l, f, ti, segs, hn, hT, sgs):
        nc = self.nc
        xb = self.tile_bufs(ti)
        it = 0
        for blk in range(11):
            w = self.ws.get(4096)
            wv = w[:, 0:4096].rearrange("p (a k c) -> p a k c", a=2, k=KC)
            for fc in range(2):
                fch = blk * 2 + fc
                for si, (g0, n, l0) in enumerate(segs):
                    if si == 0:
                        pg, pu = self.P[(it % 2)], self.P[2 + (it % 2)]
                    else:
                        pg, pu = self.P[4], self.P[5]

                    def mm(pt, a):
                        ins = None
                        for kc in range(KC):
                            ins = nc.tensor.matmul(pt[:, 0:n], wv[:, a, kc, fc * 128:(fc + 1) * 128],
                                                   hn[:, kc, l0:l0 + n], start=(kc == 0), stop=(kc == KC - 1))
                        return ins
                    self.pe(lambda: mm(pg, 0), r=[w, hn], w=[pg])
                    self.pe(lambda: mm(pu, 1), r=[w, hn], w=[pu])
                    sg = sgs[it % 2]
                    self.act(lambda: nc.scalar.activation(sg[:, 0:n], pg[:, 0:n], AF.Silu), r=[pg], w=[sg])
                    self.dve(lambda: nc.vector.tensor_tensor(out=hT[:, fch, l0:l0 + n], in0=sg[:, 0:n],
                                                             in1=pu[:, 0:n], op=ALU.mult), r=[sg, pu], w=[hT])
                it += 1
        it = 0
        for blk in range(8):
            w = self.ws.get(2816)
            wv = w[:, 0:2816].rearrange("p (k c) -> p k c", k=FC)
            dch = blk
            for si, (g0, n, l0) in enumerate(segs):
                po = self.P[6 + (it % 2)] if si == 0 else self.P[4 + (it % 2)]

                def mm():
                    ins = None
                    for k in range(FC):
                        ins = nc.tensor.matmul(po[:, 0:n], wv[:, k, :], hT[:, k, l0:l0 + n],
                                               start=(k == 0), stop=(k == FC - 1))
                    return ins
                self.pe(mm, r=[w, hT], w=[po])
                xs = self.xT[:, dch, g0:g0 + n]
                self.dve(lambda: nc.vector.scalar_tensor_tensor(out=xs, in0=po[:, 0:n], scalar=0.5, in1=xs,
                                                                op0=ALU.mult, op1=ALU.add), r=[po] + xb, w=xb)
            it += 1

    def ffn(self, l, f):
        self.begin()
        hns = [self.T("hn%d" % i, [128, KC, 528], BF16) for i in range(2)]
        hT = self.T("hT", [128, FC, 528], BF16)
        sq = self.T("sq", [128, KC, 528], BF16)
        rstd = self.T("rstd", [128, 528], F32)
        sgs = [self.T("sg%d" % i, [128, 512], F32) for i in range(2)]
        gidx = l if f == 0 else 8 + l
        for ti, segs in enumerate(self.tiles()):
            hn = hns[ti % 2]
            for (g0, n, l0) in segs:
                self.rmsnorm_cols(self.tile_bufs(ti), g0, n, gidx, hn, l0, sq, rstd)
            self.ffn_tile(l, f, ti, segs, hn, hT, sgs)
        self.end()

    def consts(self):
        nc = self.nc
        self.pool(lambda: nc.gpsimd.memset(self.ident[:], 0.0), w=[self.ident])
        self.pool(lambda: nc.gpsimd.affine_select(self.ident[:], self.ident[:], pattern=[[-1, 128]],
                                                  compare_op=ALU.not_equal, fill=1.0, base=0, channel_multiplier=1),
                  r=[self.ident], w=[self.ident])
        self.pool(lambda: nc.gpsimd.memset(self.ones_bf[:], 1.0), w=[self.ones_bf])
        self.pool(lambda: nc.gpsimd.memset(self.ones_f[:], 1.0), w=[self.ones_f])
        self.dve(lambda: nc.vector.tensor_copy(self.ident_bf[:], self.ident[:]), r=[self.ident], w=[self.ident_bf])
        self.dma(self.gn[:], self.gains, r=[self.Bin], w=[self.gn])
        self.dma(self.mk[:].rearrange("p a b -> p (a b)"), self.masks, r=[self.Bin], w=[self.mk])

    def load_x(self):
        nc = self.nc
        self.begin()
        xins = [self.T("xin%d" % i, [128, D], F32) for i in range(2)]
        ntt = self.seq // 128
        for tt in range(ntt + 1):
            xin = xins[tt % 2]
            if tt < ntt:
                rows, src, c0 = 128, self.x_p[tt * 128:(tt + 1) * 128, :], tt * 128
                xb = self.xTb[tt // 4]
            else:
                rows, src, c0 = NS, self.x_s, self.seq
                xb = self.xTb[self.nt]
            self.dma(xin[0:rows, :], src, r=[self.Bin], w=[xin])
            for half in range(2):
                ps = self.P[(tt % 2) * 2 + half]

                def tr():
                    ins = None
                    for j in range(4):
                        kc = half * 4 + j
                        ins = nc.tensor.transpose(ps[:, j * 128:j * 128 + rows], xin[0:rows, kc * 128:(kc + 1) * 128],
                                                  self.ident[0:rows, 0:rows])
                    return ins
                self.pe(tr, r=[xin, self.ident], w=[ps])
                pv = ps[:, :].rearrange("p (j c) -> p j c", j=4)[:, :, 0:rows]
                dst = self.xT[:, half * 4:half * 4 + 4, c0:c0 + rows]
                if half == 0:
                    self.dve(lambda: nc.vector.tensor_copy(dst, pv), r=[ps], w=[xb])
                else:
                    self.act(lambda: nc.scalar.copy(dst, pv), r=[ps], w=[xb])
        self.end()

    def final(self):
        nc = self.nc
        self.begin()
        sq = self.T("sq", [128, KC, 528], BF16)
        rstd = self.T("rstd", [128, 528], F32)
        yos = [self.T("yo%d" % i, [128, D], F32) for i in range(2)]
        for ti, segs in enumerate(self.tiles()):
            xb = self.tile_bufs(ti)
            for (c0, n, l0) in segs:
                ps = self.P[7]
                self.act(lambda: nc.scalar.activation(sq[:, :, l0:l0 + n], self.xT[:, :, c0:c0 + n], AF.Square),
                         r=xb, w=[sq])

                def mm():
                    ins = None
                    for kc in range(KC):
                        ins = nc.tensor.matmul(ps[:, 0:n], self.ones_bf[:], sq[:, kc, l0:l0 + n],
                                               start=(kc == 0), stop=(kc == KC - 1))
                    return ins
                self.pe(mm, r=[sq, self.ones_bf], w=[ps])
                self.act(lambda: nc.scalar.activation(rstd[:, l0:l0 + n], ps[:, 0:n], AF.Ln, bias=EPS, scale=1.0 / D),
                         r=[ps], w=[rstd])
                self.act(lambda: nc.scalar.activation(rstd[:, l0:l0 + n], rstd[:, l0:l0 + n], AF.Exp, scale=-0.5),
                         r=[rstd], w=[rstd])
                for kc in range(KC):
                    self.dve(lambda kc=kc: nc.vector.scalar_tensor_tensor(
                        out=self.xT[:, kc, c0:c0 + n], in0=self.xT[:, kc, c0:c0 + n], scalar=self.gcol(12, kc),
                        in1=rstd[:, l0:l0 + n], op0=ALU.mult, op1=ALU.mult), r=xb + [rstd, self.gn], w=xb)
        ntt = self.seq // 128
        for tt in range(ntt + 1):
            yo = yos[tt % 2]
            if tt < ntt:
                rows, dst, c0 = 128, self.y_p[tt * 128:(tt + 1) * 128, :], tt * 128
                xb = self.xTb[tt // 4]
            else:
                rows, dst, c0 = NS, self.y_s, self.seq
                xb = self.xTb[self.nt]
            for half in range(2):
                ps = self.P[(tt % 2) * 2 + half]

                def tr():
                    ins = None
                    for j in range(4):
                        kc = half * 4 + j
                        ins = nc.tensor.transpose(ps[0:rows, j * 128:(j + 1) * 128], self.xT[:, kc, c0:c0 + rows],
                                                  self.ident[:])
                    return ins
                self.pe(tr, r=[xb, self.ident], w=[ps])
                if half == 0:
                    self.dve(lambda: nc.vector.tensor_copy(yo[0:rows, 0:512], ps[0:rows, :]), r=[ps], w=[yo])
                else:
                    self.act(lambda: nc.scalar.copy(yo[0:rows, 512:1024], ps[0:rows, :]), r=[ps], w=[yo])
            self.dma(dst, yo[0:rows, :], r=[yo], w=[self.Bout])
        self.end()

    def mixer(self, l):
        if l % 2 == 0:
            self.even_mixer(l)
        else:
            self.odd_mixer(l)

    def emit(self):
        self.consts()
        self.load_x()
        for l in self.layers:
            if self.cfg.get("ffn", True):
                self.ffn(l, 0)
            if not self.cfg.get("nomix"):
                self.mixer(l)
            if self.cfg.get("ffn", True):
                self.ffn(l, 1)
        self.final()
        self.S.finish()


def even_alloc(self, nc_):
    G = {}
    T = self.T
    G["hn"] = T("hn", [128, KC, nc_], BF16)
    G["sqm"] = T("sqm", [128, KC, nc_], BF16)
    G["rstd"] = T("rstd", [128, nc_], F32)
    G["qT"] = T("qT", [128, 4, nc_], BF16)
    G["acc"] = [T("acc%d" % i, [128, nc_], F32) for i in range(2)]
    G["cq"] = T("cq", [128, 12, nc_], F32)
    G["zs"] = T("zs", [128, 4, nc_], BF16)
    oT = Tile(self.nc, "oT", [128, 4, nc_], F32, handle=G["hn"].t.bitcast(F32).reshape([128, 4, nc_]))
    oT.b = G["hn"].b
    G["oT"] = oT
    G["cw"] = T("cw", [128, 12, 4], F32)
    G["sm"] = T("sm", [128, 17], F32)
    G["eA"] = T("eA", [128, 4], F32)
    G["sq4"] = T("sq4", [128, 2, nc_], BF16)
    return G


def even_common(self, G, e):
    nc = self.nc
    self.dma(G["cw"][:].rearrange("p a b -> p (a b)"), self.econv[e], r=[self.Bin], w=[G["cw"]])
    self.dma(G["sm"][:], self.esm[e], r=[self.Bin], w=[G["sm"]])
    self.act(lambda: nc.scalar.activation(G["eA"][:], G["sm"][:, 0:4], AF.Exp), r=[G["sm"]], w=[G["eA"]])
    self.dve(lambda: nc.vector.tensor_scalar_mul(G["eA"][:], G["eA"][:], -1.0), r=[G["eA"]], w=[G["eA"]])


def even_inproj_fm(self, G, n, conv_fn, k_dst):
    nc = self.nc
    hn = G["hn"]
    j = 0
    for blk in range(6):
        nb = EW_N[blk]
        w = self.ws.get(nb)
        ncols = nb // KC
        wv = w[:, 0:nb].rearrange("p (k c) -> p k c", k=KC)
        for cc in range(ncols // 128):
            ps = self.P[j % 2]

            def mm():
                ins = None
                for kc in range(KC):
                    ins = nc.tensor.matmul(ps[:, 0:n], wv[:, kc, cc * 128:(cc + 1) * 128], hn[:, kc, 0:n],
                                           start=(kc == 0), stop=(kc == KC - 1))
                return ins
            self.pe(mm, r=[w, hn], w=[ps])
            if j < 4:
                self.act(lambda: nc.scalar.copy(G["qT"][:, j, 0:n], ps[:, 0:n]), r=[ps], w=[G["qT"]])
            elif j == 4:
                self.act(lambda: nc.scalar.copy(k_dst, ps[:, 0:n]), r=[ps], w=[G["kTa"]])
            elif j < 17:
                conv_fn(j - 5, ps)
            else:
                self.act(lambda: nc.scalar.activation(G["zs"][:, j - 17, 0:n], ps[:, 0:n], AF.Silu),
                         r=[ps], w=[G["zs"]])
            j += 1
    assert j == EV_FM


def gates_from_ba(self, G, ba_ps_ap, rows, beta_dst, g_dst, tmp, rd, wr):
    nc = self.nc
    sm = G["sm"]
    self.act(lambda: nc.scalar.activation(beta_dst, ba_ps_ap[:, 0:4], AF.Sigmoid), r=rd, w=wr)
    self.dve(lambda: nc.vector.tensor_tensor(out=tmp[0:rows, 0:4], in0=ba_ps_ap[:, 4:8], in1=sm[0:rows, 4:8], op=ALU.add),
             r=rd + [sm], w=[tmp])
    self.act(lambda: nc.scalar.activation(tmp[0:rows, 0:4], tmp[0:rows, 0:4], AF.Exp), r=[tmp], w=[tmp])
    self.act(lambda: nc.scalar.activation(tmp[0:rows, 0:4], tmp[0:rows, 0:4], AF.Ln, bias=1.0, scale=1.0), r=[tmp], w=[tmp])
    self.dve(lambda: nc.vector.tensor_tensor(out=g_dst, in0=tmp[0:rows, 0:4], in1=G["eA"][0:rows, :], op=ALU.mult),
             r=[tmp, G["eA"]], w=wr)


def l2norm_heads(self, G, n):
    nc = self.nc
    cq, sq4 = G["cq"], G["sq4"]
    for c in range(8):
        ps = self.P[c % 2]
        self.act(lambda: nc.scalar.activation(sq4[:, c % 2, 0:n], cq[:, c, 0:n], AF.Square), r=[cq], w=[sq4])
        self.pe(lambda: nc.tensor.matmul(ps[:, 0:n], self.ones_bf[:], sq4[:, c % 2, 0:n], start=True, stop=True),
                r=[sq4, self.ones_bf], w=[ps])
        rs = G["acc"][c % 2]
        self.act(lambda: nc.scalar.activation(rs[:, 0:n], ps[:, 0:n], AF.Ln, bias=EPS, scale=1.0), r=[ps], w=[rs])
        self.act(lambda: nc.scalar.activation(rs[:, 0:n], rs[:, 0:n], AF.Exp, scale=-0.5,
                                              bias=(math.log(128.0 ** -0.5) if c < 4 else 0.0)), r=[rs], w=[rs])
        self.dve(lambda: nc.vector.tensor_tensor(out=cq[:, c, 0:n], in0=cq[:, c, 0:n], in1=rs[:, 0:n], op=ALU.mult),
                 r=[cq, rs], w=[cq])


def gdn_post(self, G, n, e):
    nc = self.nc
    oT, sq4, mix = G["oT"], G["sq4"], G["sqm"]
    for h in range(4):
        ps = self.P[h % 2]
        self.act(lambda: nc.scalar.activation(sq4[:, h % 2, 0:n], oT[:, h, 0:n], AF.Square), r=[oT], w=[sq4])
        self.pe(lambda: nc.tensor.matmul(ps[:, 0:n], self.ones_bf[:], sq4[:, h % 2, 0:n], start=True, stop=True),
                r=[sq4, self.ones_bf], w=[ps])
        rs = G["acc"][h % 2]
        self.act(lambda: nc.scalar.activation(rs[:, 0:n], ps[:, 0:n], AF.Ln, bias=EPS, scale=1.0 / 128.0), r=[ps], w=[rs])
        self.act(lambda: nc.scalar.activation(rs[:, 0:n], rs[:, 0:n], AF.Exp, scale=-0.5), r=[rs], w=[rs])
        self.dve(lambda: nc.vector.scalar_tensor_tensor(out=rs[:, 0:n], in0=oT[:, h, 0:n], scalar=G["sm"][:, 16:17],
                                                        in1=rs[:, 0:n], op0=ALU.mult, op1=ALU.mult),
                 r=[oT, rs, G["sm"]], w=[rs])
        self.dve(lambda: nc.vector.tensor_tensor(out=mix[:, 4 + h, 0:n], in0=rs[:, 0:n], in1=G["zs"][:, h, 0:n], op=ALU.mult),
                 r=[rs, G["zs"]], w=[mix])


def even_outproj(self, G, segs_bufs, c0, n):
    nc = self.nc
    mix = G["sqm"]
    for blk in range(2):
        w = self.ws.get(4096)
        wv = w[:, 0:4096].rearrange("p (k c) -> p k c", k=KC)
        for dc in range(4):
            dch = blk * 4 + dc
            ps = self.P[dch % 2]

            def mm():
                ins = None
                for kc in range(KC):
                    ins = nc.tensor.matmul(ps[:, 0:n], wv[:, kc, dc * 128:(dc + 1) * 128], mix[:, kc, 0:n],
                                           start=(kc == 0), stop=(kc == KC - 1))
                return ins
            self.pe(mm, r=[w, mix], w=[ps])
            xs = self.xT[:, dch, c0:c0 + n]
            self.dve(lambda: nc.vector.tensor_tensor(out=xs, in0=xs, in1=ps[:, 0:n], op=ALU.add),
                     r=[ps] + segs_bufs, w=segs_bufs)


def swa_block(self, G, W, qb, ql):
    nc = self.nc
    mix = G["sqm"]
    sm = G["sm"]
    k0 = (qb - 1) * 128 if qb > 0 else 0
    nk = 256 if qb > 0 else 128
    nkt = nk // 128

    def pairgen(c):
        q = c % 2
        s_ps, t_ps, o_ps = self.P[2 + q], self.P[4 + q], self.P[6 + q]
        sb, pn, PT, col = W["sb"][q], W["pn"][q], W["PT"][q], W["col"][q]
        tv = t_ps[:, :].bitcast(BF16)
        for kv in range(2):
            h = kv * 4 + c
            rows = slice(kv * 64, (kv + 1) * 64)
            yield self.pe(lambda: nc.tensor.matmul(s_ps[:, 0:nk], G["qT"][rows, c, ql:ql + 128], G["kTa"][rows, k0:k0 + nk],
                                                   start=True, stop=True), r=[G["qT"], G["kTa"]], w=[s_ps])
            yield self.dve(lambda: nc.vector.scalar_tensor_tensor(out=sb[:, 0:nk], in0=s_ps[:, 0:nk], scalar=0.125,
                                                                  in1=G["bias"][:, h, 256 - nk:256], op0=ALU.mult, op1=ALU.add),
                           r=[s_ps, G["bias"]], w=[sb])
            yield self.dve(lambda: nc.vector.reduce_max(out=col[:, 0:1], in_=sb[:, 0:nk], axis=AX.X), r=[sb], w=[col])
            yield self.dve(lambda: nc.vector.tensor_tensor(out=col[:, 0:1], in0=col[:, 0:1], in1=sm[:, 8 + h:9 + h], op=ALU.max),
                           r=[col, sm], w=[col])
            yield self.dve(lambda: nc.vector.tensor_scalar_mul(col[:, 1:2], col[:, 0:1], -1.0), r=[col], w=[col])
            yield self.act(lambda: nc.scalar.activation(sb[:, 0:nk], sb[:, 0:nk], AF.Exp, bias=col[:, 1:2], scale=1.0),
                           r=[sb, col], w=[sb])
            yield self.dve(lambda: nc.vector.reduce_sum(out=col[:, 2:3], in_=sb[:, 0:nk], axis=AX.X), r=[sb], w=[col])
            yield self.act(lambda: nc.scalar.activation(col[:, 3:4], sm[:, 8 + h:9 + h], AF.Exp, bias=col[:, 1:2], scale=1.0),
                           r=[sm, col], w=[col])
            yield self.dve(lambda: nc.vector.tensor_tensor(out=col[:, 2:3], in0=col[:, 2:3], in1=col[:, 3:4], op=ALU.add),
                           r=[col], w=[col])
            yield self.dve(lambda: nc.vector.reciprocal(col[:, 4:5], col[:, 2:3]), r=[col], w=[col])
            yield self.dve(lambda: nc.vector.tensor_scalar_mul(pn[:, 0:nk], sb[:, 0:nk], col[:, 4:5]), r=[sb, col], w=[pn])

            def tr():
                ins = None
                for kt in range(nkt):
                    ins = nc.tensor.transpose(tv[:, kt * 128:(kt + 1) * 128], pn[:, kt * 128:(kt + 1) * 128],
                                              self.ident_bf[:])
                return ins
            yield self.pe(tr, r=[pn, self.ident_bf], w=[t_ps])
            yield self.act(lambda: nc.scalar.copy(PT[:, 0:nk], tv[:, 0:nk]), r=[t_ps], w=[PT])

            def pv():
                ins = None
                for kt in range(nkt):
                    ins = nc.tensor.matmul(o_ps[:, 0:128], G["Va"][:, k0 // 128 + kt, kv * 128:(kv + 1) * 128],
                                           PT[:, kt * 128:(kt + 1) * 128],
                                           start=(kv == 0 and kt == 0), stop=(kv == 1 and kt == nkt - 1))
                return ins
            yield self.pe(pv, r=[G["Va"], PT], w=[o_ps])
        yield self.act(lambda: nc.scalar.copy(mix[:, c, ql:ql + 128], o_ps[:, 0:128]), r=[o_ps], w=[mix])

    for pr in range(2):
        gens = [pairgen(2 * pr), pairgen(2 * pr + 1)]
        while gens:
            for g in list(gens):
                try:
                    next(g)
                except StopIteration:
                    gens.remove(g)


def gdn_subtile(self, G, W, st, cs):
    nc = self.nc
    cq = G["cq"]
    mk = self.mk
    ident = self.ident
    beta_t, g_t = G["beta_t"], G["g_t"]
    sc = W["sc"]
    R = W["R"]
    P = self.P
    Sst, Sbf, oT = G["Sst"], G["Sbf"], G["oT"]
    self.pe(lambda: nc.tensor.matmul(P[4][:, 0:4], mk[:, 0, :], g_t[:, st, :], start=True, stop=True), r=[mk, g_t], w=[P[4]])
    self.pe(lambda: nc.tensor.matmul(P[4][:, 4:8], mk[:, 4, :], g_t[:, st, :], start=True, stop=True), r=[mk, g_t], w=[P[4]])
    self.dve(lambda: nc.vector.tensor_copy(sc[:, 0:8], P[4][:, 0:8]), r=[P[4]], w=[sc])
    self.dve(lambda: nc.vector.tensor_tensor(out=sc[:, 8:12], in0=sc[:, 4:8], in1=sc[:, 0:4], op=ALU.subtract), r=[sc], w=[sc])
    self.act(lambda: nc.scalar.activation(sc[:, 8:12], sc[:, 8:12], AF.Exp), r=[sc], w=[sc])
    self.act(lambda: nc.scalar.activation(sc[:, 12:16], sc[:, 0:4], AF.Exp), r=[sc], w=[sc])
    self.dve(lambda: nc.vector.tensor_tensor(out=sc[:, 16:20], in0=sc[:, 12:16], in1=beta_t[:, st, :], op=ALU.mult),
             r=[sc, beta_t], w=[sc])
    self.dve(lambda: nc.vector.tensor_tensor(out=R[:, :, 0:128], in0=mk[:, 0, :].unsqueeze(1).to_broadcast([128, 4, 128]),
                                             in1=g_t[:, st, :].unsqueeze(2).to_broadcast([128, 4, 128]), op=ALU.mult),
             r=[mk, g_t], w=[R])
    self.dve(lambda: nc.vector.tensor_tensor(out=R[:, :, 128:256], in0=ident[:].unsqueeze(1).to_broadcast([128, 4, 128]),
                                             in1=beta_t[:, st, :].unsqueeze(2).to_broadcast([128, 4, 128]), op=ALU.mult),
             r=[ident, beta_t], w=[R])
    for hh in range(2):
        self.pe(lambda: nc.tensor.matmul(P[hh][:, 0:512], self.ones_f[:], R[:, 2 * hh:2 * hh + 2, :], start=True, stop=True),
                r=[self.ones_f, R], w=[P[hh]])
    bcS = W["R"]
    self.dve(lambda: nc.vector.tensor_copy(bcS[:, 0:2, :], P[0][:, 0:512].rearrange("p (a b) -> p a b", a=2)), r=[P[0]], w=[bcS])
    self.act(lambda: nc.scalar.copy(bcS[:, 2:4, :], P[1][:, 0:512].rearrange("p (a b) -> p a b", a=2)), r=[P[1]], w=[bcS])

    def gcbc(h):
        return bcS[:, h, 0:128]

    def betabc(h):
        return bcS[:, h, 128:256]
    for h in range(4):
        self.act(lambda: nc.scalar.activation(sc[:, 20 + 2 * h:22 + 2 * h], gcbc(h)[:, 63:128:64], AF.Exp), r=[bcS], w=[sc])

    def head(h):
        q = h % 2
        tw = W["set"][q]
        PA, PN, PT_ = P[2 + q], P[4 + q], P[6 + q]
        kn = cq[:, 4 + h, cs:cs + 128]
        qn = cq[:, h, cs:cs + 128]
        vv = cq[:, 8 + h, cs:cs + 128]
        bc = bcS
        yield self.pe(lambda: nc.tensor.transpose(PA[:, 0:128], kn, ident[:]), r=[cq, ident], w=[PA])
        yield self.pe(lambda: nc.tensor.transpose(PA[:, 128:256], vv, ident[:]), r=[cq, ident], w=[PA])
        yield self.pe(lambda: nc.tensor.matmul(PA[:, 256:384], kn, kn, start=True, stop=True), r=[cq], w=[PA])
        yield self.pe(lambda: nc.tensor.matmul(PA[:, 384:512], kn, qn, start=True, stop=True), r=[cq], w=[PA])
        yield self.dve(lambda: nc.vector.tensor_scalar_mul(tw["kbg"][:], PA[:, 0:128], sc[:, 16 + h:17 + h]), r=[PA, sc], w=[tw["kbg"]])
        yield self.act(lambda: nc.scalar.mul(tw["kd"][:], PA[:, 0:128], sc[:, 8 + h:9 + h]), r=[PA, sc], w=[tw["kd"]])
        yield self.dve(lambda: nc.vector.tensor_scalar_mul(tw["vb"][:], PA[:, 128:256], beta_t[:, st, h:h + 1]),
                       r=[PA, beta_t], w=[tw["vb"]])
        yield self.dve(lambda: nc.vector.scalar_tensor_tensor(out=tw["tA"][:], in0=gcbc(h), scalar=sc[:, h:h + 1], in1=mk[:, 1, :],
                                                              op0=ALU.subtract, op1=ALU.max), r=[bc, sc, mk], w=[tw["tA"]])
        yield self.act(lambda: nc.scalar.activation(tw["Es"][:], tw["tA"][:], AF.Exp, scale=-1.0), r=[tw["tA"]], w=[tw["Es"]])
        yield self.dve(lambda: nc.vector.scalar_tensor_tensor(out=tw["tB"][:], in0=gcbc(h), scalar=sc[:, h:h + 1], in1=mk[:, 2, :],
                                                              op0=ALU.subtract, op1=ALU.min), r=[bc, sc, mk], w=[tw["tB"]])
        yield self.act(lambda: nc.scalar.activation(tw["ETd"][:], tw["tB"][:], AF.Exp), r=[tw["tB"]], w=[tw["ETd"]])
        yield self.dve(lambda: nc.vector.tensor_tensor(out=tw["ETs"][:], in0=tw["ETd"][:], in1=mk[:, 3, :], op=ALU.mult),
                       r=[tw["ETd"], mk], w=[tw["ETs"]])
        Pc, Qc, X, Y = tw["Pm"][0], tw["Qm"][0], tw["X"][0], tw["Y"][0]
        yield self.dve(lambda: nc.vector.scalar_tensor_tensor(out=Qc[:], in0=PA[:, 256:384], scalar=beta_t[:, st, h:h + 1],
                                                              in1=tw["Es"][:], op0=ALU.mult, op1=ALU.mult),
                       r=[PA, beta_t, tw["Es"]], w=[Qc])
        yield self.dve(lambda: nc.vector.tensor_tensor(out=tw["tA"][:], in0=PA[:, 256:384], in1=tw["ETs"][:], op=ALU.mult),
                       r=[PA, tw["ETs"]], w=[tw["tA"]])
        yield self.dve(lambda: nc.vector.tensor_tensor(out=Pc[:], in0=tw["tA"][:], in1=betabc(h), op=ALU.mult),
                       r=[tw["tA"], bc], w=[Pc])
        yield self.dve(lambda: nc.vector.tensor_tensor(out=tw["AqkT"][:], in0=PA[:, 384:512], in1=tw["ETd"][:], op=ALU.mult),
                       r=[PA, tw["ETd"]], w=[tw["AqkT"]])
        yield self.act(lambda: nc.scalar.activation(tw["tB"][:], gcbc(h), AF.Exp), r=[bc], w=[tw["tB"]])
        yield self.dve(lambda: nc.vector.tensor_tensor(out=tw["qgT"][:], in0=qn, in1=tw["tB"][:], op=ALU.mult),
                       r=[cq, tw["tB"]], w=[tw["qgT"]])
        yield self.dve(lambda: nc.vector.tensor_tensor(out=X[:], in0=ident[:], in1=Pc[:], op=ALU.subtract), r=[ident, Pc], w=[X])
        for k in range(1, 6):
            bk = PN
            last = (k == 5)
            Pn, Qn, Xn = tw["Pm"][k % 2], tw["Qm"][k % 2], tw["X"][k % 2]

            def sqr():
                ins = nc.tensor.matmul(bk[:, 128:256], Pc[:], Qc[:], start=True, stop=True)
                if not last:
                    ins = nc.tensor.matmul(bk[:, 0:128], Qc[:], Pc[:], start=True, stop=True)
                return ins
            yield self.pe(sqr, r=[Pc, Qc], w=[bk])
            yield self.dve(lambda: nc.vector.tensor_copy(Qn[:], bk[:, 128:256]), r=[bk], w=[Qn])
            if not last:
                yield self.act(lambda: nc.scalar.copy(Pn[:], bk[:, 0:128]), r=[bk], w=[Pn])
            yield self.pe(lambda: nc.tensor.matmul(bk[:, 256:384], Qn[:], X[:], start=True, stop=True), r=[X, Qn], w=[bk])
            yield self.dve(lambda: nc.vector.tensor_tensor(out=Xn[:], in0=X[:], in1=bk[:, 256:384], op=ALU.add), r=[X, bk], w=[Xn])
            Pc, Qc, X = Pn, Qn, Xn
        yield self.pe(lambda: nc.tensor.matmul(PT_[:, 0:128], X[:], tw["vb"][:], start=True, stop=True), r=[X, tw["vb"]], w=[PT_])
        yield self.pe(lambda: nc.tensor.matmul(PT_[:, 128:256], tw["kbg"][:], X[:], start=True, stop=True), r=[X, tw["kbg"]], w=[PT_])
        yield self.act(lambda: nc.scalar.copy(tw["u"][:], PT_[:, 0:128]), r=[PT_], w=[tw["u"]])
        yield self.dve(lambda: nc.vector.tensor_copy(tw["wTa"][:, 0:64], PT_[:, 128:192]), r=[PT_], w=[tw["wTa"]])
        yield self.dve(lambda: nc.vector.tensor_copy(tw["wTz"][:, 64:128], PT_[:, 192:256]), r=[PT_], w=[tw["wTz"]])
        for ck in range(2):
            rr = slice(ck * 64, (ck + 1) * 64)
            if ck == 0:
                yield self.pe(lambda: nc.tensor.matmul(PT_[0:64, 256:384], tw["wTa"][:, 0:64], Sbf[:, h, :], start=True, stop=True),
                              r=[tw["wTa"], Sbf], w=[PT_])
            else:
                yield self.pe(lambda: nc.tensor.matmul(PT_[:, 256:384], tw["wTz"][:], Sbf[:, h, :], start=True, stop=True),
                              r=[tw["wTz"], Sbf], w=[PT_])
            yield self.dve(lambda: nc.vector.tensor_tensor(out=tw["vnew"][rr, :], in0=tw["u"][rr, :], in1=PT_[rr, 256:384],
                                                           op=ALU.subtract), r=[tw["u"], PT_], w=[tw["vnew"]])

            def omm():
                nc.tensor.matmul(PA[:, ck * 64:(ck + 1) * 64], Sbf[:, h, :], tw["qgT"][:, rr], start=True, stop=False)
                return nc.tensor.matmul(PA[:, ck * 64:(ck + 1) * 64], tw["vnew"][rr, :], tw["AqkT"][rr, rr],
                                        start=False, stop=True)
            yield self.pe(omm, r=[Sbf, tw["qgT"], tw["vnew"], tw["AqkT"]], w=[PA])
            yield self.act(lambda: nc.scalar.copy(oT[:, h, cs + ck * 64:cs + (ck + 1) * 64], PA[:, ck * 64:(ck + 1) * 64]),
                           r=[PA], w=[oT])
            yield self.pe(lambda: nc.tensor.matmul(PT_[:, 384:512], tw["kd"][rr, :], tw["vnew"][rr, :], start=True, stop=True),
                          r=[tw["kd"], tw["vnew"]], w=[PT_])
            yield self.dve(lambda: nc.vector.scalar_tensor_tensor(out=Sst[:, h, :], in0=Sst[:, h, :],
                                                                  scalar=sc[:, 20 + 2 * h + ck:21 + 2 * h + ck],
                                                                  in1=PT_[:, 384:512], op0=ALU.mult, op1=ALU.add),
                           r=[Sst, sc, PT_], w=[Sst])
            yield self.act(lambda: nc.scalar.copy(Sbf[:, h, :], Sst[:, h, :]), r=[Sst], w=[Sbf])

    for pair in range(2):
        gens = [head(2 * pair), head(2 * pair + 1)]
        while gens:
            for g in list(gens):
                try:
                    next(g)
                except StopIteration:
                    gens.remove(g)


def even_mixer(self, l):
    nc = self.nc
    e = l // 2
    nsub = self.seq // 128
    self.begin()
    G = even_alloc(self, 512)
    T = self.T
    G["pre"] = [T("pre%d" % i, [128, 515], F32) for i in range(2)]
    G["kTa"] = T("kTa", [128, self.seq], BF16)
    G["Va"] = T("Va", [128, nsub, 256], BF16)
    G["carry"] = T("carry", [128, 12, 3], F32)
    U = T("U", [128, 2432], F32)
    G["bias"] = Tile(nc, "bias", [128, 8, 256], F32, handle=U.t[:, 0:2048].rearrange("p (a b) -> p a b", a=8))
    G["Sst"] = T("Sst", [128, 4, 128], F32)
    G["Sbf"] = T("Sbf", [128, 4, 128], BF16)
    G["beta_t"] = T("beta_t", [128, 4, 4], F32)
    G["g_t"] = T("g_t", [128, 4, 4], F32)
    gtmp = T("gtmp", [128, 4], F32)
    kvo = T("kvo", [128, 256], F32)
    gco = T("gco", [128, 1536], F32)
    W = {"sb": [T("sb%d" % i, [128, 256], F32) for i in range(2)],
         "pn": [T("pn%d" % i, [128, 256], BF16) for i in range(2)],
         "PT": [T("PT%d" % i, [128, 256], BF16) for i in range(2)],
         "col": [T("col%d" % i, [128, 8], F32) for i in range(2)],
         "sc": T("sc", [128, 40], F32),
         "R": T("R", [128, 4, 256], F32),
         "set": []}
    d = {}
    for nm in ("kbg", "vb", "tA", "Es", "tB", "ETd", "ETs", "u"):
        d[nm] = T("%s0" % nm, [128, 128], F32)
    for nm in ("kd", "AqkT", "qgT", "wTa", "wTz", "vnew"):
        d[nm] = T("%s0" % nm, [128, 128], BF16)
    for nm in ("Pm", "Qm", "X", "Y"):
        d[nm] = [T("%s0_%d" % (nm, j), [128, 128], F32) for j in range(2)]
    W["set"].append(d)
    self.dve(lambda: nc.vector.memset(d["wTz"][:], 0.0), w=[d["wTz"]])
    d2 = {}
    off = [0]

    def uview(name, dt):
        if dt == F32:
            v = U.t[:, off[0]:off[0] + 128]
            off[0] += 128
        else:
            v = U.t[:, off[0]:off[0] + 64].bitcast(BF16)
            off[0] += 64
        return Tile(nc, name, [128, 128], dt, handle=v)
    for nm in ("kbg", "vb", "tA", "Es", "tB", "ETd", "ETs", "u"):
        d2[nm] = uview(nm + "1", F32)
    for nm in ("Pm", "Qm", "X", "Y"):
        d2[nm] = [uview("%s1_%d" % (nm, j), F32) for j in range(2)]
    for nm in ("kd", "AqkT", "qgT", "wTa", "wTz", "vnew"):
        d2[nm] = uview(nm + "1", BF16)
    assert off[0] == 2432
    W["set"].append(d2)
    even_common(self, G, e)
    self.dve(lambda: nc.vector.memset(G["carry"][:], 0.0), w=[G["carry"]])
    self.dve(lambda: nc.vector.memset(G["Sst"][:], 0.0), w=[G["Sst"]])
    self.dve(lambda: nc.vector.memset(G["Sbf"][:], 0.0), w=[G["Sbf"]])
    cw = G["cw"]
    for ti in range(self.nt):
        c0 = ti * 512
        xb = [self.xTb[ti]]
        self.S.barrier()
        self.dma(G["bias"][:].rearrange("p a b -> p (a b)"), self.swabias, r=[self.Bin], w=[G["bias"]])
        self.rmsnorm_cols(xb, c0, 512, 4 + l, G["hn"], 0, G["sqm"], G["rstd"])
        cnt = [0]
        pend_silu = [None]

        def conv_fn(ch, ps):
            pre = G["pre"][cnt[0] % 2]
            acc = G["acc"][cnt[0] % 2]
            cnt[0] += 1
            self.dve(lambda: nc.vector.tensor_copy(pre[:, 0:3], G["carry"][:, ch, :]), r=[G["carry"]], w=[pre])
            self.act(lambda: nc.scalar.copy(pre[:, 3:515], ps[:, 0:512]), r=[ps], w=[pre])
            if pend_silu[0] is not None:
                pend_silu[0]()
                pend_silu[0] = None
            self.dve(lambda: nc.vector.tensor_copy(G["carry"][:, ch, :], pre[:, 512:515]), r=[pre], w=[G["carry"]])
            self.dve(lambda: nc.vector.tensor_scalar_mul(acc[:], pre[:, 0:512], cw[:, ch, 0:1]), r=[pre, cw], w=[acc])
            for i in range(1, 4):
                self.dve(lambda: nc.vector.scalar_tensor_tensor(out=acc[:], in0=pre[:, i:i + 512], scalar=cw[:, ch, i:i + 1],
                                                                 in1=acc[:], op0=ALU.mult, op1=ALU.add), r=[pre, cw, acc], w=[acc])
            def silu_(ch=ch, acc=acc):
                self.act(lambda: nc.scalar.activation(G["cq"][:, ch, :], acc[:], AF.Silu), r=[acc], w=[G["cq"]])
            pend_silu[0] = silu_
        even_inproj_fm(self, G, 512, conv_fn, G["kTa"][:, c0:c0 + 512])
        if pend_silu[0] is not None:
            pend_silu[0]()
            pend_silu[0] = None
        w = self.ws.get(EW_N[6])
        wv = w[:, 0:EW_N[6]].rearrange("p (k c) -> p k c", k=KC)
        for st in range(4):
            gs = ti * 4 + st
            ps = self.P[2 + st % 2]

            def mm():
                ins = None
                for kc in range(KC):
                    ins = nc.tensor.matmul(ps[:, 0:EV_TOK], G["hn"][:, kc, st * 128:(st + 1) * 128], wv[:, kc, :],
                                           start=(kc == 0), stop=(kc == KC - 1))
                return ins
            self.pe(mm, r=[w, G["hn"]], w=[ps])
            self.act(lambda: nc.scalar.copy(G["Va"][:, gs, :], ps[:, 0:256]), r=[ps], w=[G["Va"]])
            gates_from_ba(self, G, ps[:, 384:392], 128, G["beta_t"][:, st, :], G["g_t"][:, st, :], gtmp, [ps],
                          [G["beta_t"], G["g_t"]])
            if gs == nsub - 1 and not self.cfg.get("skip_kvo"):
                self.act(lambda: nc.scalar.copy(kvo[:, 0:128], ps[:, 256:384]), r=[ps], w=[kvo])
                self.act(lambda: nc.scalar.copy(kvo[:, 128:256], ps[:, 0:128]), r=[ps], w=[kvo])
                self.dve(lambda: nc.vector.tensor_tensor(out=kvo[:, 128:256], in0=kvo[:, 128:256], in1=ps[:, 128:256], op=ALU.add),
                         r=[ps, kvo], w=[kvo])
                pass
        if self.cfg.get("skip_swa") or self.cfg.get("skip_gdn") or self.cfg.get("gdn_stop", 6) < 6:
            self.dve(lambda: nc.vector.memset(G["sqm"][:], 0.0), w=[G["sqm"]])
            self.dve(lambda: nc.vector.memset(G["oT"][:], 0.0), w=[G["oT"]])
        if not self.cfg.get("skip_swa"):
            for qi in range(4):
                swa_block(self, G, W, ti * 4 + qi, qi * 128)
        self.S.barrier()
        self.dve(lambda: nc.vector.memset(W["set"][1]["wTz"][:], 0.0), w=[W["set"][1]["wTz"]])
        l2norm_heads(self, G, 512)
        if not self.cfg.get("skip_gdn"):
            for st in range(4):
                gdn_subtile(self, G, W, st, st * 128)
        gdn_post(self, G, 512, e)
        even_outproj(self, G, xb, c0, 512)
    self.out_dma(self.o_pk[e], kvo[:, 0:128], r=[kvo])
    self.out_dma(self.o_pv[e], kvo[:, 128:256], r=[kvo])
    for ch in range(12):
        self.out_dma(self.o_pgc[e][:, ch * 128:(ch + 1) * 128].rearrange("i p -> p i"), G["carry"][:, ch, :], r=[G["carry"]])
    self.out_dma(self.o_pgs[e].rearrange("h k v -> k h v"), G["Sst"][:], r=[G["Sst"]])
    self.end()
    if self.cfg.get("skip_dec"):
        for b in range(7):
            self.ws.get(EW_N[b])
        for b in range(2):
            self.ws.get(4096)
    else:
        even_decode(self, l)


def even_decode(self, l):
    nc = self.nc
    e = l // 2
    n = NS
    c0 = self.seq
    P = self.P
    ident, mk = self.ident, self.mk
    self.begin()
    T = self.T
    G = even_alloc(self, n)
    G["kTa"] = T("kTs", [128, n], BF16)
    xx = T("xx", [128, 12, n, 4], F32)
    hs = T("hs", [48, 1536], F32)
    gnew = T("gnew", [n, 1536], F32)
    knv = T("knv", [n, 256], F32)
    gts = T("gts", [n, 16], F32)
    Kc = T("Kc", [128, n, 128], F32)
    Vc = T("Vc", [128, n, 128], F32)
    Qb = T("Qb", [128, n, 8], F32)
    KTb = [T("KTb%d" % i, [128, 128], F32) for i in range(2)]
    sT = T("sT", [128, 128], F32)
    kTn = T("kTn", [128, n], F32)
    vTn = T("vTn", [128, n], F32)
    sf = T("sf", [128, 132], F32)
    pnf = T("pnf", [128, 132], F32)
    col = T("dcol", [128, 8], F32)
    PTs = T("PTs", [128, 128], F32)
    R2 = T("R2", [128, 128], F32)
    ov = T("ov", [128, n, 8], F32)
    dbias = T("dbias", [128, 129], F32)
    skc = T("skc", [128, 1], F32)
    R3 = T("R3", [n, 2, 4, n], F32)
    bcs = T("bcs", [128, 2, 4, n], F32)
    Sd = T("Sd", [128, n, 4, 128], F32)
    vnT = T("vnT", [128, 4, n], F32)
    t1 = T("t1", [128, 4, n], F32)
    krow = T("krow", [n, 512], F32)
    vrow = T("vrow", [n, 512], F32)
    Km = [T("Km%d" % i, [n, 512], F32) for i in range(2)]
    tS = T("tS", [128, 4, 128], F32)
    mix = G["sqm"]
    even_common(self, G, e)
    self.dma(dbias[:], self.decbias, r=[self.Bin], w=[dbias])
    self.dma(skc[:], self.esk[e], r=[self.Bin], w=[skc])
    self.dma(hs[:], self.s_gc[e].rearrange("b i c -> (b i) c"), r=[self.Bin], w=[hs])
    self.dma(Kc[:], self.c_k[e].rearrange("b w f -> w b f"), r=[self.Bin], w=[Kc])
    self.dma(Vc[:], self.c_v[e].rearrange("b w f -> w b f"), r=[self.Bin], w=[Vc])
    self.dma(Sd[:], self.s_gs[e].rearrange("b h k v -> k b h v"), r=[self.Bin], w=[Sd])
    for ch in range(12):
        ps = P[2 + ch % 2]
        self.pe(lambda: nc.tensor.transpose(ps[:, 0:48], hs[:, ch * 128:(ch + 1) * 128], ident[0:48, 0:48]),
                r=[hs, ident], w=[ps])
        self.dve(lambda: nc.vector.tensor_copy(xx[:, ch, :, 0:3], ps[:, 0:48].rearrange("p (b i) -> p b i", i=3)),
                 r=[ps], w=[xx])
    xb = [self.xTb[self.nt]]
    self.rmsnorm_cols(xb, c0, n, 4 + l, G["hn"], 0, G["sqm"], G["rstd"])
    cw = G["cw"]
    cnt = [0]

    def conv_fn(ch, ps):
        acc = G["acc"][cnt[0] % 2]
        cnt[0] += 1
        self.act(lambda: nc.scalar.copy(xx[:, ch, :, 3], ps[:, 0:n]), r=[ps], w=[xx])
        self.dve(lambda: nc.vector.tensor_scalar_mul(acc[:, 0:n], xx[:, ch, :, 0], cw[:, ch, 0:1]), r=[xx, cw], w=[acc])
        for i in range(1, 4):
            self.dve(lambda: nc.vector.scalar_tensor_tensor(out=acc[:, 0:n], in0=xx[:, ch, :, i], scalar=cw[:, ch, i:i + 1],
                                                            in1=acc[:, 0:n], op0=ALU.mult, op1=ALU.add), r=[xx, cw, acc], w=[acc])
        self.act(lambda: nc.scalar.activation(G["cq"][:, ch, 0:n], acc[:, 0:n], AF.Silu), r=[acc], w=[G["cq"]])
        pt = P[6]
        self.pe(lambda: nc.tensor.transpose(pt[0:n, (ch % 4) * 128:(ch % 4 + 1) * 128], xx[:, ch, :, 3], ident[:]),
                r=[xx, ident], w=[pt])
        self.dve(lambda: nc.vector.tensor_copy(gnew[:, ch * 128:(ch + 1) * 128], pt[0:n, (ch % 4) * 128:(ch % 4 + 1) * 128]),
                 r=[pt], w=[gnew])
    even_inproj_fm(self, G, n, conv_fn, G["kTa"][:, 0:n])
    w = self.ws.get(EW_N[6])
    wv = w[:, 0:EW_N[6]].rearrange("p (k c) -> p k c", k=KC)
    ps = P[2]

    def mm():
        ins = None
        for kc in range(KC):
            ins = nc.tensor.matmul(ps[0:n, 0:EV_TOK], G["hn"][:, kc, 0:n], wv[:, kc, :], start=(kc == 0), stop=(kc == KC - 1))
        return ins
    self.pe(mm, r=[w, G["hn"]], w=[ps])
    self.act(lambda: nc.scalar.copy(knv[:, 0:128], ps[0:n, 256:384]), r=[ps], w=[knv])
    self.act(lambda: nc.scalar.copy(knv[:, 128:256], ps[0:n, 0:128]), r=[ps], w=[knv])
    self.dve(lambda: nc.vector.tensor_tensor(out=knv[:, 128:256], in0=knv[:, 128:256], in1=ps[0:n, 128:256], op=ALU.add),
             r=[ps, knv], w=[knv])
    gates_from_ba(self, G, ps[0:n, 384:392], n, gts[:, 0:4], gts[:, 4:8], gts_tmp(gts), [ps], [gts])
    self.act(lambda: nc.scalar.activation(gts[:, 8:12], gts[:, 4:8], AF.Exp), r=[gts], w=[gts])
    self.dve(lambda: nc.vector.memset(Qb[:], 0.0), w=[Qb])
    for c in range(4):
        self.dve(lambda: nc.vector.tensor_copy(Qb[0:64, :, c], G["qT"][0:64, c, 0:n]), r=[G["qT"]], w=[Qb])
        self.dve(lambda: nc.vector.tensor_copy(Qb[64:128, :, 4 + c], G["qT"][64:128, c, 0:n]), r=[G["qT"]], w=[Qb])
    self.dve(lambda: nc.vector.tensor_copy(kTn[:], G["kTa"][:, 0:n]), r=[G["kTa"]], w=[kTn])
    for b in range(n):
        kp = P[b % 2]
        kt = KTb[b % 2]
        self.pe(lambda: nc.tensor.transpose(kp[:, 0:128], Kc[:, b, :], ident[:]), r=[Kc, ident], w=[kp])
        self.act(lambda: nc.scalar.copy(kt[:], kp[:, 0:128]), r=[kp], w=[kt])
        self.pe(lambda: nc.tensor.matmul(P[4][:, b * 8:(b + 1) * 8], kt[:], Qb[:, b, :], start=True, stop=True),
                r=[kt, Qb], w=[P[4]])
    self.dve(lambda: nc.vector.tensor_copy(sT[:], P[4][:, 0:128]), r=[P[4]], w=[sT])
    self.pe(lambda: nc.tensor.transpose(P[5][:, 0:128], sT[:], ident[:]), r=[sT, ident], w=[P[5]])
    self.pe(lambda: nc.tensor.matmul(P[5][:, 128:128 + n], Qb[:].rearrange("p b h -> p (b h)"), kTn[:], start=True, stop=True),
            r=[Qb, kTn], w=[P[5]])
    self.dve(lambda: nc.vector.tensor_tensor(out=sf[:, 0:n], in0=P[5][:, 128:128 + n], in1=mk[:, 5, 0:n], op=ALU.mult),
             r=[P[5], mk], w=[sf])
    self.dve(lambda: nc.vector.reduce_sum(out=col[:, 5:6], in_=sf[:, 0:n], axis=AX.X), r=[sf], w=[col])
    self.dve(lambda: nc.vector.scalar_tensor_tensor(out=sf[:, 0:128], in0=P[5][:, 0:128], scalar=0.125, in1=dbias[:, 0:128],
                                                    op0=ALU.mult, op1=ALU.add), r=[P[5], dbias], w=[sf])
    self.dve(lambda: nc.vector.scalar_tensor_tensor(out=sf[:, 128:129], in0=col[:, 5:6], scalar=0.125, in1=dbias[:, 128:129],
                                                    op0=ALU.mult, op1=ALU.add), r=[col, dbias], w=[sf])
    self.dve(lambda: nc.vector.reduce_max(out=col[:, 0:1], in_=sf[:, 0:129], axis=AX.X), r=[sf], w=[col])
    self.dve(lambda: nc.vector.tensor_tensor(out=col[:, 0:1], in0=col[:, 0:1], in1=skc[:, 0:1], op=ALU.max), r=[col, skc], w=[col])
    self.dve(lambda: nc.vector.tensor_scalar_mul(col[:, 1:2], col[:, 0:1], -1.0), r=[col], w=[col])
    self.act(lambda: nc.scalar.activation(sf[:, 0:129], sf[:, 0:129], AF.Exp, bias=col[:, 1:2], scale=1.0), r=[sf, col], w=[sf])
    self.dve(lambda: nc.vector.reduce_sum(out=col[:, 2:3], in_=sf[:, 0:129], axis=AX.X), r=[sf], w=[col])
    self.act(lambda: nc.scalar.activation(col[:, 3:4], skc[:, 0:1], AF.Exp, bias=col[:, 1:2], scale=1.0), r=[skc, col], w=[col])
    self.dve(lambda: nc.vector.tensor_tensor(out=col[:, 2:3], in0=col[:, 2:3], in1=col[:, 3:4], op=ALU.add), r=[col], w=[col])
    self.dve(lambda: nc.vector.reciprocal(col[:, 4:5], col[:, 2:3]), r=[col], w=[col])
    self.dve(lambda: nc.vector.tensor_scalar_mul(pnf[:, 0:129], sf[:, 0:129], col[:, 4:5]), r=[sf, col], w=[pnf])
    self.pe(lambda: nc.tensor.transpose(P[4][:, 128:256], pnf[:, 0:128], ident[:]), r=[pnf, ident], w=[P[4]])
    self.act(lambda: nc.scalar.copy(PTs[:], P[4][:, 128:256]), r=[P[4]], w=[PTs])
    for b in range(n):
        self.pe(lambda: nc.tensor.matmul(P[6][:, 256 + b * 8:256 + (b + 1) * 8], Vc[:, b, :], PTs[:, b * 8:(b + 1) * 8],
                                         start=True, stop=True), r=[Vc, PTs], w=[P[6]])
    self.pe(lambda: nc.tensor.transpose(P[7][:, 0:n], knv[:, 128:256], ident[0:n, 0:n]), r=[knv, ident], w=[P[7]])
    self.dve(lambda: nc.vector.tensor_copy(vTn[:], P[7][:, 0:n]), r=[P[7]], w=[vTn])
    self.dve(lambda: nc.vector.tensor_scalar_mul(R2[:], ident[:], pnf[:, 128:129]), r=[ident, pnf], w=[R2])
    self.pe(lambda: nc.tensor.matmul(P[7][:, 128:256], self.ones_f[:], R2[:], start=True, stop=True), r=[self.ones_f, R2], w=[P[7]])
    self.dve(lambda: nc.vector.tensor_tensor(out=ov[:], in0=P[7][:, 128:256].rearrange("p (b h) -> p b h", h=8),
                                             in1=vTn[:].unsqueeze(2).to_broadcast([128, n, 8]), op=ALU.mult),
             r=[P[7], vTn], w=[ov])
    self.dve(lambda: nc.vector.tensor_tensor(out=ov[:], in0=ov[:], in1=P[6][:, 256:384].rearrange("p (b h) -> p b h", h=8), op=ALU.add),
             r=[ov, P[6]], w=[ov])
    for c in range(4):
        self.dve(lambda: nc.vector.tensor_copy(mix[0:64, c, 0:n], ov[0:64, :, c]), r=[ov], w=[mix])
        self.dve(lambda: nc.vector.tensor_copy(mix[64:128, c, 0:n], ov[64:128, :, 4 + c]), r=[ov], w=[mix])
    l2norm_heads(self, G, n)
    cq = G["cq"]
    for t in range(2):
        src = gts[:, 0:4] if t == 0 else gts[:, 8:12]
        self.dve(lambda: nc.vector.tensor_tensor(out=R3[:, t, :, :], in0=src.unsqueeze(2).to_broadcast([n, 4, n]),
                                                 in1=ident[0:n, 0:n].unsqueeze(1).to_broadcast([n, 4, n]), op=ALU.mult),
                 r=[gts, ident], w=[R3])
    self.pe(lambda: nc.tensor.matmul(P[0][:, 0:128], self.ones_f[0:n, :], R3[:].rearrange("p t h b -> p (t h b)"),
                                     start=True, stop=True), r=[self.ones_f, R3], w=[P[0]])
    self.dve(lambda: nc.vector.tensor_copy(bcs[:].rearrange("p t h b -> p (t h b)"), P[0][:, 0:128]), r=[P[0]], w=[bcs])
    for b in range(n):
        for h in range(4):
            self.pe(lambda: nc.tensor.matmul(P[1][:, h * n + b:h * n + b + 1], Sd[:, b, h, :], cq[:, 4 + h, b:b + 1],
                                             start=True, stop=True), r=[Sd, cq], w=[P[1]])
    kSv = P[1][:, 0:4 * n].rearrange("p (h b) -> p h b", h=4)
    self.dve(lambda: nc.vector.tensor_tensor(out=t1[:], in0=kSv, in1=bcs[:, 1, :, :], op=ALU.mult), r=[P[1], bcs], w=[t1])
    self.dve(lambda: nc.vector.tensor_tensor(out=t1[:], in0=cq[:, 8:12, 0:n], in1=t1[:], op=ALU.subtract), r=[cq, t1], w=[t1])
    self.dve(lambda: nc.vector.tensor_tensor(out=vnT[:], in0=t1[:], in1=bcs[:, 0, :, :], op=ALU.mult), r=[t1, bcs], w=[vnT])
    for h in range(4):
        self.pe(lambda: nc.tensor.transpose(P[2][0:n, h * 128:(h + 1) * 128], cq[:, 4 + h, 0:n], ident[:]), r=[cq, ident], w=[P[2]])
        self.pe(lambda: nc.tensor.transpose(P[3][0:n, h * 128:(h + 1) * 128], vnT[:, h, :], ident[:]), r=[vnT, ident], w=[P[3]])
    self.act(lambda: nc.scalar.copy(krow[:], P[2][0:n, :]), r=[P[2]], w=[krow])
    self.dve(lambda: nc.vector.tensor_copy(vrow[:], P[3][0:n, :]), r=[P[3]], w=[vrow])
    for b in range(n):
        km = Km[b % 2]
        pp = P[4 + b % 2]
        self.dve(lambda: nc.vector.tensor_scalar_mul(km[:], krow[:], ident[0:n, b:b + 1]), r=[krow, ident], w=[km])

        def mm4():
            ins = None
            for h in range(4):
                ins = nc.tensor.matmul(pp[:, h * 128:(h + 1) * 128], km[:, h * 128:(h + 1) * 128], vrow[:, h * 128:(h + 1) * 128],
                                       start=True, stop=True)
            return ins
        self.pe(mm4, r=[km, vrow], w=[pp])
        self.dve(lambda: nc.vector.tensor_tensor(out=tS[:], in0=Sd[:, b, :, :],
                                                 in1=bcs[:, 1, :, b:b + 1].to_broadcast([128, 4, 128]), op=ALU.mult),
                 r=[Sd, bcs], w=[tS])
        self.dve(lambda: nc.vector.tensor_tensor(out=Sd[:, b, :, :], in0=tS[:], in1=pp[:, :].rearrange("p (h v) -> p h v", h=4), op=ALU.add),
                 r=[tS, pp], w=[Sd])
    for b in range(n):
        for h in range(4):
            self.pe(lambda: nc.tensor.matmul(P[1][:, 64 + h * n + b:64 + h * n + b + 1], Sd[:, b, h, :], cq[:, h, b:b + 1],
                                             start=True, stop=True), r=[Sd, cq], w=[P[1]])
    self.act(lambda: nc.scalar.copy(G["oT"][:, :, 0:n], P[1][:, 64:64 + 4 * n].rearrange("p (h b) -> p h b", h=4)), r=[P[1]], w=[G["oT"]])
    gdn_post(self, G, n, e)
    even_outproj(self, G, xb, c0, n)
    self.out_dma(self.o_sk[e][:, 0:127, :], self.c_k[e][:, 1:128, :], r=[self.Bin])
    self.out_dma(self.o_sv[e][:, 0:127, :], self.c_v[e][:, 1:128, :], r=[self.Bin])
    self.out_dma(self.o_sk[e][:, 127, :], knv[:, 0:128], r=[knv])
    self.out_dma(self.o_sv[e][:, 127, :], knv[:, 128:256], r=[knv])
    self.out_dma(self.o_sgc[e][:, 0:2, :], self.s_gc[e][:, 1:3, :], r=[self.Bin])
    self.out_dma(self.o_sgc[e][:, 2, :], gnew[:], r=[gnew])
    self.out_dma(self.o_sgs[e].rearrange("b h k v -> k b h v"), Sd[:], r=[Sd])
    self.end()


def gts_tmp(gts):
    class _V:
        b = gts.b

        def __getitem__(self, k):
            rows, cols = k
            return gts.t[rows, 12 + cols.start:12 + cols.stop]
    return _V()


Prog.even_mixer = even_mixer


OW_DT = 256
H_C, P_C, N_C, G_C = 32, 64, 128, 4


def odd_consts(self, O, e):
    nc = self.nc
    self.dma(O["cw"][:].rearrange("p a b -> p (a b)"), self.oconv[e], r=[self.Bin], w=[O["cw"]])
    self.dma(O["hb"][:], self.ohead[e], r=[self.Bin], w=[O["hb"]])
    self.dma(O["fc"][:], self.ofeat[e], r=[self.Bin], w=[O["fc"]])
    self.act(lambda: nc.scalar.activation(O["hb"][:, 32:64], O["hb"][:, 32:64], AF.Exp), r=[O["hb"]], w=[O["hb"]])
    self.dve(lambda: nc.vector.tensor_scalar_mul(O["hb"][:, 32:64], O["hb"][:, 32:64], -1.0), r=[O["hb"]], w=[O["hb"]])


def odd_alloc(self, n, sqm=None):
    T = self.T
    O = {}
    O["hn"] = T("hn", [128, KC, n], BF16)
    O["sqm"] = sqm if sqm is not None else T("sqm", [128, KC, n], BF16)
    O["rstd"] = T("rstd", [128, n], F32)
    O["cw"] = T("ocw", [128, 24, 4], F32)
    O["hb"] = T("ohb", [128, 96], F32)
    O["fc"] = T("ofc", [128, 56], F32)
    O["acc"] = [T("oacc%d" % i, [128, n], F32) for i in range(2)]
    O["yT"] = T("yT", [128, 16, n], BF16)
    O["BT"] = T("BT", [128, 4, n], BF16)
    O["CT"] = T("CT", [128, 4, n], BF16)
    O["xc"] = [T("xc%d" % i, [128, n], BF16) for i in range(1)]
    return O


def odd_dt(self, O, ps_ap, rows, dt_dst, a_dst, rd, wr):
    nc = self.nc
    hb = O["hb"]
    self.dve(lambda: nc.vector.tensor_tensor(out=dt_dst, in0=ps_ap, in1=hb[0:rows, 0:32], op=ALU.add), r=rd + [hb], w=wr)
    self.act(lambda: nc.scalar.activation(dt_dst, dt_dst, AF.Exp), r=wr, w=wr)
    self.act(lambda: nc.scalar.activation(dt_dst, dt_dst, AF.Ln, bias=1.0, scale=1.0), r=wr, w=wr)
    self.dve(lambda: nc.vector.tensor_tensor(out=a_dst, in0=dt_dst, in1=hb[0:rows, 32:64], op=ALU.mult), r=wr + [hb], w=wr)


def odd_gate_norm_out(self, O, xbufs, c0, n, zfn):
    nc = self.nc
    yT, fc = O["yT"], O["fc"]
    sq = O["sqm"]
    for j in range(16):
        def consume(ps, j=j):
            zs = O["acc"][j % 2]
            self.act(lambda: nc.scalar.activation(zs[:, 0:n], ps[:, 0:n], AF.Silu), r=[ps], w=[zs])
            self.dve(lambda: nc.vector.tensor_tensor(out=yT[:, j, 0:n], in0=yT[:, j, 0:n], in1=zs[:, 0:n], op=ALU.mult),
                     r=[yT, zs], w=[yT])
        zfn(j, consume)
    for g in range(4):
        ps = self.P[6 + g % 2]
        for jj in range(4):
            j = g * 4 + jj
            self.act(lambda: nc.scalar.activation(sq[:, jj, 0:n], yT[:, j, 0:n], AF.Square), r=[yT], w=[sq])

        def mm():
            ins = None
            for jj in range(4):
                ins = nc.tensor.matmul(ps[:, 0:n], self.ones_bf[:], sq[:, jj, 0:n], start=(jj == 0), stop=(jj == 3))
            return ins
        self.pe(mm, r=[sq, self.ones_bf], w=[ps])
        rs = O["rstd"]
        self.act(lambda: nc.scalar.activation(rs[:, 0:n], ps[:, 0:n], AF.Ln, bias=EPS, scale=1.0 / 512.0), r=[ps], w=[rs])
        self.act(lambda: nc.scalar.activation(rs[:, 0:n], rs[:, 0:n], AF.Exp, scale=-0.5), r=[rs], w=[rs])
        for jj in range(4):
            j = g * 4 + jj
            self.dve(lambda: nc.vector.scalar_tensor_tensor(out=yT[:, j, 0:n], in0=yT[:, j, 0:n], scalar=fc[:, 24 + j:25 + j],
                                                            in1=rs[:, 0:n], op0=ALU.mult, op1=ALU.mult), r=[yT, fc, rs], w=[yT])
    for blk in range(4):
        w = self.ws.get(4096)
        wv = w[:, 0:4096].rearrange("p (k c) -> p k c", k=16)
        for dc in range(2):
            dch = blk * 2 + dc
            ps = self.P[dch % 2]

            def mm2():
                ins = None
                for k in range(16):
                    ins = nc.tensor.matmul(ps[:, 0:n], wv[:, k, dc * 128:(dc + 1) * 128], yT[:, k, 0:n],
                                           start=(k == 0), stop=(k == 15))
                return ins
            self.pe(mm2, r=[w, yT], w=[ps])
            xs = self.xT[:, dch, c0:c0 + n]
            self.dve(lambda: nc.vector.tensor_tensor(out=xs, in0=xs, in1=ps[:, 0:n], op=ALU.add), r=[ps] + xbufs, w=xbufs)


def odd_fm_chunk(self, O, w, cc, n, ps):
    nc = self.nc
    hn = O["hn"]
    wv = w[:, 0:4096].rearrange("p (k c) -> p k c", k=KC)

    def mm():
        ins = None
        for kc in range(KC):
            ins = nc.tensor.matmul(ps[:, 0:n], wv[:, kc, cc * 128:(cc + 1) * 128], hn[:, kc, 0:n],
                                   start=(kc == 0), stop=(kc == KC - 1))
        return ins
    self.pe(mm, r=[w, hn], w=[ps])


def odd_mixer(self, l):
    nc = self.nc
    e = l // 2
    P = self.P
    ident, mk = self.ident, self.mk
    self.begin()
    T = self.T
    xd = T("xd", [128, 4, 2048], BF16)
    xdts = [Tile(self.nc, "xdt%d" % i, [128, 2048], BF16, handle=xd.t[:, 2 * i, :]) for i in range(2)]
    xdds = [Tile(self.nc, "xdd%d" % i, [128, 2048], BF16, handle=xd.t[:, 2 * i + 1, :]) for i in range(2)]
    sqa = Tile(self.nc, "sqa", [128, KC, 512], BF16, handle=xd.t.reshape([128, 16, 512])[:, 0:KC, :])
    O = odd_alloc(self, 512, sqm=sqa)
    pre = [T("opre%d" % i, [128, 515], F32) for i in range(2)]
    carry = T("ocarry", [128, 24, 3], F32)
    xst = T("xst", [128, 4, 2048], BF16)
    Btok = T("Btok", [128, 4, 512], BF16)
    dtv = T("dtv", [128, 4, 32], F32)
    av = T("av", [128, 4, 32], F32)
    sms = [T("osm%d" % i, [128, 6, 32], F32) for i in range(2)]
    R4 = [T("R4_%d" % i, [128, 4, 128], F32) for i in range(2)]
    Lt = [T("Lt%d" % i, [128, 4, 128], F32) for i in range(2)]
    Wb = [T("Wb%d" % i, [128, 4, 128], BF16) for i in range(2)]
    ytk = T("ytk", [128, 2048], BF16)
    tt = [T("ott%d" % i, [128, 512], F32) for i in range(2)]
    xcs = [Tile(self.nc, "xcs%d" % i, [128, 512], BF16, handle=tt[i].t.bitcast(BF16)[:, 0:512]) for i in range(2)]
    ST = T("ST", [128, 2048], F32)
    STb = T("STb", [128, 2048], BF16)
    sto = Tile(self.nc, "sto", [128, 32, 128], F32, handle=xst.t.bitcast(F32).reshape([128, 32, 128]))
    sto.b = xst.b
    odd_consts(self, O, e)
    self.dve(lambda: nc.vector.memset(carry[:], 0.0), w=[carry])
    self.dve(lambda: nc.vector.memset(ST[:], 0.0), w=[ST])
    self.dve(lambda: nc.vector.memset(STb[:], 0.0), w=[STb])
    cw, fc, hb = O["cw"], O["fc"], O["hb"]
    for ti in range(self.nt):
        c0 = ti * 512
        xb = [self.xTb[ti]]
        self.S.barrier()
        self.rmsnorm_cols(xb, c0, 512, 4 + l, O["hn"], 0, O["sqm"], O["rstd"])
        w = self.ws.get(OW_DT)
        wv = w[:, 0:OW_DT].rearrange("p (k c) -> p k c", k=KC)
        for st in range(4):
            ps = P[2 + st % 2]

            def mm():
                ins = None
                for kc in range(KC):
                    ins = nc.tensor.matmul(ps[:, 0:32], O["hn"][:, kc, st * 128:(st + 1) * 128], wv[:, kc, :],
                                           start=(kc == 0), stop=(kc == KC - 1))
                return ins
            self.pe(mm, r=[w, O["hn"]], w=[ps])
            odd_dt(self, O, ps[:, 0:32], 128, dtv[:, st, :], av[:, st, :], [ps], [dtv, av])
        pending = [None]
        pend_silu = [None]
        for blk in range(6):
            w = self.ws.get(4096)
            for cc in range(4):
                ch = blk * 4 + cc
                ps = P[ch % 2]
                odd_fm_chunk(self, O, w, cc, 512, ps)
                pr = pre[ch % 2]
                acc = O["acc"][ch % 2]
                self.dve(lambda: nc.vector.tensor_copy(pr[:, 0:3], carry[:, ch, :]), r=[carry], w=[pr])
                self.act(lambda: nc.scalar.copy(pr[:, 3:515], ps[:, 0:512]), r=[ps], w=[pr])
                if pend_silu[0] is not None:
                    pend_silu[0]()
                    pend_silu[0] = None
                self.dve(lambda: nc.vector.tensor_copy(carry[:, ch, :], pr[:, 512:515]), r=[pr], w=[carry])
                self.dve(lambda: nc.vector.tensor_scalar_mul(acc[:], pr[:, 0:512], cw[:, ch, 0:1]), r=[pr, cw], w=[acc])
                for i in range(1, 4):
                    self.dve(lambda: nc.vector.scalar_tensor_tensor(out=acc[:], in0=pr[:, i:i + 512], scalar=cw[:, ch, i:i + 1],
                                                                     in1=acc[:], op0=ALU.mult, op1=ALU.add), r=[pr, cw, acc], w=[acc])
                if ch < 20:
                    dst = xcs[ch % 2] if ch < 16 else None
                    tgt = dst[:, 0:512] if ch < 16 else O["BT"][:, ch - 16, 0:512]
                    tb = dst if ch < 16 else O["BT"]
                    def silu_(ch=ch, tgt=tgt, tb=tb, acc=acc):
                        self.act(lambda: nc.scalar.activation(tgt, acc[:], AF.Silu, bias=fc[:, ch:ch + 1], scale=1.0),
                                 r=[acc, fc], w=[tb])
                    pend_silu[0] = silu_

                    def fin(ch=ch, tgt=tgt, tb=tb):
                        tp = P[4 + ch % 2]
                        tv = tp[:, :].bitcast(BF16)

                        def tr():
                            ins = None
                            for st in range(4):
                                ins = nc.tensor.transpose(tv[:, st * 128:(st + 1) * 128], tgt[:, st * 128:(st + 1) * 128], self.ident_bf[:])
                            return ins
                        self.pe(tr, r=[tb, self.ident_bf], w=[tp])
                        src = tv[:, 0:512].rearrange("p (s c) -> p s c", s=4)
                        if ch < 16:
                            self.dve(lambda: nc.vector.tensor_copy(xst[:, :, ch * 128:(ch + 1) * 128], src), r=[tp], w=[xst])
                        else:
                            self.dve(lambda: nc.vector.tensor_copy(Btok[:, :, (ch - 16) * 128:(ch - 15) * 128], src), r=[tp], w=[Btok])
                    if pending[0] is not None:
                        pending[0]()
                    pending[0] = fin
                else:
                    def silu_c(ch=ch, acc=acc):
                        self.act(lambda: nc.scalar.activation(O["CT"][:, ch - 20, 0:512], acc[:], AF.Silu, bias=fc[:, ch:ch + 1], scale=1.0),
                                 r=[acc, fc], w=[O["CT"]])
                    pend_silu[0] = silu_c
                    if pending[0] is not None:
                        pending[0]()
                        pending[0] = None
        if pend_silu[0] is not None:
            pend_silu[0]()
            pend_silu[0] = None
        def prepgen(st):
            sm = sms[st % 2]
            xdt, xdd = xdts[st % 2], xdds[st % 2]
            acs, acl, dte, cd, eac, dtd = [sm[:, i, :] for i in range(6)]
            yield self.pe(lambda: nc.tensor.matmul(P[7][:, 256:288], mk[:, 6, :], av[:, st, :], start=True, stop=True), r=[mk, av], w=[P[7]])
            yield self.pe(lambda: nc.tensor.matmul(P[7][:, 288:320], self.ones_f[:], av[:, st, :], start=True, stop=True),
                          r=[self.ones_f, av], w=[P[7]])
            yield self.dve(lambda: nc.vector.tensor_copy(sm[:, 0:2, :], P[7][:, 256:320].rearrange("p (a b) -> p a b", a=2)), r=[P[7]], w=[sm])
            yield self.dve(lambda: nc.vector.tensor_tensor(out=dte, in0=acl, in1=acs, op=ALU.subtract), r=[sm], w=[sm])
            yield self.act(lambda: nc.scalar.activation(dte, dte, AF.Exp), r=[sm], w=[sm])
            yield self.act(lambda: nc.scalar.activation(cd, acl, AF.Exp), r=[sm], w=[sm])
            yield self.act(lambda: nc.scalar.activation(eac, acs, AF.Exp), r=[sm], w=[sm])
            yield self.dve(lambda: nc.vector.tensor_tensor(out=dtd, in0=dte, in1=dtv[:, st, :], op=ALU.mult), r=[sm, dtv], w=[sm])
            xv = xst[:, st, :].rearrange("p (h q) -> p h q", q=P_C)
            for q4 in range(4):
                hs = slice(q4 * 8, (q4 + 1) * 8)
                yield self.dve(lambda: nc.vector.tensor_tensor(out=xdt[:, q4 * 512:(q4 + 1) * 512].rearrange("p (h q) -> p h q", q=P_C),
                                                               in0=xv[:, hs, :], in1=dtv[:, st, hs].unsqueeze(2).to_broadcast([128, 8, P_C]),
                                                               op=ALU.mult), r=[xst, dtv], w=[xdt])
                yield self.dve(lambda: nc.vector.tensor_tensor(out=xdd[:, q4 * 512:(q4 + 1) * 512].rearrange("p (h q) -> p h q", q=P_C),
                                                               in0=xv[:, hs, :], in1=dtd[:, hs].unsqueeze(2).to_broadcast([128, 8, P_C]),
                                                               op=ALU.mult), r=[xst, sm], w=[xdd])

        for _ in prepgen(0):
            pass
        for st in range(4):
            cs = st * 128
            sm = sms[st % 2]
            xdt, xdd = xdts[st % 2], xdds[st % 2]
            acs, acl, dte, cd, eac, dtd = [sm[:, i, :] for i in range(6)]

            def group(g):
                q = g % 2
                bcB, yoffB, yps, scB = P[q], P[2 + q], P[4 + q], P[6 + q]
                r4, lt, wb = R4[q], Lt[q], Wb[q]
                yield self.pe(lambda: nc.tensor.matmul(scB[:, 0:128], O["BT"][:, g, cs:cs + 128], O["CT"][:, g, cs:cs + 128], start=True, stop=True),
                              r=[O["BT"], O["CT"]], w=[scB])
                for hf in range(2):
                    h0 = g * 8 + hf * 4
                    yield self.dve(lambda: nc.vector.tensor_tensor(out=r4[:], in0=mk[:, 6, :].unsqueeze(1).to_broadcast([128, 4, 128]),
                                                                   in1=av[:, st, h0:h0 + 4].unsqueeze(2).to_broadcast([128, 4, 128]), op=ALU.mult),
                                   r=[mk, av], w=[r4])
                    yield self.pe(lambda: nc.tensor.matmul(bcB[:, 0:512], self.ones_f[:], r4[:].rearrange("p a b -> p (a b)"), start=True, stop=True),
                                  r=[self.ones_f, r4], w=[bcB])
                    yield self.dve(lambda: nc.vector.tensor_tensor(out=lt[:], in0=bcB[:, 0:512].rearrange("p (a b) -> p a b", a=4),
                                                                   in1=acs[:, h0:h0 + 4].unsqueeze(2).to_broadcast([128, 4, 128]),
                                                                   op=ALU.subtract), r=[bcB, sm], w=[lt])
                    yield self.dve(lambda: nc.vector.tensor_tensor(out=lt[:], in0=lt[:], in1=mk[:, 7, :].unsqueeze(1).to_broadcast([128, 4, 128]),
                                                                   op=ALU.min), r=[lt, mk], w=[lt])
                    yield self.act(lambda: nc.scalar.activation(lt[:], lt[:], AF.Exp), r=[lt], w=[lt])
                    yield self.dve(lambda: nc.vector.tensor_tensor(out=wb[:], in0=lt[:], in1=scB[:, 0:128].unsqueeze(1).to_broadcast([128, 4, 128]),
                                                                   op=ALU.mult), r=[lt, scB], w=[wb])

                    def ymm():
                        ins = None
                        for hh in range(4):
                            h = h0 + hh
                            ins = nc.tensor.matmul(yps[:, (hf * 4 + hh) * 64:(hf * 4 + hh + 1) * 64], wb[:, hh, :], xdt[:, h * 64:(h + 1) * 64],
                                                   start=True, stop=True)
                        return ins
                    yield self.pe(ymm, r=[wb, xdt], w=[yps])
                yield self.pe(lambda: nc.tensor.matmul(yoffB[:, 0:512], O["CT"][:, g, cs:cs + 128], STb[:, g * 512:(g + 1) * 512], start=True, stop=True),
                              r=[O["CT"], STb], w=[yoffB])
                t = tt[q]
                gsl = slice(g * 512, (g + 1) * 512)
                yield self.dve(lambda: nc.vector.tensor_tensor(out=t[:].rearrange("p (h q) -> p h q", q=P_C),
                                                               in0=yoffB[:, 0:512].rearrange("p (h q) -> p h q", q=P_C),
                                                               in1=eac[:, g * 8:(g + 1) * 8].unsqueeze(2).to_broadcast([128, 8, P_C]), op=ALU.mult),
                               r=[yoffB, sm], w=[t])
                yield self.dve(lambda: nc.vector.tensor_tensor(out=t[:], in0=t[:], in1=yps[:, 0:512], op=ALU.add), r=[t, yps], w=[t])
                t2 = O["acc"][q]
                yield self.dve(lambda: nc.vector.tensor_tensor(out=t2[:].rearrange("p (h q) -> p h q", q=P_C),
                                                               in0=xst[:, st, gsl].rearrange("p (h q) -> p h q", q=P_C),
                                                               in1=hb[:, 64 + g * 8:64 + (g + 1) * 8].unsqueeze(2).to_broadcast([128, 8, P_C]), op=ALU.mult),
                               r=[xst, hb], w=[t2])
                yield self.dve(lambda: nc.vector.tensor_tensor(out=ytk[:, gsl], in0=t[:], in1=t2[:], op=ALU.add), r=[t, t2], w=[ytk])
                yield self.pe(lambda: nc.tensor.matmul(bcB[:, 0:512], Btok[:, st, g * 128:(g + 1) * 128], xdd[:, gsl], start=True, stop=True),
                              r=[Btok, xdd], w=[bcB])
                yield self.dve(lambda: nc.vector.tensor_tensor(out=ST[:, gsl].rearrange("p (h q) -> p h q", q=P_C),
                                                               in0=ST[:, gsl].rearrange("p (h q) -> p h q", q=P_C),
                                                               in1=cd[:, g * 8:(g + 1) * 8].unsqueeze(2).to_broadcast([128, 8, P_C]), op=ALU.mult),
                               r=[ST, sm], w=[ST])
                yield self.dve(lambda: nc.vector.tensor_tensor(out=ST[:, gsl], in0=ST[:, gsl], in1=bcB[:, 0:512], op=ALU.add), r=[ST, bcB], w=[ST])
                yield self.act(lambda: nc.scalar.copy(STb[:, gsl], ST[:, gsl]), r=[ST], w=[STb])

            for pr in range(2):
                gens = [group(2 * pr), group(2 * pr + 1)]
                if pr == 0 and st + 1 < 4:
                    gens.append(prepgen(st + 1))
                while gens:
                    for gg in list(gens):
                        try:
                            next(gg)
                        except StopIteration:
                            gens.remove(gg)
            for hf in range(2):
                tp = P[hf]
                tv = tp[:, :].bitcast(BF16)

                def tr2():
                    ins = None
                    for j in range(8):
                        ch = hf * 8 + j
                        ins = nc.tensor.transpose(tv[:, j * 128:(j + 1) * 128], ytk[:, ch * 128:(ch + 1) * 128], self.ident_bf[:])
                    return ins
                self.pe(tr2, r=[ytk, self.ident_bf], w=[tp])
                self.act(lambda: nc.scalar.copy(O["yT"][:, hf * 8:(hf + 1) * 8, cs:cs + 128], tv[:, 0:1024].rearrange("p (j c) -> p j c", j=8)),
                         r=[tp], w=[O["yT"]])
        zw = [None]

        def zfn(j, consume):
            if j % 4 == 0:
                zw[0] = self.ws.get(4096)
            ps = P[2 + j % 2]
            odd_fm_chunk(self, O, zw[0], j % 4, 512, ps)
            consume(ps)
        odd_gate_norm_out(self, O, xb, c0, 512, zfn)
    for ch in range(24):
        self.out_dma(self.o_psc[e][:, ch * 128:(ch + 1) * 128].rearrange("i p -> p i"), carry[:, ch, :], r=[carry])
    for c in range(16):
        ps = P[c % 2]
        self.pe(lambda: nc.tensor.transpose(ps[:, 0:128], ST[:, c * 128:(c + 1) * 128], ident[:]), r=[ST, ident], w=[ps])
        self.dve(lambda: nc.vector.tensor_copy(sto[:, c, :], ps[:, 0:128]), r=[ps], w=[sto])
    self.out_dma(self.o_pss[e].rearrange("(c q) n -> q c n", q=128), sto[:, 0:16, :], r=[sto])
    self.end()
    odd_decode(self, l)


def odd_decode(self, l):
    nc = self.nc
    e = l // 2
    n = NS
    c0 = self.seq
    P = self.P
    ident, mk = self.ident, self.mk
    self.begin()
    T = self.T
    O = odd_alloc(self, n)
    xx = T("oxx", [128, 24, n, 4], F32)
    hs = T("ohs", [48, 3072], F32)
    gnew = T("ognew", [n, 3072], F32)
    xsT = T("xsT", [128, 16, n], F32)
    BCs = T("BCs", [128, 8, n], F32)
    dts = T("dts", [n, 64], F32)
    BCt = T("BCt", [n, 2, 512], F32)
    BCm = [T("BCm%d" % i, [n, 2, 512], F32) for i in range(2)]
    R5 = T("R5", [n, 2, 2, 16, n], F32)
    onesH = T("onesH", [n, 2, 128], F32)
    cols = T("ocols", [128, 2, 16, n], F32)
    xdc = T("xdc", [128, 16, n], F32)
    Sb = [T("Sb%d" % i, [128, 16, 128], F32) for i in range(2)]
    tS = T("otS", [128, 16, 128], F32)
    ysum = T("ysum", [128, 16, n], F32)
    odd_consts(self, O, e)
    cw, fc, hb = O["cw"], O["fc"], O["hb"]
    self.dma(hs[:], self.s_sc[e].rearrange("b i c -> (b i) c"), r=[self.Bin], w=[hs])
    self.dve(lambda: nc.vector.memset(onesH[:], 0.0), w=[onesH])
    self.dve(lambda: nc.vector.memset(onesH[:, 0, 0:64], 1.0), r=[onesH], w=[onesH])
    self.dve(lambda: nc.vector.memset(onesH[:, 1, 64:128], 1.0), r=[onesH], w=[onesH])
    for ch in range(24):
        ps = P[2 + ch % 2]
        self.pe(lambda: nc.tensor.transpose(ps[:, 0:48], hs[:, ch * 128:(ch + 1) * 128], ident[0:48, 0:48]), r=[hs, ident], w=[ps])
        self.dve(lambda: nc.vector.tensor_copy(xx[:, ch, :, 0:3], ps[:, 0:48].rearrange("p (b i) -> p b i", i=3)), r=[ps], w=[xx])
    xb = [self.xTb[self.nt]]
    self.rmsnorm_cols(xb, c0, n, 4 + l, O["hn"], 0, O["sqm"], O["rstd"])
    w = self.ws.get(OW_DT)
    wv = w[:, 0:OW_DT].rearrange("p (k c) -> p k c", k=KC)
    ps = P[2]

    def mm():
        ins = None
        for kc in range(KC):
            ins = nc.tensor.matmul(ps[0:n, 0:32], O["hn"][:, kc, 0:n], wv[:, kc, :], start=(kc == 0), stop=(kc == KC - 1))
        return ins
    self.pe(mm, r=[w, O["hn"]], w=[ps])
    odd_dt(self, O, ps[0:n, 0:32], n, dts[:, 0:32], dts[:, 32:64], [ps], [dts])
    self.act(lambda: nc.scalar.activation(dts[:, 32:64], dts[:, 32:64], AF.Exp), r=[dts], w=[dts])
    for blk in range(6):
        w = self.ws.get(4096)
        for cc in range(4):
            ch = blk * 4 + cc
            ps = P[ch % 2]
            odd_fm_chunk(self, O, w, cc, n, ps)
            acc = O["acc"][ch % 2]
            self.act(lambda: nc.scalar.copy(xx[:, ch, :, 3], ps[:, 0:n]), r=[ps], w=[xx])
            self.dve(lambda: nc.vector.tensor_scalar_mul(acc[:, 0:n], xx[:, ch, :, 0], cw[:, ch, 0:1]), r=[xx, cw], w=[acc])
            for i in range(1, 4):
                self.dve(lambda: nc.vector.scalar_tensor_tensor(out=acc[:, 0:n], in0=xx[:, ch, :, i], scalar=cw[:, ch, i:i + 1],
                                                                in1=acc[:, 0:n], op0=ALU.mult, op1=ALU.add), r=[xx, cw, acc], w=[acc])
            if ch < 16:
                dst, db = xsT[:, ch, :], xsT
            else:
                dst, db = BCs[:, ch - 16, :], BCs
            self.act(lambda: nc.scalar.activation(dst, acc[:, 0:n], AF.Silu, bias=fc[:, ch:ch + 1], scale=1.0), r=[acc, fc], w=[db])
            pt = P[6]
            self.pe(lambda: nc.tensor.transpose(pt[0:n, (ch % 4) * 128:(ch % 4 + 1) * 128], xx[:, ch, :, 3], ident[:]), r=[xx, ident], w=[pt])
            self.dve(lambda: nc.vector.tensor_copy(gnew[:, ch * 128:(ch + 1) * 128], pt[0:n, (ch % 4) * 128:(ch % 4 + 1) * 128]), r=[pt], w=[gnew])
    for t in range(2):
        pt = P[4 + t]
        for g in range(4):
            self.pe(lambda: nc.tensor.transpose(pt[0:n, g * 128:(g + 1) * 128], BCs[:, t * 4 + g, :], ident[:]), r=[BCs, ident], w=[pt])
        self.dve(lambda: nc.vector.tensor_copy(BCt[:, t, :], pt[0:n, 0:512]), r=[pt], w=[BCt])
    for t in range(2):
        src = dts[:, t * 32:(t + 1) * 32].rearrange("p (hp h2) -> p h2 hp", h2=2)
        self.dve(lambda: nc.vector.tensor_tensor(out=R5[:, t], in0=src.unsqueeze(3).to_broadcast([n, 2, 16, n]),
                                                 in1=ident[0:n, 0:n].unsqueeze(1).unsqueeze(1).to_broadcast([n, 2, 16, n]), op=ALU.mult),
                 r=[dts, ident], w=[R5])

        def cmm():
            ins = None
            for h2 in range(2):
                ins = nc.tensor.matmul(P[7][:, t * 256:(t + 1) * 256], onesH[:, h2, :], R5[:, t, h2].rearrange("p a b -> p (a b)"),
                                       start=(h2 == 0), stop=(h2 == 1))
            return ins
        self.pe(cmm, r=[onesH, R5], w=[P[7]])
    self.dve(lambda: nc.vector.tensor_copy(cols[:].rearrange("p t a b -> p (t a b)"), P[7][:, 0:512]), r=[P[7]], w=[cols])
    self.dve(lambda: nc.vector.tensor_tensor(out=xdc[:], in0=cols[:, 0], in1=xsT[:], op=ALU.mult), r=[cols, xsT], w=[xdc])
    sview = self.s_ss[e].rearrange("b (c q) n -> b q c n", q=128)
    oview = self.o_sss[e].rearrange("b (c q) n -> b q c n", q=128)
    self.dma(Sb[0][:], sview[0], r=[self.Bin], w=[Sb[0]])
    for b in range(n):
        S_ = Sb[b % 2]
        if b + 1 < n:
            self.dma(Sb[(b + 1) % 2][:], sview[b + 1], r=[self.Bin], w=[Sb[(b + 1) % 2]])
        bm = BCm[b % 2]
        self.dve(lambda: nc.vector.tensor_scalar_mul(bm[:].rearrange("p a b -> p (a b)"), BCt[:].rearrange("p a b -> p (a b)"),
                                                     ident[0:n, b:b + 1]), r=[BCt, ident], w=[bm])
        pB, pC = P[(b % 2) * 2], P[(b % 2) * 2 + 1]
        self.pe(lambda: nc.tensor.matmul(pB[:, 0:512], self.ones_f[0:n, :], bm[:, 0, :], start=True, stop=True), r=[self.ones_f, bm], w=[pB])
        self.pe(lambda: nc.tensor.matmul(pC[:, 0:512], self.ones_f[0:n, :], bm[:, 1, :], start=True, stop=True), r=[self.ones_f, bm], w=[pC])
        s4 = S_[:].rearrange("p (g r) n -> p g r n", r=4)
        t4 = tS[:].rearrange("p (g r) n -> p g r n", r=4)
        self.dve(lambda: nc.vector.tensor_tensor(out=tS[:], in0=S_[:], in1=cols[:, 1, :, b:b + 1].to_broadcast([128, 16, 128]), op=ALU.mult),
                 r=[S_, cols], w=[tS])
        self.dve(lambda: nc.vector.tensor_tensor(out=s4, in0=pB[:, 0:512].rearrange("p (g n) -> p g n", g=4).unsqueeze(2).to_broadcast([128, 4, 4, 128]),
                                                 in1=xdc[:, :, b].rearrange("p (g r) -> p g r", r=4).unsqueeze(3).to_broadcast([128, 4, 4, 128]),
                                                 op=ALU.mult), r=[pB, xdc], w=[S_])
        self.dve(lambda: nc.vector.tensor_tensor(out=S_[:], in0=S_[:], in1=tS[:], op=ALU.add), r=[S_, tS], w=[S_])
        self.dve(lambda: nc.vector.tensor_tensor(out=t4, in0=s4, in1=pC[:, 0:512].rearrange("p (g n) -> p g n", g=4).unsqueeze(2).to_broadcast([128, 4, 4, 128]),
                                                 op=ALU.mult), r=[S_, pC], w=[tS])
        self.dve(lambda: nc.vector.reduce_sum(out=ysum[:, :, b], in_=tS[:], axis=AX.X), r=[tS], w=[ysum])
        self.dma(oview[b], S_[:], r=[S_], w=[self.Bout])
    self.dve(lambda: nc.vector.tensor_tensor(out=xdc[:], in0=xsT[:], in1=fc[:, 40:56].unsqueeze(2).to_broadcast([128, 16, n]), op=ALU.mult),
             r=[xsT, fc], w=[xdc])
    self.dve(lambda: nc.vector.tensor_tensor(out=O["yT"][:, :, 0:n], in0=ysum[:], in1=xdc[:], op=ALU.add), r=[ysum, xdc], w=[O["yT"]])
    zw = [None]

    def zfn(j, consume):
        if j % 4 == 0:
            zw[0] = self.ws.get(4096)
        ps = P[2 + j % 2]
        odd_fm_chunk(self, O, zw[0], j % 4, n, ps)
        consume(ps)
    odd_gate_norm_out(self, O, xb, c0, n, zfn)
    self.out_dma(self.o_ssc[e][:, 0:2, :], self.s_sc[e][:, 1:3, :], r=[self.Bin])
    self.out_dma(self.o_ssc[e][:, 2, :], gnew[:], r=[gnew])
    self.end()


Prog.odd_mixer = odd_mixer


def tile_k(w, cb):
    K, N = w.shape
    return np.ascontiguousarray(w.reshape(K // 128, 128, N // cb, cb).transpose(2, 1, 0, 3))


def t5_bucket_np(dist):
    max_exact = 16
    df = np.maximum(dist, max_exact).astype(np.float32)
    large = max_exact + (np.log(df / max_exact) / math.log(128 / max_exact) * (32 - max_exact)).astype(np.int32)
    return np.where(dist < max_exact, dist, np.minimum(large, 31))


def static_masks():
    i = np.arange(128)[:, None]
    j = np.arange(128)[None, :]
    same = (i // 64) == (j // 64)
    m = np.zeros((128, 8, 128), np.float32)
    m[:, 0] = ((i <= j) & same)
    m[:, 1] = np.where((j < i) & same, 0.0, 1e30)
    m[:, 2] = np.where((j >= i) & same, 0.0, -1e30)
    m[:, 3] = ((j > i) & same)
    m[:, 4] = same
    m[:, 5, 0:16] = (np.arange(128)[:, None] // 8) == np.arange(16)[None, :]
    m[:, 6] = (i <= j)
    m[:, 7] = np.where(j >= i, 0.0, -1e30)
    return m.reshape(128, 8 * 128)


def prep_shared(inp):
    sh = {}
    g = np.zeros((128, 13, KC), np.float32)
    for l in range(4):
        g[:, l] = inp["norm_ff1"][l].reshape(KC, 128).T
        g[:, 4 + l] = inp["norm_mix"][l].reshape(KC, 128).T
        g[:, 8 + l] = inp["norm_ff2"][l].reshape(KC, 128).T
    g[:, 12] = inp["norm_final"].reshape(KC, 128).T
    sh["gains"] = g.reshape(128, 13 * KC)
    sh["masks"] = static_masks()
    wgu = np.empty((DEPTH, 2, 11, 128, 2, KC, 256), np.float32)
    wd = np.empty((DEPTH, 2, 8, 128, FC, 128), np.float32)
    for l in range(DEPTH):
        for f, (kg, ku, kd) in enumerate((("ff1_gate", "ff1_up", "ff1_down"), ("ff2_gate", "ff2_up", "ff2_down"))):
            wgu[l, f, :, :, 0] = tile_k(inp[kg][l], 256)
            wgu[l, f, :, :, 1] = tile_k(inp[ku][l], 256)
            wd[l, f] = tile_k(inp[kd][l], 128)
    sh["wgu"] = wgu.reshape(DEPTH, 2, 11, 128, 4096)
    sh["wd"] = wd.reshape(DEPTH, 2, 8, 128, 2816)
    ewin = np.zeros((2, 128, EW_TOT), np.float32)
    ewout = np.zeros((2, 2, 128, 4096), np.float32)
    econv = np.zeros((2, 128, 12, 4), np.float32)
    esm = np.zeros((2, 128, 17), np.float32)
    esk = np.zeros((2, 128, 1), np.float32)
    for e in range(2):
        W = inp["even_w_in"][e]
        qa, ka, va = W[:, 0:512], W[:, 512:640], W[:, 640:768]
        qkvb, zb, bb = W[:, 768:2304], W[:, 2304:2816], W[:, 2816:2824]
        cols = []
        for c in range(4):
            cols.append(np.concatenate([qa[:, c * 64:(c + 1) * 64], qa[:, 256 + c * 64:256 + (c + 1) * 64]], 1))
        cols.append(ka)
        cols.append(qkvb)
        cols.append(zb)
        fm = np.concatenate(cols, 1)
        assert fm.shape[1] == EV_FM * 128
        zero = np.zeros((1024, 64), np.float32)
        tok = np.concatenate([va[:, 0:64], zero, zero, va[:, 64:128], ka, bb], 1)
        assert tok.shape[1] == EV_TOK
        for b in range(5):
            ewin[e, :, EW_OFF[b]:EW_OFF[b] + 4096] = tile_k(fm[:, b * 512:(b + 1) * 512], 512)[0].reshape(128, 4096)
        ewin[e, :, EW_OFF[5]:EW_OFF[5] + 1024] = tile_k(fm[:, 2560:2688], 128)[0].reshape(128, 1024)
        ewin[e, :, EW_OFF[6]:] = tile_k(tok, EV_TOK)[0].reshape(128, 8 * EV_TOK)
        Wo = inp["even_w_out"][e]
        rows = []
        for c in range(4):
            rows.append(Wo[c * 64:(c + 1) * 64])
            rows.append(Wo[256 + c * 64:256 + (c + 1) * 64])
        rows.append(Wo[512:])
        Wp = np.concatenate(rows, 0)
        ewout[e] = tile_k(Wp, 512).reshape(2, 128, 4096)
        econv[e] = inp["gdn_conv_w"][e].reshape(4, 12, 128).transpose(2, 1, 0)
        esm[e, :, 0:4] = inp["gdn_A_log"][e][None, :]
        esm[e, :, 4:8] = inp["gdn_dt_bias"][e][None, :]
        esm[e, :, 8:16] = inp["swa_sinks"][e][None, :]
        esm[e, :, 16] = inp["gdn_norm"][e]
        esk[e, :, 0] = np.tile(inp["swa_sinks"][e], 16)
    sh["ewin"], sh["ewout"] = ewin, ewout
    sh["econv"] = econv.reshape(2, 128, 48)
    sh["esm"], sh["esk"] = esm, esk
    owin = np.zeros((2, 128, OW_TOT), np.float32)
    owout = np.zeros((2, 4, 128, 4096), np.float32)
    oconv = np.zeros((2, 128, 24, 4), np.float32)
    ohead = np.zeros((2, 128, 96), np.float32)
    ofeat = np.zeros((2, 128, 56), np.float32)
    for e in range(2):
        W = inp["ssd_w_in"][e]
        z, xbc, dtw = W[:, 0:2048], W[:, 2048:5120], W[:, 5120:5152]
        owin[e, :, 0:256] = tile_k(dtw, 32)[0].reshape(128, 256)
        fm = np.concatenate([xbc, z], 1)
        for b in range(10):
            owin[e, :, 256 + b * 4096:256 + (b + 1) * 4096] = tile_k(fm[:, b * 512:(b + 1) * 512], 512)[0].reshape(128, 4096)
        Wo = inp["ssd_w_out"][e]
        owout[e] = np.ascontiguousarray(Wo.reshape(16, 128, 4, 256).transpose(2, 1, 0, 3)).reshape(4, 128, 4096)
        oconv[e] = inp["ssd_conv_w"][e].reshape(4, 24, 128).transpose(2, 1, 0)
        ohead[e, :, 0:32] = inp["ssd_dt_bias"][e][None, :]
        ohead[e, :, 32:64] = inp["ssd_A_log"][e][None, :]
        ohead[e, :, 64:96] = inp["ssd_D"][e][None, :]
        ofeat[e, :, 0:24] = inp["ssd_conv_b"][e].reshape(24, 128).T
        ofeat[e, :, 24:40] = inp["ssd_norm"][e].reshape(16, 128).T
        ofeat[e, :, 40:56] = np.repeat(inp["ssd_D"][e].reshape(16, 2), 64, axis=1).T
    sh["owin"], sh["owout"] = owin, owout
    sh["oconv"] = oconv.reshape(2, 128, 96)
    sh["ohead"], sh["ofeat"] = ohead, ofeat
    rb = inp["rel_bias"]
    i = np.arange(128)[:, None]
    j = np.arange(256)[None, :]
    d = 128 + i - j
    valid = (d >= 0) & (d <= 128)
    bk = t5_bucket_np(np.clip(d, 0, 128))
    sb = np.where(valid[:, None, :], rb[bk].transpose(0, 2, 1), np.float32(NEG)).astype(np.float32)
    sh["swabias"] = np.ascontiguousarray(sb).reshape(128, 8 * 256)
    dd = 128 - np.arange(129)
    db = rb[t5_bucket_np(dd)]
    sh["decbias"] = np.ascontiguousarray(np.tile(db.T, (16, 1))).astype(np.float32)
    return sh


def core_inputs(inp, c, seq):
    m = {}
    m["x_p"] = np.ascontiguousarray(inp["x_prompt"][c][:seq])
    sl = slice(c * NS, (c + 1) * NS)
    m["x_s"] = np.ascontiguousarray(inp["x_sample"][sl, 0])
    m["c_k"] = np.ascontiguousarray(inp["cache_swa_k"][:, sl]).reshape(2, NS, 128, 128)
    m["c_v"] = np.ascontiguousarray(inp["cache_swa_v"][:, sl]).reshape(2, NS, 128, 128)
    m["s_gc"] = np.ascontiguousarray(inp["state_gdn_conv"][:, sl])
    m["s_gs"] = np.ascontiguousarray(inp["state_gdn_ssm"][:, sl])
    m["s_sc"] = np.ascontiguousarray(inp["state_ssd_conv"][:, sl])
    m["s_ss"] = np.ascontiguousarray(inp["state_ssd_ssm"][:, sl]).reshape(2, NS, 2048, 128)
    return m


_PROG_CACHE = {}


def get_prog(cfg):
    key = tuple(sorted((k, str(v)) for k, v in cfg.items()))
    if key not in _PROG_CACHE:
        _PROG_CACHE[key] = Prog(dict(cfg))
    return _PROG_CACHE[key]


def kernel(**inp):
    cfg = {"ntiles": 4, "depth": DEPTH}
    prog = get_prog(cfg)
    inp = {k: np.asarray(v) for k, v in inp.items()}
    sh = prep_shared(inp)
    in_maps = []
    for c in range(N_CORES):
        m = dict(sh)
        m.update(core_inputs(inp, c, SEQ))
        in_maps.append(m)
    res = run_bass_kernel_spmd(prog.nc, in_maps, core_ids=list(range(N_CORES)))
    R = res.results
    st1 = lambda k: np.stack([r[k] for r in R], 1)
    ct1 = lambda k: np.concatenate([r[k] for r in R], 1)
    y_p = np.stack([r["y_p"] for r in R], 0)
    y_s = np.concatenate([r["y_s"] for r in R], 0)[:, None, :]
    p_k = st1("o_pk").reshape(2, N_CORES, 128, 2, 64)
    p_v = st1("o_pv").reshape(2, N_CORES, 128, 2, 64)
    p_gc = st1("o_pgc")
    p_gs = st1("o_pgs")
    p_sc = st1("o_psc")
    p_ss = st1("o_pss").reshape(2, N_CORES, 32, 64, 128)
    s_k = ct1("o_sk").reshape(2, N_CORES * NS, 128, 2, 64)
    s_v = ct1("o_sv").reshape(2, N_CORES * NS, 128, 2, 64)
    s_gc = ct1("o_sgc")
    s_gs = ct1("o_sgs")
    s_sc = ct1("o_ssc")
    s_ss = ct1("o_sss").reshape(2, N_CORES * NS, 32, 64, 128)
    return (y_p, y_s, p_k, p_v, p_gc, p_gs, p_sc, p_ss, s_k, s_v, s_gc, s_gs, s_sc, s_ss)
```

```python
import math
import numpy as np
import concourse.bass as bass
import concourse.mybir as mybir
from concourse.bass_utils import run_bass_kernel_spmd

F32 = mybir.dt.float32
BF16 = mybir.dt.bfloat16
ALU = mybir.AluOpType
AF = mybir.ActivationFunctionType
AX = mybir.AxisListType

D = 1024
KC = 8
SEQ = 2048
NS = 16
DFF = 2816
FC = 22
EPS = 1e-6
N_CORES = 8
DEPTH = 4


class Buf:
    __slots__ = ("name", "w", "r", "excl")

    def __init__(self, name, excl=False):
        self.name = name
        self.w = None
        self.r = []
        self.excl = excl


class Sched:
    def __init__(self, nc, n_dma_sems=48):
        self.nc = nc
        self.E = {"pe": nc.tensor, "dve": nc.vector, "act": nc.scalar, "pool": nc.gpsimd, "sp": nc.sync}
        self.sems = {}
        self.cnt = {}
        for e in self.E:
            self.sems[e] = nc.alloc_semaphore("s_" + e)
            self.cnt[e] = 0
        self.dsems = []
        for i in range(n_dma_sems):
            k = "d%d" % i
            self.sems[k] = nc.alloc_semaphore("s_" + k)
            self.cnt[k] = 0
            self.dsems.append(k)
        self.dnext = 0
        self.seen = {e: {} for e in self.E}
        self.n_wait = 0
        self.n_ops = 0

    def _wait(self, eng, tick):
        if tick is None:
            return
        k, v = tick
        if self.seen[eng].get(k, 0) >= v:
            return
        self.E[eng].wait_ge(self.sems[k], v)
        self.seen[eng][k] = v
        self.n_wait += 1

    def _deps(self, eng, reads, writes):
        need = {}
        for b in reads:
            if b.w is not None:
                k, v = b.w
                if need.get(k, 0) < v:
                    need[k] = v
            if b.excl:
                for (k, v) in b.r:
                    if k != eng and need.get(k, 0) < v:
                        need[k] = v
        for b in writes:
            if b.w is not None:
                k, v = b.w
                if need.get(k, 0) < v:
                    need[k] = v
            for (k, v) in b.r:
                if need.get(k, 0) < v:
                    need[k] = v
        for k, v in need.items():
            self._wait(eng, (k, v))

    def _commit(self, tick, reads, writes):
        for b in reads:
            b.r.append(tick)
            if len(b.r) > 16:
                m = {}
                for (k, v) in b.r:
                    if m.get(k, 0) < v:
                        m[k] = v
                b.r = list(m.items())
        for b in writes:
            b.w = tick
            b.r = []

    def op(self, eng, fn, reads=(), writes=()):
        self._deps(eng, reads, writes)
        ins = fn()
        self.cnt[eng] += 1
        ins.then_inc(self.sems[eng], 1)
        tick = (eng, self.cnt[eng])
        self._commit(tick, reads, writes)
        self.n_ops += 1
        return tick

    def new_sem(self, k):
        self.sems[k] = self.nc.alloc_semaphore("s_" + k)
        self.cnt[k] = 0

    def barrier(self):
        for e in ("pe", "dve", "act", "sp"):
            for o in ("pe", "dve", "act", "pool"):
                if o != e and self.cnt[o] > 0:
                    self._wait(e, (o, self.cnt[o]))
            for k in self.dsems:
                if self.cnt[k] > 0:
                    self._wait(e, (k, self.cnt[k]))

    def dma(self, q, out=None, in_=None, reads=(), writes=(), multi=None, sem=None):
        pairs = multi if multi is not None else [(out, in_)]
        if sem is not None:
            k = sem
        else:
            k = self.dsems[self.dnext]
            self.dnext = (self.dnext + 1) % len(self.dsems)
        if self.cnt[k] > 0:
            self._wait(q, (k, self.cnt[k]))
        self._deps(q, reads, writes)
        for (o, i) in pairs:
            self.E[q].dma_start(out=o, in_=i, allow_slow_non_contiguous=True).then_inc(self.sems[k], 16)
            self.cnt[k] += 16
        tick = (k, self.cnt[k])
        self._commit(tick, reads, writes)
        return tick

    def finish(self):
        for e in ("pe", "dve", "act", "pool"):
            if self.cnt[e] > 0:
                self._wait("sp", (e, self.cnt[e]))
        for k in list(self.sems):
            if k not in self.E and self.cnt[k] > 0:
                self._wait("sp", (k, self.cnt[k]))


class Tile:
    def __init__(self, nc, name, shape, dtype, psum=False, handle=None):
        if handle is not None:
            self.t = handle
        elif psum:
            self.t = nc.alloc_psum_tensor(name, list(shape), dtype)
        else:
            self.t = nc.alloc_sbuf_tensor(name, list(shape), dtype)
        self.b = Buf(name, excl=psum)

    def __getitem__(self, k):
        return self.t[k]


SLOT_ELEMS = 4096
N_SLOTS = 3


class WStream:
    def __init__(self, nc, S, plan):
        self.nc, self.S = nc, S
        self.plan = plan
        self.slots = [Tile(nc, "wslot%d" % i, [128, SLOT_ELEMS], BF16) for i in range(N_SLOTS)]
        self.dram_buf = Buf("wdram")
        for i in range(N_SLOTS):
            S.new_sem("w%d" % i)
        self.issued = 0
        self.used = 0

    def _issue(self):
        i = self.issued
        ap, n = self.plan[i]
        sl = self.slots[i % N_SLOTS]
        self.S.dma("pool", out=sl[:, 0:n], in_=ap, reads=[self.dram_buf], writes=[sl.b], sem="w%d" % (i % N_SLOTS))
        self.issued += 1

    def get(self, expect_n):
        i = self.used
        while self.issued <= min(i + N_SLOTS - 2, len(self.plan) - 1):
            self._issue()
        assert self.plan[i][1] == expect_n, (i, self.plan[i][1], expect_n)
        self.used += 1
        return self.slots[i % N_SLOTS]


import contextlib

H_A, KV_A, HD_A = 8, 2, 64
H_B = 4
NEG = -1e30
EV_FM = 21
EV_TOK = 392
EW_OFF = [0, 4096, 8192, 12288, 16384, 20480, 21504]
EW_N = [4096, 4096, 4096, 4096, 4096, 1024, 8 * EV_TOK]
EW_TOT = 21504 + 8 * EV_TOK
OW_TOT = 256 + 10 * 4096


class Prog:
    def __init__(self, cfg):
        self.cfg = cfg
        self.nt = cfg.get("ntiles", 4)
        self.seq = self.nt * 512
        self.ntok = self.seq + NS
        self.depth = cfg.get("depth", DEPTH)
        self.layers = cfg.get("layers", None) or list(range(self.depth))
        nc = bass.Bass("TRN2", target_bir_lowering=False)
        self.nc = nc
        self.S = Sched(nc)
        self.stack = None
        self.declare_io()
        self.alloc()
        self.plan_weights()
        self.emit()

    def din(self, name, shape, dtype=F32):
        return self.nc.dram_tensor(name, list(shape), dtype, kind="ExternalInput").ap()

    def dout(self, name, shape, dtype=F32):
        return self.nc.dram_tensor(name, list(shape), dtype, kind="ExternalOutput").ap()

    def declare_io(self):
        sq = self.seq
        self.x_p = self.din("x_p", [sq, D])
        self.x_s = self.din("x_s", [NS, D])
        self.gains = self.din("gains", [128, 13 * KC])
        self.masks = self.din("masks", [128, 8 * 128])
        self.wgu = self.din("wgu", [DEPTH, 2, 11, 128, 4096])
        self.wd = self.din("wd", [DEPTH, 2, 8, 128, 2816])
        self.ewin = self.din("ewin", [2, 128, EW_TOT])
        self.ewout = self.din("ewout", [2, 2, 128, 4096])
        self.econv = self.din("econv", [2, 128, 48])
        self.esm = self.din("esm", [2, 128, 17])
        self.esk = self.din("esk", [2, 128, 1])
        self.swabias = self.din("swabias", [128, 8 * 256])
        self.decbias = self.din("decbias", [128, 129])
        self.c_k = self.din("c_k", [2, NS, 128, 128])
        self.c_v = self.din("c_v", [2, NS, 128, 128])
        self.s_gc = self.din("s_gc", [2, NS, 3, 1536])
        self.s_gs = self.din("s_gs", [2, NS, 4, 128, 128])
        self.owin = self.din("owin", [2, 128, OW_TOT])
        self.owout = self.din("owout", [2, 4, 128, 4096])
        self.oconv = self.din("oconv", [2, 128, 96])
        self.ohead = self.din("ohead", [2, 128, 96])
        self.ofeat = self.din("ofeat", [2, 128, 56])
        self.s_sc = self.din("s_sc", [2, NS, 3, 3072])
        self.s_ss = self.din("s_ss", [2, NS, 2048, 128])
        self.o_psc = self.dout("o_psc", [2, 3, 3072])
        self.o_pss = self.dout("o_pss", [2, 2048, 128])
        self.o_ssc = self.dout("o_ssc", [2, NS, 3, 3072])
        self.o_sss = self.dout("o_sss", [2, NS, 2048, 128])
        self.y_p = self.dout("y_p", [sq, D])
        self.y_s = self.dout("y_s", [NS, D])
        self.o_pk = self.dout("o_pk", [2, 128, 128])
        self.o_pv = self.dout("o_pv", [2, 128, 128])
        self.o_pgc = self.dout("o_pgc", [2, 3, 1536])
        self.o_pgs = self.dout("o_pgs", [2, 4, 128, 128])
        self.o_sk = self.dout("o_sk", [2, NS, 128, 128])
        self.o_sv = self.dout("o_sv", [2, NS, 128, 128])
        self.o_sgc = self.dout("o_sgc", [2, NS, 3, 1536])
        self.o_sgs = self.dout("o_sgs", [2, NS, 4, 128, 128])
        self.Bin = Buf("dram_in")
        self.Bout = Buf("dram_out")

    def alloc(self):
        nc = self.nc
        self.xT = Tile(nc, "xT", [128, KC, self.ntok], F32)
        self.xTb = [Buf("xT_t%d" % i) for i in range(self.nt)] + [Buf("xT_s")]
        self.ident = Tile(nc, "ident", [128, 128], F32)
        self.ident_bf = Tile(nc, "ident_bf", [128, 128], BF16)
        self.ones_bf = Tile(nc, "ones_bf", [128, 128], BF16)
        self.ones_f = Tile(nc, "ones_f", [128, 128], F32)
        self.gn = Tile(nc, "gn", [128, 13 * KC], F32)
        self.mk = Tile(nc, "mk", [128, 8, 128], F32)
        self.P = [Tile(nc, "ps%d" % i, [128, 512], F32, psum=True) for i in range(8)]

    def begin(self):
        assert self.stack is None
        self.stack = contextlib.ExitStack()
        self.nalloc = 0
        self.deferred = []

    def out_dma(self, dst, src, r=()):
        self.deferred.append((dst, src, list(r)))

    def end(self):
        for (dst, src, r) in self.deferred:
            self.dma(dst, src, r=r, w=[self.Bout])
        self.deferred = []
        self.S.barrier()
        self.stack.close()
        self.stack = None

    def T(self, name, shape, dtype):
        self.nalloc += 1
        h = self.stack.enter_context(self.nc.sbuf_tensor("%s_%d" % (name, self.S.n_ops), list(shape), dtype))
        return Tile(self.nc, name, shape, dtype, handle=h)

    def ffn_blocks(self, l, f):
        out = []
        for b in range(11):
            out.append((self.wgu[l, f, b], 4096))
        for b in range(8):
            out.append((self.wd[l, f, b], 2816))
        return out

    def tiles(self):
        ts = []
        for t in range(self.nt):
            segs = [(t * 512, 512, 0)]
            if t == self.nt - 1:
                segs.append((self.seq, NS, 512))
            ts.append(segs)
        return ts

    def even_blocks(self, e):
        out = []
        for _ in range(self.nt + 1):
            for b in range(7):
                out.append((self.ewin[e, :, EW_OFF[b]:EW_OFF[b] + EW_N[b]], EW_N[b]))
            for b in range(2):
                out.append((self.ewout[e, b], 4096))
        return out

    def odd_blocks(self, e):
        out = []
        for _ in range(self.nt + 1):
            out.append((self.owin[e, :, 0:256], 256))
            for b in range(10):
                out.append((self.owin[e, :, 256 + b * 4096:256 + (b + 1) * 4096], 4096))
            for b in range(4):
                out.append((self.owout[e, b], 4096))
        return out

    def mixer_blocks(self, l):
        if l % 2 == 0:
            return self.even_blocks(l // 2)
        return self.odd_blocks(l // 2)

    def plan_weights(self):
        plan = []
        ffn = self.cfg.get("ffn", True)
        for l in self.layers:
            if ffn:
                for _ in self.tiles():
                    plan += self.ffn_blocks(l, 0)
            if not self.cfg.get("nomix"):
                plan += self.mixer_blocks(l)
            if ffn:
                for _ in self.tiles():
                    plan += self.ffn_blocks(l, 1)
        self.ws = WStream(self.nc, self.S, plan)

    def bl(self, xs):
        return [getattr(x, 'b', x) for x in xs]

    def dve(self, fn, r=(), w=()):
        return self.S.op("dve", fn, self.bl(r), self.bl(w))

    def act(self, fn, r=(), w=()):
        return self.S.op("act", fn, self.bl(r), self.bl(w))

    def pe(self, fn, r=(), w=()):
        return self.S.op("pe", fn, self.bl(r), self.bl(w))

    def pool(self, fn, r=(), w=()):
        return self.S.op("pool", fn, self.bl(r), self.bl(w))

    def dma(self, out, in_, r=(), w=(), q="sp"):
        return self.S.dma(q, out=out, in_=in_, reads=self.bl(r), writes=self.bl(w))

    def gcol(self, idx, kc):
        return self.gn[:, idx * KC + kc: idx * KC + kc + 1]

    def tile_bufs(self, ti):
        b = [self.xTb[ti]]
        if ti == self.nt - 1:
            b.append(self.xTb[self.nt])
        return b

    def col_stats(self, src_fn, nchunk, n, ps, sq, ones, scale, bias_ln, out_ap, exp_bias=0.0, rd=()):
        nc = self.nc
        for c in range(nchunk):
            self.act(lambda c=c: nc.scalar.activation(sq[:, c, 0:n], src_fn(c), AF.Square), r=rd, w=[sq])

        def mm():
            ins = None
            for c in range(nchunk):
                ins = nc.tensor.matmul(ps[:, 0:n], ones[:], sq[:, c, 0:n], start=(c == 0), stop=(c == nchunk - 1))
            return ins
        self.pe(mm, r=[sq, ones], w=[ps])
        return ps

    def rmsnorm_cols(self, xbufs, c0, n, gidx, hn, l0, sq, rstd):
        nc = self.nc
        ps = self.P[7]
        self.act(lambda: nc.scalar.activation(sq[:, :, l0:l0 + n], self.xT[:, :, c0:c0 + n], AF.Square),
                 r=xbufs, w=[sq])

        def mm():
            ins = None
            for kc in range(KC):
                ins = nc.tensor.matmul(ps[:, 0:n], self.ones_bf[:], sq[:, kc, l0:l0 + n],
                                       start=(kc == 0), stop=(kc == KC - 1))
            return ins
        self.pe(mm, r=[sq, self.ones_bf], w=[ps])
        self.act(lambda: nc.scalar.activation(rstd[:, l0:l0 + n], ps[:, 0:n], AF.Ln, bias=EPS, scale=1.0 / D),
                 r=[ps], w=[rstd])
        self.act(lambda: nc.scalar.activation(rstd[:, l0:l0 + n], rstd[:, l0:l0 + n], AF.Exp, scale=-0.5),
                 r=[rstd], w=[rstd])
        for kc in range(KC):
            self.dve(lambda kc=kc: nc.vector.scalar_tensor_tensor(
                out=hn[:, kc, l0:l0 + n], in0=self.xT[:, kc, c0:c0 + n], scalar=self.gcol(gidx, kc),
                in1=rstd[:, l0:l0 + n], op0=ALU.mult, op1=ALU.mult),
                r=xbufs + [rstd, self.gn], w=[hn])

    def ffn_tile(self, l, f, ti, segs, hn, hT, sgs):
        nc = self.nc
        xb = self.tile_bufs(ti)
        it = 0
        for blk in range(11):
            w = self.ws.get(4096)
            wv = w[:, 0:4096].rearrange("p (a k c) -> p a k c", a=2, k=KC)
            for fc in range(2):
                fch = blk * 2 + fc
                for si, (g0, n, l0) in enumerate(segs):
                    if si == 0:
                        pg, pu = self.P[(it % 2)], self.P[2 + (it % 2)]
                    else:
                        pg, pu = self.P[4], self.P[5]

                    def mm(pt, a):
                        ins = None
                        for kc in range(KC):
                            ins = nc.tensor.matmul(pt[:, 0:n], wv[:, a, kc, fc * 128:(fc + 1) * 128],
                                                   hn[:, kc, l0:l0 + n], start=(kc == 0), stop=(kc == KC - 1))
                        return ins
                    self.pe(lambda: mm(pg, 0), r=[w, hn], w=[pg])
                    self.pe(lambda: mm(pu, 1), r=[w, hn], w=[pu])
                    sg = sgs[it % 2]
                    self.act(lambda: nc.scalar.activation(sg[:, 0:n], pg[:, 0:n], AF.Silu), r=[pg], w=[sg])
                    self.dve(lambda: nc.vector.tensor_tensor(out=hT[:, fch, l0:l0 + n], in0=sg[:, 0:n],
                                                             in1=pu[:, 0:n], op=ALU.mult), r=[sg, pu], w=[hT])
                it += 1
        it = 0
        for blk in range(8):
            w = self.ws.get(2816)
            wv = w[:, 0:2816].rearrange("p (k c) -> p k c", k=FC)
            dch = blk
            for si, (g0, n, l0) in enumerate(segs):
                po = self.P[6 + (it % 2)] if si == 0 else self.P[4 + (it % 2)]

                def mm():
                    ins = None
                    for k in range(FC):
                        ins = nc.tensor.matmul(po[:, 0:n], wv[:, k, :], hT[:, k, l0:l0 + n],
                                               start=(k == 0), stop=(k == FC - 1))
                    return ins
                self.pe(mm, r=[w, hT], w=[po])
                xs = self.xT[:, dch, g0:g0 + n]
                self.dve(lambda: nc.vector.scalar_tensor_tensor(out=xs, in0=po[:, 0:n], scalar=0.5, in1=xs,
                                                                op0=ALU.mult, op1=ALU.add), r=[po] + xb, w=xb)
            it += 1

    def ffn(self, l, f):
        self.begin()
        hns = [self.T("hn%d" % i, [128, KC, 528], BF16) for i in range(2)]
        hT = self.T("hT", [128, FC, 528], BF16)
        sq = self.T("sq", [128, KC, 528], BF16)
        rstd = self.T("rstd", [128, 528], F32)
        sgs = [self.T("sg%d" % i, [128, 512], F32) for i in range(2)]
        gidx = l if f == 0 else 8 + l
        for ti, segs in enumerate(self.tiles()):
            hn = hns[ti % 2]
            for (g0, n, l0) in segs:
                self.rmsnorm_cols(self.tile_bufs(ti), g0, n, gidx, hn, l0, sq, rstd)
            self.ffn_tile(l, f, ti, segs, hn, hT, sgs)
        self.end()

    def consts(self):
        nc = self.nc
        self.pool(lambda: nc.gpsimd.memset(self.ident[:], 0.0), w=[self.ident])
        self.pool(lambda: nc.gpsimd.affine_select(self.ident[:], self.ident[:], pattern=[[-1, 128]],
                                                  compare_op=ALU.not_equal, fill=1.0, base=0, channel_multiplier=1),
                  r=[self.ident], w=[self.ident])
        self.pool(lambda: nc.gpsimd.memset(self.ones_bf[:], 1.0), w=[self.ones_bf])
        self.pool(lambda: nc.gpsimd.memset(self.ones_f[:], 1.0), w=[self.ones_f])
        self.dve(lambda: nc.vector.tensor_copy(self.ident_bf[:], self.ident[:]), r=[self.ident], w=[self.ident_bf])
        self.dma(self.gn[:], self.gains, r=[self.Bin], w=[self.gn])
        self.dma(self.mk[:].rearrange("p a b -> p (a b)"), self.masks, r=[self.Bin], w=[self.mk])

    def load_x(self):
        nc = self.nc
        self.begin()
        xins = [self.T("xin%d" % i, [128, D], F32) for i in range(2)]
        ntt = self.seq // 128
        for tt in range(ntt + 1):
            xin = xins[tt % 2]
            if tt < ntt:
                rows, src, c0 = 128, self.x_p[tt * 128:(tt + 1) * 128, :], tt * 128
                xb = self.xTb[tt // 4]
            else:
                rows, src, c0 = NS, self.x_s, self.seq
                xb = self.xTb[self.nt]
            self.dma(xin[0:rows, :], src, r=[self.Bin], w=[xin])
            for half in range(2):
                ps = self.P[(tt % 2) * 2 + half]

                def tr():
                    ins = None
                    for j in range(4):
                        kc = half * 4 + j
                        ins = nc.tensor.transpose(ps[:, j * 128:j * 128 + rows], xin[0:rows, kc * 128:(kc + 1) * 128],
                                                  self.ident[0:rows, 0:rows])
                    return ins
                self.pe(tr, r=[xin, self.ident], w=[ps])
                pv = ps[:, :].rearrange("p (j c) -> p j c", j=4)[:, :, 0:rows]
                dst = self.xT[:, half * 4:half * 4 + 4, c0:c0 + rows]
                if half == 0:
                    self.dve(lambda: nc.vector.tensor_copy(dst, pv), r=[ps], w=[xb])
                else:
                    self.act(lambda: nc.scalar.copy(dst, pv), r=[ps], w=[xb])
        self.end()

    def final(self):
        nc = self.nc
        self.begin()
        sq = self.T("sq", [128, KC, 528], BF16)
        rstd = self.T("rstd", [128, 528], F32)
        yos = [self.T("yo%d" % i, [128, D], F32) for i in range(2)]
        for ti, segs in enumerate(self.tiles()):
            xb = self.tile_bufs(ti)
            for (c0, n, l0) in segs:
                ps = self.P[7]
                self.act(lambda: nc.scalar.activation(sq[:, :, l0:l0 + n], self.xT[:, :, c0:c0 + n], AF.Square),
                         r=xb, w=[sq])

                def mm():
                    ins = None
                    for kc in range(KC):
                        ins = nc.tensor.matmul(ps[:, 0:n], self.ones_bf[:], sq[:, kc, l0:l0 + n],
                                               start=(kc == 0), stop=(kc == KC - 1))
                    return ins
                self.pe(mm, r=[sq, self.ones_bf], w=[ps])
                self.act(lambda: nc.scalar.activation(rstd[:, l0:l0 + n], ps[:, 0:n], AF.Ln, bias=EPS, scale=1.0 / D),
                         r=[ps], w=[rstd])
                self.act(lambda: nc.scalar.activation(rstd[:, l0:l0 + n], rstd[:, l0:l0 + n], AF.Exp, scale=-0.5),
                         r=[rstd], w=[rstd])
                for kc in range(KC):
                    self.dve(lambda kc=kc: nc.vector.scalar_tensor_tensor(
                        out=self.xT[:, kc, c0:c0 + n], in0=self.xT[:, kc, c0:c0 + n], scalar=self.gcol(12, kc),
                        in1=rstd[:, l0:l0 + n], op0=ALU.mult, op1=ALU.mult), r=xb + [rstd, self.gn], w=xb)
        ntt = self.seq // 128
        for tt in range(ntt + 1):
            yo = yos[tt % 2]
            if tt < ntt:
                rows, dst, c0 = 128, self.y_p[tt * 128:(tt + 1) * 128, :], tt * 128
                xb = self.xTb[tt // 4]
            else:
                rows, dst, c0 = NS, self.y_s, self.seq
                xb = self.xTb[self.nt]
            for half in range(2):
                ps = self.P[(tt % 2) * 2 + half]

                def tr():
                    ins = None
                    for j in range(4):
                        kc = half * 4 + j
                        ins = nc.tensor.transpose(ps[0:rows, j * 128:(j + 1) * 128], self.xT[:, kc, c0:c0 + rows],
                                                  self.ident[:])
                    return ins
                self.pe(tr, r=[xb, self.ident], w=[ps])
                if half == 0:
                    self.dve(lambda: nc.vector.tensor_copy(yo[0:rows, 0:512], ps[0:rows, :]), r=[ps], w=[yo])
                else:
                    self.act(lambda: nc.scalar.copy(yo[0:rows, 512:1024], ps[0:rows, :]), r=[ps], w=[yo])
            self.dma(dst, yo[0:rows, :], r=[yo], w=[self.Bout])
        self.end()

    def mixer(self, l):
        if l % 2 == 0:
            self.even_mixer(l)
        else:
            self.odd_mixer(l)

    def emit(self):
        self.consts()
        self.load_x()
        for l in self.layers:
            if self.cfg.get("ffn", True):
                self.ffn(l, 0)
            if not self.cfg.get("nomix"):
                self.mixer(l)
            if self.cfg.get("ffn", True):
                self.ffn(l, 1)
        self.final()
        self.S.finish()


def even_alloc(self, nc_):
    G = {}
    T = self.T
    G["hn"] = T("hn", [128, KC, nc_], BF16)
    G["sqm"] = T("sqm", [128, KC, nc_], BF16)
    G["rstd"] = T("rstd", [128, nc_], F32)
    G["qT"] = T("qT", [128, 4, nc_], BF16)
    G["acc"] = [T("acc%d" % i, [128, nc_], F32) for i in range(2)]
    G["cq"] = T("cq", [128, 12, nc_], F32)
    G["zs"] = T("zs", [128, 4, nc_], BF16)
    oT = Tile(self.nc, "oT", [128, 4, nc_], F32, handle=G["hn"].t.bitcast(F32).reshape([128, 4, nc_]))
    oT.b = G["hn"].b
    G["oT"] = oT
    G["cw"] = T("cw", [128, 12, 4], F32)
    G["sm"] = T("sm", [128, 17], F32)
    G["eA"] = T("eA", [128, 4], F32)
    G["sq4"] = T("sq4", [128, 2, nc_], BF16)
    return G


def even_common(self, G, e):
    nc = self.nc
    self.dma(G["cw"][:].rearrange("p a b -> p (a b)"), self.econv[e], r=[self.Bin], w=[G["cw"]])
    self.dma(G["sm"][:], self.esm[e], r=[self.Bin], w=[G["sm"]])
    self.act(lambda: nc.scalar.activation(G["eA"][:], G["sm"][:, 0:4], AF.Exp), r=[G["sm"]], w=[G["eA"]])
    self.dve(lambda: nc.vector.tensor_scalar_mul(G["eA"][:], G["eA"][:], -1.0), r=[G["eA"]], w=[G["eA"]])


def even_inproj_fm(self, G, n, conv_fn, k_dst):
    nc = self.nc
    hn = G["hn"]
    j = 0
    for blk in range(6):
        nb = EW_N[blk]
        w = self.ws.get(nb)
        ncols = nb // KC
        wv = w[:, 0:nb].rearrange("p (k c) -> p k c", k=KC)
        for cc in range(ncols // 128):
            ps = self.P[j % 2]

            def mm():
                ins = None
                for kc in range(KC):
                    ins = nc.tensor.matmul(ps[:, 0:n], wv[:, kc, cc * 128:(cc + 1) * 128], hn[:, kc, 0:n],
                                           start=(kc == 0), stop=(kc == KC - 1))
                return ins
            self.pe(mm, r=[w, hn], w=[ps])
            if j < 4:
                self.act(lambda: nc.scalar.copy(G["qT"][:, j, 0:n], ps[:, 0:n]), r=[ps], w=[G["qT"]])
            elif j == 4:
                self.act(lambda: nc.scalar.copy(k_dst, ps[:, 0:n]), r=[ps], w=[G["kTa"]])
            elif j < 17:
                conv_fn(j - 5, ps)
            else:
                self.act(lambda: nc.scalar.activation(G["zs"][:, j - 17, 0:n], ps[:, 0:n], AF.Silu),
                         r=[ps], w=[G["zs"]])
            j += 1
    assert j == EV_FM


def gates_from_ba(self, G, ba_ps_ap, rows, beta_dst, g_dst, tmp, rd, wr):
    nc = self.nc
    sm = G["sm"]
    self.act(lambda: nc.scalar.activation(beta_dst, ba_ps_ap[:, 0:4], AF.Sigmoid), r=rd, w=wr)
    self.dve(lambda: nc.vector.tensor_tensor(out=tmp[0:rows, 0:4], in0=ba_ps_ap[:, 4:8], in1=sm[0:rows, 4:8], op=ALU.add),
             r=rd + [sm], w=[tmp])
    self.act(lambda: nc.scalar.activation(tmp[0:rows, 0:4], tmp[0:rows, 0:4], AF.Exp), r=[tmp], w=[tmp])
    self.act(lambda: nc.scalar.activation(tmp[0:rows, 0:4], tmp[0:rows, 0:4], AF.Ln, bias=1.0, scale=1.0), r=[tmp], w=[tmp])
    self.dve(lambda: nc.vector.tensor_tensor(out=g_dst, in0=tmp[0:rows, 0:4], in1=G["eA"][0:rows, :], op=ALU.mult),
             r=[tmp, G["eA"]], w=wr)


def l2norm_heads(self, G, n):
    nc = self.nc
    cq, sq4 = G["cq"], G["sq4"]
    for c0 in range(0, 8, 2):
        for c in (c0, c0 + 1):
            self.act(lambda: nc.scalar.activation(sq4[:, c % 2, 0:n], cq[:, c, 0:n], AF.Square), r=[cq], w=[sq4])
        for c in (c0, c0 + 1):
            ps = self.P[c % 2]
            self.pe(lambda: nc.tensor.matmul(ps[:, 0:n], self.ones_bf[:], sq4[:, c % 2, 0:n], start=True, stop=True),
                    r=[sq4, self.ones_bf], w=[ps])
        for c in (c0, c0 + 1):
            ps = self.P[c % 2]
            rs = G["acc"][c % 2]
            self.act(lambda: nc.scalar.activation(rs[:, 0:n], ps[:, 0:n], AF.Ln, bias=EPS, scale=1.0), r=[ps], w=[rs])
            self.act(lambda: nc.scalar.activation(rs[:, 0:n], rs[:, 0:n], AF.Exp, scale=-0.5,
                                                  bias=(math.log(128.0 ** -0.5) if c < 4 else 0.0)), r=[rs], w=[rs])
        for c in (c0, c0 + 1):
            rs = G["acc"][c % 2]
            self.dve(lambda: nc.vector.tensor_tensor(out=cq[:, c, 0:n], in0=cq[:, c, 0:n], in1=rs[:, 0:n], op=ALU.mult),
                     r=[cq, rs], w=[cq])


def gdn_post(self, G, n, e):
    nc = self.nc
    oT, sq4, mix = G["oT"], G["sq4"], G["sqm"]
    for h0 in range(0, 4, 2):
        for h in (h0, h0 + 1):
            self.act(lambda: nc.scalar.activation(sq4[:, h % 2, 0:n], oT[:, h, 0:n], AF.Square), r=[oT], w=[sq4])
        for h in (h0, h0 + 1):
            ps = self.P[h % 2]
            self.pe(lambda: nc.tensor.matmul(ps[:, 0:n], self.ones_bf[:], sq4[:, h % 2, 0:n], start=True, stop=True),
                    r=[sq4, self.ones_bf], w=[ps])
        for h in (h0, h0 + 1):
            ps = self.P[h % 2]
            rs = G["acc"][h % 2]
            self.act(lambda: nc.scalar.activation(rs[:, 0:n], ps[:, 0:n], AF.Ln, bias=EPS, scale=1.0 / 128.0), r=[ps], w=[rs])
            self.act(lambda: nc.scalar.activation(rs[:, 0:n], rs[:, 0:n], AF.Exp, scale=-0.5), r=[rs], w=[rs])
        for h in (h0, h0 + 1):
            rs = G["acc"][h % 2]
            self.dve(lambda: nc.vector.scalar_tensor_tensor(out=rs[:, 0:n], in0=oT[:, h, 0:n], scalar=G["sm"][:, 16:17],
                                                            in1=rs[:, 0:n], op0=ALU.mult, op1=ALU.mult),
                     r=[oT, rs, G["sm"]], w=[rs])
            self.dve(lambda: nc.vector.tensor_tensor(out=mix[:, 4 + h, 0:n], in0=rs[:, 0:n], in1=G["zs"][:, h, 0:n], op=ALU.mult),
                     r=[rs, G["zs"]], w=[mix])


def even_outproj(self, G, segs_bufs, c0, n):
    nc = self.nc
    mix = G["sqm"]
    for blk in range(2):
        w = self.ws.get(4096)
        wv = w[:, 0:4096].rearrange("p (k c) -> p k c", k=KC)
        for dc in range(4):
            dch = blk * 4 + dc
            ps = self.P[dch % 2]

            def mm():
                ins = None
                for kc in range(KC):
                    ins = nc.tensor.matmul(ps[:, 0:n], wv[:, kc, dc * 128:(dc + 1) * 128], mix[:, kc, 0:n],
                                           start=(kc == 0), stop=(kc == KC - 1))
                return ins
            self.pe(mm, r=[w, mix], w=[ps])
            xs = self.xT[:, dch, c0:c0 + n]
            self.dve(lambda: nc.vector.tensor_tensor(out=xs, in0=xs, in1=ps[:, 0:n], op=ALU.add),
                     r=[ps] + segs_bufs, w=segs_bufs)


def swa_block(self, G, W, qb, ql):
    nc = self.nc
    mix = G["sqm"]
    sm = G["sm"]
    k0 = (qb - 1) * 128 if qb > 0 else 0
    nk = 256 if qb > 0 else 128
    nkt = nk // 128

    def pairgen(c):
        q = c % 2
        s_ps, t_ps, o_ps = self.P[2 + q], self.P[4 + q], self.P[6 + q]
        sb, pn, PT, col = W["sb"][q], W["pn"][q], W["PT"][q], W["col"][q]
        tv = t_ps[:, :].bitcast(BF16)
        for kv in range(2):
            h = kv * 4 + c
            rows = slice(kv * 64, (kv + 1) * 64)
            yield self.pe(lambda: nc.tensor.matmul(s_ps[:, 0:nk], G["qT"][rows, c, ql:ql + 128], G["kTa"][rows, k0:k0 + nk],
                                                   start=True, stop=True), r=[G["qT"], G["kTa"]], w=[s_ps])
            yield self.dve(lambda: nc.vector.scalar_tensor_tensor(out=sb[:, 0:nk], in0=s_ps[:, 0:nk], scalar=0.125,
                                                                  in1=G["bias"][:, h, 256 - nk:256], op0=ALU.mult, op1=ALU.add),
                           r=[s_ps, G["bias"]], w=[sb])
            yield self.dve(lambda: nc.vector.reduce_max(out=col[:, 0:1], in_=sb[:, 0:nk], axis=AX.X), r=[sb], w=[col])
            yield self.dve(lambda: nc.vector.tensor_tensor(out=col[:, 0:1], in0=col[:, 0:1], in1=sm[:, 8 + h:9 + h], op=ALU.max),
                           r=[col, sm], w=[col])
            yield self.dve(lambda: nc.vector.tensor_scalar_mul(col[:, 1:2], col[:, 0:1], -1.0), r=[col], w=[col])
            yield self.act(lambda: nc.scalar.activation(sb[:, 0:nk], sb[:, 0:nk], AF.Exp, bias=col[:, 1:2], scale=1.0),
                           r=[sb, col], w=[sb])
            yield self.dve(lambda: nc.vector.reduce_sum(out=col[:, 2:3], in_=sb[:, 0:nk], axis=AX.X), r=[sb], w=[col])
            yield self.act(lambda: nc.scalar.activation(col[:, 3:4], sm[:, 8 + h:9 + h], AF.Exp, bias=col[:, 1:2], scale=1.0),
                           r=[sm, col], w=[col])
            yield self.dve(lambda: nc.vector.tensor_tensor(out=col[:, 2:3], in0=col[:, 2:3], in1=col[:, 3:4], op=ALU.add),
                           r=[col], w=[col])
            yield self.dve(lambda: nc.vector.reciprocal(col[:, 4:5], col[:, 2:3]), r=[col], w=[col])
            yield self.dve(lambda: nc.vector.tensor_scalar_mul(pn[:, 0:nk], sb[:, 0:nk], col[:, 4:5]), r=[sb, col], w=[pn])

            def tr():
                ins = None
                for kt in range(nkt):
                    ins = nc.tensor.transpose(tv[:, kt * 128:(kt + 1) * 128], pn[:, kt * 128:(kt + 1) * 128],
                                              self.ident_bf[:])
                return ins
            yield self.pe(tr, r=[pn, self.ident_bf], w=[t_ps])
            yield self.act(lambda: nc.scalar.copy(PT[:, 0:nk], tv[:, 0:nk]), r=[t_ps], w=[PT])

            def pv():
                ins = None
                for kt in range(nkt):
                    ins = nc.tensor.matmul(o_ps[:, 0:128], G["Va"][:, k0 // 128 + kt, kv * 128:(kv + 1) * 128],
                                           PT[:, kt * 128:(kt + 1) * 128],
                                           start=(kv == 0 and kt == 0), stop=(kv == 1 and kt == nkt - 1))
                return ins
            yield self.pe(pv, r=[G["Va"], PT], w=[o_ps])
        yield self.act(lambda: nc.scalar.copy(mix[:, c, ql:ql + 128], o_ps[:, 0:128]), r=[o_ps], w=[mix])

    for pr in range(2):
        gens = [pairgen(2 * pr), pairgen(2 * pr + 1)]
        while gens:
            for g in list(gens):
                try:
                    next(g)
                except StopIteration:
                    gens.remove(g)


def gdn_subtile(self, G, W, st, cs):
    nc = self.nc
    cq = G["cq"]
    mk = self.mk
    ident = self.ident
    beta_t, g_t = G["beta_t"], G["g_t"]
    sc = W["sc"]
    R = W["R"]
    P = self.P
    Sst, Sbf, oT = G["Sst"], G["Sbf"], G["oT"]
    self.pe(lambda: nc.tensor.matmul(P[4][:, 0:4], mk[:, 0, :], g_t[:, st, :], start=True, stop=True), r=[mk, g_t], w=[P[4]])
    self.pe(lambda: nc.tensor.matmul(P[4][:, 4:8], mk[:, 4, :], g_t[:, st, :], start=True, stop=True), r=[mk, g_t], w=[P[4]])
    self.dve(lambda: nc.vector.tensor_copy(sc[:, 0:8], P[4][:, 0:8]), r=[P[4]], w=[sc])
    self.dve(lambda: nc.vector.tensor_tensor(out=sc[:, 8:12], in0=sc[:, 4:8], in1=sc[:, 0:4], op=ALU.subtract), r=[sc], w=[sc])
    self.act(lambda: nc.scalar.activation(sc[:, 8:12], sc[:, 8:12], AF.Exp), r=[sc], w=[sc])
    self.act(lambda: nc.scalar.activation(sc[:, 12:16], sc[:, 0:4], AF.Exp), r=[sc], w=[sc])
    self.dve(lambda: nc.vector.tensor_tensor(out=sc[:, 16:20], in0=sc[:, 12:16], in1=beta_t[:, st, :], op=ALU.mult),
             r=[sc, beta_t], w=[sc])
    self.dve(lambda: nc.vector.tensor_tensor(out=R[:, :, 0:128], in0=mk[:, 0, :].unsqueeze(1).to_broadcast([128, 4, 128]),
                                             in1=g_t[:, st, :].unsqueeze(2).to_broadcast([128, 4, 128]), op=ALU.mult),
             r=[mk, g_t], w=[R])
    self.dve(lambda: nc.vector.tensor_tensor(out=R[:, :, 128:256], in0=ident[:].unsqueeze(1).to_broadcast([128, 4, 128]),
                                             in1=beta_t[:, st, :].unsqueeze(2).to_broadcast([128, 4, 128]), op=ALU.mult),
             r=[ident, beta_t], w=[R])
    for hh in range(2):
        self.pe(lambda: nc.tensor.matmul(P[hh][:, 0:512], self.ones_f[:], R[:, 2 * hh:2 * hh + 2, :], start=True, stop=True),
                r=[self.ones_f, R], w=[P[hh]])
    bcS = W["R"]
    self.dve(lambda: nc.vector.tensor_copy(bcS[:, 0:2, :], P[0][:, 0:512].rearrange("p (a b) -> p a b", a=2)), r=[P[0]], w=[bcS])
    self.act(lambda: nc.scalar.copy(bcS[:, 2:4, :], P[1][:, 0:512].rearrange("p (a b) -> p a b", a=2)), r=[P[1]], w=[bcS])

    def gcbc(h):
        return bcS[:, h, 0:128]

    def betabc(h):
        return bcS[:, h, 128:256]
    for h in range(4):
        self.act(lambda: nc.scalar.activation(sc[:, 20 + 2 * h:22 + 2 * h], gcbc(h)[:, 63:128:64], AF.Exp), r=[bcS], w=[sc])

    def head(h):
        q = h % 2
        tw = W["set"][q]
        PA, PN, PT_ = P[2 + q], P[4 + q], P[6 + q]
        kn = cq[:, 4 + h, cs:cs + 128]
        qn = cq[:, h, cs:cs + 128]
        vv = cq[:, 8 + h, cs:cs + 128]
        bc = bcS
        yield self.pe(lambda: nc.tensor.transpose(PA[:, 0:128], kn, ident[:]), r=[cq, ident], w=[PA])
        yield self.pe(lambda: nc.tensor.transpose(PA[:, 128:256], vv, ident[:]), r=[cq, ident], w=[PA])
        yield self.pe(lambda: nc.tensor.matmul(PA[:, 256:384], kn, kn, start=True, stop=True), r=[cq], w=[PA])
        yield self.pe(lambda: nc.tensor.matmul(PA[:, 384:512], kn, qn, start=True, stop=True), r=[cq], w=[PA])
        yield self.dve(lambda: nc.vector.tensor_scalar_mul(tw["kbg"][:], PA[:, 0:128], sc[:, 16 + h:17 + h]), r=[PA, sc], w=[tw["kbg"]])
        yield self.act(lambda: nc.scalar.mul(tw["kd"][:], PA[:, 0:128], sc[:, 8 + h:9 + h]), r=[PA, sc], w=[tw["kd"]])
        yield self.dve(lambda: nc.vector.tensor_scalar_mul(tw["vb"][:], PA[:, 128:256], beta_t[:, st, h:h + 1]),
                       r=[PA, beta_t], w=[tw["vb"]])
        yield self.dve(lambda: nc.vector.scalar_tensor_tensor(out=tw["tA"][:], in0=gcbc(h), scalar=sc[:, h:h + 1], in1=mk[:, 1, :],
                                                              op0=ALU.subtract, op1=ALU.max), r=[bc, sc, mk], w=[tw["tA"]])
        yield self.act(lambda: nc.scalar.activation(tw["Es"][:], tw["tA"][:], AF.Exp, scale=-1.0), r=[tw["tA"]], w=[tw["Es"]])
        yield self.dve(lambda: nc.vector.scalar_tensor_tensor(out=tw["tB"][:], in0=gcbc(h), scalar=sc[:, h:h + 1], in1=mk[:, 2, :],
                                                              op0=ALU.subtract, op1=ALU.min), r=[bc, sc, mk], w=[tw["tB"]])
        yield self.act(lambda: nc.scalar.activation(tw["ETd"][:], tw["tB"][:], AF.Exp), r=[tw["tB"]], w=[tw["ETd"]])
        yield self.dve(lambda: nc.vector.tensor_tensor(out=tw["ETs"][:], in0=tw["ETd"][:], in1=mk[:, 3, :], op=ALU.mult),
                       r=[tw["ETd"], mk], w=[tw["ETs"]])
        Pc, Qc, X, Y = tw["Pm"][0], tw["Qm"][0], tw["X"][0], tw["Y"][0]
        yield self.dve(lambda: nc.vector.scalar_tensor_tensor(out=Qc[:], in0=PA[:, 256:384], scalar=beta_t[:, st, h:h + 1],
                                                              in1=tw["Es"][:], op0=ALU.mult, op1=ALU.mult),
                       r=[PA, beta_t, tw["Es"]], w=[Qc])
        yield self.dve(lambda: nc.vector.tensor_tensor(out=tw["tA"][:], in0=PA[:, 256:384], in1=tw["ETs"][:], op=ALU.mult),
                       r=[PA, tw["ETs"]], w=[tw["tA"]])
        yield self.dve(lambda: nc.vector.tensor_tensor(out=Pc[:], in0=tw["tA"][:], in1=betabc(h), op=ALU.mult),
                       r=[tw["tA"], bc], w=[Pc])
        yield self.dve(lambda: nc.vector.tensor_tensor(out=tw["AqkT"][:], in0=PA[:, 384:512], in1=tw["ETd"][:], op=ALU.mult),
                       r=[PA, tw["ETd"]], w=[tw["AqkT"]])
        yield self.act(lambda: nc.scalar.activation(tw["tB"][:], gcbc(h), AF.Exp), r=[bc], w=[tw["tB"]])
        yield self.dve(lambda: nc.vector.tensor_tensor(out=tw["qgT"][:], in0=qn, in1=tw["tB"][:], op=ALU.mult),
                       r=[cq, tw["tB"]], w=[tw["qgT"]])
        yield self.dve(lambda: nc.vector.tensor_tensor(out=X[:], in0=ident[:], in1=Pc[:], op=ALU.subtract), r=[ident, Pc], w=[X])
        for k in range(1, 6):
            bk = PN
            last = (k == 5)
            Pn, Qn, Xn = tw["Pm"][k % 2], tw["Qm"][k % 2], tw["X"][k % 2]

            def sqr():
                ins = nc.tensor.matmul(bk[:, 128:256], Pc[:], Qc[:], start=True, stop=True)
                if not last:
                    ins = nc.tensor.matmul(bk[:, 0:128], Qc[:], Pc[:], start=True, stop=True)
                return ins
            yield self.pe(sqr, r=[Pc, Qc], w=[bk])
            yield self.dve(lambda: nc.vector.tensor_copy(Qn[:], bk[:, 128:256]), r=[bk], w=[Qn])
            if not last:
                yield self.act(lambda: nc.scalar.copy(Pn[:], bk[:, 0:128]), r=[bk], w=[Pn])
            yield self.pe(lambda: nc.tensor.matmul(bk[:, 256:384], Qn[:], X[:], start=True, stop=True), r=[X, Qn], w=[bk])
            yield self.dve(lambda: nc.vector.tensor_tensor(out=Xn[:], in0=X[:], in1=bk[:, 256:384], op=ALU.add), r=[X, bk], w=[Xn])
            Pc, Qc, X = Pn, Qn, Xn
        yield self.pe(lambda: nc.tensor.matmul(PT_[:, 0:128], X[:], tw["vb"][:], start=True, stop=True), r=[X, tw["vb"]], w=[PT_])
        yield self.pe(lambda: nc.tensor.matmul(PT_[:, 128:256], tw["kbg"][:], X[:], start=True, stop=True), r=[X, tw["kbg"]], w=[PT_])
        yield self.act(lambda: nc.scalar.copy(tw["u"][:], PT_[:, 0:128]), r=[PT_], w=[tw["u"]])
        yield self.dve(lambda: nc.vector.tensor_copy(tw["wTa"][:, 0:64], PT_[:, 128:192]), r=[PT_], w=[tw["wTa"]])
        yield self.dve(lambda: nc.vector.tensor_copy(tw["wTz"][:, 64:128], PT_[:, 192:256]), r=[PT_], w=[tw["wTz"]])
        for ck in range(2):
            rr = slice(ck * 64, (ck + 1) * 64)
            if ck == 0:
                yield self.pe(lambda: nc.tensor.matmul(PT_[0:64, 256:384], tw["wTa"][:, 0:64], Sbf[:, h, :], start=True, stop=True),
                              r=[tw["wTa"], Sbf], w=[PT_])
            else:
                yield self.pe(lambda: nc.tensor.matmul(PT_[:, 256:384], tw["wTz"][:], Sbf[:, h, :], start=True, stop=True),
                              r=[tw["wTz"], Sbf], w=[PT_])
            yield self.dve(lambda: nc.vector.tensor_tensor(out=tw["vnew"][rr, :], in0=tw["u"][rr, :], in1=PT_[rr, 256:384],
                                                           op=ALU.subtract), r=[tw["u"], PT_], w=[tw["vnew"]])

            def omm():
                nc.tensor.matmul(PA[:, ck * 64:(ck + 1) * 64], Sbf[:, h, :], tw["qgT"][:, rr], start=True, stop=False)
                return nc.tensor.matmul(PA[:, ck * 64:(ck + 1) * 64], tw["vnew"][rr, :], tw["AqkT"][rr, rr],
                                        start=False, stop=True)
            yield self.pe(omm, r=[Sbf, tw["qgT"], tw["vnew"], tw["AqkT"]], w=[PA])
            yield self.act(lambda: nc.scalar.copy(oT[:, h, cs + ck * 64:cs + (ck + 1) * 64], PA[:, ck * 64:(ck + 1) * 64]),
                           r=[PA], w=[oT])
            yield self.pe(lambda: nc.tensor.matmul(PT_[:, 384:512], tw["kd"][rr, :], tw["vnew"][rr, :], start=True, stop=True),
                          r=[tw["kd"], tw["vnew"]], w=[PT_])
            yield self.dve(lambda: nc.vector.scalar_tensor_tensor(out=Sst[:, h, :], in0=Sst[:, h, :],
                                                                  scalar=sc[:, 20 + 2 * h + ck:21 + 2 * h + ck],
                                                                  in1=PT_[:, 384:512], op0=ALU.mult, op1=ALU.add),
                           r=[Sst, sc, PT_], w=[Sst])
            yield self.act(lambda: nc.scalar.copy(Sbf[:, h, :], Sst[:, h, :]), r=[Sst], w=[Sbf])

    for pair in range(2):
        gens = [head(2 * pair), head(2 * pair + 1)]
        while gens:
            for g in list(gens):
                try:
                    next(g)
                except StopIteration:
                    gens.remove(g)


def even_mixer(self, l):
    nc = self.nc
    e = l // 2
    nsub = self.seq // 128
    self.begin()
    G = even_alloc(self, 512)
    T = self.T
    G["pre"] = [T("pre%d" % i, [128, 515], F32) for i in range(2)]
    G["kTa"] = T("kTa", [128, self.seq], BF16)
    G["Va"] = T("Va", [128, nsub, 256], BF16)
    G["carry"] = T("carry", [128, 12, 3], F32)
    U = T("U", [128, 2432], F32)
    G["bias"] = Tile(nc, "bias", [128, 8, 256], F32, handle=U.t[:, 0:2048].rearrange("p (a b) -> p a b", a=8))
    G["Sst"] = T("Sst", [128, 4, 128], F32)
    G["Sbf"] = T("Sbf", [128, 4, 128], BF16)
    G["beta_t"] = T("beta_t", [128, 4, 4], F32)
    G["g_t"] = T("g_t", [128, 4, 4], F32)
    gtmp = T("gtmp", [128, 4], F32)
    kvo = T("kvo", [128, 256], F32)
    gco = T("gco", [128, 1536], F32)
    W = {"sb": [T("sb%d" % i, [128, 256], F32) for i in range(2)],
         "pn": [T("pn%d" % i, [128, 256], BF16) for i in range(2)],
         "PT": [T("PT%d" % i, [128, 256], BF16) for i in range(2)],
         "col": [T("col%d" % i, [128, 8], F32) for i in range(2)],
         "sc": T("sc", [128, 40], F32),
         "R": T("R", [128, 4, 256], F32),
         "set": []}
    d = {}
    for nm in ("kbg", "vb", "tA", "Es", "tB", "ETd", "ETs", "u"):
        d[nm] = T("%s0" % nm, [128, 128], F32)
    for nm in ("kd", "AqkT", "qgT", "wTa", "wTz", "vnew"):
        d[nm] = T("%s0" % nm, [128, 128], BF16)
    for nm in ("Pm", "Qm", "X", "Y"):
        d[nm] = [T("%s0_%d" % (nm, j), [128, 128], F32) for j in range(2)]
    W["set"].append(d)
    self.dve(lambda: nc.vector.memset(d["wTz"][:], 0.0), w=[d["wTz"]])
    d2 = {}
    off = [0]

    def uview(name, dt):
        if dt == F32:
            v = U.t[:, off[0]:off[0] + 128]
            off[0] += 128
        else:
            v = U.t[:, off[0]:off[0] + 64].bitcast(BF16)
            off[0] += 64
        return Tile(nc, name, [128, 128], dt, handle=v)
    for nm in ("kbg", "vb", "tA", "Es", "tB", "ETd", "ETs", "u"):
        d2[nm] = uview(nm + "1", F32)
    for nm in ("Pm", "Qm", "X", "Y"):
        d2[nm] = [uview("%s1_%d" % (nm, j), F32) for j in range(2)]
    for nm in ("kd", "AqkT", "qgT", "wTa", "wTz", "vnew"):
        d2[nm] = uview(nm + "1", BF16)
    assert off[0] == 2432
    W["set"].append(d2)
    even_common(self, G, e)
    self.dve(lambda: nc.vector.memset(G["carry"][:], 0.0), w=[G["carry"]])
    self.dve(lambda: nc.vector.memset(G["Sst"][:], 0.0), w=[G["Sst"]])
    self.dve(lambda: nc.vector.memset(G["Sbf"][:], 0.0), w=[G["Sbf"]])
    cw = G["cw"]
    for ti in range(self.nt):
        c0 = ti * 512
        xb = [self.xTb[ti]]
        self.S.barrier()
        self.dma(G["bias"][:].rearrange("p a b -> p (a b)"), self.swabias, r=[self.Bin], w=[G["bias"]])
        self.rmsnorm_cols(xb, c0, 512, 4 + l, G["hn"], 0, G["sqm"], G["rstd"])
        cnt = [0]
        pend_silu = [None]

        def conv_fn(ch, ps):
            pre = G["pre"][cnt[0] % 2]
            acc = G["acc"][cnt[0] % 2]
            cnt[0] += 1
            self.dve(lambda: nc.vector.tensor_copy(pre[:, 0:3], G["carry"][:, ch, :]), r=[G["carry"]], w=[pre])
            self.act(lambda: nc.scalar.copy(pre[:, 3:515], ps[:, 0:512]), r=[ps], w=[pre])
            if pend_silu[0] is not None:
                pend_silu[0]()
                pend_silu[0] = None
            self.dve(lambda: nc.vector.tensor_copy(G["carry"][:, ch, :], pre[:, 512:515]), r=[pre], w=[G["carry"]])
            self.dve(lambda: nc.vector.tensor_scalar_mul(acc[:], pre[:, 0:512], cw[:, ch, 0:1]), r=[pre, cw], w=[acc])
            for i in range(1, 4):
                self.dve(lambda: nc.vector.scalar_tensor_tensor(out=acc[:], in0=pre[:, i:i + 512], scalar=cw[:, ch, i:i + 1],
                                                                 in1=acc[:], op0=ALU.mult, op1=ALU.add), r=[pre, cw, acc], w=[acc])
            def silu_(ch=ch, acc=acc):
                self.act(lambda: nc.scalar.activation(G["cq"][:, ch, :], acc[:], AF.Silu), r=[acc], w=[G["cq"]])
            pend_silu[0] = silu_
        even_inproj_fm(self, G, 512, conv_fn, G["kTa"][:, c0:c0 + 512])
        if pend_silu[0] is not None:
            pend_silu[0]()
            pend_silu[0] = None
        w = self.ws.get(EW_N[6])
        wv = w[:, 0:EW_N[6]].rearrange("p (k c) -> p k c", k=KC)
        for st in range(4):
            gs = ti * 4 + st
            ps = self.P[2 + st % 2]

            def mm():
                ins = None
                for kc in range(KC):
                    ins = nc.tensor.matmul(ps[:, 0:EV_TOK], G["hn"][:, kc, st * 128:(st + 1) * 128], wv[:, kc, :],
                                           start=(kc == 0), stop=(kc == KC - 1))
                return ins
            self.pe(mm, r=[w, G["hn"]], w=[ps])
            self.act(lambda: nc.scalar.copy(G["Va"][:, gs, :], ps[:, 0:256]), r=[ps], w=[G["Va"]])
            gates_from_ba(self, G, ps[:, 384:392], 128, G["beta_t"][:, st, :], G["g_t"][:, st, :], gtmp, [ps],
                          [G["beta_t"], G["g_t"]])
            if gs == nsub - 1 and not self.cfg.get("skip_kvo"):
                self.act(lambda: nc.scalar.copy(kvo[:, 0:128], ps[:, 256:384]), r=[ps], w=[kvo])
                self.act(lambda: nc.scalar.copy(kvo[:, 128:256], ps[:, 0:128]), r=[ps], w=[kvo])
                self.dve(lambda: nc.vector.tensor_tensor(out=kvo[:, 128:256], in0=kvo[:, 128:256], in1=ps[:, 128:256], op=ALU.add),
                         r=[ps, kvo], w=[kvo])
                pass
        if self.cfg.get("skip_swa") or self.cfg.get("skip_gdn") or self.cfg.get("gdn_stop", 6) < 6:
            self.dve(lambda: nc.vector.memset(G["sqm"][:], 0.0), w=[G["sqm"]])
            self.dve(lambda: nc.vector.memset(G["oT"][:], 0.0), w=[G["oT"]])
        if not self.cfg.get("skip_swa"):
            for qi in range(4):
                swa_block(self, G, W, ti * 4 + qi, qi * 128)
        self.S.barrier()
        self.dve(lambda: nc.vector.memset(W["set"][1]["wTz"][:], 0.0), w=[W["set"][1]["wTz"]])
        l2norm_heads(self, G, 512)
        if not self.cfg.get("skip_gdn"):
            for st in range(4):
                gdn_subtile(self, G, W, st, st * 128)
        gdn_post(self, G, 512, e)
        even_outproj(self, G, xb, c0, 512)
    self.out_dma(self.o_pk[e], kvo[:, 0:128], r=[kvo])
    self.out_dma(self.o_pv[e], kvo[:, 128:256], r=[kvo])
    for ch in range(12):
        self.out_dma(self.o_pgc[e][:, ch * 128:(ch + 1) * 128].rearrange("i p -> p i"), G["carry"][:, ch, :], r=[G["carry"]])
    self.out_dma(self.o_pgs[e].rearrange("h k v -> k h v"), G["Sst"][:], r=[G["Sst"]])
    self.end()
    if self.cfg.get("skip_dec"):
        for b in range(7):
            self.ws.get(EW_N[b])
        for b in range(2):
            self.ws.get(4096)
    else:
        even_decode(self, l)


def even_decode(self, l):
    nc = self.nc
    e = l // 2
    n = NS
    c0 = self.seq
    P = self.P
    ident, mk = self.ident, self.mk
    self.begin()
    T = self.T
    G = even_alloc(self, n)
    G["kTa"] = T("kTs", [128, n], BF16)
    xx = T("xx", [128, 12, n, 4], F32)
    hs = T("hs", [48, 1536], F32)
    gnew = T("gnew", [n, 1536], F32)
    knv = T("knv", [n, 256], F32)
    gts = T("gts", [n, 16], F32)
    Kc = T("Kc", [128, n, 128], F32)
    Vc = T("Vc", [128, n, 128], F32)
    Qb = T("Qb", [128, n, 8], F32)
    KTb = [T("KTb%d" % i, [128, 128], F32) for i in range(2)]
    sT = T("sT", [128, 128], F32)
    kTn = T("kTn", [128, n], F32)
    vTn = T("vTn", [128, n], F32)
    sf = T("sf", [128, 132], F32)
    pnf = T("pnf", [128, 132], F32)
    col = T("dcol", [128, 8], F32)
    PTs = T("PTs", [128, 128], F32)
    R2 = T("R2", [128, 128], F32)
    ov = T("ov", [128, n, 8], F32)
    dbias = T("dbias", [128, 129], F32)
    skc = T("skc", [128, 1], F32)
    R3 = T("R3", [n, 2, 4, n], F32)
    bcs = T("bcs", [128, 2, 4, n], F32)
    Sd = T("Sd", [128, n, 4, 128], F32)
    vnT = T("vnT", [128, 4, n], F32)
    t1 = T("t1", [128, 4, n], F32)
    krow = T("krow", [n, 512], F32)
    vrow = T("vrow", [n, 512], F32)
    Km = [T("Km%d" % i, [n, 512], F32) for i in range(2)]
    tS = T("tS", [128, 4, 128], F32)
    mix = G["sqm"]
    even_common(self, G, e)
    self.dma(dbias[:], self.decbias, r=[self.Bin], w=[dbias])
    self.dma(skc[:], self.esk[e], r=[self.Bin], w=[skc])
    self.dma(hs[:], self.s_gc[e].rearrange("b i c -> (b i) c"), r=[self.Bin], w=[hs])
    self.dma(Kc[:], self.c_k[e].rearrange("b w f -> w b f"), r=[self.Bin], w=[Kc])
    self.dma(Vc[:], self.c_v[e].rearrange("b w f -> w b f"), r=[self.Bin], w=[Vc])
    self.dma(Sd[:], self.s_gs[e].rearrange("b h k v -> k b h v"), r=[self.Bin], w=[Sd])
    for ch in range(12):
        ps = P[2 + ch % 2]
        self.pe(lambda: nc.tensor.transpose(ps[:, 0:48], hs[:, ch * 128:(ch + 1) * 128], ident[0:48, 0:48]),
                r=[hs, ident], w=[ps])
        self.dve(lambda: nc.vector.tensor_copy(xx[:, ch, :, 0:3], ps[:, 0:48].rearrange("p (b i) -> p b i", i=3)),
                 r=[ps], w=[xx])
    xb = [self.xTb[self.nt]]
    self.rmsnorm_cols(xb, c0, n, 4 + l, G["hn"], 0, G["sqm"], G["rstd"])
    cw = G["cw"]
    cnt = [0]

    def conv_fn(ch, ps):
        acc = G["acc"][cnt[0] % 2]
        cnt[0] += 1
        self.act(lambda: nc.scalar.copy(xx[:, ch, :, 3], ps[:, 0:n]), r=[ps], w=[xx])
        self.dve(lambda: nc.vector.tensor_scalar_mul(acc[:, 0:n], xx[:, ch, :, 0], cw[:, ch, 0:1]), r=[xx, cw], w=[acc])
        for i in range(1, 4):
            self.dve(lambda: nc.vector.scalar_tensor_tensor(out=acc[:, 0:n], in0=xx[:, ch, :, i], scalar=cw[:, ch, i:i + 1],
                                                            in1=acc[:, 0:n], op0=ALU.mult, op1=ALU.add), r=[xx, cw, acc], w=[acc])
        self.act(lambda: nc.scalar.activation(G["cq"][:, ch, 0:n], acc[:, 0:n], AF.Silu), r=[acc], w=[G["cq"]])
        pt = P[6]
        self.pe(lambda: nc.tensor.transpose(pt[0:n, (ch % 4) * 128:(ch % 4 + 1) * 128], xx[:, ch, :, 3], ident[:]),
                r=[xx, ident], w=[pt])
        self.dve(lambda: nc.vector.tensor_copy(gnew[:, ch * 128:(ch + 1) * 128], pt[0:n, (ch % 4) * 128:(ch % 4 + 1) * 128]),
                 r=[pt], w=[gnew])
    even_inproj_fm(self, G, n, conv_fn, G["kTa"][:, 0:n])
    w = self.ws.get(EW_N[6])
    wv = w[:, 0:EW_N[6]].rearrange("p (k c) -> p k c", k=KC)
    ps = P[2]

    def mm():
        ins = None
        for kc in range(KC):
            ins = nc.tensor.matmul(ps[0:n, 0:EV_TOK], G["hn"][:, kc, 0:n], wv[:, kc, :], start=(kc == 0), stop=(kc == KC - 1))
        return ins
    self.pe(mm, r=[w, G["hn"]], w=[ps])
    self.act(lambda: nc.scalar.copy(knv[:, 0:128], ps[0:n, 256:384]), r=[ps], w=[knv])
    self.act(lambda: nc.scalar.copy(knv[:, 128:256], ps[0:n, 0:128]), r=[ps], w=[knv])
    self.dve(lambda: nc.vector.tensor_tensor(out=knv[:, 128:256], in0=knv[:, 128:256], in1=ps[0:n, 128:256], op=ALU.add),
             r=[ps, knv], w=[knv])
    gates_from_ba(self, G, ps[0:n, 384:392], n, gts[:, 0:4], gts[:, 4:8], gts_tmp(gts), [ps], [gts])
    self.act(lambda: nc.scalar.activation(gts[:, 8:12], gts[:, 4:8], AF.Exp), r=[gts], w=[gts])
    self.dve(lambda: nc.vector.memset(Qb[:], 0.0), w=[Qb])
    for c in range(4):
        self.dve(lambda: nc.vector.tensor_copy(Qb[0:64, :, c], G["qT"][0:64, c, 0:n]), r=[G["qT"]], w=[Qb])
        self.dve(lambda: nc.vector.tensor_copy(Qb[64:128, :, 4 + c], G["qT"][64:128, c, 0:n]), r=[G["qT"]], w=[Qb])
    self.dve(lambda: nc.vector.tensor_copy(kTn[:], G["kTa"][:, 0:n]), r=[G["kTa"]], w=[kTn])
    for b in range(n):
        kp = P[b % 2]
        kt = KTb[b % 2]
        self.pe(lambda: nc.tensor.transpose(kp[:, 0:128], Kc[:, b, :], ident[:]), r=[Kc, ident], w=[kp])
        self.act(lambda: nc.scalar.copy(kt[:], kp[:, 0:128]), r=[kp], w=[kt])
        self.pe(lambda: nc.tensor.matmul(P[4][:, b * 8:(b + 1) * 8], kt[:], Qb[:, b, :], start=True, stop=True),
                r=[kt, Qb], w=[P[4]])
    self.dve(lambda: nc.vector.tensor_copy(sT[:], P[4][:, 0:128]), r=[P[4]], w=[sT])
    self.pe(lambda: nc.tensor.transpose(P[5][:, 0:128], sT[:], ident[:]), r=[sT, ident], w=[P[5]])
    self.pe(lambda: nc.tensor.matmul(P[5][:, 128:128 + n], Qb[:].rearrange("p b h -> p (b h)"), kTn[:], start=True, stop=True),
            r=[Qb, kTn], w=[P[5]])
    self.dve(lambda: nc.vector.tensor_tensor(out=sf[:, 0:n], in0=P[5][:, 128:128 + n], in1=mk[:, 5, 0:n], op=ALU.mult),
             r=[P[5], mk], w=[sf])
    self.dve(lambda: nc.vector.reduce_sum(out=col[:, 5:6], in_=sf[:, 0:n], axis=AX.X), r=[sf], w=[col])
    self.dve(lambda: nc.vector.scalar_tensor_tensor(out=sf[:, 0:128], in0=P[5][:, 0:128], scalar=0.125, in1=dbias[:, 0:128],
                                                    op0=ALU.mult, op1=ALU.add), r=[P[5], dbias], w=[sf])
    self.dve(lambda: nc.vector.scalar_tensor_tensor(out=sf[:, 128:129], in0=col[:, 5:6], scalar=0.125, in1=dbias[:, 128:129],
                                                    op0=ALU.mult, op1=ALU.add), r=[col, dbias], w=[sf])
    self.dve(lambda: nc.vector.reduce_max(out=col[:, 0:1], in_=sf[:, 0:129], axis=AX.X), r=[sf], w=[col])
    self.dve(lambda: nc.vector.tensor_tensor(out=col[:, 0:1], in0=col[:, 0:1], in1=skc[:, 0:1], op=ALU.max), r=[col, skc], w=[col])
    self.dve(lambda: nc.vector.tensor_scalar_mul(col[:, 1:2], col[:, 0:1], -1.0), r=[col], w=[col])
    self.act(lambda: nc.scalar.activation(sf[:, 0:129], sf[:, 0:129], AF.Exp, bias=col[:, 1:2], scale=1.0), r=[sf, col], w=[sf])
    self.dve(lambda: nc.vector.reduce_sum(out=col[:, 2:3], in_=sf[:, 0:129], axis=AX.X), r=[sf], w=[col])
    self.act(lambda: nc.scalar.activation(col[:, 3:4], skc[:, 0:1], AF.Exp, bias=col[:, 1:2], scale=1.0), r=[skc, col], w=[col])
    self.dve(lambda: nc.vector.tensor_tensor(out=col[:, 2:3], in0=col[:, 2:3], in1=col[:, 3:4], op=ALU.add), r=[col], w=[col])
    self.dve(lambda: nc.vector.reciprocal(col[:, 4:5], col[:, 2:3]), r=[col], w=[col])
    self.dve(lambda: nc.vector.tensor_scalar_mul(pnf[:, 0:129], sf[:, 0:129], col[:, 4:5]), r=[sf, col], w=[pnf])
    self.pe(lambda: nc.tensor.transpose(P[4][:, 128:256], pnf[:, 0:128], ident[:]), r=[pnf, ident], w=[P[4]])
    self.act(lambda: nc.scalar.copy(PTs[:], P[4][:, 128:256]), r=[P[4]], w=[PTs])
    for b in range(n):
        self.pe(lambda: nc.tensor.matmul(P[6][:, 256 + b * 8:256 + (b + 1) * 8], Vc[:, b, :], PTs[:, b * 8:(b + 1) * 8],
                                         start=True, stop=True), r=[Vc, PTs], w=[P[6]])
    self.pe(lambda: nc.tensor.transpose(P[7][:, 0:n], knv[:, 128:256], ident[0:n, 0:n]), r=[knv, ident], w=[P[7]])
    self.dve(lambda: nc.vector.tensor_copy(vTn[:], P[7][:, 0:n]), r=[P[7]], w=[vTn])
    self.dve(lambda: nc.vector.tensor_scalar_mul(R2[:], ident[:], pnf[:, 128:129]), r=[ident, pnf], w=[R2])
    self.pe(lambda: nc.tensor.matmul(P[7][:, 128:256], self.ones_f[:], R2[:], start=True, stop=True), r=[self.ones_f, R2], w=[P[7]])
    self.dve(lambda: nc.vector.tensor_tensor(out=ov[:], in0=P[7][:, 128:256].rearrange("p (b h) -> p b h", h=8),
                                             in1=vTn[:].unsqueeze(2).to_broadcast([128, n, 8]), op=ALU.mult),
             r=[P[7], vTn], w=[ov])
    self.dve(lambda: nc.vector.tensor_tensor(out=ov[:], in0=ov[:], in1=P[6][:, 256:384].rearrange("p (b h) -> p b h", h=8), op=ALU.add),
             r=[ov, P[6]], w=[ov])
    for c in range(4):
        self.dve(lambda: nc.vector.tensor_copy(mix[0:64, c, 0:n], ov[0:64, :, c]), r=[ov], w=[mix])
        self.dve(lambda: nc.vector.tensor_copy(mix[64:128, c, 0:n], ov[64:128, :, 4 + c]), r=[ov], w=[mix])
    l2norm_heads(self, G, n)
    cq = G["cq"]
    for t in range(2):
        src = gts[:, 0:4] if t == 0 else gts[:, 8:12]
        self.dve(lambda: nc.vector.tensor_tensor(out=R3[:, t, :, :], in0=src.unsqueeze(2).to_broadcast([n, 4, n]),
                                                 in1=ident[0:n, 0:n].unsqueeze(1).to_broadcast([n, 4, n]), op=ALU.mult),
                 r=[gts, ident], w=[R3])
    self.pe(lambda: nc.tensor.matmul(P[0][:, 0:128], self.ones_f[0:n, :], R3[:].rearrange("p t h b -> p (t h b)"),
                                     start=True, stop=True), r=[self.ones_f, R3], w=[P[0]])
    self.dve(lambda: nc.vector.tensor_copy(bcs[:].rearrange("p t h b -> p (t h b)"), P[0][:, 0:128]), r=[P[0]], w=[bcs])
    for b in range(n):
        for h in range(4):
            self.pe(lambda: nc.tensor.matmul(P[1][:, h * n + b:h * n + b + 1], Sd[:, b, h, :], cq[:, 4 + h, b:b + 1],
                                             start=True, stop=True), r=[Sd, cq], w=[P[1]])
    kSv = P[1][:, 0:4 * n].rearrange("p (h b) -> p h b", h=4)
    self.dve(lambda: nc.vector.tensor_tensor(out=t1[:], in0=kSv, in1=bcs[:, 1, :, :], op=ALU.mult), r=[P[1], bcs], w=[t1])
    self.dve(lambda: nc.vector.tensor_tensor(out=t1[:], in0=cq[:, 8:12, 0:n], in1=t1[:], op=ALU.subtract), r=[cq, t1], w=[t1])
    self.dve(lambda: nc.vector.tensor_tensor(out=vnT[:], in0=t1[:], in1=bcs[:, 0, :, :], op=ALU.mult), r=[t1, bcs], w=[vnT])
    for h in range(4):
        self.pe(lambda: nc.tensor.transpose(P[2][0:n, h * 128:(h + 1) * 128], cq[:, 4 + h, 0:n], ident[:]), r=[cq, ident], w=[P[2]])
        self.pe(lambda: nc.tensor.transpose(P[3][0:n, h * 128:(h + 1) * 128], vnT[:, h, :], ident[:]), r=[vnT, ident], w=[P[3]])
    self.act(lambda: nc.scalar.copy(krow[:], P[2][0:n, :]), r=[P[2]], w=[krow])
    self.dve(lambda: nc.vector.tensor_copy(vrow[:], P[3][0:n, :]), r=[P[3]], w=[vrow])
    for b in range(n):
        km = Km[b % 2]
        pp = P[4 + b % 2]
        self.dve(lambda: nc.vector.tensor_scalar_mul(km[:], krow[:], ident[0:n, b:b + 1]), r=[krow, ident], w=[km])

        def mm4():
            ins = None
            for h in range(4):
                ins = nc.tensor.matmul(pp[:, h * 128:(h + 1) * 128], km[:, h * 128:(h + 1) * 128], vrow[:, h * 128:(h + 1) * 128],
                                       start=True, stop=True)
            return ins
        self.pe(mm4, r=[km, vrow], w=[pp])
        self.dve(lambda: nc.vector.tensor_tensor(out=tS[:], in0=Sd[:, b, :, :],
                                                 in1=bcs[:, 1, :, b:b + 1].to_broadcast([128, 4, 128]), op=ALU.mult),
                 r=[Sd, bcs], w=[tS])
        self.dve(lambda: nc.vector.tensor_tensor(out=Sd[:, b, :, :], in0=tS[:], in1=pp[:, :].rearrange("p (h v) -> p h v", h=4), op=ALU.add),
                 r=[tS, pp], w=[Sd])
    for b in range(n):
        for h in range(4):
            self.pe(lambda: nc.tensor.matmul(P[1][:, 64 + h * n + b:64 + h * n + b + 1], Sd[:, b, h, :], cq[:, h, b:b + 1],
                                             start=True, stop=True), r=[Sd, cq], w=[P[1]])
    self.act(lambda: nc.scalar.copy(G["oT"][:, :, 0:n], P[1][:, 64:64 + 4 * n].rearrange("p (h b) -> p h b", h=4)), r=[P[1]], w=[G["oT"]])
    gdn_post(self, G, n, e)
    even_outproj(self, G, xb, c0, n)
    self.out_dma(self.o_sk[e][:, 0:127, :], self.c_k[e][:, 1:128, :], r=[self.Bin])
    self.out_dma(self.o_sv[e][:, 0:127, :], self.c_v[e][:, 1:128, :], r=[self.Bin])
    self.out_dma(self.o_sk[e][:, 127, :], knv[:, 0:128], r=[knv])
    self.out_dma(self.o_sv[e][:, 127, :], knv[:, 128:256], r=[knv])
    self.out_dma(self.o_sgc[e][:, 0:2, :], self.s_gc[e][:, 1:3, :], r=[self.Bin])
    self.out_dma(self.o_sgc[e][:, 2, :], gnew[:], r=[gnew])
    self.out_dma(self.o_sgs[e].rearrange("b h k v -> k b h v"), Sd[:], r=[Sd])
    self.end()


def gts_tmp(gts):
    class _V:
        b = gts.b

        def __getitem__(self, k):
            rows, cols = k
            return gts.t[rows, 12 + cols.start:12 + cols.stop]
    return _V()


Prog.even_mixer = even_mixer


OW_DT = 256
H_C, P_C, N_C, G_C = 32, 64, 128, 4


def odd_consts(self, O, e):
    nc = self.nc
    self.dma(O["cw"][:].rearrange("p a b -> p (a b)"), self.oconv[e], r=[self.Bin], w=[O["cw"]])
    self.dma(O["hb"][:], self.ohead[e], r=[self.Bin], w=[O["hb"]])
    self.dma(O["fc"][:], self.ofeat[e], r=[self.Bin], w=[O["fc"]])
    self.act(lambda: nc.scalar.activation(O["hb"][:, 32:64], O["hb"][:, 32:64], AF.Exp), r=[O["hb"]], w=[O["hb"]])
    self.dve(lambda: nc.vector.tensor_scalar_mul(O["hb"][:, 32:64], O["hb"][:, 32:64], -1.0), r=[O["hb"]], w=[O["hb"]])


def odd_alloc(self, n, sqm=None):
    T = self.T
    O = {}
    O["hn"] = T("hn", [128, KC, n], BF16)
    O["sqm"] = sqm if sqm is not None else T("sqm", [128, KC, n], BF16)
    O["rstd"] = T("rstd", [128, n], F32)
    O["cw"] = T("ocw", [128, 24, 4], F32)
    O["hb"] = T("ohb", [128, 96], F32)
    O["fc"] = T("ofc", [128, 56], F32)
    O["acc"] = [T("oacc%d" % i, [128, n], F32) for i in range(2)]
    O["yT"] = T("yT", [128, 16, n], BF16)
    O["BT"] = T("BT", [128, 4, n], BF16)
    O["CT"] = T("CT", [128, 4, n], BF16)
    O["xc"] = [T("xc%d" % i, [128, n], BF16) for i in range(1)]
    return O


def odd_dt(self, O, ps_ap, rows, dt_dst, a_dst, rd, wr):
    nc = self.nc
    hb = O["hb"]
    self.dve(lambda: nc.vector.tensor_tensor(out=dt_dst, in0=ps_ap, in1=hb[0:rows, 0:32], op=ALU.add), r=rd + [hb], w=wr)
    self.act(lambda: nc.scalar.activation(dt_dst, dt_dst, AF.Exp), r=wr, w=wr)
    self.act(lambda: nc.scalar.activation(dt_dst, dt_dst, AF.Ln, bias=1.0, scale=1.0), r=wr, w=wr)
    self.dve(lambda: nc.vector.tensor_tensor(out=a_dst, in0=dt_dst, in1=hb[0:rows, 32:64], op=ALU.mult), r=wr + [hb], w=wr)


def odd_gate_norm_out(self, O, xbufs, c0, n, zfn):
    nc = self.nc
    yT, fc = O["yT"], O["fc"]
    sq = O["sqm"]
    for j in range(16):
        def consume(ps, j=j):
            zs = O["acc"][j % 2]
            self.act(lambda: nc.scalar.activation(zs[:, 0:n], ps[:, 0:n], AF.Silu), r=[ps], w=[zs])
            self.dve(lambda: nc.vector.tensor_tensor(out=yT[:, j, 0:n], in0=yT[:, j, 0:n], in1=zs[:, 0:n], op=ALU.mult),
                     r=[yT, zs], w=[yT])
        zfn(j, consume)
    for g in range(4):
        ps = self.P[6 + g % 2]
        for jj in range(4):
            j = g * 4 + jj
            self.act(lambda: nc.scalar.activation(sq[:, jj, 0:n], yT[:, j, 0:n], AF.Square), r=[yT], w=[sq])

        def mm():
            ins = None
            for jj in range(4):
                ins = nc.tensor.matmul(ps[:, 0:n], self.ones_bf[:], sq[:, jj, 0:n], start=(jj == 0), stop=(jj == 3))
            return ins
        self.pe(mm, r=[sq, self.ones_bf], w=[ps])
        rs = O["rstd"]
        self.act(lambda: nc.scalar.activation(rs[:, 0:n], ps[:, 0:n], AF.Ln, bias=EPS, scale=1.0 / 512.0), r=[ps], w=[rs])
        self.act(lambda: nc.scalar.activation(rs[:, 0:n], rs[:, 0:n], AF.Exp, scale=-0.5), r=[rs], w=[rs])
        for jj in range(4):
            j = g * 4 + jj
            self.dve(lambda: nc.vector.scalar_tensor_tensor(out=yT[:, j, 0:n], in0=yT[:, j, 0:n], scalar=fc[:, 24 + j:25 + j],
                                                            in1=rs[:, 0:n], op0=ALU.mult, op1=ALU.mult), r=[yT, fc, rs], w=[yT])
    for blk in range(4):
        w = self.ws.get(4096)
        wv = w[:, 0:4096].rearrange("p (k c) -> p k c", k=16)
        for dc in range(2):
            dch = blk * 2 + dc
            ps = self.P[dch % 2]

            def mm2():
                ins = None
                for k in range(16):
                    ins = nc.tensor.matmul(ps[:, 0:n], wv[:, k, dc * 128:(dc + 1) * 128], yT[:, k, 0:n],
                                           start=(k == 0), stop=(k == 15))
                return ins
            self.pe(mm2, r=[w, yT], w=[ps])
            xs = self.xT[:, dch, c0:c0 + n]
            self.dve(lambda: nc.vector.tensor_tensor(out=xs, in0=xs, in1=ps[:, 0:n], op=ALU.add), r=[ps] + xbufs, w=xbufs)


def odd_fm_chunk(self, O, w, cc, n, ps):
    nc = self.nc
    hn = O["hn"]
    wv = w[:, 0:4096].rearrange("p (k c) -> p k c", k=KC)

    def mm():
        ins = None
        for kc in range(KC):
            ins = nc.tensor.matmul(ps[:, 0:n], wv[:, kc, cc * 128:(cc + 1) * 128], hn[:, kc, 0:n],
                                   start=(kc == 0), stop=(kc == KC - 1))
        return ins
    self.pe(mm, r=[w, hn], w=[ps])


def odd_mixer(self, l):
    nc = self.nc
    e = l // 2
    P = self.P
    ident, mk = self.ident, self.mk
    self.begin()
    T = self.T
    xd = T("xd", [128, 4, 2048], BF16)
    xdts = [Tile(self.nc, "xdt%d" % i, [128, 2048], BF16, handle=xd.t[:, 2 * i, :]) for i in range(2)]
    xdds = [Tile(self.nc, "xdd%d" % i, [128, 2048], BF16, handle=xd.t[:, 2 * i + 1, :]) for i in range(2)]
    sqa = Tile(self.nc, "sqa", [128, KC, 512], BF16, handle=xd.t.reshape([128, 16, 512])[:, 0:KC, :])
    O = odd_alloc(self, 512, sqm=sqa)
    pre = [T("opre%d" % i, [128, 515], F32) for i in range(2)]
    carry = T("ocarry", [128, 24, 3], F32)
    xst = T("xst", [128, 4, 2048], BF16)
    Btok = T("Btok", [128, 4, 512], BF16)
    dtv = T("dtv", [128, 4, 32], F32)
    av = T("av", [128, 4, 32], F32)
    sms = [T("osm%d" % i, [128, 6, 32], F32) for i in range(2)]
    R4 = [T("R4_%d" % i, [128, 4, 128], F32) for i in range(2)]
    Lt = [T("Lt%d" % i, [128, 4, 128], F32) for i in range(2)]
    Wb = [T("Wb%d" % i, [128, 4, 128], BF16) for i in range(2)]
    ytk = T("ytk", [128, 2048], BF16)
    tt = [T("ott%d" % i, [128, 512], F32) for i in range(2)]
    xcs = [Tile(self.nc, "xcs%d" % i, [128, 512], BF16, handle=tt[i].t.bitcast(BF16)[:, 0:512]) for i in range(2)]
    ST = T("ST", [128, 2048], F32)
    STb = T("STb", [128, 2048], BF16)
    sto = Tile(self.nc, "sto", [128, 32, 128], F32, handle=xst.t.bitcast(F32).reshape([128, 32, 128]))
    sto.b = xst.b
    odd_consts(self, O, e)
    self.dve(lambda: nc.vector.memset(carry[:], 0.0), w=[carry])
    self.dve(lambda: nc.vector.memset(ST[:], 0.0), w=[ST])
    self.dve(lambda: nc.vector.memset(STb[:], 0.0), w=[STb])
    cw, fc, hb = O["cw"], O["fc"], O["hb"]
    for ti in range(self.nt):
        c0 = ti * 512
        xb = [self.xTb[ti]]
        self.S.barrier()
        self.rmsnorm_cols(xb, c0, 512, 4 + l, O["hn"], 0, O["sqm"], O["rstd"])
        w = self.ws.get(OW_DT)
        wv = w[:, 0:OW_DT].rearrange("p (k c) -> p k c", k=KC)
        for st in range(4):
            ps = P[2 + st % 2]

            def mm():
                ins = None
                for kc in range(KC):
                    ins = nc.tensor.matmul(ps[:, 0:32], O["hn"][:, kc, st * 128:(st + 1) * 128], wv[:, kc, :],
                                           start=(kc == 0), stop=(kc == KC - 1))
                return ins
            self.pe(mm, r=[w, O["hn"]], w=[ps])
            odd_dt(self, O, ps[:, 0:32], 128, dtv[:, st, :], av[:, st, :], [ps], [dtv, av])
        pending = [None]
        pend_silu = [None]
        for blk in range(6):
            w = self.ws.get(4096)
            for cc in range(4):
                ch = blk * 4 + cc
                ps = P[ch % 2]
                odd_fm_chunk(self, O, w, cc, 512, ps)
                pr = pre[ch % 2]
                acc = O["acc"][ch % 2]
                self.dve(lambda: nc.vector.tensor_copy(pr[:, 0:3], carry[:, ch, :]), r=[carry], w=[pr])
                self.act(lambda: nc.scalar.copy(pr[:, 3:515], ps[:, 0:512]), r=[ps], w=[pr])
                if pend_silu[0] is not None:
                    pend_silu[0]()
                    pend_silu[0] = None
                self.dve(lambda: nc.vector.tensor_copy(carry[:, ch, :], pr[:, 512:515]), r=[pr], w=[carry])
                self.dve(lambda: nc.vector.tensor_scalar_mul(acc[:], pr[:, 0:512], cw[:, ch, 0:1]), r=[pr, cw], w=[acc])
                for i in range(1, 4):
                    self.dve(lambda: nc.vector.scalar_tensor_tensor(out=acc[:], in0=pr[:, i:i + 512], scalar=cw[:, ch, i:i + 1],
                                                                     in1=acc[:], op0=ALU.mult, op1=ALU.add), r=[pr, cw, acc], w=[acc])
                if ch < 20:
                    dst = xcs[ch % 2] if ch < 16 else None
                    tgt = dst[:, 0:512] if ch < 16 else O["BT"][:, ch - 16, 0:512]
                    tb = dst if ch < 16 else O["BT"]
                    def silu_(ch=ch, tgt=tgt, tb=tb, acc=acc):
                        self.act(lambda: nc.scalar.activation(tgt, acc[:], AF.Silu, bias=fc[:, ch:ch + 1], scale=1.0),
                                 r=[acc, fc], w=[tb])
                    pend_silu[0] = silu_

                    def fin(ch=ch, tgt=tgt, tb=tb):
                        tp = P[4 + ch % 2]
                        tv = tp[:, :].bitcast(BF16)

                        def tr():
                            ins = None
                            for st in range(4):
                                ins = nc.tensor.transpose(tv[:, st * 128:(st + 1) * 128], tgt[:, st * 128:(st + 1) * 128], self.ident_bf[:])
                            return ins
                        self.pe(tr, r=[tb, self.ident_bf], w=[tp])
                        src = tv[:, 0:512].rearrange("p (s c) -> p s c", s=4)
                        if ch < 16:
                            self.dve(lambda: nc.vector.tensor_copy(xst[:, :, ch * 128:(ch + 1) * 128], src), r=[tp], w=[xst])
                        else:
                            self.dve(lambda: nc.vector.tensor_copy(Btok[:, :, (ch - 16) * 128:(ch - 15) * 128], src), r=[tp], w=[Btok])
                    if pending[0] is not None:
                        pending[0]()
                    pending[0] = fin
                else:
                    def silu_c(ch=ch, acc=acc):
                        self.act(lambda: nc.scalar.activation(O["CT"][:, ch - 20, 0:512], acc[:], AF.Silu, bias=fc[:, ch:ch + 1], scale=1.0),
                                 r=[acc, fc], w=[O["CT"]])
                    pend_silu[0] = silu_c
                    if pending[0] is not None:
                        pending[0]()
                        pending[0] = None
        if pend_silu[0] is not None:
            pend_silu[0]()
            pend_silu[0] = None
        def prepgen(st):
            sm = sms[st % 2]
            xdt, xdd = xdts[st % 2], xdds[st % 2]
            acs, acl, dte, cd, eac, dtd = [sm[:, i, :] for i in range(6)]
            yield self.pe(lambda: nc.tensor.matmul(P[7][:, 256:288], mk[:, 6, :], av[:, st, :], start=True, stop=True), r=[mk, av], w=[P[7]])
            yield self.pe(lambda: nc.tensor.matmul(P[7][:, 288:320], self.ones_f[:], av[:, st, :], start=True, stop=True),
                          r=[self.ones_f, av], w=[P[7]])
            yield self.dve(lambda: nc.vector.tensor_copy(sm[:, 0:2, :], P[7][:, 256:320].rearrange("p (a b) -> p a b", a=2)), r=[P[7]], w=[sm])
            yield self.dve(lambda: nc.vector.tensor_tensor(out=dte, in0=acl, in1=acs, op=ALU.subtract), r=[sm], w=[sm])
            yield self.act(lambda: nc.scalar.activation(dte, dte, AF.Exp), r=[sm], w=[sm])
            yield self.act(lambda: nc.scalar.activation(cd, acl, AF.Exp), r=[sm], w=[sm])
            yield self.act(lambda: nc.scalar.activation(eac, acs, AF.Exp), r=[sm], w=[sm])
            yield self.dve(lambda: nc.vector.tensor_tensor(out=dtd, in0=dte, in1=dtv[:, st, :], op=ALU.mult), r=[sm, dtv], w=[sm])
            xv = xst[:, st, :].rearrange("p (h q) -> p h q", q=P_C)
            for q4 in range(4):
                hs = slice(q4 * 8, (q4 + 1) * 8)
                yield self.dve(lambda: nc.vector.tensor_tensor(out=xdt[:, q4 * 512:(q4 + 1) * 512].rearrange("p (h q) -> p h q", q=P_C),
                                                               in0=xv[:, hs, :], in1=dtv[:, st, hs].unsqueeze(2).to_broadcast([128, 8, P_C]),
                                                               op=ALU.mult), r=[xst, dtv], w=[xdt])
                yield self.dve(lambda: nc.vector.tensor_tensor(out=xdd[:, q4 * 512:(q4 + 1) * 512].rearrange("p (h q) -> p h q", q=P_C),
                                                               in0=xv[:, hs, :], in1=dtd[:, hs].unsqueeze(2).to_broadcast([128, 8, P_C]),
                                                               op=ALU.mult), r=[xst, sm], w=[xdd])

        for _ in prepgen(0):
            pass
        for st in range(4):
            cs = st * 128
            sm = sms[st % 2]
            xdt, xdd = xdts[st % 2], xdds[st % 2]
            acs, acl, dte, cd, eac, dtd = [sm[:, i, :] for i in range(6)]

            def group(g):
                q = g % 2
                bcB, yoffB, yps, scB = P[q], P[2 + q], P[4 + q], P[6 + q]
                r4, lt, wb = R4[q], Lt[q], Wb[q]
                yield self.pe(lambda: nc.tensor.matmul(scB[:, 0:128], O["BT"][:, g, cs:cs + 128], O["CT"][:, g, cs:cs + 128], start=True, stop=True),
                              r=[O["BT"], O["CT"]], w=[scB])
                for hf in range(2):
                    h0 = g * 8 + hf * 4
                    yield self.dve(lambda: nc.vector.tensor_tensor(out=r4[:], in0=mk[:, 6, :].unsqueeze(1).to_broadcast([128, 4, 128]),
                                                                   in1=av[:, st, h0:h0 + 4].unsqueeze(2).to_broadcast([128, 4, 128]), op=ALU.mult),
                                   r=[mk, av], w=[r4])
                    yield self.pe(lambda: nc.tensor.matmul(bcB[:, 0:512], self.ones_f[:], r4[:].rearrange("p a b -> p (a b)"), start=True, stop=True),
                                  r=[self.ones_f, r4], w=[bcB])
                    yield self.dve(lambda: nc.vector.tensor_tensor(out=lt[:], in0=bcB[:, 0:512].rearrange("p (a b) -> p a b", a=4),
                                                                   in1=acs[:, h0:h0 + 4].unsqueeze(2).to_broadcast([128, 4, 128]),
                                                                   op=ALU.subtract), r=[bcB, sm], w=[lt])
                    yield self.dve(lambda: nc.vector.tensor_tensor(out=lt[:], in0=lt[:], in1=mk[:, 7, :].unsqueeze(1).to_broadcast([128, 4, 128]),
                                                                   op=ALU.min), r=[lt, mk], w=[lt])
                    yield self.act(lambda: nc.scalar.activation(lt[:], lt[:], AF.Exp), r=[lt], w=[lt])
                    yield self.dve(lambda: nc.vector.tensor_tensor(out=wb[:], in0=lt[:], in1=scB[:, 0:128].unsqueeze(1).to_broadcast([128, 4, 128]),
                                                                   op=ALU.mult), r=[lt, scB], w=[wb])

                    def ymm():
                        ins = None
                        for hh in range(4):
                            h = h0 + hh
                            ins = nc.tensor.matmul(yps[:, (hf * 4 + hh) * 64:(hf * 4 + hh + 1) * 64], wb[:, hh, :], xdt[:, h * 64:(h + 1) * 64],
                                                   start=True, stop=True)
                        return ins
                    yield self.pe(ymm, r=[wb, xdt], w=[yps])
                yield self.pe(lambda: nc.tensor.matmul(yoffB[:, 0:512], O["CT"][:, g, cs:cs + 128], STb[:, g * 512:(g + 1) * 512], start=True, stop=True),
                              r=[O["CT"], STb], w=[yoffB])
                t = tt[q]
                gsl = slice(g * 512, (g + 1) * 512)
                yield self.dve(lambda: nc.vector.tensor_tensor(out=t[:].rearrange("p (h q) -> p h q", q=P_C),
                                                               in0=yoffB[:, 0:512].rearrange("p (h q) -> p h q", q=P_C),
                                                               in1=eac[:, g * 8:(g + 1) * 8].unsqueeze(2).to_broadcast([128, 8, P_C]), op=ALU.mult),
                               r=[yoffB, sm], w=[t])
                yield self.dve(lambda: nc.vector.tensor_tensor(out=t[:], in0=t[:], in1=yps[:, 0:512], op=ALU.add), r=[t, yps], w=[t])
                t2 = O["acc"][q]
                yield self.dve(lambda: nc.vector.tensor_tensor(out=t2[:].rearrange("p (h q) -> p h q", q=P_C),
                                                               in0=xst[:, st, gsl].rearrange("p (h q) -> p h q", q=P_C),
                                                               in1=hb[:, 64 + g * 8:64 + (g + 1) * 8].unsqueeze(2).to_broadcast([128, 8, P_C]), op=ALU.mult),
                               r=[xst, hb], w=[t2])
                yield self.dve(lambda: nc.vector.tensor_tensor(out=ytk[:, gsl], in0=t[:], in1=t2[:], op=ALU.add), r=[t, t2], w=[ytk])
                yield self.pe(lambda: nc.tensor.matmul(bcB[:, 0:512], Btok[:, st, g * 128:(g + 1) * 128], xdd[:, gsl], start=True, stop=True),
                              r=[Btok, xdd], w=[bcB])
                yield self.dve(lambda: nc.vector.tensor_tensor(out=ST[:, gsl].rearrange("p (h q) -> p h q", q=P_C),
                                                               in0=ST[:, gsl].rearrange("p (h q) -> p h q", q=P_C),
                                                               in1=cd[:, g * 8:(g + 1) * 8].unsqueeze(2).to_broadcast([128, 8, P_C]), op=ALU.mult),
                               r=[ST, sm], w=[ST])
                yield self.dve(lambda: nc.vector.tensor_tensor(out=ST[:, gsl], in0=ST[:, gsl], in1=bcB[:, 0:512], op=ALU.add), r=[ST, bcB], w=[ST])
                yield self.act(lambda: nc.scalar.copy(STb[:, gsl], ST[:, gsl]), r=[ST], w=[STb])

            for pr in range(2):
                gens = [group(2 * pr), group(2 * pr + 1)]
                if pr == 0 and st + 1 < 4:
                    gens.append(prepgen(st + 1))
                while gens:
                    for gg in list(gens):
                        try:
                            next(gg)
                        except StopIteration:
                            gens.remove(gg)
            for hf in range(2):
                tp = P[hf]
                tv = tp[:, :].bitcast(BF16)

                def tr2():
                    ins = None
                    for j in range(8):
                        ch = hf * 8 + j
                        ins = nc.tensor.transpose(tv[:, j * 128:(j + 1) * 128], ytk[:, ch * 128:(ch + 1) * 128], self.ident_bf[:])
                    return ins
                self.pe(tr2, r=[ytk, self.ident_bf], w=[tp])
                self.act(lambda: nc.scalar.copy(O["yT"][:, hf * 8:(hf + 1) * 8, cs:cs + 128], tv[:, 0:1024].rearrange("p (j c) -> p j c", j=8)),
                         r=[tp], w=[O["yT"]])
        zw = [None]

        def zfn(j, consume):
            if j % 4 == 0:
                zw[0] = self.ws.get(4096)
            ps = P[2 + j % 2]
            odd_fm_chunk(self, O, zw[0], j % 4, 512, ps)
            consume(ps)
        odd_gate_norm_out(self, O, xb, c0, 512, zfn)
    for ch in range(24):
        self.out_dma(self.o_psc[e][:, ch * 128:(ch + 1) * 128].rearrange("i p -> p i"), carry[:, ch, :], r=[carry])
    for c in range(16):
        ps = P[c % 2]
        self.pe(lambda: nc.tensor.transpose(ps[:, 0:128], ST[:, c * 128:(c + 1) * 128], ident[:]), r=[ST, ident], w=[ps])
        self.dve(lambda: nc.vector.tensor_copy(sto[:, c, :], ps[:, 0:128]), r=[ps], w=[sto])
    self.out_dma(self.o_pss[e].rearrange("(c q) n -> q c n", q=128), sto[:, 0:16, :], r=[sto])
    self.end()
    odd_decode(self, l)


def odd_decode(self, l):
    nc = self.nc
    e = l // 2
    n = NS
    c0 = self.seq
    P = self.P
    ident, mk = self.ident, self.mk
    self.begin()
    T = self.T
    O = odd_alloc(self, n)
    xx = T("oxx", [128, 24, n, 4], F32)
    hs = T("ohs", [48, 3072], F32)
    gnew = T("ognew", [n, 3072], F32)
    xsT = T("xsT", [128, 16, n], F32)
    BCs = T("BCs", [128, 8, n], F32)
    dts = T("dts", [n, 64], F32)
    BCt = T("BCt", [n, 2, 512], F32)
    BCm = [T("BCm%d" % i, [n, 2, 512], F32) for i in range(2)]
    R5 = T("R5", [n, 2, 2, 16, n], F32)
    onesH = T("onesH", [n, 2, 128], F32)
    cols = T("ocols", [128, 2, 16, n], F32)
    xdc = T("xdc", [128, 16, n], F32)
    Sb = [T("Sb%d" % i, [128, 16, 128], F32) for i in range(2)]
    tS = T("otS", [128, 16, 128], F32)
    ysum = T("ysum", [128, 16, n], F32)
    odd_consts(self, O, e)
    cw, fc, hb = O["cw"], O["fc"], O["hb"]
    self.dma(hs[:], self.s_sc[e].rearrange("b i c -> (b i) c"), r=[self.Bin], w=[hs])
    self.dve(lambda: nc.vector.memset(onesH[:], 0.0), w=[onesH])
    self.dve(lambda: nc.vector.memset(onesH[:, 0, 0:64], 1.0), r=[onesH], w=[onesH])
    self.dve(lambda: nc.vector.memset(onesH[:, 1, 64:128], 1.0), r=[onesH], w=[onesH])
    for ch in range(24):
        ps = P[2 + ch % 2]
        self.pe(lambda: nc.tensor.transpose(ps[:, 0:48], hs[:, ch * 128:(ch + 1) * 128], ident[0:48, 0:48]), r=[hs, ident], w=[ps])
        self.dve(lambda: nc.vector.tensor_copy(xx[:, ch, :, 0:3], ps[:, 0:48].rearrange("p (b i) -> p b i", i=3)), r=[ps], w=[xx])
    xb = [self.xTb[self.nt]]
    self.rmsnorm_cols(xb, c0, n, 4 + l, O["hn"], 0, O["sqm"], O["rstd"])
    w = self.ws.get(OW_DT)
    wv = w[:, 0:OW_DT].rearrange("p (k c) -> p k c", k=KC)
    ps = P[2]

    def mm():
        ins = None
        for kc in range(KC):
            ins = nc.tensor.matmul(ps[0:n, 0:32], O["hn"][:, kc, 0:n], wv[:, kc, :], start=(kc == 0), stop=(kc == KC - 1))
        return ins
    self.pe(mm, r=[w, O["hn"]], w=[ps])
    odd_dt(self, O, ps[0:n, 0:32], n, dts[:, 0:32], dts[:, 32:64], [ps], [dts])
    self.act(lambda: nc.scalar.activation(dts[:, 32:64], dts[:, 32:64], AF.Exp), r=[dts], w=[dts])
    for blk in range(6):
        w = self.ws.get(4096)
        for cc in range(4):
            ch = blk * 4 + cc
            ps = P[ch % 2]
            odd_fm_chunk(self, O, w, cc, n, ps)
            acc = O["acc"][ch % 2]
            self.act(lambda: nc.scalar.copy(xx[:, ch, :, 3], ps[:, 0:n]), r=[ps], w=[xx])
            self.dve(lambda: nc.vector.tensor_scalar_mul(acc[:, 0:n], xx[:, ch, :, 0], cw[:, ch, 0:1]), r=[xx, cw], w=[acc])
            for i in range(1, 4):
                self.dve(lambda: nc.vector.scalar_tensor_tensor(out=acc[:, 0:n], in0=xx[:, ch, :, i], scalar=cw[:, ch, i:i + 1],
                                                                in1=acc[:, 0:n], op0=ALU.mult, op1=ALU.add), r=[xx, cw, acc], w=[acc])
            if ch < 16:
                dst, db = xsT[:, ch, :], xsT
            else:
                dst, db = BCs[:, ch - 16, :], BCs
            self.act(lambda: nc.scalar.activation(dst, acc[:, 0:n], AF.Silu, bias=fc[:, ch:ch + 1], scale=1.0), r=[acc, fc], w=[db])
            pt = P[6]
            self.pe(lambda: nc.tensor.transpose(pt[0:n, (ch % 4) * 128:(ch % 4 + 1) * 128], xx[:, ch, :, 3], ident[:]), r=[xx, ident], w=[pt])
            self.dve(lambda: nc.vector.tensor_copy(gnew[:, ch * 128:(ch + 1) * 128], pt[0:n, (ch % 4) * 128:(ch % 4 + 1) * 128]), r=[pt], w=[gnew])
    for t in range(2):
        pt = P[4 + t]
        for g in range(4):
            self.pe(lambda: nc.tensor.transpose(pt[0:n, g * 128:(g + 1) * 128], BCs[:, t * 4 + g, :], ident[:]), r=[BCs, ident], w=[pt])
        self.dve(lambda: nc.vector.tensor_copy(BCt[:, t, :], pt[0:n, 0:512]), r=[pt], w=[BCt])
    for t in range(2):
        src = dts[:, t * 32:(t + 1) * 32].rearrange("p (hp h2) -> p h2 hp", h2=2)
        self.dve(lambda: nc.vector.tensor_tensor(out=R5[:, t], in0=src.unsqueeze(3).to_broadcast([n, 2, 16, n]),
                                                 in1=ident[0:n, 0:n].unsqueeze(1).unsqueeze(1).to_broadcast([n, 2, 16, n]), op=ALU.mult),
                 r=[dts, ident], w=[R5])

        def cmm():
            ins = None
            for h2 in range(2):
                ins = nc.tensor.matmul(P[7][:, t * 256:(t + 1) * 256], onesH[:, h2, :], R5[:, t, h2].rearrange("p a b -> p (a b)"),
                                       start=(h2 == 0), stop=(h2 == 1))
            return ins
        self.pe(cmm, r=[onesH, R5], w=[P[7]])
    self.dve(lambda: nc.vector.tensor_copy(cols[:].rearrange("p t a b -> p (t a b)"), P[7][:, 0:512]), r=[P[7]], w=[cols])
    self.dve(lambda: nc.vector.tensor_tensor(out=xdc[:], in0=cols[:, 0], in1=xsT[:], op=ALU.mult), r=[cols, xsT], w=[xdc])
    sview = self.s_ss[e].rearrange("b (c q) n -> b q c n", q=128)
    oview = self.o_sss[e].rearrange("b (c q) n -> b q c n", q=128)
    self.dma(Sb[0][:], sview[0], r=[self.Bin], w=[Sb[0]])
    for b in range(n):
        S_ = Sb[b % 2]
        if b + 1 < n:
            self.dma(Sb[(b + 1) % 2][:], sview[b + 1], r=[self.Bin], w=[Sb[(b + 1) % 2]])
        bm = BCm[b % 2]
        self.dve(lambda: nc.vector.tensor_scalar_mul(bm[:].rearrange("p a b -> p (a b)"), BCt[:].rearrange("p a b -> p (a b)"),
                                                     ident[0:n, b:b + 1]), r=[BCt, ident], w=[bm])
        pB, pC = P[(b % 2) * 2], P[(b % 2) * 2 + 1]
        self.pe(lambda: nc.tensor.matmul(pB[:, 0:512], self.ones_f[0:n, :], bm[:, 0, :], start=True, stop=True), r=[self.ones_f, bm], w=[pB])
        self.pe(lambda: nc.tensor.matmul(pC[:, 0:512], self.ones_f[0:n, :], bm[:, 1, :], start=True, stop=True), r=[self.ones_f, bm], w=[pC])
        s4 = S_[:].rearrange("p (g r) n -> p g r n", r=4)
        t4 = tS[:].rearrange("p (g r) n -> p g r n", r=4)
        self.dve(lambda: nc.vector.tensor_tensor(out=tS[:], in0=S_[:], in1=cols[:, 1, :, b:b + 1].to_broadcast([128, 16, 128]), op=ALU.mult),
                 r=[S_, cols], w=[tS])
        self.dve(lambda: nc.vector.tensor_tensor(out=s4, in0=pB[:, 0:512].rearrange("p (g n) -> p g n", g=4).unsqueeze(2).to_broadcast([128, 4, 4, 128]),
                                                 in1=xdc[:, :, b].rearrange("p (g r) -> p g r", r=4).unsqueeze(3).to_broadcast([128, 4, 4, 128]),
                                                 op=ALU.mult), r=[pB, xdc], w=[S_])
        self.dve(lambda: nc.vector.tensor_tensor(out=S_[:], in0=S_[:], in1=tS[:], op=ALU.add), r=[S_, tS], w=[S_])
        self.dve(lambda: nc.vector.tensor_tensor(out=t4, in0=s4, in1=pC[:, 0:512].rearrange("p (g n) -> p g n", g=4).unsqueeze(2).to_broadcast([128, 4, 4, 128]),
                                                 op=ALU.mult), r=[S_, pC], w=[tS])
        self.dve(lambda: nc.vector.reduce_sum(out=ysum[:, :, b], in_=tS[:], axis=AX.X), r=[tS], w=[ysum])
        self.dma(oview[b], S_[:], r=[S_], w=[self.Bout])
    self.dve(lambda: nc.vector.tensor_tensor(out=xdc[:], in0=xsT[:], in1=fc[:, 40:56].unsqueeze(2).to_broadcast([128, 16, n]), op=ALU.mult),
             r=[xsT, fc], w=[xdc])
    self.dve(lambda: nc.vector.tensor_tensor(out=O["yT"][:, :, 0:n], in0=ysum[:], in1=xdc[:], op=ALU.add), r=[ysum, xdc], w=[O["yT"]])
    zw = [None]

    def zfn(j, consume):
        if j % 4 == 0:
            zw[0] = self.ws.get(4096)
        ps = P[2 + j % 2]
        odd_fm_chunk(self, O, zw[0], j % 4, n, ps)
        consume(ps)
    odd_gate_norm_out(self, O, xb, c0, n, zfn)
    self.out_dma(self.o_ssc[e][:, 0:2, :], self.s_sc[e][:, 1:3, :], r=[self.Bin])
    self.out_dma(self.o_ssc[e][:, 2, :], gnew[:], r=[gnew])
    self.end()


Prog.odd_mixer = odd_mixer


def tile_k(w, cb):
    K, N = w.shape
    return np.ascontiguousarray(w.reshape(K // 128, 128, N // cb, cb).transpose(2, 1, 0, 3))


def t5_bucket_np(dist):
    max_exact = 16
    df = np.maximum(dist, max_exact).astype(np.float32)
    large = max_exact + (np.log(df / max_exact) / math.log(128 / max_exact) * (32 - max_exact)).astype(np.int32)
    return np.where(dist < max_exact, dist, np.minimum(large, 31))


def static_masks():
    i = np.arange(128)[:, None]
    j = np.arange(128)[None, :]
    same = (i // 64) == (j // 64)
    m = np.zeros((128, 8, 128), np.float32)
    m[:, 0] = ((i <= j) & same)
    m[:, 1] = np.where((j < i) & same, 0.0, 1e30)
    m[:, 2] = np.where((j >= i) & same, 0.0, -1e30)
    m[:, 3] = ((j > i) & same)
    m[:, 4] = same
    m[:, 5, 0:16] = (np.arange(128)[:, None] // 8) == np.arange(16)[None, :]
    m[:, 6] = (i <= j)
    m[:, 7] = np.where(j >= i, 0.0, -1e30)
    return m.reshape(128, 8 * 128)


def prep_shared(inp):
    sh = {}
    g = np.zeros((128, 13, KC), np.float32)
    for l in range(4):
        g[:, l] = inp["norm_ff1"][l].reshape(KC, 128).T
        g[:, 4 + l] = inp["norm_mix"][l].reshape(KC, 128).T
        g[:, 8 + l] = inp["norm_ff2"][l].reshape(KC, 128).T
    g[:, 12] = inp["norm_final"].reshape(KC, 128).T
    sh["gains"] = g.reshape(128, 13 * KC)
    sh["masks"] = static_masks()
    wgu = np.empty((DEPTH, 2, 11, 128, 2, KC, 256), np.float32)
    wd = np.empty((DEPTH, 2, 8, 128, FC, 128), np.float32)
    for l in range(DEPTH):
        for f, (kg, ku, kd) in enumerate((("ff1_gate", "ff1_up", "ff1_down"), ("ff2_gate", "ff2_up", "ff2_down"))):
            wgu[l, f, :, :, 0] = tile_k(inp[kg][l], 256)
            wgu[l, f, :, :, 1] = tile_k(inp[ku][l], 256)
            wd[l, f] = tile_k(inp[kd][l], 128)
    sh["wgu"] = wgu.reshape(DEPTH, 2, 11, 128, 4096)
    sh["wd"] = wd.reshape(DEPTH, 2, 8, 128, 2816)
    ewin = np.zeros((2, 128, EW_TOT), np.float32)
    ewout = np.zeros((2, 2, 128, 4096), np.float32)
    econv = np.zeros((2, 128, 12, 4), np.float32)
    esm = np.zeros((2, 128, 17), np.float32)
    esk = np.zeros((2, 128, 1), np.float32)
    for e in range(2):
        W = inp["even_w_in"][e]
        qa, ka, va = W[:, 0:512], W[:, 512:640], W[:, 640:768]
        qkvb, zb, bb = W[:, 768:2304], W[:, 2304:2816], W[:, 2816:2824]
        cols = []
        for c in range(4):
            cols.append(np.concatenate([qa[:, c * 64:(c + 1) * 64], qa[:, 256 + c * 64:256 + (c + 1) * 64]], 1))
        cols.append(ka)
        cols.append(qkvb)
        cols.append(zb)
        fm = np.concatenate(cols, 1)
        assert fm.shape[1] == EV_FM * 128
        zero = np.zeros((1024, 64), np.float32)
        tok = np.concatenate([va[:, 0:64], zero, zero, va[:, 64:128], ka, bb], 1)
        assert tok.shape[1] == EV_TOK
        for b in range(5):
            ewin[e, :, EW_OFF[b]:EW_OFF[b] + 4096] = tile_k(fm[:, b * 512:(b + 1) * 512], 512)[0].reshape(128, 4096)
        ewin[e, :, EW_OFF[5]:EW_OFF[5] + 1024] = tile_k(fm[:, 2560:2688], 128)[0].reshape(128, 1024)
        ewin[e, :, EW_OFF[6]:] = tile_k(tok, EV_TOK)[0].reshape(128, 8 * EV_TOK)
        Wo = inp["even_w_out"][e]
        rows = []
        for c in range(4):
            rows.append(Wo[c * 64:(c + 1) * 64])
            rows.append(Wo[256 + c * 64:256 + (c + 1) * 64])
        rows.append(Wo[512:])
        Wp = np.concatenate(rows, 0)
        ewout[e] = tile_k(Wp, 512).reshape(2, 128, 4096)
        econv[e] = inp["gdn_conv_w"][e].reshape(4, 12, 128).transpose(2, 1, 0)
        esm[e, :, 0:4] = inp["gdn_A_log"][e][None, :]
        esm[e, :, 4:8] = inp["gdn_dt_bias"][e][None, :]
        esm[e, :, 8:16] = inp["swa_sinks"][e][None, :]
        esm[e, :, 16] = inp["gdn_norm"][e]
        esk[e, :, 0] = np.tile(inp["swa_sinks"][e], 16)
    sh["ewin"], sh["ewout"] = ewin, ewout
    sh["econv"] = econv.reshape(2, 128, 48)
    sh["esm"], sh["esk"] = esm, esk
    owin = np.zeros((2, 128, OW_TOT), np.float32)
    owout = np.zeros((2, 4, 128, 4096), np.float32)
    oconv = np.zeros((2, 128, 24, 4), np.float32)
    ohead = np.zeros((2, 128, 96), np.float32)
    ofeat = np.zeros((2, 128, 56), np.float32)
    for e in range(2):
        W = inp["ssd_w_in"][e]
        z, xbc, dtw = W[:, 0:2048], W[:, 2048:5120], W[:, 5120:5152]
        owin[e, :, 0:256] = tile_k(dtw, 32)[0].reshape(128, 256)
        fm = np.concatenate([xbc, z], 1)
        for b in range(10):
            owin[e, :, 256 + b * 4096:256 + (b + 1) * 4096] = tile_k(fm[:, b * 512:(b + 1) * 512], 512)[0].reshape(128, 4096)
        Wo = inp["ssd_w_out"][e]
        owout[e] = np.ascontiguousarray(Wo.reshape(16, 128, 4, 256).transpose(2, 1, 0, 3)).reshape(4, 128, 4096)
        oconv[e] = inp["ssd_conv_w"][e].reshape(4, 24, 128).transpose(2, 1, 0)
        ohead[e, :, 0:32] = inp["ssd_dt_bias"][e][None, :]
        ohead[e, :, 32:64] = inp["ssd_A_log"][e][None, :]
        ohead[e, :, 64:96] = inp["ssd_D"][e][None, :]
        ofeat[e, :, 0:24] = inp["ssd_conv_b"][e].reshape(24, 128).T
        ofeat[e, :, 24:40] = inp["ssd_norm"][e].reshape(16, 128).T
        ofeat[e, :, 40:56] = np.repeat(inp["ssd_D"][e].reshape(16, 2), 64, axis=1).T
    sh["owin"], sh["owout"] = owin, owout
    sh["oconv"] = oconv.reshape(2, 128, 96)
    sh["ohead"], sh["ofeat"] = ohead, ofeat
    rb = inp["rel_bias"]
    i = np.arange(128)[:, None]
    j = np.arange(256)[None, :]
    d = 128 + i - j
    valid = (d >= 0) & (d <= 128)
    bk = t5_bucket_np(np.clip(d, 0, 128))
    sb = np.where(valid[:, None, :], rb[bk].transpose(0, 2, 1), np.float32(NEG)).astype(np.float32)
    sh["swabias"] = np.ascontiguousarray(sb).reshape(128, 8 * 256)
    dd = 128 - np.arange(129)
    db = rb[t5_bucket_np(dd)]
    sh["decbias"] = np.ascontiguousarray(np.tile(db.T, (16, 1))).astype(np.float32)
    return sh


def core_inputs(inp, c, seq):
    m = {}
    m["x_p"] = np.ascontiguousarray(inp["x_prompt"][c][:seq])
    sl = slice(c * NS, (c + 1) * NS)
    m["x_s"] = np.ascontiguousarray(inp["x_sample"][sl, 0])
    m["c_k"] = np.ascontiguousarray(inp["cache_swa_k"][:, sl]).reshape(2, NS, 128, 128)
    m["c_v"] = np.ascontiguousarray(inp["cache_swa_v"][:, sl]).reshape(2, NS, 128, 128)
    m["s_gc"] = np.ascontiguousarray(inp["state_gdn_conv"][:, sl])
    m["s_gs"] = np.ascontiguousarray(inp["state_gdn_ssm"][:, sl])
    m["s_sc"] = np.ascontiguousarray(inp["state_ssd_conv"][:, sl])
    m["s_ss"] = np.ascontiguousarray(inp["state_ssd_ssm"][:, sl]).reshape(2, NS, 2048, 128)
    return m


_PROG_CACHE = {}


def get_prog(cfg):
    key = tuple(sorted((k, str(v)) for k, v in cfg.items()))
    if key not in _PROG_CACHE:
        _PROG_CACHE[key] = Prog(dict(cfg))
    return _PROG_CACHE[key]


def kernel(**inp):
    cfg = {"ntiles": 4, "depth": DEPTH}
    prog = get_prog(cfg)
    inp = {k: np.asarray(v) for k, v in inp.items()}
    sh = prep_shared(inp)
    in_maps = []
    for c in range(N_CORES):
        m = dict(sh)
        m.update(core_inputs(inp, c, SEQ))
        in_maps.append(m)
    res = run_bass_kernel_spmd(prog.nc, in_maps, core_ids=list(range(N_CORES)))
    R = res.results
    st1 = lambda k: np.stack([r[k] for r in R], 1)
    ct1 = lambda k: np.concatenate([r[k] for r in R], 1)
    y_p = np.stack([r["y_p"] for r in R], 0)
    y_s = np.concatenate([r["y_s"] for r in R], 0)[:, None, :]
    p_k = st1("o_pk").reshape(2, N_CORES, 128, 2, 64)
    p_v = st1("o_pv").reshape(2, N_CORES, 128, 2, 64)
    p_gc = st1("o_pgc")
    p_gs = st1("o_pgs")
    p_sc = st1("o_psc")
    p_ss = st1("o_pss").reshape(2, N_CORES, 32, 64, 128)
    s_k = ct1("o_sk").reshape(2, N_CORES * NS, 128, 2, 64)
    s_v = ct1("o_sv").reshape(2, N_CORES * NS, 128, 2, 64)
    s_gc = ct1("o_sgc")
    s_gs = ct1("o_sgs")
    s_sc = ct1("o_ssc")
    s_ss = ct1("o_sss").reshape(2, N_CORES * NS, 32, 64, 128)
    return (y_p, y_s, p_k, p_v, p_gc, p_gs, p_sc, p_ss, s_k, s_v, s_gc, s_gs, s_sc, s_ss)
```
